# Optimizing a Trainium2 kernel written in Bass

```python
import math
import jax, jax.numpy as jnp
from jax import lax
import numpy as np


D_MODEL = 1024
BATCH = 8
SEQ = 8192
DEPTH = 2

D_MIX = D_MODEL
HEAD_DIM = 64
RWKV_WIDTH = 3 * D_MIX // 8
RWKV_HEADS = RWKV_WIDTH // HEAD_DIM
DECAY_LORA = 64
ICLR_LORA = 64
GATE_LORA = 128
ATTN_WIDTH = 3 * D_MIX // 8
ATTN_HEADS = ATTN_WIDTH // HEAD_DIM
DILATED_PATTERNS = ((128, 1), (512, 4), (2048, 16))
S5_WIDTH = D_MIX - RWKV_WIDTH - ATTN_WIDTH
S5_GROUP = 16
S5_GROUPS = S5_WIDTH // S5_GROUP
S5_STATE = 64
RWKV_IN = 3 * RWKV_WIDTH + DECAY_LORA + ICLR_LORA + GATE_LORA
ATTN_IN = 3 * ATTN_WIDTH
IN_WIDTH = RWKV_IN + ATTN_IN + S5_WIDTH
D_FF = ((8 * D_MODEL // 3 + 255) // 256) * 256
FFN_RESIDUAL = 0.5
N_DIR = 2
NORM_EPS = 1e-6
LNX_EPS = 64e-5
L2_EPS = 1e-12
NEG_INF = -1e30

kernel_name = 'hymba_rwkv7_dilated_alibi_s5_macaron'


def _rms_norm(x, g):
    xf = x.astype(jnp.float32)
    y = xf * lax.rsqrt(jnp.mean(xf * xf, axis=-1, keepdims=True) + NORM_EPS)
    return (y * g.astype(jnp.float32)).astype(x.dtype)


def _swiglu(x, w_gate, w_up, w_down):
    return (jax.nn.silu(x @ w_gate) * (x @ w_up)) @ w_down


def _shift_prev(z):
    return jnp.pad(z, ((0, 0), (1, 0), (0, 0)))[:, :-1]


def _shift_next(z):
    return jnp.pad(z, ((0, 0), (0, 1), (0, 0)))[:, 1:]


def _alibi_slopes(n):
    def pow2(m):
        start = 2.0 ** (-8.0 / m)
        return [start ** (i + 1) for i in range(m)]
    if n & (n - 1) == 0:
        return pow2(n)
    c = 2 ** int(math.floor(math.log2(n)))
    return pow2(c) + _alibi_slopes(2 * c)[0::2][:n - c]


def _wkv7_scan(r, w, k, v, kkn, kka, reverse):
    bsz, _, nh, n = r.shape

    def step(state, inp):
        r_t, w_t, k_t, v_t, kkn_t, kka_t = inp
        sa = jnp.einsum('bhvk,bhk->bhv', state, kkn_t)
        state = (state * w_t[:, :, None, :] + sa[..., None] * kka_t[:, :, None, :]
                 + v_t[..., None] * k_t[:, :, None, :])
        return state, jnp.einsum('bhvk,bhk->bhv', state, r_t)

    xs = tuple(jnp.moveaxis(t, 1, 0) for t in (r, w, k, v, kkn, kka))
    state0 = jnp.zeros((bsz, nh, n, n), jnp.float32)
    _, y = lax.scan(step, state0, xs, reverse=reverse)
    return jnp.moveaxis(y, 0, 1)


def _rwkv7_time_mix(z, mu_prev, mu_next, w0, w2, a0, a2, g2, k_k, k_a, r_k, lnx_w, lnx_b):
    dtype = z.dtype
    z = z.astype(jnp.float32)
    bsz, seq, _ = z.shape
    z = z + mu_prev * (_shift_prev(z) - z) + mu_next * (_shift_next(z) - z)
    W = RWKV_WIDTH
    r, k, v, zw, za, zg = jnp.split(
        z, [W, 2 * W, 3 * W, 3 * W + DECAY_LORA, 3 * W + DECAY_LORA + ICLR_LORA], axis=-1)
    heads = lambda t: t.reshape(bsz, seq, RWKV_HEADS, HEAD_DIM)
    kk = heads(k * k_k)
    kk = kk * lax.rsqrt(jnp.sum(kk * kk, axis=-1, keepdims=True) + L2_EPS)
    rh, kh, vh = heads(r), heads(k), heads(v)
    k_a_h = k_a.reshape(RWKV_HEADS, HEAD_DIM)
    tw = jnp.tanh(zw)
    ys, bonus = [], []
    for d in range(N_DIR):
        log_w = -jax.nn.softplus(-(w0[d] + tw @ w2[d])) - 0.5
        decay = heads(jnp.exp(-jnp.exp(log_w)))
        a = heads(jax.nn.sigmoid(a0[d] + za @ a2[d]))
        kd = kh * (1.0 + (a - 1.0) * k_a_h)
        ys.append(_wkv7_scan(rh, decay, kd, vh, -kk, kk * a, reverse=(d == 1)))
        bonus.append(jnp.sum(rh * kd * r_k, axis=-1, keepdims=True) * vh)
    y = ys[0] + ys[1]
    mean = jnp.mean(y, axis=-1, keepdims=True)
    var = jnp.mean(jnp.square(y - mean), axis=-1, keepdims=True)
    y = (y - mean) * lax.rsqrt(var + LNX_EPS)
    y = y.reshape(bsz, seq, W) * lnx_w + lnx_b + (bonus[0] + bonus[1]).reshape(bsz, seq, W)
    g = jax.nn.sigmoid(zg) @ g2
    return (y * g).astype(dtype)


def _dilated_band_attention(q, k, v, slopes, dilation, half):
    bsz, seq, nh, dh = q.shape
    n = seq // dilation
    nb = -(-n // half)
    npad = nb * half
    scale = dh ** -0.5

    def to_sub(t):
        t = t.reshape(bsz, n, dilation, nh, dh).transpose(0, 2, 3, 1, 4)
        return jnp.pad(t, ((0, 0), (0, 0), (0, 0), (0, npad - n), (0, 0)))

    def windows(t):
        tp = jnp.pad(to_sub(t), ((0, 0), (0, 0), (0, 0), (half, half), (0, 0)))
        tp = tp.reshape(bsz, dilation, nh, nb + 2, half, dh)
        return jnp.concatenate([tp[:, :, :, :-2], tp[:, :, :, 1:-1], tp[:, :, :, 2:]], axis=4)

    qb = to_sub(q).reshape(bsz, dilation, nh, nb, half, dh)
    kw, vw = windows(k), windows(v)
    s = jnp.einsum('brhiqc,brhikc->brhiqk', qb, kw).astype(jnp.float32) * scale
    rel = jnp.arange(3 * half)[None, :] - half - jnp.arange(half)[:, None]
    kidx = (jnp.arange(nb)[:, None] - 1) * half + jnp.arange(3 * half)[None, :]
    valid = (jnp.abs(rel) <= half)[None] & ((kidx >= 0) & (kidx < n))[:, None, :]
    bias = -slopes[:, None, None] * (jnp.abs(rel) * dilation).astype(jnp.float32)[None]
    logits = jnp.where(valid, s + bias[None, None, :, None], NEG_INF)
    lse = jax.nn.logsumexp(logits, axis=-1)
    p = jnp.exp(logits - lse[..., None])
    o = jnp.einsum('brhiqk,brhikc->brhiqc', p, vw.astype(jnp.float32))
    o = o.reshape(bsz, dilation, nh, npad, dh)[:, :, :, :n].transpose(0, 3, 1, 2, 4)
    lse = lse.reshape(bsz, dilation, nh, npad)[:, :, :, :n].transpose(0, 3, 1, 2)
    return o.reshape(bsz, seq, nh, dh), lse.reshape(bsz, seq, nh)


def _dilated_attention(q, k, v):
    slopes = jnp.asarray(_alibi_slopes(q.shape[2]), jnp.float32)
    outs, lses = [], []
    for window, dilation in DILATED_PATTERNS:
        o, l = _dilated_band_attention(q, k, v, slopes, dilation, window // (2 * dilation))
        outs.append(o)
        lses.append(l)
    wts = jax.nn.softmax(jnp.stack(lses, axis=0), axis=0)
    out = jnp.sum(wts[..., None] * jnp.stack(outs, axis=0), axis=0)
    return out.astype(q.dtype)


def _ssm_combine(left, right):
    a_l, b_l = left
    a_r, b_r = right
    return a_r * a_l, a_r * b_l + b_r


def _s5_mix(u, a_re, a_im, log_step, b_re, b_im, c_re, c_im, d_skip, glu_w, glu_b):
    dtype = u.dtype
    bsz, seq, _ = u.shape
    f32 = jnp.float32
    uf = u.astype(f32).reshape(bsz, seq, S5_GROUPS, S5_GROUP)
    lam = lax.complex(a_re.astype(f32), a_im.astype(f32))
    step = jnp.exp(log_step.astype(f32))[..., None]
    lam_bar = jnp.exp(lam * step)
    zoh = (lam_bar - 1.0) / lam
    bu = jnp.einsum('bsgc,gpc->bsgp', uf.astype(jnp.complex64),
                    lax.complex(b_re.astype(f32), b_im.astype(f32)))
    y = d_skip.astype(f32).reshape(S5_GROUPS, S5_GROUP) * uf
    for d in range(N_DIR):
        a_el = jnp.broadcast_to(lam_bar[d], (1, seq, S5_GROUPS, S5_STATE))
        _, hs = lax.associative_scan(_ssm_combine, (a_el, zoh[d] * bu), axis=1, reverse=(d == 1))
        c = lax.complex(c_re[d].astype(f32), c_im[d].astype(f32))
        y = y + jnp.einsum('bsgp,gcp->bsgc', hs, c).real
    y = jax.nn.gelu(y.reshape(bsz, seq, S5_WIDTH))
    zg = y @ glu_w.astype(f32) + glu_b.astype(f32)
    out = zg[..., :S5_WIDTH] * jax.nn.sigmoid(zg[..., S5_WIDTH:])
    return out.astype(dtype)


def setup_inputs(seed: int = 0) -> dict:
    key = jax.random.key(seed)
    keys = iter(jax.random.split(key, 48))
    f32 = jnp.float32
    L = DEPTH
    nrm = lambda shape, s: s * jax.random.normal(next(keys), shape, f32)
    gain = lambda shape: 1.0 + nrm(shape, 0.02)
    W, G, P = RWKV_WIDTH, S5_GROUPS, S5_STATE
    inp = {}
    inp['x'] = nrm((BATCH, SEQ, D_MODEL), 1.0)
    inp['ffn1_norm_g'] = gain((L, D_MODEL))
    inp['ffn1_w_gate'] = nrm((L, D_MODEL, D_FF), D_MODEL ** -0.5)
    inp['ffn1_w_up'] = nrm((L, D_MODEL, D_FF), D_MODEL ** -0.5)
    inp['ffn1_w_down'] = nrm((L, D_FF, D_MODEL), D_FF ** -0.5)
    inp['mix_norm_g'] = gain((L, D_MODEL))
    inp['w_in'] = nrm((L, D_MODEL, IN_WIDTH), D_MODEL ** -0.5)
    inp['w_out'] = nrm((L, D_MIX, D_MODEL), D_MIX ** -0.5)
    inp['rwkv_mu_prev'] = jax.random.uniform(next(keys), (L, RWKV_IN), f32, 0.0, 0.5)
    inp['rwkv_mu_next'] = jax.random.uniform(next(keys), (L, RWKV_IN), f32, 0.0, 0.5)
    inp['rwkv_decay_w0'] = jnp.linspace(-6.0, -1.0, W, dtype=f32) + nrm((L, N_DIR, W), 0.1)
    inp['rwkv_decay_w2'] = nrm((L, N_DIR, DECAY_LORA, W), 0.1 * DECAY_LORA ** -0.5)
    inp['rwkv_iclr_a0'] = nrm((L, N_DIR, W), 0.1)
    inp['rwkv_iclr_a2'] = nrm((L, N_DIR, ICLR_LORA, W), 0.5 * ICLR_LORA ** -0.5)
    inp['rwkv_gate_w2'] = nrm((L, GATE_LORA, W), GATE_LORA ** -0.5)
    inp['rwkv_k_k'] = 0.85 + nrm((L, W), 0.02)
    inp['rwkv_k_a'] = gain((L, W))
    inp['rwkv_r_k'] = nrm((L, RWKV_HEADS, HEAD_DIM), 0.1)
    inp['rwkv_lnx_w'] = gain((L, W))
    inp['rwkv_lnx_b'] = nrm((L, W), 0.02)
    inp['s5_a_re'] = -0.5 + nrm((L, N_DIR, G, P), 0.01)
    inp['s5_a_im'] = math.pi * jnp.arange(P, dtype=f32) + nrm((L, N_DIR, G, P), 0.01)
    inp['s5_log_step'] = jax.random.uniform(next(keys), (L, N_DIR, G), f32, math.log(1e-3), math.log(1e-1))
    inp['s5_b_re'] = nrm((L, G, P, S5_GROUP), (2.0 * S5_GROUP) ** -0.5)
    inp['s5_b_im'] = nrm((L, G, P, S5_GROUP), (2.0 * S5_GROUP) ** -0.5)
    inp['s5_c_re'] = nrm((L, N_DIR, G, S5_GROUP, P), (2.0 * P) ** -0.5)
    inp['s5_c_im'] = nrm((L, N_DIR, G, S5_GROUP, P), (2.0 * P) ** -0.5)
    inp['s5_d'] = nrm((L, S5_WIDTH), 1.0)
    inp['s5_glu_w'] = nrm((L, S5_WIDTH, 2 * S5_WIDTH), S5_WIDTH ** -0.5)
    inp['s5_glu_b'] = nrm((L, 2 * S5_WIDTH), 0.02)
    inp['ffn2_norm_g'] = gain((L, D_MODEL))
    inp['ffn2_w_gate'] = nrm((L, D_MODEL, D_FF), D_MODEL ** -0.5)
    inp['ffn2_w_up'] = nrm((L, D_MODEL, D_FF), D_MODEL ** -0.5)
    inp['ffn2_w_down'] = nrm((L, D_FF, D_MODEL), D_FF ** -0.5)
    inp['final_norm_g'] = gain((D_MODEL,))
    return inp


def reference(x, ffn1_norm_g, ffn1_w_gate, ffn1_w_up, ffn1_w_down, mix_norm_g, w_in, w_out,
              rwkv_mu_prev, rwkv_mu_next, rwkv_decay_w0, rwkv_decay_w2, rwkv_iclr_a0, rwkv_iclr_a2,
              rwkv_gate_w2, rwkv_k_k, rwkv_k_a, rwkv_r_k, rwkv_lnx_w, rwkv_lnx_b,
              s5_a_re, s5_a_im, s5_log_step, s5_b_re, s5_b_im, s5_c_re, s5_c_im, s5_d,
              s5_glu_w, s5_glu_b, ffn2_norm_g, ffn2_w_gate, ffn2_w_up, ffn2_w_down, final_norm_g):
    bsz, seq, _ = x.shape
    h = x
    for l in range(DEPTH):
        h = h + FFN_RESIDUAL * _swiglu(_rms_norm(h, ffn1_norm_g[l]), ffn1_w_gate[l], ffn1_w_up[l], ffn1_w_down[l])
        z = _rms_norm(h, mix_norm_g[l]) @ w_in[l]
        z_rwkv, z_attn, z_s5 = jnp.split(z, [RWKV_IN, RWKV_IN + ATTN_IN], axis=-1)
        o_rwkv = _rwkv7_time_mix(z_rwkv, rwkv_mu_prev[l], rwkv_mu_next[l], rwkv_decay_w0[l],
                                 rwkv_decay_w2[l], rwkv_iclr_a0[l], rwkv_iclr_a2[l], rwkv_gate_w2[l],
                                 rwkv_k_k[l], rwkv_k_a[l], rwkv_r_k[l], rwkv_lnx_w[l], rwkv_lnx_b[l])
        q, k, v = (t.reshape(bsz, seq, ATTN_HEADS, HEAD_DIM) for t in jnp.split(z_attn, 3, axis=-1))
        o_attn = _dilated_attention(q, k, v).reshape(bsz, seq, ATTN_WIDTH)
        o_s5 = _s5_mix(z_s5, s5_a_re[l], s5_a_im[l], s5_log_step[l], s5_b_re[l], s5_b_im[l],
                       s5_c_re[l], s5_c_im[l], s5_d[l], s5_glu_w[l], s5_glu_b[l])
        h = h + jnp.concatenate([o_rwkv, o_attn, o_s5], axis=-1) @ w_out[l]
        h = h + FFN_RESIDUAL * _swiglu(_rms_norm(h, ffn2_norm_g[l]), ffn2_w_gate[l], ffn2_w_up[l], ffn2_w_down[l])
    return _rms_norm(h, final_norm_g)
```

```python
import numpy as np
import concourse.bass as bass
import concourse.mybir as mybir
from concourse.bass_utils import run_bass_kernel_spmd

F32 = mybir.dt.float32
BF16 = mybir.dt.bfloat16
I32 = mybir.dt.int32
AF = mybir.ActivationFunctionType
ALU = mybir.AluOpType
AX = mybir.AxisListType

ENGS = ("pe", "act", "dve", "pool", "sp")


class Tok:
    __slots__ = ("w", "r", "name")

    def __init__(self, name=""):
        self.w = None
        self.r = {}
        self.name = name


class Sched:
    def __init__(self, nc, lanes_sp=8, lanes_pool=6, lanes_act=2, same_engine_sync=True):
        self.nc = nc
        self.ops = {e: [] for e in ENGS}
        self.cnt = {}
        self.sems = {}
        self.seen = {e: {} for e in ENGS}
        self.same = same_engine_sync
        self._ctx = []
        for e in ("pe", "act", "dve", "pool"):
            self._mksem(e)
        self.lanes = {"sp": [], "pool": [], "act": []}
        for q, n in (("sp", lanes_sp), ("pool", lanes_pool), ("act", lanes_act)):
            for i in range(n):
                nm = f"ln_{q}{i}"
                self._mksem(nm)
                self.lanes[q].append(nm)
        self.lane_rr = {"sp": 0, "pool": 0, "act": 0}
        self.n_instr = 0

    def _mksem(self, name):
        cm = self.nc.semaphore(name)
        s = cm.__enter__()
        self._ctx.append(cm)
        self.sems[name] = s
        self.cnt[name] = 0

    def _collect(self, eng, reads, writes):
        need = {}

        def add(src, val):
            if src == eng and (eng == "pe" or not self.same or eng == "sp"):
                return
            if need.get(src, 0) < val:
                need[src] = val
        for t in reads:
            if t.w is not None:
                add(*t.w)
        for t in writes:
            if t.w is not None:
                add(*t.w)
            for s, v in t.r.items():
                add(s, v)
        out = []
        seen = self.seen[eng]
        for s, v in need.items():
            if seen.get(s, 0) < v:
                seen[s] = v
                out.append((self.sems[s], v))
        return out

    def op(self, eng, fn, reads=(), writes=()):
        waits = self._collect(eng, reads, writes)
        self.cnt[eng] += 1
        c = self.cnt[eng]
        sem = self.sems[eng]

        def emit(E, waits=waits, fn=fn, sem=sem):
            for s, v in waits:
                E.wait_ge(s, v)
            fn(E).then_inc(sem, 1)
        self.ops[eng].append(emit)
        for t in reads:
            t.r[eng] = c
        for t in writes:
            t.w = (eng, c)
            t.r = {}
        self.n_instr += 1

    def dma(self, q, out, in_, reads=(), writes=(), **kw):
        lanes = self.lanes[q]
        ln = lanes[self.lane_rr[q] % len(lanes)]
        self.lane_rr[q] += 1
        waits = self._collect(q, reads, writes)
        prev = self.cnt[ln]
        if prev and self.seen[q].get(ln, 0) < prev:
            self.seen[q][ln] = prev
            waits.append((self.sems[ln], prev))
        self.cnt[ln] += 16
        c = self.cnt[ln]
        sem = self.sems[ln]

        def emit(E, waits=waits, sem=sem, out=out, in_=in_, kw=kw):
            for s, v in waits:
                E.wait_ge(s, v)
            E.dma_start(out=out, in_=in_, **kw).then_inc(sem, 16)
        self.ops[q].append(emit)
        for t in reads:
            t.r[ln] = c
        for t in writes:
            t.w = (ln, c)
            t.r = {}
        self.n_instr += 1

    def finish(self, final_toks):
        nc = self.nc
        fin = []
        need = {}
        for t in final_toks:
            if t.w is not None and need.get(t.w[0], 0) < t.w[1]:
                need[t.w[0]] = t.w[1]
        for s, v in self.cnt.items():
            if v and need.get(s, 0) < v:
                need[s] = v
        for s, v in need.items():
            fin.append((self.sems[s], v))
        ops = self.ops
        with nc.Block() as block:
            @block.tensor
            def _(E):
                for f in ops["pe"]:
                    f(E)

            @block.scalar
            def _(E):
                for f in ops["act"]:
                    f(E)

            @block.vector
            def _(E):
                for f in ops["dve"]:
                    f(E)

            @block.gpsimd
            def _(E):
                for f in ops["pool"]:
                    f(E)

            @block.sync
            def _(E):
                for f in ops["sp"]:
                    f(E)
                for s, v in fin:
                    E.wait_ge(s, v)
        for cm in reversed(self._ctx):
            cm.__exit__(None, None, None)


class Alloc:
    def __init__(self, nc):
        self.nc = nc
        self._ctx = []

    def sb(self, name, shape, dt):
        cm = self.nc.sbuf_tensor(name, list(shape), dt)
        t = cm.__enter__()
        self._ctx.append(cm)
        return t

    def ps(self, name, shape, dt):
        cm = self.nc.psum_tensor(name, list(shape), dt)
        t = cm.__enter__()
        self._ctx.append(cm)
        return t

    def close(self):
        for cm in reversed(self._ctx):
            cm.__exit__(None, None, None)


class SchedI(Sched):
    def __init__(self, nc, **kw):
        super().__init__(nc, **kw)
        self.E = {"pe": nc.tensor, "act": nc.scalar, "dve": nc.vector, "pool": nc.gpsimd, "sp": nc.sync}

    limit = 10 ** 9

    def op(self, eng, fn, reads=(), writes=()):
        if self.n_instr >= self.limit:
            return
        waits = self._collect(eng, reads, writes)
        self.cnt[eng] += 1
        c = self.cnt[eng]
        E = self.E[eng]
        for s, v in waits:
            E.wait_ge(s, v)
        fn(E).then_inc(self.sems[eng], 1)
        for t in reads:
            t.r[eng] = c
        for t in writes:
            t.w = (eng, c)
            t.r = {}
        self.n_instr += 1

    def dma(self, q, out, in_, reads=(), writes=(), **kw):
        if self.n_instr >= self.limit:
            return
        lanes = self.lanes[q]
        ln = lanes[self.lane_rr[q] % len(lanes)]
        self.lane_rr[q] += 1
        waits = self._collect(q, reads, writes)
        prev = self.cnt[ln]
        if prev and self.seen[q].get(ln, 0) < prev:
            self.seen[q][ln] = prev
            waits.append((self.sems[ln], prev))
        self.cnt[ln] += 16
        c = self.cnt[ln]
        E = self.E[q]
        for s, v in waits:
            E.wait_ge(s, v)
        E.dma_start(out=out, in_=in_, **kw).then_inc(self.sems[ln], 16)
        for t in reads:
            t.r[ln] = c
        for t in writes:
            t.w = (ln, c)
            t.r = {}
        self.n_instr += 1

    def barrier(self):
        for e in ("pe", "act", "dve", "pool", "sp"):
            E = self.E[e]
            for s, v in self.cnt.items():
                if v and s != e and self.seen[e].get(s, 0) < v:
                    self.seen[e][s] = v
                    E.wait_ge(self.sems[s], v)
                if s == e and v and e != "sp":
                    if self.seen[e].get(s, 0) < v:
                        self.seen[e][s] = v
                        E.wait_ge(self.sems[s], v)

    def finish(self, final_toks=()):
        self.barrier()
        for cm in reversed(self._ctx):
            cm.__exit__(None, None, None)


from contextlib import ExitStack

D = 1024
DFF = 2816
NFF = DFF // 128
KD = D // 128


class Phase:
    _uid = [0]

    def __init__(self, nc, S):
        self.nc, self.S = nc, S
        self.es = ExitStack()
        self.n = 0
        Phase._uid[0] += 1
        self.uid = Phase._uid[0]

    def sb(self, shape, dt, name=None):
        self.n += 1
        return self.es.enter_context(self.nc.sbuf_tensor(name or f"t{self.uid}_{self.n}", list(shape), dt))

    def ps(self, shape, dt, name=None):
        self.n += 1
        return self.es.enter_context(self.nc.psum_tensor(name or f"p{self.uid}_{self.n}", list(shape), dt))

    def close(self):
        self.S.barrier()
        self.es.close()


def make_ident(nc, S, P, dt=BF16):
    identf = P.sb([128, 128], F32)
    ident = P.sb([128, 128], dt)
    t = Tok()
    S.op("pool", lambda E: E.memset(identf[:], 1.0), writes=[t])
    S.op("pool", lambda E: E.affine_select(out=identf[:], in_=identf[:], pattern=[[-1, 128]], base=0,
                                           channel_multiplier=1, compare_op=ALU.is_equal, fill=0.0),
         reads=[t], writes=[t])
    S.op("dve", lambda E: E.tensor_copy(out=ident[:], in_=identf[:]), reads=[t], writes=[t])
    return ident, identf, t


def load_w_bf16(S, dst, src, tok, rows_per=128, col_split=2):
    K = dst.shape[1]
    N = dst.shape[2]
    cs = N // col_split
    for k in range(K):
        for c in range(col_split):
            S.dma("pool", dst[:, k, c * cs:(c + 1) * cs], src[k * 128:(k + 1) * 128, c * cs:(c + 1) * cs],
                  writes=[tok])


def rms_prep(S, P, ht, t_h, s, gt, t_g, xn, t_xn, junk, t_junk, stat, t_stat, mhalf, t_mh):
    S.op("dve", lambda E: E.scalar_tensor_tensor(out=junk[:], in0=ht[:, s, :], scalar=1.0 / D, in1=ht[:, s, :],
                                                 op0=ALU.mult, op1=ALU.mult, accum_out=stat[:, 0:1]),
         reads=[t_h], writes=[t_junk, t_stat])
    S.op("dve", lambda E: E.tensor_scalar(out=stat[:, 1:2], in0=stat[:, 0:1], scalar1=1e-6, scalar2=None,
                                          op0=ALU.add), reads=[t_stat], writes=[t_stat])
    S.op("pool", lambda E: E.tensor_tensor(out=stat[:, 2:3], in0=stat[:, 1:2], in1=mhalf[:, 0:1], op=ALU.pow),
         reads=[t_stat, t_mh], writes=[t_stat])
    S.op("dve", lambda E: E.scalar_tensor_tensor(out=xn[:], in0=ht[:, s, :], scalar=stat[:, 2:3], in1=gt[:],
                                                 op0=ALU.mult, op1=ALU.mult),
         reads=[t_h, t_stat, t_g], writes=[t_xn])


def phase_ffn(nc, S, h_in, h_out, g, wg, wu, wd, toks_in, toks_out, T):
    P = Phase(nc, S)
    NT = T // 128
    NS = 4
    NSUP = NT // NS
    hv_in = h_in.rearrange("(n p) d -> p n d", p=128)
    hv_out = h_out.rearrange("(n p) d -> p n d", p=128)
    ident, _, t_id = make_ident(nc, S, P)
    wg_b = P.sb([128, KD, DFF], BF16); t_wg = Tok()
    wu_b = P.sb([128, KD, DFF], BF16); t_wu = Tok()
    wd_b = P.sb([128, NFF, D], BF16); t_wd = Tok()
    load_w_bf16(S, wg_b, wg, t_wg)
    load_w_bf16(S, wu_b, wu, t_wu)
    load_w_bf16(S, wd_b, wd, t_wd, col_split=1)
    gt = P.sb([128, D], F32); t_g = Tok()
    S.dma("sp", gt[:], g.partition_broadcast(128), writes=[t_g])
    mhalf = P.sb([128, 1], F32); t_mh = Tok()
    S.op("pool", lambda E: E.memset(mhalf[:], -0.5), writes=[t_mh])
    ht = P.sb([128, NS, D], F32); t_ht = [Tok() for _ in range(NS)]
    xn = [P.sb([128, D], BF16) for _ in range(2)]; t_xn = [Tok(), Tok()]
    junk = P.sb([128, D], BF16); t_junk = Tok()
    stat = [P.sb([128, 4], F32) for _ in range(2)]; t_stat = [Tok(), Tok()]
    xnT = P.sb([128, KD, NS * 128], BF16); t_xnT = Tok()
    hT = P.sb([128, NFF, NS * 128], BF16); t_hT = [Tok() for _ in range(NFF)]
    sg = [P.sb([128, NS * 128], BF16) for _ in range(2)]; t_sg = [Tok(), Tok()]
    pT = [P.ps([128, KD, 128], BF16) for _ in range(2)]; t_pT = [Tok(), Tok()]
    pG = [P.ps([128, 512], F32) for _ in range(2)]; t_pG = [Tok(), Tok()]
    pU = [P.ps([128, 512], F32) for _ in range(2)]; t_pU = [Tok(), Tok()]
    pD = [P.ps([128, 512], F32) for _ in range(2)]; t_pD = [Tok(), Tok()]
    it = 0
    for st in range(NSUP):
        for s in range(NS):
            n = st * NS + s
            S.dma("sp", ht[:, s, :], hv_in[:, n, :], reads=[toks_in[n]], writes=[t_ht[s]])
        for s in range(NS):
            b = s % 2
            rms_prep(S, P, ht, t_ht[s], s, gt, t_g, xn[b], t_xn[b], junk, t_junk, stat[b], t_stat[b], mhalf, t_mh)
            for k in range(KD):
                S.op("pe", lambda E, k=k, b=b: E.transpose(out=pT[b][:, k, :], in_=xn[b][:, k * 128:(k + 1) * 128],
                                                           identity=ident[:]),
                     reads=[t_xn[b], t_id], writes=[t_pT[b]])
            S.op("dve", lambda E, b=b, s=s: E.tensor_copy(out=xnT[:, :, s * 128:(s + 1) * 128], in_=pT[b][:]),
                 reads=[t_pT[b]], writes=[t_xnT])
        for f in range(NFF):
            b = f % 2
            for k in range(KD):
                S.op("pe", lambda E, k=k, f=f, b=b: E.matmul(pG[b][:], lhsT=wg_b[:, k, f * 128:(f + 1) * 128],
                                                             rhs=xnT[:, k, :], start=(k == 0), stop=(k == KD - 1)),
                     reads=[t_wg, t_xnT], writes=[t_pG[b]])
            for k in range(KD):
                S.op("pe", lambda E, k=k, f=f, b=b: E.matmul(pU[b][:], lhsT=wu_b[:, k, f * 128:(f + 1) * 128],
                                                             rhs=xnT[:, k, :], start=(k == 0), stop=(k == KD - 1)),
                     reads=[t_wu, t_xnT], writes=[t_pU[b]])
            S.op("act", lambda E, b=b: E.activation(out=sg[b][:], in_=pG[b][:], func=AF.Silu),
                 reads=[t_pG[b]], writes=[t_sg[b]])
            S.op("dve", lambda E, b=b, f=f: E.tensor_tensor(out=hT[:, f, :], in0=pU[b][:], in1=sg[b][:], op=ALU.mult),
                 reads=[t_pU[b], t_sg[b]], writes=[t_hT[f]])
        for s in range(NS):
            n = st * NS + s
            for c in range(2):
                b = it % 2
                it += 1
                for f in range(NFF):
                    S.op("pe", lambda E, f=f, s=s, c=c, b=b: E.matmul(
                        pD[b][:], lhsT=hT[:, f, s * 128:(s + 1) * 128], rhs=wd_b[:, f, c * 512:(c + 1) * 512],
                        start=(f == 0), stop=(f == NFF - 1)),
                        reads=[t_wd, t_hT[f]], writes=[t_pD[b]])
                S.op("dve", lambda E, s=s, c=c, b=b: E.scalar_tensor_tensor(
                    out=ht[:, s, c * 512:(c + 1) * 512], in0=pD[b][:], scalar=0.5,
                    in1=ht[:, s, c * 512:(c + 1) * 512], op0=ALU.mult, op1=ALU.add),
                    reads=[t_pD[b]], writes=[t_ht[s]])
            S.dma("sp", hv_out[:, n, :], ht[:, s, :], reads=[t_ht[s]], writes=[toks_out[n]])
    P.close()


RWKV_IN = 1408
ATT_Q0 = 1408
ATT_V0 = 2176
S5_0 = 2560
INW = 2816


def phase_win(nc, S, h_in, g, win, zr, qk, vtm, us5, toks_in, T):
    P = Phase(nc, S)
    NT = T // 128
    NS = 4
    NSUP = NT // NS
    hv_in = h_in.rearrange("(n p) d -> p n d", p=128)
    ident, _, t_id = make_ident(nc, S, P)
    w_b = P.sb([128, KD, INW], BF16); t_w = Tok()
    load_w_bf16(S, w_b, win, t_w)
    gt = P.sb([128, D], F32); t_g = Tok()
    S.dma("sp", gt[:], g.partition_broadcast(128), writes=[t_g])
    mhalf = P.sb([128, 1], F32); t_mh = Tok()
    S.op("pool", lambda E: E.memset(mhalf[:], -0.5), writes=[t_mh])
    ht = [P.sb([128, NS, D], F32) for _ in range(2)]; t_ht = [[Tok() for _ in range(NS)] for _ in range(2)]
    xn = [P.sb([128, D], BF16) for _ in range(2)]; t_xn = [Tok(), Tok()]
    junk = P.sb([128, D], BF16); t_junk = Tok()
    stat = [P.sb([128, 4], F32) for _ in range(2)]; t_stat = [Tok(), Tok()]
    xnT = [P.sb([128, KD, NS * 128], BF16) for _ in range(2)]; t_xnT = [Tok(), Tok()]
    stf = [P.sb([128, 512], F32) for _ in range(4)]; t_stf = [Tok() for _ in range(4)]
    stb = [P.sb([128, 512], BF16) for _ in range(4)]; t_stb = [Tok() for _ in range(4)]
    pT = [P.ps([128, KD, 128], BF16) for _ in range(2)]; t_pT = [Tok(), Tok()]
    pZ = [P.ps([128, 512], F32) for _ in range(4)]; t_pZ = [Tok() for _ in range(4)]
    t_out = Tok()
    chunks = []
    for c in range(11):
        chunks.append((c * 128, zr, c * 128, False))
    for c in range(6):
        chunks.append((ATT_Q0 + c * 128, qk, c * 128, True))
    for c in range(2):
        chunks.append((S5_0 + c * 128, us5, c * 128, False))
    it = 0
    ib = 0
    iff = 0
    for st in range(NSUP):
        hb = st % 2
        for s in range(NS):
            n = st * NS + s
            S.dma("sp", ht[hb][:, s, :], hv_in[:, n, :], reads=[toks_in[n]], writes=[t_ht[hb][s]])
        for s in range(NS):
            b = s % 2
            rms_prep(S, P, ht[hb], t_ht[hb][s], s, gt, t_g, xn[b], t_xn[b], junk, t_junk, stat[b], t_stat[b], mhalf, t_mh)
            for k in range(KD):
                S.op("pe", lambda E, k=k, b=b: E.transpose(out=pT[b][:, k, :], in_=xn[b][:, k * 128:(k + 1) * 128],
                                                           identity=ident[:]),
                     reads=[t_xn[b], t_id], writes=[t_pT[b]])
            S.op("dve", lambda E, b=b, s=s, hb=hb: E.tensor_copy(out=xnT[hb][:, :, s * 128:(s + 1) * 128], in_=pT[b][:]),
                 reads=[t_pT[b]], writes=[t_xnT[hb]])
        tsl = slice(st * 512, (st + 1) * 512)
        for (c0, dst, r0, isb) in chunks:
            pb = it % 4
            it += 1
            for k in range(KD):
                S.op("pe", lambda E, k=k, c0=c0, pb=pb, hb=hb: E.matmul(
                    pZ[pb][:], lhsT=w_b[:, k, c0:c0 + 128], rhs=xnT[hb][:, k, :], start=(k == 0), stop=(k == KD - 1)),
                    reads=[t_w, t_xnT[hb]], writes=[t_pZ[pb]])
            if isb:
                sb_ = ib % 4
                ib += 1
                S.op("act", lambda E, pb=pb, sb_=sb_: E.activation(out=stb[sb_][:], in_=pZ[pb][:], func=AF.Copy),
                     reads=[t_pZ[pb]], writes=[t_stb[sb_]])
                S.dma("sp", dst[r0:r0 + 128, tsl], stb[sb_][:], reads=[t_stb[sb_]], writes=[t_out])
            else:
                sf = iff % 4
                iff += 1
                eng = "act" if iff % 2 else "dve"
                if eng == "act":
                    S.op("act", lambda E, pb=pb, sf=sf: E.activation(out=stf[sf][:], in_=pZ[pb][:], func=AF.Copy),
                         reads=[t_pZ[pb]], writes=[t_stf[sf]])
                else:
                    S.op("dve", lambda E, pb=pb, sf=sf: E.tensor_copy(out=stf[sf][:], in_=pZ[pb][:]),
                         reads=[t_pZ[pb]], writes=[t_stf[sf]])
                S.dma("sp", dst[r0:r0 + 128, tsl], stf[sf][:], reads=[t_stf[sf]], writes=[t_out])
        for s in range(NS):
            n = st * NS + s
            pb = it % 4
            it += 1
            for k in range(KD):
                S.op("pe", lambda E, k=k, s=s, pb=pb, hb=hb: E.matmul(
                    pZ[pb][:, 0:384], lhsT=xnT[hb][:, k, s * 128:(s + 1) * 128], rhs=w_b[:, k, ATT_V0:ATT_V0 + 384],
                    start=(k == 0), stop=(k == KD - 1)),
                    reads=[t_w, t_xnT[hb]], writes=[t_pZ[pb]])
            sb_ = ib % 4
            ib += 1
            S.op("dve", lambda E, pb=pb, sb_=sb_: E.tensor_copy(out=stb[sb_][:, 0:384], in_=pZ[pb][:, 0:384]),
                 reads=[t_pZ[pb]], writes=[t_stb[sb_]])
            S.dma("sp", vtm[n * 128:(n + 1) * 128, :], stb[sb_][:, 0:384], reads=[t_stb[sb_]], writes=[t_out])
    P.close()


def phase_wout(nc, S, h_in, h_out, mixT, wout, toks_in, toks_out, T):
    P = Phase(nc, S)
    NT = T // 128
    NS = 4
    NSUP = NT // NS
    hv_in = h_in.rearrange("(n p) d -> p n d", p=128)
    hv_out = h_out.rearrange("(n p) d -> p n d", p=128)
    mv = mixT.rearrange("(k p) t -> p k t", p=128)
    w_b = P.sb([128, KD, D], BF16); t_w = Tok()
    load_w_bf16(S, w_b, wout, t_w, col_split=1)
    ht = [P.sb([128, NS, D], F32) for _ in range(2)]; t_ht = [[Tok() for _ in range(NS)] for _ in range(2)]
    ml = [P.sb([128, KD, 512], BF16) for _ in range(2)]; t_ml = [Tok(), Tok()]
    pD = [P.ps([128, 512], F32) for _ in range(4)]; t_pD = [Tok() for _ in range(4)]
    it = 0
    for st in range(NSUP):
        hb = st % 2
        S.dma("sp", ml[hb][:], mv[:, :, st * 512:(st + 1) * 512], writes=[t_ml[hb]])
        for s in range(NS):
            n = st * NS + s
            S.dma("sp", ht[hb][:, s, :], hv_in[:, n, :], reads=[toks_in[n]], writes=[t_ht[hb][s]])
        for s in range(NS):
            n = st * NS + s
            for c in range(2):
                b = it % 4
                it += 1
                for k in range(KD):
                    S.op("pe", lambda E, k=k, s=s, c=c, b=b, hb=hb: E.matmul(
                        pD[b][:], lhsT=ml[hb][:, k, s * 128:(s + 1) * 128], rhs=w_b[:, k, c * 512:(c + 1) * 512],
                        start=(k == 0), stop=(k == KD - 1)),
                        reads=[t_w, t_ml[hb]], writes=[t_pD[b]])
                S.op("dve", lambda E, s=s, c=c, b=b, hb=hb: E.tensor_tensor(
                    out=ht[hb][:, s, c * 512:(c + 1) * 512], in0=pD[b][:], in1=ht[hb][:, s, c * 512:(c + 1) * 512],
                    op=ALU.add), reads=[t_pD[b]], writes=[t_ht[hb][s]])
            S.dma("sp", hv_out[:, n, :], ht[hb][:, s, :], reads=[t_ht[hb][s]], writes=[toks_out[n]])
    P.close()


def phase_final(nc, S, h_in, out, g, toks_in, toks_out, T):
    P = Phase(nc, S)
    NT = T // 128
    hv_in = h_in.rearrange("(n p) d -> p n d", p=128)
    hv_out = out.rearrange("(n p) d -> p n d", p=128)
    gt = P.sb([128, D], F32); t_g = Tok()
    S.dma("sp", gt[:], g.partition_broadcast(128), writes=[t_g])
    mhalf = P.sb([128, 1], F32); t_mh = Tok()
    S.op("pool", lambda E: E.memset(mhalf[:], -0.5), writes=[t_mh])
    ht = [P.sb([128, 1, D], F32) for _ in range(4)]; t_ht = [Tok() for _ in range(4)]
    xo = [P.sb([128, D], F32) for _ in range(4)]; t_xo = [Tok() for _ in range(4)]
    junk = P.sb([128, D], BF16); t_junk = Tok()
    stat = [P.sb([128, 4], F32) for _ in range(4)]; t_stat = [Tok() for _ in range(4)]
    for n in range(NT):
        b = n % 4
        S.dma("sp", ht[b][:, 0, :], hv_in[:, n, :], reads=[toks_in[n]], writes=[t_ht[b]])
        rms_prep(S, P, ht[b], t_ht[b], 0, gt, t_g, xo[b], t_xo[b], junk, t_junk, stat[b], t_stat[b], mhalf, t_mh)
        S.dma("pool", hv_out[:, n, :], xo[b][:], reads=[t_xo[b]], writes=[toks_out[n]])
    P.close()


ALIBI = [0.25, 0.0625, 0.015625, 0.00390625, 0.5, 0.125]
DILS = [1, 4, 16]
NEG = -1.0e30


def phase_attn(nc, S, qk, vtm, mixT, T):
    P = Phase(nc, S)
    NB1 = T // 128
    dfi = P.sb([128, 128], I32)
    dff = P.sb([128, 128], F32)
    Dk = P.sb([128, 3, 128], F32)
    Mk = P.sb([128, 3, 128], F32)
    t_c = Tok()
    S.op("pool", lambda E: E.iota(dfi[:], pattern=[[1, 128]], base=0, channel_multiplier=-1), writes=[t_c])
    S.op("dve", lambda E: E.tensor_copy(out=dff[:], in_=dfi[:]), reads=[t_c], writes=[t_c])
    S.op("dve", lambda E: E.tensor_scalar(out=Dk[:, 0, :], in0=dff[:], scalar1=128.0, scalar2=None, op0=ALU.add),
         reads=[t_c], writes=[t_c])
    S.op("dve", lambda E: E.tensor_scalar(out=Dk[:, 1, :], in0=dff[:], scalar1=-1.0, scalar2=None, op0=ALU.mult),
         reads=[t_c], writes=[t_c])
    S.op("dve", lambda E: E.tensor_tensor(out=Dk[:, 1, :], in0=Dk[:, 1, :], in1=dff[:], op=ALU.max),
         reads=[t_c], writes=[t_c])
    S.op("dve", lambda E: E.tensor_scalar(out=Dk[:, 2, :], in0=dff[:], scalar1=-1.0, scalar2=128.0, op0=ALU.mult,
                                          op1=ALU.add), reads=[t_c], writes=[t_c])
    S.op("dve", lambda E: E.tensor_scalar(out=Mk[:], in0=Dk[:], scalar1=64.0, scalar2=NEG, op0=ALU.is_gt,
                                          op1=ALU.mult), reads=[t_c], writes=[t_c])
    sel = P.sb([65, 64], F32)
    S.op("dve", lambda E: E.memset(sel[:], 0.0), writes=[t_c])
    S.op("dve", lambda E: E.memset(sel[64:65, :], 1.0), reads=[t_c], writes=[t_c])

    qT = P.sb([128, T], BF16); t_q = Tok()
    kT = P.sb([128, T], BF16); t_k = Tok()
    vt = [P.sb([128, NB1, 2, 65], BF16) for _ in range(3)]; t_v = [Tok() for _ in range(3)]
    acc = P.sb([65, 2, T], F32)
    t_acc = [[Tok() for _ in range(NB1)] for _ in range(2)]
    biasT = P.sb([128, 2, 3, 3, 128], F32); t_b = Tok()
    NBF = 4
    sc = [P.sb([128, 3, 128], F32) for _ in range(NBF)]; t_sc = [Tok() for _ in range(NBF)]
    pr = [P.sb([128, 3, 128], BF16) for _ in range(NBF)]; t_pr = [Tok() for _ in range(NBF)]
    rec = [P.sb([64, 512], F32) for _ in range(2)]; t_rec = [Tok(), Tok()]
    ob = [P.sb([64, 512], BF16) for _ in range(2)]; t_ob = [Tok(), Tok()]
    pS = [P.ps([128, 3, 128], F32) for _ in range(NBF)]; t_pS = [Tok() for _ in range(NBF)]
    pOb = [P.ps([128, 512], F32) for _ in range(NBF)]; t_pO = [Tok() for _ in range(NBF)]
    pO = [x[0:65, 0:128] for x in pOb]
    pB = [x[0:64, 0:512] for x in pOb]; t_pB = t_pO
    t_out = Tok()
    it = 0
    for hp in range(3):
        S.dma("sp", qT[:], qk[128 * hp:128 * hp + 128, :], writes=[t_q])
        S.dma("sp", kT[:], qk[384 + 128 * hp:384 + 128 * hp + 128, :], writes=[t_k])
        for pi, d in enumerate(DILS):
            S.op("pool", lambda E, pi=pi: E.memset(vt[pi][:], 1.0), writes=[t_v[pi]])
            nm = T // d // 128
            vv = vtm.rearrange("(m j r) (h c) -> j r m h c", j=128, r=d, c=64)
            for r in range(d):
                for h2 in range(2):
                    S.dma("sp", vt[pi][:, r * nm:(r + 1) * nm, h2, 0:64], vv[:, r, :, 2 * hp + h2, :],
                          writes=[t_v[pi]])
        for h2 in range(2):
            for pi, d in enumerate(DILS):
                sl = -ALIBI[2 * hp + h2] * d
                S.op("dve", lambda E, h2=h2, pi=pi, sl=sl: E.scalar_tensor_tensor(
                    out=biasT[:, h2, pi, :, :], in0=Dk[:], scalar=sl, in1=Mk[:], op0=ALU.mult, op1=ALU.add),
                    reads=[t_c], writes=[t_b])
        for h2 in range(2):
            rows = slice(64 * h2, 64 * h2 + 64)
            stages = []
            for pi, d in enumerate(DILS):
                nb = T // d // 128
                for r in range(d):
                    for b in range(nb):
                        bi = it % NBF
                        it += 1
                        kts = [kt for kt in (b - 1, b, b + 1) if 0 <= kt < nb]
                        k0 = kts[0] - (b - 1)
                        nk = len(kts)
                        qs = slice(r + d * 128 * b, r + d * 128 * b + d * 127 + 1, d)
                        blks = sorted(set(range((r + d * 128 * b) // 128, (r + d * 128 * (b + 1) - d) // 128 + 1)))

                        def stA(bi=bi, kts=kts, k0=k0, nk=nk, qs=qs, r=r, d=d, pi=pi, rows=rows, h2=h2):
                            for ki, kt in enumerate(kts):
                                ks = slice(r + d * 128 * kt, r + d * 128 * kt + d * 127 + 1, d)
                                S.op("pe", lambda E, ki=ki, ks=ks: E.matmul(
                                    pS[bi][:, ki, :], lhsT=kT[rows, ks], rhs=qT[rows, qs], start=True, stop=True),
                                    reads=[t_q, t_k], writes=[t_pS[bi]])
                            S.op("dve", lambda E: E.scalar_tensor_tensor(
                                out=sc[bi][:, 0:nk, :], in0=pS[bi][:, 0:nk, :], scalar=0.125,
                                in1=biasT[:, h2, pi, k0:k0 + nk, :], op0=ALU.mult, op1=ALU.add),
                                reads=[t_pS[bi], t_b], writes=[t_sc[bi]])
                            S.op("act", lambda E: E.activation(out=pr[bi][:, 0:nk, :], in_=sc[bi][:, 0:nk, :],
                                                               func=AF.Exp),
                                 reads=[t_sc[bi]], writes=[t_pr[bi]])

                        def stB(bi=bi, kts=kts, nk=nk, qs=qs, r=r, nb=nb, pi=pi, h2=h2, blks=blks):
                            for ki, kt in enumerate(kts):
                                S.op("pe", lambda E, ki=ki, kt=kt: E.matmul(
                                    pO[bi], lhsT=vt[pi][:, r * nb + kt, h2, :], rhs=pr[bi][:, ki, :],
                                    start=(ki == 0), stop=(ki == nk - 1)),
                                    reads=[t_v[pi], t_pr[bi]], writes=[t_pO[bi]])
                            at = [t_acc[h2][x] for x in blks]
                            if pi == 0:
                                S.op("act", lambda E: E.activation(out=acc[:, h2, qs], in_=pO[bi], func=AF.Copy),
                                     reads=[t_pO[bi]], writes=at)
                            else:
                                S.op("dve", lambda E: E.tensor_tensor(out=acc[:, h2, qs], in0=pO[bi],
                                                                      in1=acc[:, h2, qs], op=ALU.add),
                                     reads=[t_pO[bi]], writes=at)
                        stages.append((stA, stB))
            SKEW = NBF - 1
            for i in range(len(stages) + SKEW):
                if i < len(stages):
                    stages[i][0]()
                if i - SKEW >= 0:
                    stages[i - SKEW][1]()
            head = 2 * hp + h2
            for c in range(T // 512):
                bi = c % 2
                cs = slice(c * 512, (c + 1) * 512)
                at = t_acc[h2][4 * c:4 * c + 4]
                S.op("pe", lambda E, bi=bi, h2=h2, cs=cs: E.matmul(pB[bi], lhsT=sel[:], rhs=acc[:, h2, cs],
                                                                   start=True, stop=True),
                     reads=at + [t_c], writes=[t_pB[bi]])
                S.op("dve", lambda E, bi=bi: E.reciprocal(out=rec[bi][:], in_=pB[bi]),
                     reads=[t_pB[bi]], writes=[t_rec[bi]])
                S.op("dve", lambda E, bi=bi, h2=h2, cs=cs: E.tensor_tensor(out=ob[bi][:], in0=acc[0:64, h2, cs],
                                                                          in1=rec[bi][:], op=ALU.mult),
                     reads=at + [t_rec[bi]], writes=[t_ob[bi]])
                S.dma("sp", mixT[384 + 64 * head:384 + 64 * head + 64, cs], ob[bi][:], reads=[t_ob[bi]],
                      writes=[t_out])
    P.close()


def rsl(lo, hi):
    return slice(hi - 1, (lo - 1) if lo > 0 else None, -1)


TWO_PI = 6.283185307179586
MAGIC = 12582912.0


def phase_s5(nc, S, us5, mixT, prm, T, C=256):
    P = Phase(nc, S)
    NCH = T // C
    NCMB = 16
    a_re, a_im, lstep, b_re, b_im, c_re, c_im, dsk, glu_w, glu_b = prm
    ident, identf, t_id = make_ident(nc, S, P)
    t_s = Tok()

    def dv(fn, eng="dve"):
        S.op(eng, fn, reads=[t_s, t_id], writes=[t_s])

    prs = P.sb([128, 40, 16], F32)
    cosT = P.sb([128, NCMB, C], F32)
    sinT = P.sb([128, NCMB, C], F32)
    BT = P.sb([128, NCMB, 2, 128], BF16)
    CT = P.sb([128, NCMB, 2, 128], BF16)
    gw = P.sb([128, 2, 512], BF16)
    gb = P.sb([128, 4], F32)
    dk = P.sb([128, 2], F32)
    ub = P.sb([128, 2, T], BF16); t_ub = Tok()
    ybwd = P.sb([128, 2, T], F32); t_yb = [Tok() for _ in range(NCH)]
    gi = P.sb([128, 2, NCMB], F32); t_gi = [Tok() for _ in range(NCMB)]
    banks = [P.ps([128, 512], F32) for _ in range(6)]
    P2 = Phase(nc, S)
    stg = P2.sb([16, 3, 128], F32)
    lst = P2.sb([16, 2], F32)
    S.dma("sp", stg[:, 0, :], a_re.rearrange("d (j g) p -> (d j) (g p)", g=2), writes=[t_s])
    S.dma("sp", stg[:, 1, :], a_im.rearrange("d (j g) p -> (d j) (g p)", g=2), writes=[t_s])
    S.dma("sp", lst[:], lstep.rearrange("d (j g) -> (d j) g", g=2), writes=[t_s])
    for g2 in range(2):
        dv(lambda E, g2=g2: E.tensor_copy(out=stg[:, 2, 64 * g2:64 * g2 + 64],
                                          in_=lst[:, g2:g2 + 1].to_broadcast([16, 64])))
    pst = banks[0][:, 0:48].rearrange("p (a b) -> p a b", a=3)
    for i in range(3):
        S.op("pe", lambda E, i=i: E.transpose(out=pst[:, i, :], in_=stg[:, i, :], identity=identf[0:16, 0:16]),
             reads=[t_s, t_id], writes=[t_s])
    nm = {}

    def V(name):
        if name not in nm:
            nm[name] = len(nm)
        return prs[:, nm[name], :]
    dv(lambda E: E.tensor_copy(out=prs[:, 0:3, :], in_=pst))
    nm.update({"are": 0, "aim": 1, "lst": 2})
    S.op("act", lambda E: E.activation(out=V("step"), in_=V("lst"), func=AF.Exp), reads=[t_s], writes=[t_s])
    dv(lambda E: E.tensor_tensor(out=V("ar"), in0=V("are"), in1=V("step"), op=ALU.mult))
    dv(lambda E: E.tensor_tensor(out=V("th"), in0=V("aim"), in1=V("step"), op=ALU.mult))
    S.op("act", lambda E: E.activation(out=V("rho"), in_=V("ar"), func=AF.Exp), reads=[t_s], writes=[t_s])

    def sin_of(dst, src, shift):
        dv(lambda E: E.tensor_scalar(out=V("k1"), in0=V(src), scalar1=1.0 / TWO_PI, scalar2=shift / TWO_PI,
                                     op0=ALU.mult, op1=ALU.add))
        dv(lambda E: E.tensor_scalar(out=V("k2"), in0=V("k1"), scalar1=MAGIC, scalar2=None, op0=ALU.add))
        dv(lambda E: E.tensor_scalar(out=V("k3"), in0=V("k2"), scalar1=-MAGIC, scalar2=None, op0=ALU.add))
        dv(lambda E: E.tensor_tensor(out=V("k1"), in0=V("k1"), in1=V("k3"), op=ALU.subtract))
        S.op("act", lambda E: E.activation(out=V(dst), in_=V("k1"), func=AF.Sin, scale=TWO_PI),
             reads=[t_s], writes=[t_s])
    sin_of("sn", "th", 0.0)
    sin_of("cs", "th", TWO_PI / 4)
    dv(lambda E: E.tensor_tensor(out=V("lr"), in0=V("rho"), in1=V("cs"), op=ALU.mult))
    dv(lambda E: E.tensor_tensor(out=V("li"), in0=V("rho"), in1=V("sn"), op=ALU.mult))
    dv(lambda E: E.tensor_scalar(out=V("nr"), in0=V("lr"), scalar1=-1.0, scalar2=None, op0=ALU.add))
    dv(lambda E: E.tensor_tensor(out=V("d1"), in0=V("are"), in1=V("are"), op=ALU.mult))
    dv(lambda E: E.tensor_tensor(out=V("d2"), in0=V("aim"), in1=V("aim"), op=ALU.mult))
    dv(lambda E: E.tensor_tensor(out=V("d1"), in0=V("d1"), in1=V("d2"), op=ALU.add))
    dv(lambda E: E.reciprocal(out=V("rd"), in_=V("d1")))
    dv(lambda E: E.tensor_tensor(out=V("z1"), in0=V("nr"), in1=V("are"), op=ALU.mult))
    dv(lambda E: E.tensor_tensor(out=V("z2"), in0=V("li"), in1=V("aim"), op=ALU.mult))
    dv(lambda E: E.tensor_tensor(out=V("z1"), in0=V("z1"), in1=V("z2"), op=ALU.add))
    dv(lambda E: E.tensor_tensor(out=V("zr"), in0=V("z1"), in1=V("rd"), op=ALU.mult))
    dv(lambda E: E.tensor_tensor(out=V("z1"), in0=V("li"), in1=V("are"), op=ALU.mult))
    dv(lambda E: E.tensor_tensor(out=V("z2"), in0=V("nr"), in1=V("aim"), op=ALU.mult))
    dv(lambda E: E.tensor_tensor(out=V("z1"), in0=V("z1"), in1=V("z2"), op=ALU.subtract))
    dv(lambda E: E.tensor_tensor(out=V("zi"), in0=V("z1"), in1=V("rd"), op=ALU.mult))

    tmpA = P2.sb([128, NCMB, C // 2], F32)
    tmpB = P2.sb([128, NCMB, C // 2], F32)
    dv(lambda E: E.memset(cosT[:, :, 0:1], 1.0))
    dv(lambda E: E.memset(sinT[:, :, 0:1], 0.0))
    dv(lambda E: E.tensor_copy(out=V("wr"), in_=V("cs")))
    dv(lambda E: E.tensor_copy(out=V("wi"), in_=V("sn")))
    L = 1
    while L < C:
        wrb = V("wr").unsqueeze(2).to_broadcast([128, NCMB, L])
        wib = V("wi").unsqueeze(2).to_broadcast([128, NCMB, L])
        dv(lambda E, L=L, wrb=wrb: E.tensor_tensor(out=tmpA[:, :, 0:L], in0=cosT[:, :, 0:L], in1=wrb, op=ALU.mult))
        dv(lambda E, L=L, wib=wib: E.tensor_tensor(out=tmpB[:, :, 0:L], in0=sinT[:, :, 0:L], in1=wib, op=ALU.mult))
        dv(lambda E, L=L: E.tensor_tensor(out=cosT[:, :, L:2 * L], in0=tmpA[:, :, 0:L], in1=tmpB[:, :, 0:L],
                                          op=ALU.subtract))
        dv(lambda E, L=L, wib=wib: E.tensor_tensor(out=tmpA[:, :, 0:L], in0=cosT[:, :, 0:L], in1=wib, op=ALU.mult))
        dv(lambda E, L=L, wrb=wrb: E.tensor_tensor(out=tmpB[:, :, 0:L], in0=sinT[:, :, 0:L], in1=wrb, op=ALU.mult))
        dv(lambda E, L=L: E.tensor_tensor(out=sinT[:, :, L:2 * L], in0=tmpA[:, :, 0:L], in1=tmpB[:, :, 0:L],
                                          op=ALU.add))
        dv(lambda E: E.tensor_tensor(out=V("q1"), in0=V("wr"), in1=V("wr"), op=ALU.mult))
        dv(lambda E: E.tensor_tensor(out=V("q2"), in0=V("wi"), in1=V("wi"), op=ALU.mult))
        dv(lambda E: E.tensor_tensor(out=V("q3"), in0=V("wr"), in1=V("wi"), op=ALU.mult))
        dv(lambda E: E.tensor_tensor(out=V("wr"), in0=V("q1"), in1=V("q2"), op=ALU.subtract))
        dv(lambda E: E.tensor_scalar(out=V("wi"), in0=V("q3"), scalar1=2.0, scalar2=None, op0=ALU.mult))
        L *= 2

    bst = P2.sb([128, 2, 8, 16], F32)
    S.dma("sp", bst[:, 0, :, :], b_re.rearrange("(j g) p c -> (g p) j c", g=2), writes=[t_s])
    S.dma("sp", bst[:, 1, :, :], b_im.rearrange("(j g) p c -> (g p) j c", g=2), writes=[t_s])
    bexp = P2.sb([128, 2, 128], F32)
    btmp = P2.sb([128, 16], F32)
    pX = [banks[1][:, 0:128], banks[2][:, 0:128]]
    for d in range(2):
        for j in range(8):
            cmb = d * 8 + j
            jj = j % 4
            dv(lambda E: E.memset(bexp[:], 0.0))
            for g2 in range(2):
                ps_ = slice(64 * g2, 64 * g2 + 64)
                cs_ = slice(32 * jj + 16 * g2, 32 * jj + 16 * g2 + 16)
                zr_ = prs[ps_, nm["zr"], cmb:cmb + 1]
                zi_ = prs[ps_, nm["zi"], cmb:cmb + 1]
                dv(lambda E, ps_=ps_, zi_=zi_, j=j: E.tensor_scalar(out=btmp[ps_, :], in0=bst[ps_, 1, j, :], scalar1=zi_,
                                                                    scalar2=None, op0=ALU.mult))
                dv(lambda E, ps_=ps_, cs_=cs_, zr_=zr_, j=j: E.scalar_tensor_tensor(
                    out=bexp[ps_, 0, cs_], in0=bst[ps_, 0, j, :], scalar=zr_, in1=btmp[ps_, :], op0=ALU.mult,
                    op1=ALU.subtract))
                dv(lambda E, ps_=ps_, zr_=zr_, j=j: E.tensor_scalar(out=btmp[ps_, :], in0=bst[ps_, 1, j, :], scalar1=zr_,
                                                                    scalar2=None, op0=ALU.mult))
                dv(lambda E, ps_=ps_, cs_=cs_, zi_=zi_, j=j: E.scalar_tensor_tensor(
                    out=bexp[ps_, 1, cs_], in0=bst[ps_, 0, j, :], scalar=zi_, in1=btmp[ps_, :], op0=ALU.mult,
                    op1=ALU.add))
            for ri in range(2):
                S.op("pe", lambda E, ri=ri: E.transpose(out=pX[ri], in_=bexp[:, ri, :], identity=identf[:]),
                     reads=[t_s, t_id], writes=[t_s])
                dv(lambda E, ri=ri, cmb=cmb: E.tensor_copy(out=BT[:, cmb, ri, :], in_=pX[ri]))
    cnat = P2.sb([128, 2, 2, 2, 64], F32)
    for d in range(2):
        for ri, cc in enumerate((c_re, c_im)):
            for ut in range(2):
                S.dma("sp", cnat[:, d, ri, ut, :], cc[d].rearrange("g c p -> (g c) p")[128 * ut:128 * ut + 128, :],
                      writes=[t_s])
    mki = P2.sb([128, 4, 2], I32)
    mk = P2.sb([128, 4, 2], F32)
    mk2 = P2.sb([128, 4, 2], F32)
    S.op("pool", lambda E: E.iota(mki[:], pattern=[[-32, 4], [-16, 2]], base=0, channel_multiplier=1),
         reads=[t_s], writes=[t_s])
    dv(lambda E: E.tensor_copy(out=mk[:], in_=mki[:]))
    dv(lambda E: E.tensor_scalar(out=mk2[:], in0=mk[:], scalar1=0.0, scalar2=None, op0=ALU.is_ge))
    dv(lambda E: E.tensor_scalar(out=mk[:], in0=mk[:], scalar1=15.0, scalar2=None, op0=ALU.is_le))
    dv(lambda E: E.tensor_tensor(out=mk[:], in0=mk[:], in1=mk2[:], op=ALU.mult))
    cx = P2.sb([128, 2, 64], F32)
    for d in range(2):
        for j in range(8):
            cmb = d * 8 + j
            jj = j % 4
            ut = j // 4
            for ri in range(2):
                for g2 in range(2):
                    dv(lambda E, d=d, ri=ri, ut=ut, jj=jj, g2=g2: E.tensor_scalar(
                        out=cx[:, g2, :], in0=cnat[:, d, ri, ut, :], scalar1=mk[:, jj, g2:g2 + 1], scalar2=None,
                        op0=ALU.mult))
                S.op("pe", lambda E, ri=ri: E.transpose(out=pX[ri], in_=cx[:].rearrange("p a b -> p (a b)"),
                                                        identity=identf[:]),
                     reads=[t_s, t_id], writes=[t_s])
                sgn = 1.0 if ri == 0 else -1.0
                dv(lambda E, ri=ri, cmb=cmb, sgn=sgn: E.tensor_scalar(out=CT[:, cmb, ri, :], in0=pX[ri], scalar1=sgn,
                                                                      scalar2=None, op0=ALU.mult))
    load_w_bf16(S, gw, glu_w, t_s, col_split=1)
    S.dma("sp", gb[:], glu_b.rearrange("(o p) -> p o", p=128), writes=[t_s], allow_slow_non_contiguous=True)
    S.dma("sp", dk[:], dsk.rearrange("(o p) -> p o", p=128), writes=[t_s], allow_slow_non_contiguous=True)

    for ut in range(2):
        for c4 in range(T // 2048 if T >= 2048 else 1):
            w_ = min(2048, T)
            S.dma("pool", ub[:, ut, c4 * w_:(c4 + 1) * w_], us5[128 * ut:128 * ut + 128, c4 * w_:(c4 + 1) * w_],
                  writes=[t_ub])
    S.op("dve", lambda E: E.memset(gi[:], 0.0), writes=t_gi)
    NB = 2
    P2.close()
    pBU = [banks[i][:, 0:2 * C].rearrange("p (a b) -> p a b", a=2) for i in range(NB)]; t_pBU = [Tok() for _ in range(NB)]
    m1 = [P.sb([128, 4, C], F32) for _ in range(NB)]; t_m1 = [Tok() for _ in range(NB)]
    gin = [P.sb([128, 2, C], F32) for _ in range(NB)]; t_gin = [Tok() for _ in range(NB)]
    gg = [P.sb([128, 2, C], F32) for _ in range(NB)]; t_gg = [Tok() for _ in range(NB)]
    m2 = [P.sb([128, 4, C], F32) for _ in range(NB)]; t_m2 = [Tok() for _ in range(NB)]
    ctmp = [P.sb([128, 2], F32) for _ in range(NB)]; t_ct = [Tok() for _ in range(NB)]
    hh = [P.sb([128, 4, 2, C], BF16) for _ in range(2)]; t_hh = [[Tok() for _ in range(4)] for _ in range(2)]
    pY = [banks[2 + i][:, 0:C] for i in range(2)]; t_pY = [Tok(), Tok()]
    uf = [P.sb([128, 2, C], F32) for _ in range(2)]; t_uf = [Tok(), Tok()]
    yv = [P.sb([128, C], F32) for _ in range(2)]; t_yv = [Tok(), Tok()]
    y2 = [P.sb([128, C], F32) for _ in range(2)]; t_y2 = [Tok(), Tok()]
    ygl = [P.sb([128, 2, C], BF16) for _ in range(2)]; t_yg = [[Tok(), Tok()] for _ in range(2)]
    pZ = [banks[4 + i][:, 0:C] for i in range(2)]; t_pZ = [Tok(), Tok()]
    sg = [P.sb([128, C], F32) for _ in range(2)]; t_sg = [Tok(), Tok()]
    oo = [P.sb([128, C], BF16) for _ in range(2)]; t_oo = [Tok(), Tok()]
    t_out = Tok()
    it = 0
    iy = 0
    for d in (1, 0):
        if d == 0:
            S.op("dve", lambda E: E.memset(gi[:], 0.0), writes=t_gi)
        for ci in range(NCH):
            if d == 1:
                lo, hi = T - (ci + 1) * C, T - ci * C
                tsl = rsl(lo, hi)
            else:
                lo, hi = ci * C, (ci + 1) * C
                tsl = slice(lo, hi)
            chn = lo // C
            if d == 0:
                ufb = ci % 2
                S.dma("sp", uf[ufb][:], us5.rearrange("(u p) t -> p u t", p=128)[:, :, lo:hi], writes=[t_uf[ufb]])
            for ut in range(2):
                hb = (ci * 2 + ut) % 2
                for jj in range(4):
                    j = ut * 4 + jj
                    cmb = d * 8 + j
                    b = it % NB
                    it += 1
                    for ri in range(2):
                        S.op("pe", lambda E, b=b, ri=ri, cmb=cmb, ut=ut, tsl=tsl: E.matmul(
                            pBU[b][:, ri, :], lhsT=BT[:, cmb, ri, :], rhs=ub[:, ut, tsl], start=True, stop=True),
                            reads=[t_s, t_ub], writes=[t_pBU[b]])
                    cs_ = cosT[:, cmb, :]
                    sn_ = sinT[:, cmb, :]
                    S.op("dve", lambda E, b=b, cs_=cs_: E.tensor_tensor(out=m1[b][:, 0, :], in0=pBU[b][:, 0, :], in1=cs_,
                                                                        op=ALU.mult),
                         reads=[t_pBU[b], t_s], writes=[t_m1[b]])
                    S.op("dve", lambda E, b=b, sn_=sn_: E.tensor_tensor(out=m1[b][:, 1, :], in0=pBU[b][:, 1, :], in1=sn_,
                                                                        op=ALU.mult),
                         reads=[t_pBU[b], t_s], writes=[t_m1[b]])
                    S.op("dve", lambda E, b=b, cs_=cs_: E.tensor_tensor(out=m1[b][:, 2, :], in0=pBU[b][:, 1, :], in1=cs_,
                                                                        op=ALU.mult),
                         reads=[t_pBU[b], t_s], writes=[t_m1[b]])
                    S.op("dve", lambda E, b=b, sn_=sn_: E.tensor_tensor(out=m1[b][:, 3, :], in0=pBU[b][:, 0, :], in1=sn_,
                                                                        op=ALU.mult),
                         reads=[t_pBU[b], t_s], writes=[t_m1[b]])
                    S.op("pool", lambda E, b=b: E.tensor_tensor(out=gin[b][:, 0, :], in0=m1[b][:, 0, :],
                                                                in1=m1[b][:, 1, :], op=ALU.add),
                         reads=[t_m1[b]], writes=[t_gin[b]])
                    S.op("pool", lambda E, b=b: E.tensor_tensor(out=gin[b][:, 1, :], in0=m1[b][:, 2, :],
                                                                in1=m1[b][:, 3, :], op=ALU.subtract),
                         reads=[t_m1[b]], writes=[t_gin[b]])
                    rho_b = prs[:, nm["rho"], cmb:cmb + 1].to_broadcast([128, C])
                    for ri in range(2):
                        S.op("dve", lambda E, b=b, ri=ri, cmb=cmb, rho_b=rho_b: E.tensor_tensor_scan(
                            out=gg[b][:, ri, :], data0=rho_b, data1=gin[b][:, ri, :], initial=gi[:, ri, cmb:cmb + 1],
                            op0=ALU.mult, op1=ALU.add),
                            reads=[t_gin[b], t_gi[cmb], t_s], writes=[t_gg[b]])
                    wr_ = prs[:, nm["wr"], cmb:cmb + 1]
                    wi_ = prs[:, nm["wi"], cmb:cmb + 1]
                    S.op("dve", lambda E, b=b, wi_=wi_, wr_=wr_: E.tensor_scalar(
                        out=ctmp[b][:, 0:1], in0=gg[b][:, 1, C - 1:C], scalar1=wi_, scalar2=None, op0=ALU.mult),
                        reads=[t_gg[b], t_s], writes=[t_ct[b]])
                    S.op("dve", lambda E, b=b, wi_=wi_, wr_=wr_: E.tensor_scalar(
                        out=ctmp[b][:, 1:2], in0=gg[b][:, 1, C - 1:C], scalar1=wr_, scalar2=None, op0=ALU.mult),
                        reads=[t_gg[b], t_s], writes=[t_ct[b]])
                    S.op("dve", lambda E, b=b, wr_=wr_, cmb=cmb: E.scalar_tensor_tensor(
                        out=gi[:, 0, cmb:cmb + 1], in0=gg[b][:, 0, C - 1:C], scalar=wr_, in1=ctmp[b][:, 0:1],
                        op0=ALU.mult, op1=ALU.subtract),
                        reads=[t_gg[b], t_ct[b], t_s], writes=[t_gi[cmb]])
                    S.op("dve", lambda E, b=b, wi_=wi_, cmb=cmb: E.scalar_tensor_tensor(
                        out=gi[:, 1, cmb:cmb + 1], in0=gg[b][:, 0, C - 1:C], scalar=wi_, in1=ctmp[b][:, 1:2],
                        op0=ALU.mult, op1=ALU.add),
                        reads=[t_gg[b], t_ct[b], t_s], writes=[t_gi[cmb]])
                    S.op("pool", lambda E, b=b, cs_=cs_: E.tensor_tensor(out=m2[b][:, 0, :], in0=gg[b][:, 0, :], in1=cs_,
                                                                         op=ALU.mult),
                         reads=[t_gg[b], t_s], writes=[t_m2[b]])
                    S.op("pool", lambda E, b=b, sn_=sn_: E.tensor_tensor(out=m2[b][:, 1, :], in0=gg[b][:, 1, :], in1=sn_,
                                                                         op=ALU.mult),
                         reads=[t_gg[b], t_s], writes=[t_m2[b]])
                    S.op("dve", lambda E, b=b, cs_=cs_: E.tensor_tensor(out=m2[b][:, 2, :], in0=gg[b][:, 1, :], in1=cs_,
                                                                        op=ALU.mult),
                         reads=[t_gg[b], t_s], writes=[t_m2[b]])
                    S.op("dve", lambda E, b=b, sn_=sn_: E.tensor_tensor(out=m2[b][:, 3, :], in0=gg[b][:, 0, :], in1=sn_,
                                                                        op=ALU.mult),
                         reads=[t_gg[b], t_s], writes=[t_m2[b]])
                    S.op("pool", lambda E, b=b, hb=hb, jj=jj: E.tensor_tensor(
                        out=hh[hb][:, jj, 0, :], in0=m2[b][:, 0, :], in1=m2[b][:, 1, :], op=ALU.subtract),
                        reads=[t_m2[b]], writes=[t_hh[hb][jj]])
                    S.op("pool", lambda E, b=b, hb=hb, jj=jj: E.tensor_tensor(
                        out=hh[hb][:, jj, 1, :], in0=m2[b][:, 2, :], in1=m2[b][:, 3, :], op=ALU.add),
                        reads=[t_m2[b]], writes=[t_hh[hb][jj]])
                yb_ = iy % 2
                iy += 1
                n_mm = 0
                for jj in range(4):
                    cmb = d * 8 + ut * 4 + jj
                    for ri in range(2):
                        S.op("pe", lambda E, yb_=yb_, cmb=cmb, ri=ri, hb=hb, jj=jj, n_mm=n_mm: E.matmul(
                            pY[yb_], lhsT=CT[:, cmb, ri, :], rhs=hh[hb][:, jj, ri, :], start=(n_mm == 0),
                            stop=(n_mm == 7)),
                            reads=[t_s, t_hh[hb][jj]], writes=[t_pY[yb_]])
                        n_mm += 1
                if d == 1:
                    S.op("act", lambda E, yb_=yb_, ut=ut, tsl=tsl: E.activation(out=ybwd[:, ut, tsl], in_=pY[yb_],
                                                                               func=AF.Copy),
                         reads=[t_pY[yb_]], writes=[t_yb[chn]])
                else:
                    S.op("dve", lambda E, yb_=yb_, ut=ut, tsl=tsl: E.tensor_tensor(
                        out=yv[yb_][:], in0=pY[yb_], in1=ybwd[:, ut, tsl], op=ALU.add),
                        reads=[t_pY[yb_], t_yb[chn]], writes=[t_yv[yb_]])
                    S.op("dve", lambda E, yb_=yb_, ut=ut, ufb=ufb: E.scalar_tensor_tensor(
                        out=yv[yb_][:], in0=uf[ufb][:, ut, :], scalar=dk[:, ut:ut + 1], in1=yv[yb_][:], op0=ALU.mult,
                        op1=ALU.add), reads=[t_uf[ufb], t_s, t_yv[yb_]], writes=[t_yv[yb_]])
                    S.op("pool", lambda E, yb_=yb_: E.tensor_tensor(out=y2[yb_][:], in0=yv[yb_][:], in1=yv[yb_][:],
                                                                    op=ALU.mult),
                         reads=[t_yv[yb_]], writes=[t_y2[yb_]])
                    S.op("pool", lambda E, yb_=yb_: E.tensor_scalar(out=y2[yb_][:], in0=y2[yb_][:], scalar1=0.044715,
                                                                    scalar2=1.0, op0=ALU.mult, op1=ALU.add),
                         reads=[t_y2[yb_]], writes=[t_y2[yb_]])
                    S.op("pool", lambda E, yb_=yb_: E.tensor_tensor(out=y2[yb_][:], in0=y2[yb_][:], in1=yv[yb_][:],
                                                                    op=ALU.mult),
                         reads=[t_y2[yb_], t_yv[yb_]], writes=[t_y2[yb_]])
                    S.op("act", lambda E, yb_=yb_: E.activation(out=y2[yb_][:], in_=y2[yb_][:], func=AF.Sigmoid,
                                                                scale=1.5957691216057308),
                         reads=[t_y2[yb_]], writes=[t_y2[yb_]])
                    gb_ = ci % 2
                    S.op("dve", lambda E, yb_=yb_, gb_=gb_, ut=ut: E.tensor_tensor(
                        out=ygl[gb_][:, ut, :], in0=y2[yb_][:], in1=yv[yb_][:], op=ALU.mult),
                        reads=[t_y2[yb_], t_yv[yb_]], writes=[t_yg[gb_][ut]])
            if d == 0:
                gb_ = ci % 2
                for o in range(2):
                    for half in range(2):
                        oc = o + 2 * half
                        zb = half
                        for ut in range(2):
                            S.op("pe", lambda E, zb=zb, oc=oc, ut=ut, gb_=gb_: E.matmul(
                                pZ[zb], lhsT=gw[:, ut, oc * 128:(oc + 1) * 128], rhs=ygl[gb_][:, ut, :],
                                start=(ut == 0), stop=(ut == 1)),
                                reads=[t_s, t_yg[gb_][ut]], writes=[t_pZ[zb]])
                    ob_ = o
                    S.op("act", lambda E, ob_=ob_, o=o: E.activation(out=sg[ob_][:], in_=pZ[1], func=AF.Sigmoid,
                                                                     bias=gb[:, 2 + o:3 + o]),
                         reads=[t_pZ[1], t_s], writes=[t_sg[ob_]])
                    S.op("dve", lambda E, ob_=ob_, o=o: E.scalar_tensor_tensor(
                        out=oo[ob_][:], in0=pZ[0], scalar=gb[:, o:o + 1], in1=sg[ob_][:], op0=ALU.add, op1=ALU.mult),
                        reads=[t_pZ[0], t_sg[ob_], t_s], writes=[t_oo[ob_]])
                    S.dma("sp", mixT[768 + 128 * o:768 + 128 * o + 128, lo:hi], oo[ob_][:], reads=[t_oo[ob_]],
                          writes=[t_out])
    P.close()


DEC = 0.6065306597126334
RWKV_DBG = [9]


def phase_rwkv(nc, S, zr, ydir, bon, gfm, prm, T):
    mu_p, mu_n, w0, w2, a0, a2, g2, k_k, k_a, r_k = prm
    P = Phase(nc, S)
    NBLK = T // 512
    ident, identf, t_id = make_ident(nc, S, P)
    t_c = Tok()

    def cst(fn, eng="dve"):
        S.op(eng, fn, reads=[t_c, t_id], writes=[t_c])
    onesb = P.sb([128, 128], F32)
    cst(lambda E: E.memset(onesb[:], 0.0))
    cst(lambda E: E.memset(onesb[0:64, 0:64], 1.0))
    cst(lambda E: E.memset(onesb[64:128, 64:128], 1.0))
    mskU = P.sb([128, 512], F32)
    mskL = P.sb([128, 3, 128], F32)
    cst(lambda E: E.memset(mskU[:], 1.0), "pool")
    cst(lambda E: E.memset(mskL[:], 1.0), "pool")
    for i in range(4):
        cmp_ = ALU.is_gt if i % 2 == 0 else ALU.is_ge
        cst(lambda E, i=i, cmp_=cmp_: E.affine_select(out=mskU[:, 128 * i:128 * i + 128], in_=mskU[:, 128 * i:128 * i + 128],
                                                      pattern=[[1, 128]], base=0, channel_multiplier=-1, compare_op=cmp_,
                                                      fill=0.0), "pool")
    for i in range(3):
        cst(lambda E, i=i: E.affine_select(out=mskL[:, i, :], in_=mskL[:, i, :], pattern=[[-1, 128]], base=0,
                                           channel_multiplier=1, compare_op=ALU.is_gt, fill=0.0), "pool")
    rst = P.sb([128, 512], F32)
    cst(lambda E: E.memset(rst[:], 1.0))
    for q in range(4):
        cst(lambda E, q=q: E.memset(rst[:, 128 * q:128 * q + 1], 0.0))
    mh512 = P.sb([128, 512], F32)
    cst(lambda E: E.memset(mh512[:], -0.5), "pool")
    cmu = P.sb([128, 3, 11], F32)
    S.dma("sp", cmu[:, 1, :], mu_p.rearrange("(c p) -> p c", p=128), writes=[t_c], allow_slow_non_contiguous=True)
    S.dma("sp", cmu[:, 2, :], mu_n.rearrange("(c p) -> p c", p=128), writes=[t_c], allow_slow_non_contiguous=True)
    cst(lambda E: E.tensor_tensor(out=cmu[:, 0, :], in0=cmu[:, 1, :], in1=cmu[:, 2, :], op=ALU.add))
    cst(lambda E: E.tensor_scalar(out=cmu[:, 0, :], in0=cmu[:, 0, :], scalar1=-1.0, scalar2=1.0, op0=ALU.mult, op1=ALU.add))
    w0c = P.sb([128, 2, 3], F32); a0c = P.sb([128, 2, 3], F32)
    for d in range(2):
        S.dma("sp", w0c[:, d, :], w0[d].rearrange("(c p) -> p c", p=128), writes=[t_c], allow_slow_non_contiguous=True)
        S.dma("sp", a0c[:, d, :], a0[d].rearrange("(c p) -> p c", p=128), writes=[t_c], allow_slow_non_contiguous=True)
    kkc = P.sb([128, 3], F32); kac = P.sb([128, 3], F32); omka = P.sb([128, 3], F32); rkc = P.sb([128, 3], F32)
    S.dma("sp", kkc[:], k_k.rearrange("(c p) -> p c", p=128), writes=[t_c], allow_slow_non_contiguous=True)
    S.dma("sp", kac[:], k_a.rearrange("(c p) -> p c", p=128), writes=[t_c], allow_slow_non_contiguous=True)
    S.dma("sp", rkc[:], r_k.rearrange("h k -> (h k)").rearrange("(c p) -> p c", p=128), writes=[t_c],
          allow_slow_non_contiguous=True)
    cst(lambda E: E.tensor_scalar(out=omka[:], in0=kac[:], scalar1=-1.0, scalar2=1.0, op0=ALU.mult, op1=ALU.add))
    w2a2 = P.sb([128, 2, 384], BF16)
    for d in range(2):
        S.dma("pool", w2a2[0:64, d, :], w2[d], writes=[t_c])
        S.dma("pool", w2a2[64:128, d, :], a2[d], writes=[t_c])
    g2b = P.sb([128, 384], BF16)
    S.dma("pool", g2b[:], g2, writes=[t_c])

    banks = [P.ps([128, 512], F32) for _ in range(6)]
    t_bk = [Tok() for _ in range(6)]
    bkrr = [0]

    def getbank():
        i = bkrr[0] % 6
        bkrr[0] += 1
        return banks[i], t_bk[i]
    pTr = [P.ps([128, 4, 128], BF16) for _ in range(2)]; t_pTr = [Tok(), Tok()]
    trr = [0]

    NZS = 4
    zraw = P.sb([128, NZS, 514], F32); t_zraw = [Tok() for _ in range(NZS)]
    zsi = [0]
    zm = P.sb([128, 11, 512], F32); t_zm = [Tok() for _ in range(11)]
    tmp0 = [P.sb([128, 512], F32) for _ in range(2)]; t_tmp0 = [Tok(), Tok()]
    tmp1 = [P.sb([128, 512], F32) for _ in range(2)]; t_tmp1 = [Tok(), Tok()]
    tz = P.sb([128, 512], BF16); t_tz = Tok()
    sgz = P.sb([128, 512], BF16); t_sgz = Tok()
    B1 = [P.sb([128, 512], F32) for _ in range(3)]; tB1 = [Tok() for _ in range(3)]
    B2 = [P.sb([128, 512], F32) for _ in range(3)]; tB2 = [Tok() for _ in range(3)]
    B3 = [P.sb([128, 512], F32) for _ in range(3)]; tB3 = [Tok() for _ in range(3)]
    B4 = [P.sb([128, 512], F32) for _ in range(3)]; tB4 = [Tok() for _ in range(3)]
    B5 = [P.sb([128, 512], F32) for _ in range(3)]; tB5 = [Tok() for _ in range(3)]
    B6 = [P.sb([128, 512], F32) for _ in range(3)]; tB6 = [Tok() for _ in range(3)]
    B7 = [P.sb([128, 512], F32) for _ in range(3)]; tB7 = [Tok() for _ in range(3)]
    ARb = [P.sb([128, 4, 2, 128], BF16) for _ in range(3)]; t_AR = [Tok() for _ in range(3)]
    Bt = [P.sb([128, 512], BF16) for _ in range(3)]; t_Bt = [Tok() for _ in range(3)]
    Kt = [P.sb([128, 512], BF16) for _ in range(3)]; t_Kt = [Tok() for _ in range(3)]
    Bb = [P.sb([128, 512], BF16) for _ in range(3)]; t_Bb = [Tok() for _ in range(3)]
    Kb = [P.sb([128, 512], BF16) for _ in range(3)]; t_Kb = [Tok() for _ in range(3)]
    vb = [P.sb([128, 512], BF16) for _ in range(3)]; t_vb = [Tok() for _ in range(3)]
    WCt = P.sb([128, 3, 4], F32); t_WC = Tok()
    tm = [[P.sb([128, 4, 128], BF16) for _ in range(4)] for _ in range(3)]
    t_tm = [[Tok() for _ in range(4)] for _ in range(3)]
    stg = [P.sb([128, 512], F32) for _ in range(3)]; t_stg = [Tok() for _ in range(3)]
    sgi = [0]
    MP = [P.sb([128, 512], BF16) for _ in range(6)]; t_MP = [Tok() for _ in range(6)]
    MT = [P.sb([128, 3, 128], BF16) for _ in range(2)]; t_MT = [Tok(), Tok()]
    Pm = [[P.sb([128, 3, 128], BF16) for _ in range(2)] for _ in range(2)]; t_Pm = [[Tok(), Tok()], [Tok(), Tok()]]
    PmT = [[P.sb([128, 3, 128], BF16) for _ in range(2)] for _ in range(2)]; t_PmT = [[Tok(), Tok()], [Tok(), Tok()]]
    Tm = [[P.sb([128, 3, 128], BF16) for _ in range(2)] for _ in range(2)]; t_Tm = [[Tok(), Tok()], [Tok(), Tok()]]
    X0 = P.sb([128, 6, 64], BF16); t_X0 = Tok()
    Uv = P.sb([128, 6, 64], F32); t_Uv = Tok()
    Ahb = P.sb([128, 3, 128], BF16); t_Ah = Tok()
    Ub = P.sb([128, 6, 64], BF16); t_Ub = Tok()
    Sf = P.sb([128, 3, 64], F32); t_Sf = Tok()
    Sb = P.sb([128, 3, 64], BF16); t_Sb = Tok()
    ytm = [P.sb([128, 4, 384], F32) for _ in range(2)]; t_ytm = [Tok(), Tok()]
    t_out = Tok()
    zv = zr.rearrange("(c p) t -> p c t", p=128)

    for d in range(2 if RWKV_DBG[0] > -1 else 0):
        S.op("dve", lambda E: E.memset(Sf[:], 0.0), writes=[t_Sf])
        S.op("dve", lambda E: E.memset(Sb[:], 0.0), writes=[t_Sb])
        for bi in range(NBLK):
            if d == 0:
                lo, hi = 512 * bi, 512 * bi + 512
            else:
                lo, hi = T - 512 * (bi + 1), T - 512 * bi
            loc = (lambda ap: ap) if d == 0 else None
            s0 = 1 if lo == 0 else 0
            s1 = 513 if hi == T else 514
            osl = slice(0, 512) if d == 0 else rsl(0, 512)
            for c in range(11):
                zs = zsi[0] % NZS
                zsi[0] += 1
                b = c % 2
                if lo == 0:
                    S.op("pool", lambda E, zs=zs: E.memset(zraw[:, zs, 0:1], 0.0), writes=[t_zraw[zs]])
                if hi == T:
                    S.op("pool", lambda E, zs=zs: E.memset(zraw[:, zs, 513:514], 0.0), writes=[t_zraw[zs]])
                S.dma("sp", zraw[:, zs, s0:s1], zv[:, c, lo - 1 + s0:lo - 1 + s1], writes=[t_zraw[zs]])
                S.op("act", lambda E, c=c, b=b, zs=zs: E.activation(out=tmp0[b][:], in_=zraw[:, zs, 1:513], func=AF.Copy,
                                                                   scale=cmu[:, 0, c:c + 1]),
                     reads=[t_zraw[zs], t_c], writes=[t_tmp0[b]])
                S.op("dve", lambda E, c=c, b=b, zs=zs: E.scalar_tensor_tensor(out=tmp1[b][:], in0=zraw[:, zs, 0:512],
                                                                              scalar=cmu[:, 1, c:c + 1], in1=tmp0[b][:],
                                                                              op0=ALU.mult, op1=ALU.add),
                     reads=[t_zraw[zs], t_c, t_tmp0[b]], writes=[t_tmp1[b]])
                S.op("dve", lambda E, c=c, b=b, zs=zs, osl=osl: E.scalar_tensor_tensor(
                    out=zm[:, c, osl], in0=zraw[:, zs, 2:514], scalar=cmu[:, 2, c:c + 1], in1=tmp1[b][:],
                    op0=ALU.mult, op1=ALU.add),
                    reads=[t_zraw[zs], t_c, t_tmp1[b]], writes=[t_zm[c]])
            if RWKV_DBG[0] == 0:
                continue
            S.op("act", lambda E: E.activation(out=tz[0:64, :], in_=zm[0:64, 9, :], func=AF.Tanh),
                 reads=[t_zm[9]], writes=[t_tz])
            S.op("act", lambda E: E.activation(out=tz[64:128, :], in_=zm[64:128, 9, :], func=AF.Copy),
                 reads=[t_zm[9]], writes=[t_tz])
            if d == 0:
                S.op("act", lambda E: E.activation(out=sgz[:], in_=zm[:, 10, :], func=AF.Sigmoid),
                     reads=[t_zm[10]], writes=[t_sgz])
            v4 = lambda ap: ap.rearrange("p (q t) -> p q t", q=4)
            steps = []
            cb = {}

            def ST(f):
                steps.append(f)
            for c in range(3):
                cb[c] = dict(zr=zm[:, c, :], tr=t_zm[c], zk=zm[:, 3 + c, :], tk=t_zm[3 + c], zv=zm[:, 6 + c, :],
                             tv=t_zm[6 + c])

            def s_mm(c):
                X = cb[c]
                X["pW"], X["tpW"] = getbank()
                S.op("pe", lambda E: E.matmul(X["pW"][:], lhsT=w2a2[0:64, d, 128 * c:128 * c + 128], rhs=tz[0:64, :],
                                              start=True, stop=True), reads=[t_c, t_tz], writes=[X["tpW"]])
                X["pA"], X["tpA"] = getbank()
                S.op("pe", lambda E: E.matmul(X["pA"][:], lhsT=w2a2[64:128, d, 128 * c:128 * c + 128], rhs=tz[64:128, :],
                                              start=True, stop=True), reads=[t_c, t_tz], writes=[X["tpA"]])
            ST(s_mm)

            def s_sig(c):
                X = cb[c]
                S.op("act", lambda E: E.activation(out=B1[c][:], in_=X["pW"][:], func=AF.Sigmoid, bias=w0c[:, d, c:c + 1]),
                     reads=[X["tpW"], t_c], writes=[tB1[c]])
                S.op("act", lambda E: E.activation(out=B6[c][:], in_=X["pA"][:], func=AF.Sigmoid, bias=a0c[:, d, c:c + 1]),
                     reads=[X["tpA"], t_c], writes=[tB6[c]])
            ST(s_sig)

            def s_kkv(c):
                X = cb[c]
                S.op("dve", lambda E: E.tensor_scalar(out=B4[c][:], in0=X["zk"], scalar1=kkc[:, c:c + 1], scalar2=None,
                                                      op0=ALU.mult), reads=[X["tk"], t_c], writes=[tB4[c]])
                S.op("pool", lambda E: E.tensor_tensor(out=B5[c][:], in0=B4[c][:], in1=B4[c][:], op=ALU.mult),
                     reads=[tB4[c]], writes=[tB5[c]])
                X["pN"], X["tpN"] = getbank()
                S.op("pe", lambda E: E.matmul(X["pN"][:], lhsT=onesb[:], rhs=B5[c][:], start=True, stop=True),
                     reads=[t_c, tB5[c]], writes=[X["tpN"]])
            ST(s_kkv)

            def s_cls(c):
                S.op("dve", lambda E: E.tensor_tensor_scan(out=B2[c][:], data0=rst[:], data1=B1[c][:], initial=0.0,
                                                           op0=ALU.mult, op1=ALU.add),
                     reads=[tB1[c], t_c], writes=[tB2[c]])
                S.op("pool", lambda E: E.tensor_tensor(out=B1[c][:], in0=B2[c][:], in1=B1[c][:], op=ALU.subtract),
                     reads=[tB2[c]], writes=[tB1[c]])
            ST(s_cls)

            def s_exp(c):
                S.op("act", lambda E: E.activation(out=B3[c][:], in_=B2[c][:], func=AF.Exp, scale=DEC),
                     reads=[tB2[c]], writes=[tB3[c]])
                S.op("act", lambda E: E.activation(out=B2[c][:], in_=B2[c][:], func=AF.Exp, scale=-DEC),
                     reads=[tB2[c]], writes=[tB2[c]])
                S.op("act", lambda E: E.activation(out=B1[c][:], in_=B1[c][:], func=AF.Exp, scale=-DEC),
                     reads=[tB1[c]], writes=[tB1[c]])
            ST(s_exp)

            def s_rn(c):
                X = cb[c]
                S.op("dve", lambda E: E.tensor_scalar(out=B5[c][:], in0=X["pN"][:], scalar1=1e-12, scalar2=None,
                                                      op0=ALU.add), reads=[X["tpN"]], writes=[tB5[c]])
                S.op("act", lambda E: E.activation(out=B5[c][:], in_=B5[c][:], func=AF.Sqrt),
                     reads=[tB5[c]], writes=[tB5[c]])
                S.op("dve", lambda E: E.reciprocal(out=B5[c][:], in_=B5[c][:]),
                     reads=[tB5[c]], writes=[tB5[c]])
                S.op("dve", lambda E: E.tensor_tensor(out=B4[c][:], in0=B4[c][:], in1=B5[c][:], op=ALU.mult),
                     reads=[tB4[c], tB5[c]], writes=[tB4[c]])
                S.op("dve", lambda E: E.tensor_copy(out=WCt[:, c, :], in_=B2[c][:, 127:512:128]),
                     reads=[tB2[c]], writes=[t_WC])
            ST(s_rn)

            def s_kd(c):
                X = cb[c]
                S.op("dve", lambda E: E.tensor_scalar(out=B7[c][:], in0=B6[c][:], scalar1=kac[:, c:c + 1],
                                                      scalar2=omka[:, c:c + 1], op0=ALU.mult, op1=ALU.add),
                     reads=[tB6[c], t_c], writes=[tB7[c]])
                S.op("dve", lambda E: E.tensor_tensor(out=B7[c][:], in0=B7[c][:], in1=X["zk"], op=ALU.mult),
                     reads=[tB7[c], X["tk"]], writes=[tB7[c]])
                S.op("pool", lambda E: E.tensor_tensor(out=B6[c][:], in0=B4[c][:], in1=B6[c][:], op=ALU.mult),
                     reads=[tB4[c], tB6[c]], writes=[tB6[c]])
            ST(s_kd)

            def s_ar(c):
                X = cb[c]
                S.op("dve", lambda E: E.scalar_tensor_tensor(out=ARb[c][:, :, 0, :], in0=v4(B4[c][:]), scalar=-1.0,
                                                             in1=v4(B1[c][:]), op0=ALU.mult, op1=ALU.mult),
                     reads=[tB4[c], tB1[c]], writes=[t_AR[c]])
                S.op("dve", lambda E: E.tensor_tensor(out=ARb[c][:, :, 1, :], in0=v4(X["zr"]), in1=v4(B2[c][:]),
                                                      op=ALU.mult), reads=[X["tr"], tB2[c]], writes=[t_AR[c]])
                S.op("pool", lambda E: E.tensor_tensor(out=Bt[c][:], in0=B6[c][:], in1=B3[c][:], op=ALU.mult),
                     reads=[tB6[c], tB3[c]], writes=[t_Bt[c]])
                S.op("pool", lambda E: E.tensor_tensor(out=Kt[c][:], in0=B7[c][:], in1=B3[c][:], op=ALU.mult),
                     reads=[tB7[c], tB3[c]], writes=[t_Kt[c]])
                S.op("act", lambda E: E.activation(out=vb[c][:], in_=X["zv"], func=AF.Copy),
                     reads=[X["tv"]], writes=[t_vb[c]])
            ST(s_ar)

            def s_bb(c):
                wcb = WCt[:, c, :].unsqueeze(2).to_broadcast([128, 4, 128])
                S.op("dve", lambda E: E.tensor_tensor(out=v4(Bb[c][:]), in0=v4(Bt[c][:]), in1=wcb, op=ALU.mult),
                     reads=[t_Bt[c], t_WC], writes=[t_Bb[c]])
                S.op("dve", lambda E: E.tensor_tensor(out=v4(Kb[c][:]), in0=v4(Kt[c][:]), in1=wcb, op=ALU.mult),
                     reads=[t_Kt[c], t_WC], writes=[t_Kb[c]])
            ST(s_bb)

            def s_bonus(c):
                X = cb[c]
                S.op("dve", lambda E: E.scalar_tensor_tensor(out=B5[c][:], in0=X["zr"], scalar=rkc[:, c:c + 1],
                                                             in1=B7[c][:], op0=ALU.mult, op1=ALU.mult),
                     reads=[X["tr"], tB7[c], t_c], writes=[tB5[c]])
                pBn, t_pBn = getbank()
                S.op("pe", lambda E: E.matmul(pBn[:], lhsT=onesb[:], rhs=B5[c][:], start=True, stop=True),
                     reads=[t_c, tB5[c]], writes=[t_pBn])
                si = sgi[0] % 3
                sgi[0] += 1
                S.op("dve", lambda E: E.tensor_tensor(out=stg[si][:, osl], in0=pBn[:], in1=X["zv"], op=ALU.mult),
                     reads=[t_pBn, X["tv"]], writes=[t_stg[si]])
                S.dma("sp", bon[d][128 * c:128 * c + 128, lo:hi], stg[si][:], reads=[t_stg[si]], writes=[t_out])
                if d == 0:
                    pG, t_pG = getbank()
                    S.op("pe", lambda E: E.matmul(pG[:], lhsT=g2b[:, 128 * c:128 * c + 128], rhs=sgz[:], start=True,
                                                  stop=True), reads=[t_c, t_sgz], writes=[t_pG])
                    si2 = sgi[0] % 3
                    sgi[0] += 1
                    S.op("act", lambda E: E.activation(out=stg[si2][:], in_=pG[:], func=AF.Copy),
                         reads=[t_pG], writes=[t_stg[si2]])
                    S.dma("sp", gfm[128 * c:128 * c + 128, lo:hi], stg[si2][:], reads=[t_stg[si2]], writes=[t_out])
            ST(s_bonus)

            def s_tr(c):
                for q in range(4):
                    ts_ = trr[0] % 2
                    trr[0] += 1
                    srcs = [(ARb[c][:, q, 0, :], t_AR[c]), (Bb[c][:, 128 * q:128 * q + 128], t_Bb[c]),
                            (Kb[c][:, 128 * q:128 * q + 128], t_Kb[c]), (vb[c][:, 128 * q:128 * q + 128], t_vb[c])]
                    for ai, (src, tk) in enumerate(srcs):
                        S.op("pe", lambda E, ts_=ts_, ai=ai, src=src: E.transpose(out=pTr[ts_][:, ai, :], in_=src,
                                                                                 identity=ident[:]),
                             reads=[tk, t_id], writes=[t_pTr[ts_]])
                    if q % 2 == 0:
                        S.op("act", lambda E, ts_=ts_, q=q: E.activation(out=tm[c][q][:], in_=pTr[ts_][:], func=AF.Copy),
                             reads=[t_pTr[ts_]], writes=[t_tm[c][q]])
                    else:
                        S.op("dve", lambda E, ts_=ts_, q=q: E.tensor_copy(out=tm[c][q][:], in_=pTr[ts_][:]),
                             reads=[t_pTr[ts_]], writes=[t_tm[c][q]])
            ST(s_tr)
            for st_ in steps:
                for c in range(3):
                    st_(c)
            yb = bi % 2
            for q in range(4 if RWKV_DBG[0] >= 2 else 0):
                qs = slice(128 * q, 128 * q + 128)
                for h in range(6):
                    c, h2 = h // 2, h % 2
                    rows = slice(64 * h2, 64 * h2 + 64)
                    pM, t_pM = getbank()
                    S.op("pe", lambda E, pM=pM, c=c, rows=rows, qs=qs, q=q: E.matmul(
                        pM[:, 0:256], lhsT=Bt[c][rows, qs], rhs=ARb[c][rows, q, :, :].rearrange("p a b -> p (a b)"), start=True, stop=True),
                        reads=[t_Bt[c], t_AR[c]], writes=[t_pM])
                    S.op("pe", lambda E, pM=pM, c=c, rows=rows, qs=qs, q=q: E.matmul(
                        pM[:, 256:512], lhsT=Kt[c][rows, qs], rhs=ARb[c][rows, q, :, :].rearrange("p a b -> p (a b)"), start=True, stop=True),
                        reads=[t_Kt[c], t_AR[c]], writes=[t_pM])
                    S.op("dve", lambda E, pM=pM, h=h: E.tensor_tensor(out=MP[h][:], in0=pM[:], in1=mskU[:], op=ALU.mult),
                         reads=[t_pM, t_c], writes=[t_MP[h]])
                for hg in range(2):
                    pM3, t_pM3 = getbank()
                    for j in range(3):
                        h = 2 * j + hg
                        c, h2 = j, hg
                        rows = slice(64 * h2, 64 * h2 + 64)
                        S.op("pe", lambda E, pM3=pM3, j=j, c=c, rows=rows, qs=qs, q=q: E.matmul(
                            pM3[:, 128 * j:128 * j + 128], lhsT=ARb[c][rows, q, 0, :], rhs=Bt[c][rows, qs],
                            start=True, stop=True), reads=[t_Bt[c], t_AR[c]], writes=[t_pM3])
                    S.op("dve", lambda E, pM3=pM3, hg=hg: E.tensor_tensor(
                        out=MT[hg][:], in0=pM3[:, 0:384].rearrange("p (a b) -> p a b", a=3), in1=mskL[:], op=ALU.mult),
                        reads=[t_pM3, t_c], writes=[t_MT[hg]])
                if RWKV_DBG[0] < 3:
                    continue
                cur = [0, 0]
                for hg in range(2):
                    for j in range(3):
                        h = 2 * j + hg
                        S.op("pool", lambda E, hg=hg, j=j, h=h: E.tensor_tensor(out=Tm[hg][0][:, j, :], in0=MP[h][:, 0:128],
                                                                              in1=identf[:], op=ALU.add),
                             reads=[t_MP[h], t_id], writes=[t_Tm[hg][0]])
                for lev in range(1, 7):
                    for hg in range(2):
                        pv = cur[hg]
                        nx = 1 - pv
                        pP, t_pP = getbank()
                        pPT, t_pPT = getbank()
                        for j in range(3):
                            h = 2 * j + hg
                            if lev == 1:
                                Pprev, tP = MP[h][:, 0:128], t_MP[h]
                                PTprev, tPT = MT[hg][:, j, :], t_MT[hg]
                            else:
                                Pprev, tP = Pm[hg][pv][:, j, :], t_Pm[hg][pv]
                                PTprev, tPT = PmT[hg][pv][:, j, :], t_PmT[hg][pv]
                            if lev < 6:
                                S.op("pe", lambda E, pP=pP, j=j, Pprev=Pprev, PTprev=PTprev: E.matmul(
                                    pP[:, 128 * j:128 * j + 128], lhsT=PTprev, rhs=Pprev, start=True, stop=True),
                                    reads=[tP, tPT], writes=[t_pP])
                            S.op("pe", lambda E, pPT=pPT, j=j, Pprev=Pprev, PTprev=PTprev: E.matmul(
                                pPT[:, 128 * j:128 * j + 128], lhsT=Pprev, rhs=PTprev, start=True, stop=True),
                                reads=[tP, tPT], writes=[t_pPT])
                        v3 = lambda ap: ap[:, 0:384].rearrange("p (a b) -> p a b", a=3)
                        if lev < 6:
                            S.op("act", lambda E, pP=pP, hg=hg, nx=nx: E.activation(out=Pm[hg][nx][:], in_=v3(pP),
                                                                                    func=AF.Copy),
                                 reads=[t_pP], writes=[t_Pm[hg][nx]])
                        S.op("dve", lambda E, pPT=pPT, hg=hg, nx=nx: E.tensor_copy(out=PmT[hg][nx][:], in_=v3(pPT)),
                             reads=[t_pPT], writes=[t_PmT[hg][nx]])
                        pTT, t_pTT = getbank()
                        for j in range(3):
                            S.op("pe", lambda E, pTT=pTT, j=j, hg=hg, nx=nx, pv=pv: E.matmul(
                                pTT[:, 128 * j:128 * j + 128], lhsT=PmT[hg][nx][:, j, :], rhs=Tm[hg][pv][:, j, :],
                                start=True, stop=True), reads=[t_PmT[hg][nx], t_Tm[hg][pv]], writes=[t_pTT])
                        S.op("dve", lambda E, pTT=pTT, hg=hg, nx=nx, pv=pv: E.tensor_tensor(
                            out=Tm[hg][nx][:], in0=v3(pTT), in1=Tm[hg][pv][:], op=ALU.add),
                            reads=[t_pTT, t_Tm[hg][pv]], writes=[t_Tm[hg][nx]])
                        cur[hg] = nx
                TF = [Tm[0][cur[0]], Tm[1][cur[1]]]
                tTF = [t_Tm[0][cur[0]], t_Tm[1][cur[1]]]
                if RWKV_DBG[0] < 4:
                    continue
                pX, t_pX = getbank()
                for h in range(6):
                    c, h2 = h // 2, h % 2
                    sl_ = 3 * h2 + c
                    S.op("pe", lambda E, pX=pX, h=h, c=c, h2=h2, q=q, sl_=sl_: E.matmul(
                        pX[:, 64 * sl_:64 * sl_ + 64], lhsT=MP[h][:, 256:384], rhs=tm[c][q][:, 3, 64 * h2:64 * h2 + 64],
                        start=True, stop=True), reads=[t_MP[h], t_tm[c][q]], writes=[t_pX])
                S.op("act", lambda E, pX=pX: E.activation(out=X0[:], in_=pX[:, 0:384].rearrange("p (a b) -> p a b", a=6),
                                                         func=AF.Copy), reads=[t_pX], writes=[t_X0])
                pV, t_pV = getbank()
                for h in range(6):
                    c, h2 = h // 2, h % 2
                    sl_ = 3 * h2 + c
                    S.op("pe", lambda E, pV=pV, c=c, h2=h2, sl_=sl_: E.matmul(
                        pV[:, 64 * sl_:64 * sl_ + 64], lhsT=TF[h2][:, c, :], rhs=X0[:, sl_, :], start=True, stop=True),
                        reads=[tTF[h2], t_X0], writes=[t_pV])
                S.op("act", lambda E, pV=pV: E.activation(out=Uv[:], in_=pV[:, 0:384].rearrange("p (a b) -> p a b", a=6),
                                                         func=AF.Copy), reads=[t_pV], writes=[t_Uv])
                pH, t_pH = getbank()
                for h in range(6):
                    c, h2 = h // 2, h % 2
                    S.op("pe", lambda E, pH=pH, c=c, h2=h2, q=q: E.matmul(
                        pH[64 * h2:64 * h2 + 64, 128 * c:128 * c + 128], lhsT=tm[c][q][:, 0, 64 * h2:64 * h2 + 64],
                        rhs=TF[h2][:, c, :], start=True, stop=True), reads=[tTF[h2], t_tm[c][q]], writes=[t_pH])
                S.op("dve", lambda E, pH=pH: E.tensor_copy(out=Ahb[:], in_=pH[:, 0:384].rearrange("p (a b) -> p a b", a=3)),
                     reads=[t_pH], writes=[t_Ah])
                if RWKV_DBG[0] < 5:
                    continue
                pUb = [getbank(), getbank()]
                for h2 in range(2):
                    rows = slice(64 * h2, 64 * h2 + 64)
                    for c in range(3):
                        S.op("pe", lambda E, h2=h2, c=c, rows=rows: E.matmul(
                            pUb[h2][0][:, 64 * c:64 * c + 64], lhsT=Ahb[rows, c, :], rhs=Sb[rows, c, :], start=True,
                            stop=True), reads=[t_Ah, t_Sb], writes=[pUb[h2][1]])
                for h2 in range(2):
                    S.op("dve", lambda E, h2=h2: E.tensor_tensor(
                        out=Ub[:, 3 * h2:3 * h2 + 3, :], in0=pUb[h2][0][:, 0:192].rearrange("p (a b) -> p a b", a=3),
                        in1=Uv[:, 3 * h2:3 * h2 + 3, :], op=ALU.add),
                        reads=[pUb[h2][1], t_Uv], writes=[t_Ub])
                pYb = [getbank(), getbank()]
                for h2 in range(2):
                    rows = slice(64 * h2, 64 * h2 + 64)
                    for c in range(3):
                        h = 2 * c + h2
                        sl_ = 3 * h2 + c
                        oc = slice(64 * c, 64 * c + 64)
                        S.op("pe", lambda E, h2=h2, c=c, rows=rows, q=q, oc=oc: E.matmul(
                            pYb[h2][0][:, oc], lhsT=ARb[c][rows, q, 1, :], rhs=Sb[rows, c, :], start=True, stop=False),
                            reads=[t_AR[c], t_Sb], writes=[pYb[h2][1]])
                        S.op("pe", lambda E, h2=h2, h=h, sl_=sl_, oc=oc: E.matmul(
                            pYb[h2][0][:, oc], lhsT=MP[h][:, 128:256], rhs=Ub[:, sl_, :], start=False, stop=False),
                            reads=[t_MP[h], t_Ub], writes=[pYb[h2][1]])
                        S.op("pe", lambda E, h2=h2, h=h, c=c, q=q, oc=oc: E.matmul(
                            pYb[h2][0][:, oc], lhsT=MP[h][:, 384:512], rhs=tm[c][q][:, 3, 64 * h2:64 * h2 + 64],
                            start=False, stop=True), reads=[t_MP[h], t_tm[c][q]], writes=[pYb[h2][1]])
                for h2 in range(2):
                    S.op("act", lambda E, h2=h2, yb=yb, q=q: E.activation(
                        out=ytm[yb][:, q, :].rearrange("p (c g v) -> p g c v", g=2, v=64)[:, h2, :, :],
                        in_=pYb[h2][0][:, 0:192].rearrange("p (a b) -> p a b", a=3), func=AF.Copy),
                        reads=[pYb[h2][1]], writes=[t_ytm[yb]])
                pS_, t_pS = getbank()
                for h in range(6):
                    c, h2 = h // 2, h % 2
                    orow = slice(64 * h2, 64 * h2 + 64)
                    S.op("pe", lambda E, pS_=pS_, h=h, c=c, h2=h2, orow=orow, q=q: E.matmul(
                        pS_[orow, 64 * c:64 * c + 64], lhsT=tm[c][q][:, 1, 64 * h2:64 * h2 + 64], rhs=Ub[:, 3 * h2 + c, :],
                        start=True, stop=False), reads=[t_tm[c][q], t_Ub], writes=[t_pS])
                    S.op("pe", lambda E, pS_=pS_, h=h, c=c, h2=h2, orow=orow, q=q: E.matmul(
                        pS_[orow, 64 * c:64 * c + 64], lhsT=tm[c][q][:, 2, 64 * h2:64 * h2 + 64],
                        rhs=tm[c][q][:, 3, 64 * h2:64 * h2 + 64], start=False, stop=True),
                        reads=[t_tm[c][q]], writes=[t_pS])
                for c in range(3):
                    S.op("dve", lambda E, pS_=pS_, c=c, q=q: E.scalar_tensor_tensor(
                        out=Sf[:, c, :], in0=Sf[:, c, :], scalar=WCt[:, c, q:q + 1], in1=pS_[:, 64 * c:64 * c + 64],
                        op0=ALU.mult, op1=ALU.add), reads=[t_pS, t_WC, t_Sf], writes=[t_Sf])
                S.op("act", lambda E: E.activation(out=Sb[:], in_=Sf[:], func=AF.Copy), reads=[t_Sf], writes=[t_Sb])
            lb = 512 * bi
            S.dma("sp", ydir[d][lb:lb + 512, :].rearrange("(q p) f -> p q f", p=128), ytm[yb][:], reads=[t_ytm[yb]],
                  writes=[t_out])
    P.close()


LNX_EPS = 64e-5


def phase_rwkv_combine(nc, S, ydir, bon, gfm, lnx_w, lnx_b, mixT, T):
    P = Phase(nc, S)
    NT = T // 128
    ident, identf, t_id = make_ident(nc, S, P)
    t_c = Tok()
    J = P.sb([128, 128], F32)
    S.op("pool", lambda E: E.memset(J[:], 1.0), writes=[t_c])
    S.op("pool", lambda E: E.affine_select(out=J[:], in_=J[:], pattern=[[1, 128]], base=-127, channel_multiplier=1,
                                           compare_op=ALU.is_equal, fill=0.0), reads=[t_c], writes=[t_c])
    lwt = P.sb([128, 384], F32); lbt = P.sb([128, 384], F32)
    S.dma("sp", lwt[:], lnx_w.partition_broadcast(128), writes=[t_c])
    S.dma("sp", lbt[:], lnx_b.partition_broadcast(128), writes=[t_c])
    mh = P.sb([128, 6], F32)
    S.op("pool", lambda E: E.memset(mh[:], -0.5), writes=[t_c])
    NBF = 2
    y0 = [P.sb([128, 384], F32) for _ in range(NBF)]; t_y0 = [Tok() for _ in range(NBF)]
    y1 = [P.sb([128, 384], F32) for _ in range(NBF)]; t_y1 = [Tok() for _ in range(NBF)]
    b0 = [P.sb([128, 3, 128], F32) for _ in range(NBF)]; t_b0 = [Tok() for _ in range(NBF)]
    b1 = [P.sb([128, 3, 128], F32) for _ in range(NBF)]; t_b1 = [Tok() for _ in range(NBF)]
    gt = [P.sb([128, 3, 128], F32) for _ in range(NBF)]; t_gt = [Tok() for _ in range(NBF)]
    ys = [P.sb([128, 6, 64], F32) for _ in range(NBF)]; t_ys = [Tok() for _ in range(NBF)]
    sq = [P.sb([128, 6, 64], F32) for _ in range(NBF)]; t_sq = [Tok() for _ in range(NBF)]
    st = [P.sb([128, 4, 6], F32) for _ in range(NBF)]; t_st = [Tok() for _ in range(NBF)]
    rs = [P.sb([128, 384], F32) for _ in range(NBF)]; t_rs = [Tok() for _ in range(NBF)]
    ob = [P.sb([128, 3, 128], BF16) for _ in range(NBF)]; t_ob = [Tok() for _ in range(NBF)]
    pJ = [P.ps([128, 512], F32) for _ in range(2)]; t_pJ = [Tok(), Tok()]
    pB = [P.ps([128, 512], F32) for _ in range(2)]; t_pB = [Tok(), Tok()]
    pG = [P.ps([128, 512], F32) for _ in range(2)]; t_pG = [Tok(), Tok()]
    pO = [P.ps([128, 512], F32) for _ in range(2)]; t_pO = [Tok(), Tok()]
    t_out = Tok()
    f3 = lambda ap: ap.rearrange("p (a b) -> p a b", a=6)
    for n in range(NT):
        b = n % NBF
        tl = slice(128 * n, 128 * n + 128)
        S.dma("sp", y0[b][:], ydir[0][128 * n:128 * n + 128, :], writes=[t_y0[b]])
        S.dma("sp", y1[b][:], ydir[1][T - 128 * (n + 1):T - 128 * n, :], writes=[t_y1[b]])
        S.dma("sp", b0[b][:], bon[0][:, tl].rearrange("(c p) t -> p c t", p=128), writes=[t_b0[b]])
        S.dma("sp", b1[b][:], bon[1][:, tl].rearrange("(c p) t -> p c t", p=128), writes=[t_b1[b]])
        S.dma("sp", gt[b][:], gfm[:, tl].rearrange("(c p) t -> p c t", p=128), writes=[t_gt[b]])
        S.op("pe", lambda E, b=b: E.matmul(pJ[b][:, 0:384], lhsT=J[:], rhs=y1[b][:], start=True, stop=True),
             reads=[t_c, t_y1[b]], writes=[t_pJ[b]])
        S.op("dve", lambda E, b=b: E.tensor_tensor(out=ys[b][:], in0=f3(pJ[b][:, 0:384]), in1=f3(y0[b][:]), op=ALU.add),
             reads=[t_pJ[b], t_y0[b]], writes=[t_ys[b]])
        S.op("dve", lambda E, b=b: E.tensor_reduce(out=st[b][:, 0, :], in_=ys[b][:], axis=AX.X, op=ALU.add),
             reads=[t_ys[b]], writes=[t_st[b]])
        S.op("dve", lambda E, b=b: E.tensor_scalar(out=st[b][:, 0, :], in0=st[b][:, 0, :], scalar1=1.0 / 64, scalar2=None,
                                                   op0=ALU.mult), reads=[t_st[b]], writes=[t_st[b]])
        S.op("dve", lambda E, b=b: E.tensor_tensor(out=ys[b][:], in0=ys[b][:],
                                                   in1=st[b][:, 0, :].unsqueeze(2).to_broadcast([128, 6, 64]),
                                                   op=ALU.subtract), reads=[t_st[b], t_ys[b]], writes=[t_ys[b]])
        S.op("pool", lambda E, b=b: E.tensor_tensor(out=sq[b][:], in0=ys[b][:], in1=ys[b][:], op=ALU.mult),
             reads=[t_ys[b]], writes=[t_sq[b]])
        S.op("dve", lambda E, b=b: E.tensor_reduce(out=st[b][:, 1, :], in_=sq[b][:], axis=AX.X, op=ALU.add),
             reads=[t_sq[b]], writes=[t_st[b]])
        S.op("dve", lambda E, b=b: E.tensor_scalar(out=st[b][:, 2, :], in0=st[b][:, 1, :], scalar1=1.0 / 64,
                                                   scalar2=LNX_EPS, op0=ALU.mult, op1=ALU.add),
             reads=[t_st[b]], writes=[t_st[b]])
        S.op("pool", lambda E, b=b: E.tensor_tensor(out=st[b][:, 3, :], in0=st[b][:, 2, :], in1=mh[:], op=ALU.pow),
             reads=[t_st[b], t_c], writes=[t_st[b]])
        S.op("dve", lambda E, b=b: E.tensor_tensor(out=ys[b][:], in0=ys[b][:],
                                                   in1=st[b][:, 3, :].unsqueeze(2).to_broadcast([128, 6, 64]),
                                                   op=ALU.mult), reads=[t_st[b], t_ys[b]], writes=[t_ys[b]])
        yf = ys[b][:].rearrange("p a b -> p (a b)")
        S.op("pool", lambda E, b=b, yf=yf: E.tensor_tensor(out=yf, in0=yf, in1=lwt[:], op=ALU.mult),
             reads=[t_ys[b], t_c], writes=[t_ys[b]])
        S.op("pool", lambda E, b=b, yf=yf: E.tensor_tensor(out=yf, in0=yf, in1=lbt[:], op=ALU.add),
             reads=[t_ys[b], t_c], writes=[t_ys[b]])
        S.op("dve", lambda E, b=b: E.tensor_tensor(out=b0[b][:], in0=b0[b][:], in1=b1[b][:], op=ALU.add),
             reads=[t_b0[b], t_b1[b]], writes=[t_b0[b]])
        for c in range(3):
            S.op("pe", lambda E, b=b, c=c: E.transpose(out=pB[b][:, 128 * c:128 * c + 128], in_=b0[b][:, c, :],
                                                       identity=identf[:]),
                 reads=[t_b0[b], t_id], writes=[t_pB[b]])
        for c in range(3):
            S.op("pe", lambda E, b=b, c=c: E.transpose(out=pG[b][:, 128 * c:128 * c + 128], in_=gt[b][:, c, :],
                                                       identity=identf[:]),
                 reads=[t_gt[b], t_id], writes=[t_pG[b]])
        S.op("dve", lambda E, b=b, yf=yf: E.tensor_tensor(out=rs[b][:], in0=pB[b][:, 0:384], in1=yf, op=ALU.add),
             reads=[t_pB[b], t_ys[b]], writes=[t_rs[b]])
        S.op("dve", lambda E, b=b: E.tensor_tensor(out=rs[b][:], in0=pG[b][:, 0:384], in1=rs[b][:], op=ALU.mult),
             reads=[t_pG[b], t_rs[b]], writes=[t_rs[b]])
        for c in range(3):
            S.op("pe", lambda E, b=b, c=c: E.transpose(out=pO[b][:, 128 * c:128 * c + 128],
                                                       in_=rs[b][:, 128 * c:128 * c + 128], identity=identf[:]),
                 reads=[t_rs[b], t_id], writes=[t_pO[b]])
        S.op("act", lambda E, b=b: E.activation(out=ob[b][:], in_=pO[b][:, 0:384].rearrange("p (a b) -> p a b", a=3),
                                                func=AF.Copy), reads=[t_pO[b]], writes=[t_ob[b]])
        S.dma("sp", mixT[0:384, tl].rearrange("(c p) t -> p c t", p=128), ob[b][:], reads=[t_ob[b]], writes=[t_out])
    P.close()


PARAM_SHAPES = {
    "ffn1_norm_g": [2, 1024], "ffn1_w_gate": [2, 1024, 2816], "ffn1_w_up": [2, 1024, 2816],
    "ffn1_w_down": [2, 2816, 1024], "mix_norm_g": [2, 1024], "w_in": [2, 1024, 2816], "w_out": [2, 1024, 1024],
    "rwkv_mu_prev": [2, 1408], "rwkv_mu_next": [2, 1408], "rwkv_decay_w0": [2, 2, 384],
    "rwkv_decay_w2": [2, 2, 64, 384], "rwkv_iclr_a0": [2, 2, 384], "rwkv_iclr_a2": [2, 2, 64, 384],
    "rwkv_gate_w2": [2, 128, 384], "rwkv_k_k": [2, 384], "rwkv_k_a": [2, 384], "rwkv_r_k": [2, 6, 64],
    "rwkv_lnx_w": [2, 384], "rwkv_lnx_b": [2, 384], "s5_a_re": [2, 2, 16, 64], "s5_a_im": [2, 2, 16, 64],
    "s5_log_step": [2, 2, 16], "s5_b_re": [2, 16, 64, 16], "s5_b_im": [2, 16, 64, 16],
    "s5_c_re": [2, 2, 16, 16, 64], "s5_c_im": [2, 2, 16, 16, 64], "s5_d": [2, 256], "s5_glu_w": [2, 256, 512],
    "s5_glu_b": [2, 512], "ffn2_norm_g": [2, 1024], "ffn2_w_gate": [2, 1024, 2816], "ffn2_w_up": [2, 1024, 2816],
    "ffn2_w_down": [2, 2816, 1024], "final_norm_g": [1024],
}
DEPTH = 2


def build_program(T, depth=DEPTH):
    nc = bass.Bass("TRN2", target_bir_lowering=False)
    x = nc.dram_tensor("x", [T, D], F32, kind="ExternalInput").ap()
    p = {k: nc.dram_tensor(k, list(s), F32, kind="ExternalInput").ap() for k, s in PARAM_SHAPES.items()}
    out = nc.dram_tensor("out", [T, D], F32, kind="ExternalOutput").ap()
    h = nc.dram_tensor("h_res", [T, D], F32).ap()
    zr = nc.dram_tensor("z_rwkv", [1408, T], F32).ap()
    qk = nc.dram_tensor("z_qk", [768, T], BF16).ap()
    vtm = nc.dram_tensor("z_v", [T, 384], BF16).ap()
    us5 = nc.dram_tensor("z_s5", [256, T], F32).ap()
    mixT = nc.dram_tensor("mixT", [1024, T], BF16).ap()
    ydir = nc.dram_tensor("y_dir", [2, T, 384], F32).ap()
    bon = nc.dram_tensor("bonus", [2, 384, T], F32).ap()
    gfm = nc.dram_tensor("gate", [384, T], F32).ap()
    S = SchedI(nc)
    NT = T // 128
    tk = [Tok() for _ in range(NT)]
    for l in range(depth):
        src = x if l == 0 else h
        tk2 = [Tok() for _ in range(NT)]
        phase_ffn(nc, S, src, h, p["ffn1_norm_g"][l], p["ffn1_w_gate"][l], p["ffn1_w_up"][l], p["ffn1_w_down"][l],
                  tk, tk2, T)
        tk = tk2
        phase_win(nc, S, h, p["mix_norm_g"][l], p["w_in"][l], zr, qk, vtm, us5, tk, T)
        phase_rwkv(nc, S, zr, ydir, bon, gfm,
                   [p["rwkv_mu_prev"][l], p["rwkv_mu_next"][l], p["rwkv_decay_w0"][l], p["rwkv_decay_w2"][l],
                    p["rwkv_iclr_a0"][l], p["rwkv_iclr_a2"][l], p["rwkv_gate_w2"][l], p["rwkv_k_k"][l],
                    p["rwkv_k_a"][l], p["rwkv_r_k"][l]], T)
        phase_rwkv_combine(nc, S, ydir, bon, gfm, p["rwkv_lnx_w"][l], p["rwkv_lnx_b"][l], mixT, T)
        phase_attn(nc, S, qk, vtm, mixT, T)
        phase_s5(nc, S, us5, mixT,
                 [p["s5_a_re"][l], p["s5_a_im"][l], p["s5_log_step"][l], p["s5_b_re"][l], p["s5_b_im"][l],
                  p["s5_c_re"][l], p["s5_c_im"][l], p["s5_d"][l], p["s5_glu_w"][l], p["s5_glu_b"][l]], T)
        tk2 = [Tok() for _ in range(NT)]
        phase_wout(nc, S, h, h, mixT, p["w_out"][l], tk, tk2, T)
        tk = tk2
        tk2 = [Tok() for _ in range(NT)]
        phase_ffn(nc, S, h, h, p["ffn2_norm_g"][l], p["ffn2_w_gate"][l], p["ffn2_w_up"][l], p["ffn2_w_down"][l],
                  tk, tk2, T)
        tk = tk2
    tko = [Tok() for _ in range(NT)]
    phase_final(nc, S, h, out, p["final_norm_g"], tk, tko, T)
    S.finish()
    return nc, S


def kernel(**inputs):
    x = np.ascontiguousarray(np.asarray(inputs["x"], dtype=np.float32))
    B, T, _ = x.shape
    nc, S = build_program(T)
    params = {k: np.ascontiguousarray(np.asarray(inputs[k], dtype=np.float32)) for k in PARAM_SHAPES}
    in_maps = []
    for b in range(B):
        m = {"x": x[b]}
        m.update(params)
        in_maps.append(m)
    res = run_bass_kernel_spmd(nc, in_maps, core_ids=list(range(B)))
    return np.stack([np.asarray(r["out"], dtype=np.float32) for r in res.results], axis=0)
```

```python
import numpy as np
import concourse.bass as bass
import concourse.mybir as mybir
from concourse.bass_utils import run_bass_kernel_spmd

F32 = mybir.dt.float32
BF16 = mybir.dt.bfloat16
I32 = mybir.dt.int32
AF = mybir.ActivationFunctionType
ALU = mybir.AluOpType
AX = mybir.AxisListType

ENGS = ("pe", "act", "dve", "pool", "sp")


class Tok:
    __slots__ = ("w", "r", "name")

    def __init__(self, name=""):
        self.w = None
        self.r = {}
        self.name = name


class Sched:
    def __init__(self, nc, lanes_sp=8, lanes_pool=6, lanes_act=2, same_engine_sync=True):
        self.nc = nc
        self.ops = {e: [] for e in ENGS}
        self.cnt = {}
        self.sems = {}
        self.seen = {e: {} for e in ENGS}
        self.same = same_engine_sync
        self._ctx = []
        for e in ("pe", "act", "dve", "pool"):
            self._mksem(e)
        self.lanes = {"sp": [], "pool": [], "act": []}
        for q, n in (("sp", lanes_sp), ("pool", lanes_pool), ("act", lanes_act)):
            for i in range(n):
                nm = f"ln_{q}{i}"
                self._mksem(nm)
                self.lanes[q].append(nm)
        self.lane_rr = {"sp": 0, "pool": 0, "act": 0}
        self.n_instr = 0

    def _mksem(self, name):
        cm = self.nc.semaphore(name)
        s = cm.__enter__()
        self._ctx.append(cm)
        self.sems[name] = s
        self.cnt[name] = 0

    def _collect(self, eng, reads, writes):
        need = {}

        def add(src, val):
            if src == eng and (eng == "pe" or not self.same or eng == "sp"):
                return
            if need.get(src, 0) < val:
                need[src] = val
        for t in reads:
            if t.w is not None:
                add(*t.w)
        for t in writes:
            if t.w is not None:
                add(*t.w)
            for s, v in t.r.items():
                add(s, v)
        out = []
        seen = self.seen[eng]
        for s, v in need.items():
            if seen.get(s, 0) < v:
                seen[s] = v
                out.append((self.sems[s], v))
        return out

    def op(self, eng, fn, reads=(), writes=()):
        waits = self._collect(eng, reads, writes)
        self.cnt[eng] += 1
        c = self.cnt[eng]
        sem = self.sems[eng]

        def emit(E, waits=waits, fn=fn, sem=sem):
            for s, v in waits:
                E.wait_ge(s, v)
            fn(E).then_inc(sem, 1)
        self.ops[eng].append(emit)
        for t in reads:
            t.r[eng] = c
        for t in writes:
            t.w = (eng, c)
            t.r = {}
        self.n_instr += 1

    def dma(self, q, out, in_, reads=(), writes=(), **kw):
        lanes = self.lanes[q]
        ln = lanes[self.lane_rr[q] % len(lanes)]
        self.lane_rr[q] += 1
        waits = self._collect(q, reads, writes)
        prev = self.cnt[ln]
        if prev and self.seen[q].get(ln, 0) < prev:
            self.seen[q][ln] = prev
            waits.append((self.sems[ln], prev))
        self.cnt[ln] += 16
        c = self.cnt[ln]
        sem = self.sems[ln]

        def emit(E, waits=waits, sem=sem, out=out, in_=in_, kw=kw):
            for s, v in waits:
                E.wait_ge(s, v)
            E.dma_start(out=out, in_=in_, **kw).then_inc(sem, 16)
        self.ops[q].append(emit)
        for t in reads:
            t.r[ln] = c
        for t in writes:
            t.w = (ln, c)
            t.r = {}
        self.n_instr += 1

    def finish(self, final_toks):
        nc = self.nc
        fin = []
        need = {}
        for t in final_toks:
            if t.w is not None and need.get(t.w[0], 0) < t.w[1]:
                need[t.w[0]] = t.w[1]
        for s, v in self.cnt.items():
            if v and need.get(s, 0) < v:
                need[s] = v
        for s, v in need.items():
            fin.append((self.sems[s], v))
        ops = self.ops
        with nc.Block() as block:
            @block.tensor
            def _(E):
                for f in ops["pe"]:
                    f(E)

            @block.scalar
            def _(E):
                for f in ops["act"]:
                    f(E)

            @block.vector
            def _(E):
                for f in ops["dve"]:
                    f(E)

            @block.gpsimd
            def _(E):
                for f in ops["pool"]:
                    f(E)

            @block.sync
            def _(E):
                for f in ops["sp"]:
                    f(E)
                for s, v in fin:
                    E.wait_ge(s, v)
        for cm in reversed(self._ctx):
            cm.__exit__(None, None, None)


class Alloc:
    def __init__(self, nc):
        self.nc = nc
        self._ctx = []

    def sb(self, name, shape, dt):
        cm = self.nc.sbuf_tensor(name, list(shape), dt)
        t = cm.__enter__()
        self._ctx.append(cm)
        return t

    def ps(self, name, shape, dt):
        cm = self.nc.psum_tensor(name, list(shape), dt)
        t = cm.__enter__()
        self._ctx.append(cm)
        return t

    def close(self):
        for cm in reversed(self._ctx):
            cm.__exit__(None, None, None)


class SchedI(Sched):
    def __init__(self, nc, **kw):
        super().__init__(nc, **kw)
        self.E = {"pe": nc.tensor, "act": nc.scalar, "dve": nc.vector, "pool": nc.gpsimd, "sp": nc.sync}

    limit = 10 ** 9

    def op(self, eng, fn, reads=(), writes=()):
        if self.n_instr >= self.limit:
            return
        waits = self._collect(eng, reads, writes)
        self.cnt[eng] += 1
        c = self.cnt[eng]
        E = self.E[eng]
        for s, v in waits:
            E.wait_ge(s, v)
        fn(E).then_inc(self.sems[eng], 1)
        for t in reads:
            t.r[eng] = c
        for t in writes:
            t.w = (eng, c)
            t.r = {}
        self.n_instr += 1

    def dma(self, q, out, in_, reads=(), writes=(), **kw):
        if self.n_instr >= self.limit:
            return
        lanes = self.lanes[q]
        ln = lanes[self.lane_rr[q] % len(lanes)]
        self.lane_rr[q] += 1
        waits = self._collect(q, reads, writes)
        prev = self.cnt[ln]
        if prev and self.seen[q].get(ln, 0) < prev:
            self.seen[q][ln] = prev
            waits.append((self.sems[ln], prev))
        self.cnt[ln] += 16
        c = self.cnt[ln]
        E = self.E[q]
        for s, v in waits:
            E.wait_ge(s, v)
        E.dma_start(out=out, in_=in_, **kw).then_inc(self.sems[ln], 16)
        for t in reads:
            t.r[ln] = c
        for t in writes:
            t.w = (ln, c)
            t.r = {}
        self.n_instr += 1

    def barrier(self):
        for e in ("pe", "act", "dve", "pool", "sp"):
            E = self.E[e]
            for s, v in self.cnt.items():
                if v and s != e and self.seen[e].get(s, 0) < v:
                    self.seen[e][s] = v
                    E.wait_ge(self.sems[s], v)
                if s == e and v and e != "sp":
                    if self.seen[e].get(s, 0) < v:
                        self.seen[e][s] = v
                        E.wait_ge(self.sems[s], v)

    def finish(self, final_toks=()):
        self.barrier()
        for cm in reversed(self._ctx):
            cm.__exit__(None, None, None)


from contextlib import ExitStack

D = 1024
DFF = 2816
NFF = DFF // 128
KD = D // 128


class Phase:
    _uid = [0]

    def __init__(self, nc, S):
        self.nc, self.S = nc, S
        self.es = ExitStack()
        self.n = 0
        Phase._uid[0] += 1
        self.uid = Phase._uid[0]

    def sb(self, shape, dt, name=None):
        self.n += 1
        return self.es.enter_context(self.nc.sbuf_tensor(name or f"t{self.uid}_{self.n}", list(shape), dt))

    def ps(self, shape, dt, name=None):
        self.n += 1
        return self.es.enter_context(self.nc.psum_tensor(name or f"p{self.uid}_{self.n}", list(shape), dt))

    def close(self):
        self.S.barrier()
        self.es.close()


def make_ident(nc, S, P, dt=BF16):
    identf = P.sb([128, 128], F32)
    ident = P.sb([128, 128], dt)
    t = Tok()
    S.op("pool", lambda E: E.memset(identf[:], 1.0), writes=[t])
    S.op("pool", lambda E: E.affine_select(out=identf[:], in_=identf[:], pattern=[[-1, 128]], base=0,
                                           channel_multiplier=1, compare_op=ALU.is_equal, fill=0.0),
         reads=[t], writes=[t])
    S.op("dve", lambda E: E.tensor_copy(out=ident[:], in_=identf[:]), reads=[t], writes=[t])
    return ident, identf, t


def load_w_bf16(S, dst, src, tok, rows_per=128, col_split=2):
    K = dst.shape[1]
    N = dst.shape[2]
    cs = N // col_split
    for k in range(K):
        for c in range(col_split):
            S.dma("pool", dst[:, k, c * cs:(c + 1) * cs], src[k * 128:(k + 1) * 128, c * cs:(c + 1) * cs],
                  writes=[tok])


def rms_prep(S, P, ht, t_h, s, gt, t_g, xn, t_xn, junk, t_junk, stat, t_stat, mhalf, t_mh):
    S.op("dve", lambda E: E.scalar_tensor_tensor(out=junk[:], in0=ht[:, s, :], scalar=1.0 / D, in1=ht[:, s, :],
                                                 op0=ALU.mult, op1=ALU.mult, accum_out=stat[:, 0:1]),
         reads=[t_h], writes=[t_junk, t_stat])
    S.op("dve", lambda E: E.tensor_scalar(out=stat[:, 1:2], in0=stat[:, 0:1], scalar1=1e-6, scalar2=None,
                                          op0=ALU.add), reads=[t_stat], writes=[t_stat])
    S.op("pool", lambda E: E.tensor_tensor(out=stat[:, 2:3], in0=stat[:, 1:2], in1=mhalf[:, 0:1], op=ALU.pow),
         reads=[t_stat, t_mh], writes=[t_stat])
    S.op("dve", lambda E: E.scalar_tensor_tensor(out=xn[:], in0=ht[:, s, :], scalar=stat[:, 2:3], in1=gt[:],
                                                 op0=ALU.mult, op1=ALU.mult),
         reads=[t_h, t_stat, t_g], writes=[t_xn])


def phase_ffn(nc, S, h_in, h_out, g, wg, wu, wd, toks_in, toks_out, T):
    P = Phase(nc, S)
    NT = T // 128
    NS = 4
    NSUP = NT // NS
    hv_in = h_in.rearrange("(n p) d -> p n d", p=128)
    hv_out = h_out.rearrange("(n p) d -> p n d", p=128)
    ident, _, t_id = make_ident(nc, S, P)
    wg_b = P.sb([128, KD, DFF], BF16); t_wg = Tok()
    wu_b = P.sb([128, KD, DFF], BF16); t_wu = Tok()
    wd_b = P.sb([128, NFF, D], BF16); t_wd = Tok()
    load_w_bf16(S, wg_b, wg, t_wg)
    load_w_bf16(S, wu_b, wu, t_wu)
    load_w_bf16(S, wd_b, wd, t_wd, col_split=1)
    gt = P.sb([128, D], F32); t_g = Tok()
    S.dma("sp", gt[:], g.partition_broadcast(128), writes=[t_g])
    mhalf = P.sb([128, 1], F32); t_mh = Tok()
    S.op("pool", lambda E: E.memset(mhalf[:], -0.5), writes=[t_mh])
    ht = [P.sb([128, NS, D], F32)] * 2; t_ht = [[Tok() for _ in range(NS)]] * 2
    rl = [P.sb([128, 512], F32) for _ in range(4)]; t_rl = [Tok() for _ in range(4)]
    xn = [P.sb([128, D], BF16) for _ in range(2)]; t_xn = [Tok(), Tok()]
    junk = P.sb([128, D], BF16); t_junk = Tok()
    stat = [P.sb([128, 4], F32) for _ in range(2)]; t_stat = [Tok(), Tok()]
    xnT = [P.sb([128, KD, NS * 128], BF16) for _ in range(2)]; t_xnT = [Tok(), Tok()]
    hT = P.sb([128, NFF, NS * 128], BF16); t_hT = [Tok() for _ in range(NFF)]
    sg = [P.sb([128, NS * 128], BF16) for _ in range(2)]; t_sg = [Tok(), Tok()]
    pT = [P.ps([128, KD, 128], BF16) for _ in range(2)]; t_pT = [Tok(), Tok()]
    pG = [P.ps([128, 512], F32) for _ in range(2)]; t_pG = [Tok(), Tok()]
    pU = [P.ps([128, 512], F32) for _ in range(2)]; t_pU = [Tok(), Tok()]
    pD = [P.ps([128, 512], F32) for _ in range(2)]; t_pD = [Tok(), Tok()]
    itc = [0]

    def load(st):
        hb = st % 2
        for s in range(NS):
            n = st * NS + s
            S.dma("sp", ht[hb][:, s, :], hv_in[:, n, :], reads=[toks_in[n]], writes=[t_ht[hb][s]])

    def prep(st):
        hb = st % 2
        for s in range(NS):
            b = s % 2
            rms_prep(S, P, ht[hb], t_ht[hb][s], s, gt, t_g, xn[b], t_xn[b], junk, t_junk, stat[b], t_stat[b], mhalf,
                     t_mh)
            for k in range(KD):
                S.op("pe", lambda E, k=k, b=b: E.transpose(out=pT[b][:, k, :], in_=xn[b][:, k * 128:(k + 1) * 128],
                                                           identity=ident[:]),
                     reads=[t_xn[b], t_id], writes=[t_pT[b]])
            S.op("dve", lambda E, b=b, s=s, hb=hb: E.tensor_copy(out=xnT[hb][:, :, s * 128:(s + 1) * 128], in_=pT[b][:]),
                 reads=[t_pT[b]], writes=[t_xnT[hb]])

    def gateup(st):
        hb = st % 2
        for f in range(NFF):
            b = f % 2
            for k in range(KD):
                S.op("pe", lambda E, k=k, f=f, b=b: E.matmul(pG[b][:], lhsT=wg_b[:, k, f * 128:(f + 1) * 128],
                                                             rhs=xnT[hb][:, k, :], start=(k == 0), stop=(k == KD - 1)),
                     reads=[t_wg, t_xnT[hb]], writes=[t_pG[b]])
            for k in range(KD):
                S.op("pe", lambda E, k=k, f=f, b=b: E.matmul(pU[b][:], lhsT=wu_b[:, k, f * 128:(f + 1) * 128],
                                                             rhs=xnT[hb][:, k, :], start=(k == 0), stop=(k == KD - 1)),
                     reads=[t_wu, t_xnT[hb]], writes=[t_pU[b]])
            S.op("act", lambda E, b=b: E.activation(out=sg[b][:], in_=pG[b][:], func=AF.Silu),
                 reads=[t_pG[b]], writes=[t_sg[b]])
            S.op("dve", lambda E, b=b, f=f: E.tensor_tensor(out=hT[:, f, :], in0=pU[b][:], in1=sg[b][:], op=ALU.mult),
                 reads=[t_pU[b], t_sg[b]], writes=[t_hT[f]])

    def down(st):
        for s in range(NS):
            n = st * NS + s
            for c in range(2):
                b = itc[0] % 2
                r4 = itc[0] % 4
                itc[0] += 1
                S.dma("sp", rl[r4][:], hv_in[:, n, c * 512:(c + 1) * 512], reads=[toks_in[n]], writes=[t_rl[r4]])
                for f in range(NFF):
                    S.op("pe", lambda E, f=f, s=s, c=c, b=b: E.matmul(
                        pD[b][:], lhsT=hT[:, f, s * 128:(s + 1) * 128], rhs=wd_b[:, f, c * 512:(c + 1) * 512],
                        start=(f == 0), stop=(f == NFF - 1)),
                        reads=[t_wd, t_hT[f]], writes=[t_pD[b]])
                S.op("dve", lambda E, b=b, r4=r4: E.scalar_tensor_tensor(
                    out=rl[r4][:], in0=pD[b][:], scalar=0.5, in1=rl[r4][:], op0=ALU.mult, op1=ALU.add),
                    reads=[t_pD[b], t_rl[r4]], writes=[t_rl[r4]])
                S.dma("sp", hv_out[:, n, c * 512:(c + 1) * 512], rl[r4][:], reads=[t_rl[r4]], writes=[toks_out[n]])

    load(0)
    prep(0)
    for st in range(NSUP):
        if st + 1 < NSUP:
            load(st + 1)
        gateup(st)
        if st + 1 < NSUP:
            prep(st + 1)
        down(st)
    P.close()


RWKV_IN = 1408
ATT_Q0 = 1408
ATT_V0 = 2176
S5_0 = 2560
INW = 2816


def phase_win(nc, S, h_in, g, win, zr, qk, vtm, us5, toks_in, T):
    P = Phase(nc, S)
    NT = T // 128
    NS = 4
    NSUP = NT // NS
    hv_in = h_in.rearrange("(n p) d -> p n d", p=128)
    ident, _, t_id = make_ident(nc, S, P)
    w_b = P.sb([128, KD, INW], BF16); t_w = Tok()
    load_w_bf16(S, w_b, win, t_w)
    gt = P.sb([128, D], F32); t_g = Tok()
    S.dma("sp", gt[:], g.partition_broadcast(128), writes=[t_g])
    mhalf = P.sb([128, 1], F32); t_mh = Tok()
    S.op("pool", lambda E: E.memset(mhalf[:], -0.5), writes=[t_mh])
    ht = [P.sb([128, NS, D], F32) for _ in range(2)]; t_ht = [[Tok() for _ in range(NS)] for _ in range(2)]
    xn = [P.sb([128, D], BF16) for _ in range(2)]; t_xn = [Tok(), Tok()]
    junk = P.sb([128, D], BF16); t_junk = Tok()
    stat = [P.sb([128, 4], F32) for _ in range(2)]; t_stat = [Tok(), Tok()]
    xnT = [P.sb([128, KD, NS * 128], BF16) for _ in range(2)]; t_xnT = [Tok(), Tok()]
    stf = [P.sb([128, 512], F32) for _ in range(4)]; t_stf = [Tok() for _ in range(4)]
    stb = [P.sb([128, 512], BF16) for _ in range(4)]; t_stb = [Tok() for _ in range(4)]
    pT = [P.ps([128, KD, 128], BF16) for _ in range(2)]; t_pT = [Tok(), Tok()]
    pZ = [P.ps([128, 512], F32) for _ in range(4)]; t_pZ = [Tok() for _ in range(4)]
    t_out = Tok()
    chunks = []
    for c in range(11):
        chunks.append((c * 128, zr, c * 128, False))
    for c in range(6):
        chunks.append((ATT_Q0 + c * 128, qk, c * 128, True))
    for c in range(2):
        chunks.append((S5_0 + c * 128, us5, c * 128, False))
    it = 0
    ib = 0
    iff = 0
    for st in range(NSUP):
        hb = st % 2
        for s in range(NS):
            n = st * NS + s
            S.dma("sp", ht[hb][:, s, :], hv_in[:, n, :], reads=[toks_in[n]], writes=[t_ht[hb][s]])
        for s in range(NS):
            b = s % 2
            rms_prep(S, P, ht[hb], t_ht[hb][s], s, gt, t_g, xn[b], t_xn[b], junk, t_junk, stat[b], t_stat[b], mhalf, t_mh)
            for k in range(KD):
                S.op("pe", lambda E, k=k, b=b: E.transpose(out=pT[b][:, k, :], in_=xn[b][:, k * 128:(k + 1) * 128],
                                                           identity=ident[:]),
                     reads=[t_xn[b], t_id], writes=[t_pT[b]])
            S.op("dve", lambda E, b=b, s=s, hb=hb: E.tensor_copy(out=xnT[hb][:, :, s * 128:(s + 1) * 128], in_=pT[b][:]),
                 reads=[t_pT[b]], writes=[t_xnT[hb]])
        tsl = slice(st * 512, (st + 1) * 512)
        for (c0, dst, r0, isb) in chunks:
            pb = it % 4
            it += 1
            for k in range(KD):
                S.op("pe", lambda E, k=k, c0=c0, pb=pb, hb=hb: E.matmul(
                    pZ[pb][:], lhsT=w_b[:, k, c0:c0 + 128], rhs=xnT[hb][:, k, :], start=(k == 0), stop=(k == KD - 1)),
                    reads=[t_w, t_xnT[hb]], writes=[t_pZ[pb]])
            if isb:
                sb_ = ib % 4
                ib += 1
                S.op("act", lambda E, pb=pb, sb_=sb_: E.activation(out=stb[sb_][:], in_=pZ[pb][:], func=AF.Copy),
                     reads=[t_pZ[pb]], writes=[t_stb[sb_]])
                S.dma("sp", dst[r0:r0 + 128, tsl], stb[sb_][:], reads=[t_stb[sb_]], writes=[t_out])
            else:
                sf = iff % 4
                iff += 1
                eng = "act" if iff % 2 else "dve"
                if eng == "act":
                    S.op("act", lambda E, pb=pb, sf=sf: E.activation(out=stf[sf][:], in_=pZ[pb][:], func=AF.Copy),
                         reads=[t_pZ[pb]], writes=[t_stf[sf]])
                else:
                    S.op("dve", lambda E, pb=pb, sf=sf: E.tensor_copy(out=stf[sf][:], in_=pZ[pb][:]),
                         reads=[t_pZ[pb]], writes=[t_stf[sf]])
                S.dma("sp", dst[r0:r0 + 128, tsl], stf[sf][:], reads=[t_stf[sf]], writes=[t_out])
        for s in range(NS):
            n = st * NS + s
            pb = it % 4
            it += 1
            for k in range(KD):
                S.op("pe", lambda E, k=k, s=s, pb=pb, hb=hb: E.matmul(
                    pZ[pb][:, 0:384], lhsT=xnT[hb][:, k, s * 128:(s + 1) * 128], rhs=w_b[:, k, ATT_V0:ATT_V0 + 384],
                    start=(k == 0), stop=(k == KD - 1)),
                    reads=[t_w, t_xnT[hb]], writes=[t_pZ[pb]])
            sb_ = ib % 4
            ib += 1
            S.op("dve", lambda E, pb=pb, sb_=sb_: E.tensor_copy(out=stb[sb_][:, 0:384], in_=pZ[pb][:, 0:384]),
                 reads=[t_pZ[pb]], writes=[t_stb[sb_]])
            S.dma("sp", vtm[n * 128:(n + 1) * 128, :], stb[sb_][:, 0:384], reads=[t_stb[sb_]], writes=[t_out])
    P.close()


def phase_wout(nc, S, h_in, h_out, mixT, wout, toks_in, toks_out, T):
    P = Phase(nc, S)
    NT = T // 128
    NS = 4
    NSUP = NT // NS
    hv_in = h_in.rearrange("(n p) d -> p n d", p=128)
    hv_out = h_out.rearrange("(n p) d -> p n d", p=128)
    mv = mixT.rearrange("(k p) t -> p k t", p=128)
    w_b = P.sb([128, KD, D], BF16); t_w = Tok()
    load_w_bf16(S, w_b, wout, t_w, col_split=1)
    ht = [P.sb([128, NS, D], F32) for _ in range(2)]; t_ht = [[Tok() for _ in range(NS)] for _ in range(2)]
    ml = [P.sb([128, KD, 512], BF16) for _ in range(2)]; t_ml = [Tok(), Tok()]
    pD = [P.ps([128, 512], F32) for _ in range(4)]; t_pD = [Tok() for _ in range(4)]
    it = 0
    for st in range(NSUP):
        hb = st % 2
        S.dma("sp", ml[hb][:], mv[:, :, st * 512:(st + 1) * 512], writes=[t_ml[hb]])
        for s in range(NS):
            n = st * NS + s
            S.dma("sp", ht[hb][:, s, :], hv_in[:, n, :], reads=[toks_in[n]], writes=[t_ht[hb][s]])
        for s in range(NS):
            n = st * NS + s
            for c in range(2):
                b = it % 4
                it += 1
                for k in range(KD):
                    S.op("pe", lambda E, k=k, s=s, c=c, b=b, hb=hb: E.matmul(
                        pD[b][:], lhsT=ml[hb][:, k, s * 128:(s + 1) * 128], rhs=w_b[:, k, c * 512:(c + 1) * 512],
                        start=(k == 0), stop=(k == KD - 1)),
                        reads=[t_w, t_ml[hb]], writes=[t_pD[b]])
                S.op("dve", lambda E, s=s, c=c, b=b, hb=hb: E.tensor_tensor(
                    out=ht[hb][:, s, c * 512:(c + 1) * 512], in0=pD[b][:], in1=ht[hb][:, s, c * 512:(c + 1) * 512],
                    op=ALU.add), reads=[t_pD[b]], writes=[t_ht[hb][s]])
            S.dma("sp", hv_out[:, n, :], ht[hb][:, s, :], reads=[t_ht[hb][s]], writes=[toks_out[n]])
    P.close()


def phase_final(nc, S, h_in, out, g, toks_in, toks_out, T):
    P = Phase(nc, S)
    NT = T // 128
    hv_in = h_in.rearrange("(n p) d -> p n d", p=128)
    hv_out = out.rearrange("(n p) d -> p n d", p=128)
    gt = P.sb([128, D], F32); t_g = Tok()
    S.dma("sp", gt[:], g.partition_broadcast(128), writes=[t_g])
    mhalf = P.sb([128, 1], F32); t_mh = Tok()
    S.op("pool", lambda E: E.memset(mhalf[:], -0.5), writes=[t_mh])
    ht = [P.sb([128, 1, D], F32) for _ in range(4)]; t_ht = [Tok() for _ in range(4)]
    xo = [P.sb([128, D], F32) for _ in range(4)]; t_xo = [Tok() for _ in range(4)]
    junk = P.sb([128, D], BF16); t_junk = Tok()
    stat = [P.sb([128, 4], F32) for _ in range(4)]; t_stat = [Tok() for _ in range(4)]
    for n in range(NT):
        b = n % 4
        S.dma("sp", ht[b][:, 0, :], hv_in[:, n, :], reads=[toks_in[n]], writes=[t_ht[b]])
        rms_prep(S, P, ht[b], t_ht[b], 0, gt, t_g, xo[b], t_xo[b], junk, t_junk, stat[b], t_stat[b], mhalf, t_mh)
        S.dma("pool", hv_out[:, n, :], xo[b][:], reads=[t_xo[b]], writes=[toks_out[n]])
    P.close()


ALIBI = [0.25, 0.0625, 0.015625, 0.00390625, 0.5, 0.125]
DILS = [1, 4, 16]
NEG = -1.0e30


def phase_attn(nc, S, qk, vtm, mixT, T):
    P = Phase(nc, S)
    NB1 = T // 128
    dfi = P.sb([128, 128], I32)
    dff = P.sb([128, 128], F32)
    Dk = P.sb([128, 3, 128], F32)
    Mk = P.sb([128, 3, 128], F32)
    t_c = Tok()
    S.op("pool", lambda E: E.iota(dfi[:], pattern=[[1, 128]], base=0, channel_multiplier=-1), writes=[t_c])
    S.op("dve", lambda E: E.tensor_copy(out=dff[:], in_=dfi[:]), reads=[t_c], writes=[t_c])
    S.op("dve", lambda E: E.tensor_scalar(out=Dk[:, 0, :], in0=dff[:], scalar1=128.0, scalar2=None, op0=ALU.add),
         reads=[t_c], writes=[t_c])
    S.op("dve", lambda E: E.tensor_scalar(out=Dk[:, 1, :], in0=dff[:], scalar1=-1.0, scalar2=None, op0=ALU.mult),
         reads=[t_c], writes=[t_c])
    S.op("dve", lambda E: E.tensor_tensor(out=Dk[:, 1, :], in0=Dk[:, 1, :], in1=dff[:], op=ALU.max),
         reads=[t_c], writes=[t_c])
    S.op("dve", lambda E: E.tensor_scalar(out=Dk[:, 2, :], in0=dff[:], scalar1=-1.0, scalar2=128.0, op0=ALU.mult,
                                          op1=ALU.add), reads=[t_c], writes=[t_c])
    S.op("dve", lambda E: E.tensor_scalar(out=Mk[:], in0=Dk[:], scalar1=64.0, scalar2=NEG, op0=ALU.is_gt,
                                          op1=ALU.mult), reads=[t_c], writes=[t_c])
    sel = P.sb([65, 64], F32)
    S.op("dve", lambda E: E.memset(sel[:], 0.0), writes=[t_c])
    S.op("dve", lambda E: E.memset(sel[64:65, :], 1.0), reads=[t_c], writes=[t_c])

    qT = P.sb([128, T], BF16); t_q = Tok()
    kT = P.sb([128, T], BF16); t_k = Tok()
    vt = [P.sb([128, NB1, 2, 65], BF16) for _ in range(3)]; t_v = [Tok() for _ in range(3)]
    acc = P.sb([65, 2, T], F32)
    t_acc = [[Tok() for _ in range(NB1)] for _ in range(2)]
    biasT = P.sb([128, 2, 3, 3, 128], F32); t_b = Tok()
    NBF = 4
    sc = [P.sb([128, 3, 128], F32) for _ in range(NBF)]; t_sc = [Tok() for _ in range(NBF)]
    pr = [P.sb([128, 3, 128], BF16) for _ in range(NBF)]; t_pr = [Tok() for _ in range(NBF)]
    rec = [P.sb([64, 512], F32) for _ in range(2)]; t_rec = [Tok(), Tok()]
    ob = [P.sb([64, 512], BF16) for _ in range(2)]; t_ob = [Tok(), Tok()]
    pS = [P.ps([128, 3, 128], F32) for _ in range(NBF)]; t_pS = [Tok() for _ in range(NBF)]
    pOb = [P.ps([128, 512], F32) for _ in range(NBF)]; t_pO = [Tok() for _ in range(NBF)]
    pO = [x[0:65, 0:128] for x in pOb]
    pB = [x[0:64, 0:512] for x in pOb]; t_pB = t_pO
    t_out = Tok()
    it = 0
    for hp in range(3):
        S.dma("sp", qT[:], qk[128 * hp:128 * hp + 128, :], writes=[t_q])
        S.dma("sp", kT[:], qk[384 + 128 * hp:384 + 128 * hp + 128, :], writes=[t_k])
        for pi, d in enumerate(DILS):
            S.op("pool", lambda E, pi=pi: E.memset(vt[pi][:], 1.0), writes=[t_v[pi]])
            nm = T // d // 128
            vv = vtm.rearrange("(m j r) (h c) -> j r m h c", j=128, r=d, c=64)
            for r in range(d):
                for h2 in range(2):
                    S.dma("sp", vt[pi][:, r * nm:(r + 1) * nm, h2, 0:64], vv[:, r, :, 2 * hp + h2, :],
                          writes=[t_v[pi]])
        for h2 in range(2):
            for pi, d in enumerate(DILS):
                sl = -ALIBI[2 * hp + h2] * d
                S.op("dve", lambda E, h2=h2, pi=pi, sl=sl: E.scalar_tensor_tensor(
                    out=biasT[:, h2, pi, :, :], in0=Dk[:], scalar=sl, in1=Mk[:], op0=ALU.mult, op1=ALU.add),
                    reads=[t_c], writes=[t_b])
        for h2 in range(2):
            rows = slice(64 * h2, 64 * h2 + 64)
            stages = []
            for pi, d in enumerate(DILS):
                nb = T // d // 128
                for r in range(d):
                    for b in range(nb):
                        bi = it % NBF
                        it += 1
                        kts = [kt for kt in (b - 1, b, b + 1) if 0 <= kt < nb]
                        k0 = kts[0] - (b - 1)
                        nk = len(kts)
                        qs = slice(r + d * 128 * b, r + d * 128 * b + d * 127 + 1, d)
                        blks = sorted(set(range((r + d * 128 * b) // 128, (r + d * 128 * (b + 1) - d) // 128 + 1)))

                        def stA(bi=bi, kts=kts, k0=k0, nk=nk, qs=qs, r=r, d=d, pi=pi, rows=rows, h2=h2):
                            for ki, kt in enumerate(kts):
                                ks = slice(r + d * 128 * kt, r + d * 128 * kt + d * 127 + 1, d)
                                S.op("pe", lambda E, ki=ki, ks=ks: E.matmul(
                                    pS[bi][:, ki, :], lhsT=kT[rows, ks], rhs=qT[rows, qs], start=True, stop=True),
                                    reads=[t_q, t_k], writes=[t_pS[bi]])
                            S.op("dve", lambda E: E.scalar_tensor_tensor(
                                out=sc[bi][:, 0:nk, :], in0=pS[bi][:, 0:nk, :], scalar=0.125,
                                in1=biasT[:, h2, pi, k0:k0 + nk, :], op0=ALU.mult, op1=ALU.add),
                                reads=[t_pS[bi], t_b], writes=[t_sc[bi]])
                            S.op("act", lambda E: E.activation(out=pr[bi][:, 0:nk, :], in_=sc[bi][:, 0:nk, :],
                                                               func=AF.Exp),
                                 reads=[t_sc[bi]], writes=[t_pr[bi]])

                        def stB(bi=bi, kts=kts, nk=nk, qs=qs, r=r, nb=nb, pi=pi, h2=h2, blks=blks):
                            for ki, kt in enumerate(kts):
                                S.op("pe", lambda E, ki=ki, kt=kt: E.matmul(
                                    pO[bi], lhsT=vt[pi][:, r * nb + kt, h2, :], rhs=pr[bi][:, ki, :],
                                    start=(ki == 0), stop=(ki == nk - 1)),
                                    reads=[t_v[pi], t_pr[bi]], writes=[t_pO[bi]])
                            at = [t_acc[h2][x] for x in blks]
                            if pi == 0:
                                S.op("act", lambda E: E.activation(out=acc[:, h2, qs], in_=pO[bi], func=AF.Copy),
                                     reads=[t_pO[bi]], writes=at)
                            else:
                                S.op("dve", lambda E: E.tensor_tensor(out=acc[:, h2, qs], in0=pO[bi],
                                                                      in1=acc[:, h2, qs], op=ALU.add),
                                     reads=[t_pO[bi]], writes=at)
                        stages.append((stA, stB))
            SKEW = NBF - 1
            for i in range(len(stages) + SKEW):
                if i < len(stages):
                    stages[i][0]()
                if i - SKEW >= 0:
                    stages[i - SKEW][1]()
            head = 2 * hp + h2
            for c in range(T // 512):
                bi = c % 2
                cs = slice(c * 512, (c + 1) * 512)
                at = t_acc[h2][4 * c:4 * c + 4]
                S.op("pe", lambda E, bi=bi, h2=h2, cs=cs: E.matmul(pB[bi], lhsT=sel[:], rhs=acc[:, h2, cs],
                                                                   start=True, stop=True),
                     reads=at + [t_c], writes=[t_pB[bi]])
                S.op("dve", lambda E, bi=bi: E.reciprocal(out=rec[bi][:], in_=pB[bi]),
                     reads=[t_pB[bi]], writes=[t_rec[bi]])
                S.op("dve", lambda E, bi=bi, h2=h2, cs=cs: E.tensor_tensor(out=ob[bi][:], in0=acc[0:64, h2, cs],
                                                                          in1=rec[bi][:], op=ALU.mult),
                     reads=at + [t_rec[bi]], writes=[t_ob[bi]])
                S.dma("sp", mixT[384 + 64 * head:384 + 64 * head + 64, cs], ob[bi][:], reads=[t_ob[bi]],
                      writes=[t_out])
    P.close()


def rsl(lo, hi):
    return slice(hi - 1, (lo - 1) if lo > 0 else None, -1)


TWO_PI = 6.283185307179586
MAGIC = 12582912.0


def phase_s5(nc, S, us5, mixT, prm, T, C=256):
    P = Phase(nc, S)
    NCH = T // C
    NCMB = 16
    a_re, a_im, lstep, b_re, b_im, c_re, c_im, dsk, glu_w, glu_b = prm
    ident, identf, t_id = make_ident(nc, S, P)
    t_s = Tok()

    def dv(fn, eng="dve"):
        S.op(eng, fn, reads=[t_s, t_id], writes=[t_s])

    prs = P.sb([128, 40, 16], F32)
    cosT = P.sb([128, NCMB, C], F32)
    sinT = P.sb([128, NCMB, C], F32)
    BT = P.sb([128, NCMB, 2, 128], BF16)
    CT = P.sb([128, NCMB, 2, 128], BF16)
    gw = P.sb([128, 2, 512], BF16)
    gb = P.sb([128, 4], F32)
    dk = P.sb([128, 2], F32)
    ub = P.sb([128, 2, T], BF16); t_ub = Tok()
    ybwd = P.sb([128, 2, T], F32); t_yb = [Tok() for _ in range(NCH)]
    gi = P.sb([128, 2, NCMB], F32); t_gi = [Tok() for _ in range(NCMB)]
    banks = [P.ps([128, 512], F32) for _ in range(6)]
    P2 = Phase(nc, S)
    stg = P2.sb([16, 3, 128], F32)
    lst = P2.sb([16, 2], F32)
    S.dma("sp", stg[:, 0, :], a_re.rearrange("d (j g) p -> (d j) (g p)", g=2), writes=[t_s])
    S.dma("sp", stg[:, 1, :], a_im.rearrange("d (j g) p -> (d j) (g p)", g=2), writes=[t_s])
    S.dma("sp", lst[:], lstep.rearrange("d (j g) -> (d j) g", g=2), writes=[t_s])
    for g2 in range(2):
        dv(lambda E, g2=g2: E.tensor_copy(out=stg[:, 2, 64 * g2:64 * g2 + 64],
                                          in_=lst[:, g2:g2 + 1].to_broadcast([16, 64])))
    pst = banks[0][:, 0:48].rearrange("p (a b) -> p a b", a=3)
    for i in range(3):
        S.op("pe", lambda E, i=i: E.transpose(out=pst[:, i, :], in_=stg[:, i, :], identity=identf[0:16, 0:16]),
             reads=[t_s, t_id], writes=[t_s])
    nm = {}

    def V(name):
        if name not in nm:
            nm[name] = len(nm)
        return prs[:, nm[name], :]
    dv(lambda E: E.tensor_copy(out=prs[:, 0:3, :], in_=pst))
    nm.update({"are": 0, "aim": 1, "lst": 2})
    S.op("act", lambda E: E.activation(out=V("step"), in_=V("lst"), func=AF.Exp), reads=[t_s], writes=[t_s])
    dv(lambda E: E.tensor_tensor(out=V("ar"), in0=V("are"), in1=V("step"), op=ALU.mult))
    dv(lambda E: E.tensor_tensor(out=V("th"), in0=V("aim"), in1=V("step"), op=ALU.mult))
    S.op("act", lambda E: E.activation(out=V("rho"), in_=V("ar"), func=AF.Exp), reads=[t_s], writes=[t_s])

    def sin_of(dst, src, shift):
        dv(lambda E: E.tensor_scalar(out=V("k1"), in0=V(src), scalar1=1.0 / TWO_PI, scalar2=shift / TWO_PI,
                                     op0=ALU.mult, op1=ALU.add))
        dv(lambda E: E.tensor_scalar(out=V("k2"), in0=V("k1"), scalar1=MAGIC, scalar2=None, op0=ALU.add))
        dv(lambda E: E.tensor_scalar(out=V("k3"), in0=V("k2"), scalar1=-MAGIC, scalar2=None, op0=ALU.add))
        dv(lambda E: E.tensor_tensor(out=V("k1"), in0=V("k1"), in1=V("k3"), op=ALU.subtract))
        S.op("act", lambda E: E.activation(out=V(dst), in_=V("k1"), func=AF.Sin, scale=TWO_PI),
             reads=[t_s], writes=[t_s])
    sin_of("sn", "th", 0.0)
    sin_of("cs", "th", TWO_PI / 4)
    dv(lambda E: E.tensor_tensor(out=V("lr"), in0=V("rho"), in1=V("cs"), op=ALU.mult))
    dv(lambda E: E.tensor_tensor(out=V("li"), in0=V("rho"), in1=V("sn"), op=ALU.mult))
    dv(lambda E: E.tensor_scalar(out=V("nr"), in0=V("lr"), scalar1=-1.0, scalar2=None, op0=ALU.add))
    dv(lambda E: E.tensor_tensor(out=V("d1"), in0=V("are"), in1=V("are"), op=ALU.mult))
    dv(lambda E: E.tensor_tensor(out=V("d2"), in0=V("aim"), in1=V("aim"), op=ALU.mult))
    dv(lambda E: E.tensor_tensor(out=V("d1"), in0=V("d1"), in1=V("d2"), op=ALU.add))
    dv(lambda E: E.reciprocal(out=V("rd"), in_=V("d1")))
    dv(lambda E: E.tensor_tensor(out=V("z1"), in0=V("nr"), in1=V("are"), op=ALU.mult))
    dv(lambda E: E.tensor_tensor(out=V("z2"), in0=V("li"), in1=V("aim"), op=ALU.mult))
    dv(lambda E: E.tensor_tensor(out=V("z1"), in0=V("z1"), in1=V("z2"), op=ALU.add))
    dv(lambda E: E.tensor_tensor(out=V("zr"), in0=V("z1"), in1=V("rd"), op=ALU.mult))
    dv(lambda E: E.tensor_tensor(out=V("z1"), in0=V("li"), in1=V("are"), op=ALU.mult))
    dv(lambda E: E.tensor_tensor(out=V("z2"), in0=V("nr"), in1=V("aim"), op=ALU.mult))
    dv(lambda E: E.tensor_tensor(out=V("z1"), in0=V("z1"), in1=V("z2"), op=ALU.subtract))
    dv(lambda E: E.tensor_tensor(out=V("zi"), in0=V("z1"), in1=V("rd"), op=ALU.mult))

    tmpA = P2.sb([128, NCMB, C // 2], F32)
    tmpB = P2.sb([128, NCMB, C // 2], F32)
    dv(lambda E: E.memset(cosT[:, :, 0:1], 1.0))
    dv(lambda E: E.memset(sinT[:, :, 0:1], 0.0))
    dv(lambda E: E.tensor_copy(out=V("wr"), in_=V("cs")))
    dv(lambda E: E.tensor_copy(out=V("wi"), in_=V("sn")))
    L = 1
    while L < C:
        wrb = V("wr").unsqueeze(2).to_broadcast([128, NCMB, L])
        wib = V("wi").unsqueeze(2).to_broadcast([128, NCMB, L])
        dv(lambda E, L=L, wrb=wrb: E.tensor_tensor(out=tmpA[:, :, 0:L], in0=cosT[:, :, 0:L], in1=wrb, op=ALU.mult))
        dv(lambda E, L=L, wib=wib: E.tensor_tensor(out=tmpB[:, :, 0:L], in0=sinT[:, :, 0:L], in1=wib, op=ALU.mult))
        dv(lambda E, L=L: E.tensor_tensor(out=cosT[:, :, L:2 * L], in0=tmpA[:, :, 0:L], in1=tmpB[:, :, 0:L],
                                          op=ALU.subtract))
        dv(lambda E, L=L, wib=wib: E.tensor_tensor(out=tmpA[:, :, 0:L], in0=cosT[:, :, 0:L], in1=wib, op=ALU.mult))
        dv(lambda E, L=L, wrb=wrb: E.tensor_tensor(out=tmpB[:, :, 0:L], in0=sinT[:, :, 0:L], in1=wrb, op=ALU.mult))
        dv(lambda E, L=L: E.tensor_tensor(out=sinT[:, :, L:2 * L], in0=tmpA[:, :, 0:L], in1=tmpB[:, :, 0:L],
                                          op=ALU.add))
        dv(lambda E: E.tensor_tensor(out=V("q1"), in0=V("wr"), in1=V("wr"), op=ALU.mult))
        dv(lambda E: E.tensor_tensor(out=V("q2"), in0=V("wi"), in1=V("wi"), op=ALU.mult))
        dv(lambda E: E.tensor_tensor(out=V("q3"), in0=V("wr"), in1=V("wi"), op=ALU.mult))
        dv(lambda E: E.tensor_tensor(out=V("wr"), in0=V("q1"), in1=V("q2"), op=ALU.subtract))
        dv(lambda E: E.tensor_scalar(out=V("wi"), in0=V("q3"), scalar1=2.0, scalar2=None, op0=ALU.mult))
        L *= 2

    bst = P2.sb([128, 2, 8, 16], F32)
    S.dma("sp", bst[:, 0, :, :], b_re.rearrange("(j g) p c -> (g p) j c", g=2), writes=[t_s])
    S.dma("sp", bst[:, 1, :, :], b_im.rearrange("(j g) p c -> (g p) j c", g=2), writes=[t_s])
    bexp = P2.sb([128, 2, 128], F32)
    btmp = P2.sb([128, 16], F32)
    pX = [banks[1][:, 0:128], banks[2][:, 0:128]]
    for d in range(2):
        for j in range(8):
            cmb = d * 8 + j
            jj = j % 4
            dv(lambda E: E.memset(bexp[:], 0.0))
            for g2 in range(2):
                ps_ = slice(64 * g2, 64 * g2 + 64)
                cs_ = slice(32 * jj + 16 * g2, 32 * jj + 16 * g2 + 16)
                zr_ = prs[ps_, nm["zr"], cmb:cmb + 1]
                zi_ = prs[ps_, nm["zi"], cmb:cmb + 1]
                dv(lambda E, ps_=ps_, zi_=zi_, j=j: E.tensor_scalar(out=btmp[ps_, :], in0=bst[ps_, 1, j, :], scalar1=zi_,
                                                                    scalar2=None, op0=ALU.mult))
                dv(lambda E, ps_=ps_, cs_=cs_, zr_=zr_, j=j: E.scalar_tensor_tensor(
                    out=bexp[ps_, 0, cs_], in0=bst[ps_, 0, j, :], scalar=zr_, in1=btmp[ps_, :], op0=ALU.mult,
                    op1=ALU.subtract))
                dv(lambda E, ps_=ps_, zr_=zr_, j=j: E.tensor_scalar(out=btmp[ps_, :], in0=bst[ps_, 1, j, :], scalar1=zr_,
                                                                    scalar2=None, op0=ALU.mult))
                dv(lambda E, ps_=ps_, cs_=cs_, zi_=zi_, j=j: E.scalar_tensor_tensor(
                    out=bexp[ps_, 1, cs_], in0=bst[ps_, 0, j, :], scalar=zi_, in1=btmp[ps_, :], op0=ALU.mult,
                    op1=ALU.add))
            for ri in range(2):
                S.op("pe", lambda E, ri=ri: E.transpose(out=pX[ri], in_=bexp[:, ri, :], identity=identf[:]),
                     reads=[t_s, t_id], writes=[t_s])
                dv(lambda E, ri=ri, cmb=cmb: E.tensor_copy(out=BT[:, cmb, ri, :], in_=pX[ri]))
    cnat = P2.sb([128, 2, 2, 2, 64], F32)
    for d in range(2):
        for ri, cc in enumerate((c_re, c_im)):
            for ut in range(2):
                S.dma("sp", cnat[:, d, ri, ut, :], cc[d].rearrange("g c p -> (g c) p")[128 * ut:128 * ut + 128, :],
                      writes=[t_s])
    mki = P2.sb([128, 4, 2], I32)
    mk = P2.sb([128, 4, 2], F32)
    mk2 = P2.sb([128, 4, 2], F32)
    S.op("pool", lambda E: E.iota(mki[:], pattern=[[-32, 4], [-16, 2]], base=0, channel_multiplier=1),
         reads=[t_s], writes=[t_s])
    dv(lambda E: E.tensor_copy(out=mk[:], in_=mki[:]))
    dv(lambda E: E.tensor_scalar(out=mk2[:], in0=mk[:], scalar1=0.0, scalar2=None, op0=ALU.is_ge))
    dv(lambda E: E.tensor_scalar(out=mk[:], in0=mk[:], scalar1=15.0, scalar2=None, op0=ALU.is_le))
    dv(lambda E: E.tensor_tensor(out=mk[:], in0=mk[:], in1=mk2[:], op=ALU.mult))
    cx = P2.sb([128, 2, 64], F32)
    for d in range(2):
        for j in range(8):
            cmb = d * 8 + j
            jj = j % 4
            ut = j // 4
            for ri in range(2):
                for g2 in range(2):
                    dv(lambda E, d=d, ri=ri, ut=ut, jj=jj, g2=g2: E.tensor_scalar(
                        out=cx[:, g2, :], in0=cnat[:, d, ri, ut, :], scalar1=mk[:, jj, g2:g2 + 1], scalar2=None,
                        op0=ALU.mult))
                S.op("pe", lambda E, ri=ri: E.transpose(out=pX[ri], in_=cx[:].rearrange("p a b -> p (a b)"),
                                                        identity=identf[:]),
                     reads=[t_s, t_id], writes=[t_s])
                sgn = 1.0 if ri == 0 else -1.0
                dv(lambda E, ri=ri, cmb=cmb, sgn=sgn: E.tensor_scalar(out=CT[:, cmb, ri, :], in0=pX[ri], scalar1=sgn,
                                                                      scalar2=None, op0=ALU.mult))
    load_w_bf16(S, gw, glu_w, t_s, col_split=1)
    S.dma("sp", gb[:], glu_b.rearrange("(o p) -> p o", p=128), writes=[t_s], allow_slow_non_contiguous=True)
    S.dma("sp", dk[:], dsk.rearrange("(o p) -> p o", p=128), writes=[t_s], allow_slow_non_contiguous=True)

    for ut in range(2):
        for c4 in range(T // 2048 if T >= 2048 else 1):
            w_ = min(2048, T)
            S.dma("pool", ub[:, ut, c4 * w_:(c4 + 1) * w_], us5[128 * ut:128 * ut + 128, c4 * w_:(c4 + 1) * w_],
                  writes=[t_ub])
    S.op("dve", lambda E: E.memset(gi[:], 0.0), writes=t_gi)
    NB = 3
    P2.close()
    pBU = [banks[i][:, 0:2 * C].rearrange("p (a b) -> p a b", a=2) for i in range(2)]; t_pBU = [Tok() for _ in range(2)]
    m1 = [P.sb([128, 4, C], F32) for _ in range(NB)]; t_m1 = [Tok() for _ in range(NB)]
    gin = [P.sb([128, 2, C], F32) for _ in range(NB)]; t_gin = [Tok() for _ in range(NB)]
    gg = [P.sb([128, 2, C], F32) for _ in range(NB)]; t_gg = [Tok() for _ in range(NB)]
    m2 = [P.sb([128, 4, C], F32) for _ in range(NB)]; t_m2 = [Tok() for _ in range(NB)]
    ctmp = [P.sb([128, 2], F32) for _ in range(NB)]; t_ct = [Tok() for _ in range(NB)]
    hh = [P.sb([128, 4, 2, C], BF16) for _ in range(2)]; t_hh = [[Tok() for _ in range(4)] for _ in range(2)]
    pY = [banks[2 + i][:, 0:C] for i in range(2)]; t_pY = [Tok(), Tok()]
    uf = [P.sb([128, 2, C], F32) for _ in range(2)]; t_uf = [Tok(), Tok()]
    yv = [P.sb([128, C], F32) for _ in range(2)]; t_yv = [Tok(), Tok()]
    y2 = [P.sb([128, C], F32) for _ in range(2)]; t_y2 = [Tok(), Tok()]
    ygl = [P.sb([128, 2, C], BF16) for _ in range(2)]; t_yg = [[Tok(), Tok()] for _ in range(2)]
    pZ = [banks[4 + i][:, 0:C] for i in range(2)]; t_pZ = [Tok(), Tok()]
    sg = [P.sb([128, C], F32) for _ in range(2)]; t_sg = [Tok(), Tok()]
    oo = [P.sb([128, C], BF16) for _ in range(2)]; t_oo = [Tok(), Tok()]
    t_out = Tok()
    iy = [0]
    items = []
    for d in (1, 0):
        for ci in range(NCH):
            for ut in range(2):
                for jj in range(4):
                    items.append((d, ci, ut, jj))

    def geom(d, ci):
        if d == 1:
            lo, hi = T - (ci + 1) * C, T - ci * C
            return lo, hi, rsl(lo, hi)
        lo, hi = ci * C, (ci + 1) * C
        return lo, hi, slice(lo, hi)

    def stA(i):
        d, ci, ut, jj = items[i]
        lo, hi, tsl = geom(d, ci)
        cmb = d * 8 + ut * 4 + jj
        b = i % NB
        pb = i % 2
        if d == 0 and ut == 0 and jj == 0:
            ufb = ci % 2
            S.dma("sp", uf[ufb][:], us5.rearrange("(u p) t -> p u t", p=128)[:, :, lo:hi], writes=[t_uf[ufb]])
        for ri in range(2):
            S.op("pe", lambda E, ri=ri: E.matmul(pBU[pb][:, ri, :], lhsT=BT[:, cmb, ri, :], rhs=ub[:, ut, tsl],
                                                 start=True, stop=True), reads=[t_s, t_ub], writes=[t_pBU[pb]])
        cs_ = cosT[:, cmb, :]
        sn_ = sinT[:, cmb, :]
        for k, (src, tab) in enumerate(((0, cs_), (1, sn_), (1, cs_), (0, sn_))):
            S.op("dve", lambda E, k=k, src=src, tab=tab: E.tensor_tensor(out=m1[b][:, k, :], in0=pBU[pb][:, src, :],
                                                                         in1=tab, op=ALU.mult),
                 reads=[t_pBU[pb], t_s], writes=[t_m1[b]])
        S.op("pool", lambda E: E.tensor_tensor(out=gin[b][:, 0, :], in0=m1[b][:, 0, :], in1=m1[b][:, 1, :], op=ALU.add),
             reads=[t_m1[b]], writes=[t_gin[b]])
        S.op("pool", lambda E: E.tensor_tensor(out=gin[b][:, 1, :], in0=m1[b][:, 2, :], in1=m1[b][:, 3, :],
                                               op=ALU.subtract), reads=[t_m1[b]], writes=[t_gin[b]])

    def stB(i):
        d, ci, ut, jj = items[i]
        cmb = d * 8 + ut * 4 + jj
        b = i % NB
        rho_b = prs[:, nm["rho"], cmb:cmb + 1].to_broadcast([128, C])
        for ri in range(2):
            S.op("dve", lambda E, ri=ri: E.tensor_tensor_scan(
                out=gg[b][:, ri, :], data0=rho_b, data1=gin[b][:, ri, :], initial=gi[:, ri, cmb:cmb + 1],
                op0=ALU.mult, op1=ALU.add), reads=[t_gin[b], t_gi[cmb], t_s], writes=[t_gg[b]])
        wr_ = prs[:, nm["wr"], cmb:cmb + 1]
        wi_ = prs[:, nm["wi"], cmb:cmb + 1]
        S.op("dve", lambda E: E.tensor_scalar(out=ctmp[b][:, 0:1], in0=gg[b][:, 1, C - 1:C], scalar1=wi_, scalar2=None,
                                              op0=ALU.mult), reads=[t_gg[b], t_s], writes=[t_ct[b]])
        S.op("dve", lambda E: E.tensor_scalar(out=ctmp[b][:, 1:2], in0=gg[b][:, 1, C - 1:C], scalar1=wr_, scalar2=None,
                                              op0=ALU.mult), reads=[t_gg[b], t_s], writes=[t_ct[b]])
        S.op("dve", lambda E: E.scalar_tensor_tensor(out=gi[:, 0, cmb:cmb + 1], in0=gg[b][:, 0, C - 1:C], scalar=wr_,
                                                     in1=ctmp[b][:, 0:1], op0=ALU.mult, op1=ALU.subtract),
             reads=[t_gg[b], t_ct[b], t_s], writes=[t_gi[cmb]])
        S.op("dve", lambda E: E.scalar_tensor_tensor(out=gi[:, 1, cmb:cmb + 1], in0=gg[b][:, 0, C - 1:C], scalar=wi_,
                                                     in1=ctmp[b][:, 1:2], op0=ALU.mult, op1=ALU.add),
             reads=[t_gg[b], t_ct[b], t_s], writes=[t_gi[cmb]])

    def stC(i):
        d, ci, ut, jj = items[i]
        lo, hi, tsl = geom(d, ci)
        chn = lo // C
        cmb = d * 8 + ut * 4 + jj
        b = i % NB
        hb = (ci * 2 + ut) % 2
        cs_ = cosT[:, cmb, :]
        sn_ = sinT[:, cmb, :]
        S.op("pool", lambda E: E.tensor_tensor(out=m2[b][:, 0, :], in0=gg[b][:, 0, :], in1=cs_, op=ALU.mult),
             reads=[t_gg[b], t_s], writes=[t_m2[b]])
        S.op("pool", lambda E: E.tensor_tensor(out=m2[b][:, 1, :], in0=gg[b][:, 1, :], in1=sn_, op=ALU.mult),
             reads=[t_gg[b], t_s], writes=[t_m2[b]])
        S.op("dve", lambda E: E.tensor_tensor(out=m2[b][:, 2, :], in0=gg[b][:, 1, :], in1=cs_, op=ALU.mult),
             reads=[t_gg[b], t_s], writes=[t_m2[b]])
        S.op("dve", lambda E: E.tensor_tensor(out=m2[b][:, 3, :], in0=gg[b][:, 0, :], in1=sn_, op=ALU.mult),
             reads=[t_gg[b], t_s], writes=[t_m2[b]])
        S.op("pool", lambda E: E.tensor_tensor(out=hh[hb][:, jj, 0, :], in0=m2[b][:, 0, :], in1=m2[b][:, 1, :],
                                               op=ALU.subtract), reads=[t_m2[b]], writes=[t_hh[hb][jj]])
        S.op("pool", lambda E: E.tensor_tensor(out=hh[hb][:, jj, 1, :], in0=m2[b][:, 2, :], in1=m2[b][:, 3, :],
                                               op=ALU.add), reads=[t_m2[b]], writes=[t_hh[hb][jj]])
        if jj != 3:
            return
        yb_ = iy[0] % 2
        iy[0] += 1
        n_mm = 0
        for j4 in range(4):
            cm2 = d * 8 + ut * 4 + j4
            for ri in range(2):
                S.op("pe", lambda E, cm2=cm2, ri=ri, j4=j4, n_mm=n_mm: E.matmul(
                    pY[yb_], lhsT=CT[:, cm2, ri, :], rhs=hh[hb][:, j4, ri, :], start=(n_mm == 0), stop=(n_mm == 7)),
                    reads=[t_s, t_hh[hb][j4]], writes=[t_pY[yb_]])
                n_mm += 1
        if d == 1:
            S.op("act", lambda E: E.activation(out=ybwd[:, ut, tsl], in_=pY[yb_], func=AF.Copy),
                 reads=[t_pY[yb_]], writes=[t_yb[chn]])
            return
        ufb = ci % 2
        gb_ = ci % 2
        S.op("dve", lambda E: E.tensor_tensor(out=yv[yb_][:], in0=pY[yb_], in1=ybwd[:, ut, tsl], op=ALU.add),
             reads=[t_pY[yb_], t_yb[chn]], writes=[t_yv[yb_]])
        S.op("dve", lambda E: E.scalar_tensor_tensor(out=yv[yb_][:], in0=uf[ufb][:, ut, :], scalar=dk[:, ut:ut + 1],
                                                     in1=yv[yb_][:], op0=ALU.mult, op1=ALU.add),
             reads=[t_uf[ufb], t_s, t_yv[yb_]], writes=[t_yv[yb_]])
        S.op("act", lambda E: E.activation(out=y2[yb_][:], in_=yv[yb_][:], func=AF.Square),
             reads=[t_yv[yb_]], writes=[t_y2[yb_]])
        S.op("pool", lambda E: E.tensor_scalar(out=y2[yb_][:], in0=y2[yb_][:], scalar1=0.044715, scalar2=1.0,
                                               op0=ALU.mult, op1=ALU.add), reads=[t_y2[yb_]], writes=[t_y2[yb_]])
        S.op("pool", lambda E: E.tensor_tensor(out=y2[yb_][:], in0=y2[yb_][:], in1=yv[yb_][:], op=ALU.mult),
             reads=[t_y2[yb_], t_yv[yb_]], writes=[t_y2[yb_]])
        S.op("act", lambda E: E.activation(out=y2[yb_][:], in_=y2[yb_][:], func=AF.Sigmoid, scale=1.5957691216057308),
             reads=[t_y2[yb_]], writes=[t_y2[yb_]])
        S.op("dve", lambda E: E.tensor_tensor(out=ygl[gb_][:, ut, :], in0=y2[yb_][:], in1=yv[yb_][:], op=ALU.mult),
             reads=[t_y2[yb_], t_yv[yb_]], writes=[t_yg[gb_][ut]])
        if ut != 1:
            return
        for o in range(2):
            for half in range(2):
                oc = o + 2 * half
                for u2 in range(2):
                    S.op("pe", lambda E, half=half, oc=oc, u2=u2: E.matmul(
                        pZ[half], lhsT=gw[:, u2, oc * 128:(oc + 1) * 128], rhs=ygl[gb_][:, u2, :], start=(u2 == 0),
                        stop=(u2 == 1)), reads=[t_s, t_yg[gb_][u2]], writes=[t_pZ[half]])
            S.op("act", lambda E, o=o: E.activation(out=sg[o][:], in_=pZ[1], func=AF.Sigmoid, bias=gb[:, 2 + o:3 + o]),
                 reads=[t_pZ[1], t_s], writes=[t_sg[o]])
            S.op("dve", lambda E, o=o: E.scalar_tensor_tensor(out=oo[o][:], in0=pZ[0], scalar=gb[:, o:o + 1],
                                                              in1=sg[o][:], op0=ALU.add, op1=ALU.mult),
                 reads=[t_pZ[0], t_sg[o], t_s], writes=[t_oo[o]])
            S.dma("sp", mixT[768 + 128 * o:768 + 128 * o + 128, lo:hi], oo[o][:], reads=[t_oo[o]], writes=[t_out])

    N = len(items)
    for i in range(N + 2):
        if i < N:
            stA(i)
        if 0 <= i - 1 < N:
            stB(i - 1)
        if 0 <= i - 2 < N:
            stC(i - 2)
    P.close()


DEC = 0.6065306597126334
RWKV_DBG = [9]


def phase_rwkv(nc, S, zr, ydir, bon, gfm, prm, T):
    mu_p, mu_n, w0, w2, a0, a2, g2, k_k, k_a, r_k = prm
    P = Phase(nc, S)
    NBLK = T // 512
    ident, identf, t_id = make_ident(nc, S, P)
    t_c = Tok()

    def cst(fn, eng="dve"):
        S.op(eng, fn, reads=[t_c, t_id], writes=[t_c])
    onesb = P.sb([128, 128], F32)
    cst(lambda E: E.memset(onesb[:], 0.0))
    cst(lambda E: E.memset(onesb[0:64, 0:64], 1.0))
    cst(lambda E: E.memset(onesb[64:128, 64:128], 1.0))
    mskU = P.sb([128, 512], F32)
    mskL = P.sb([128, 3, 128], F32)
    cst(lambda E: E.memset(mskU[:], 1.0), "pool")
    cst(lambda E: E.memset(mskL[:], 1.0), "pool")
    for i in range(4):
        cmp_ = ALU.is_gt if i % 2 == 0 else ALU.is_ge
        cst(lambda E, i=i, cmp_=cmp_: E.affine_select(out=mskU[:, 128 * i:128 * i + 128], in_=mskU[:, 128 * i:128 * i + 128],
                                                      pattern=[[1, 128]], base=0, channel_multiplier=-1, compare_op=cmp_,
                                                      fill=0.0), "pool")
    for i in range(3):
        cst(lambda E, i=i: E.affine_select(out=mskL[:, i, :], in_=mskL[:, i, :], pattern=[[-1, 128]], base=0,
                                           channel_multiplier=1, compare_op=ALU.is_gt, fill=0.0), "pool")
    rst = P.sb([128, 512], F32)
    cst(lambda E: E.memset(rst[:], 1.0))
    for q in range(4):
        cst(lambda E, q=q: E.memset(rst[:, 128 * q:128 * q + 1], 0.0))
    mh512 = P.sb([128, 512], F32)
    cst(lambda E: E.memset(mh512[:], -0.5), "pool")
    cmu = P.sb([128, 3, 11], F32)
    S.dma("sp", cmu[:, 1, :], mu_p.rearrange("(c p) -> p c", p=128), writes=[t_c], allow_slow_non_contiguous=True)
    S.dma("sp", cmu[:, 2, :], mu_n.rearrange("(c p) -> p c", p=128), writes=[t_c], allow_slow_non_contiguous=True)
    cst(lambda E: E.tensor_tensor(out=cmu[:, 0, :], in0=cmu[:, 1, :], in1=cmu[:, 2, :], op=ALU.add))
    cst(lambda E: E.tensor_scalar(out=cmu[:, 0, :], in0=cmu[:, 0, :], scalar1=-1.0, scalar2=1.0, op0=ALU.mult, op1=ALU.add))
    w0c = P.sb([128, 2, 3], F32); a0c = P.sb([128, 2, 3], F32)
    for d in range(2):
        S.dma("sp", w0c[:, d, :], w0[d].rearrange("(c p) -> p c", p=128), writes=[t_c], allow_slow_non_contiguous=True)
        S.dma("sp", a0c[:, d, :], a0[d].rearrange("(c p) -> p c", p=128), writes=[t_c], allow_slow_non_contiguous=True)
    kkc = P.sb([128, 3], F32); kac = P.sb([128, 3], F32); omka = P.sb([128, 3], F32); rkc = P.sb([128, 3], F32)
    S.dma("sp", kkc[:], k_k.rearrange("(c p) -> p c", p=128), writes=[t_c], allow_slow_non_contiguous=True)
    S.dma("sp", kac[:], k_a.rearrange("(c p) -> p c", p=128), writes=[t_c], allow_slow_non_contiguous=True)
    S.dma("sp", rkc[:], r_k.rearrange("h k -> (h k)").rearrange("(c p) -> p c", p=128), writes=[t_c],
          allow_slow_non_contiguous=True)
    cst(lambda E: E.tensor_scalar(out=omka[:], in0=kac[:], scalar1=-1.0, scalar2=1.0, op0=ALU.mult, op1=ALU.add))
    w2a2 = P.sb([128, 2, 384], BF16)
    for d in range(2):
        S.dma("pool", w2a2[0:64, d, :], w2[d], writes=[t_c])
        S.dma("pool", w2a2[64:128, d, :], a2[d], writes=[t_c])
    g2b = P.sb([128, 384], BF16)
    S.dma("pool", g2b[:], g2, writes=[t_c])

    banks = [P.ps([128, 512], F32) for _ in range(6)]
    t_bk = [Tok() for _ in range(6)]
    bkrr = [0]

    def getbank():
        i = bkrr[0] % 6
        bkrr[0] += 1
        return banks[i], t_bk[i]
    pTr = [P.ps([128, 4, 128], BF16) for _ in range(2)]; t_pTr = [Tok(), Tok()]
    trr = [0]

    NZS = 4
    zraw = P.sb([128, NZS, 514], F32); t_zraw = [Tok() for _ in range(NZS)]
    zsi = [0]
    zm = P.sb([128, 11, 512], F32); t_zm = [Tok() for _ in range(11)]
    tmp0 = [P.sb([128, 512], F32) for _ in range(2)]; t_tmp0 = [Tok(), Tok()]
    tmp1 = [P.sb([128, 512], F32) for _ in range(2)]; t_tmp1 = [Tok(), Tok()]
    tz = P.sb([128, 512], BF16); t_tz = Tok()
    sgz = P.sb([128, 512], BF16); t_sgz = Tok()
    B1 = [P.sb([128, 512], F32) for _ in range(3)]; tB1 = [Tok() for _ in range(3)]
    B2 = [P.sb([128, 512], F32) for _ in range(3)]; tB2 = [Tok() for _ in range(3)]
    B3 = [P.sb([128, 512], F32) for _ in range(3)]; tB3 = [Tok() for _ in range(3)]
    B4 = [P.sb([128, 512], F32) for _ in range(3)]; tB4 = [Tok() for _ in range(3)]
    B5 = [P.sb([128, 512], F32) for _ in range(3)]; tB5 = [Tok() for _ in range(3)]
    B6 = [P.sb([128, 512], F32) for _ in range(3)]; tB6 = [Tok() for _ in range(3)]
    B7 = [P.sb([128, 512], F32) for _ in range(3)]; tB7 = [Tok() for _ in range(3)]
    ARb = [P.sb([128, 4, 2, 128], BF16) for _ in range(3)]; t_AR = [Tok() for _ in range(3)]
    Bt = [P.sb([128, 512], BF16) for _ in range(3)]; t_Bt = [Tok() for _ in range(3)]
    Kt = [P.sb([128, 512], BF16) for _ in range(3)]; t_Kt = [Tok() for _ in range(3)]
    Bb = [P.sb([128, 512], BF16) for _ in range(3)]; t_Bb = [Tok() for _ in range(3)]
    Kb = [P.sb([128, 512], BF16) for _ in range(3)]; t_Kb = [Tok() for _ in range(3)]
    vb = [P.sb([128, 512], BF16) for _ in range(3)]; t_vb = [Tok() for _ in range(3)]
    WCt = P.sb([128, 3, 4], F32); t_WC = Tok()
    tm = [[P.sb([128, 4, 128], BF16) for _ in range(4)] for _ in range(3)]
    t_tm = [[Tok() for _ in range(4)] for _ in range(3)]
    stg = [P.sb([128, 512], F32) for _ in range(3)]; t_stg = [Tok() for _ in range(3)]
    sgi = [0]
    MP = [P.sb([128, 512], BF16) for _ in range(6)]; t_MP = [Tok() for _ in range(6)]
    MT = [P.sb([128, 3, 128], BF16) for _ in range(2)]; t_MT = [Tok(), Tok()]
    Pm = [[P.sb([128, 3, 128], BF16) for _ in range(2)] for _ in range(2)]; t_Pm = [[Tok(), Tok()], [Tok(), Tok()]]
    PmT = [[P.sb([128, 3, 128], BF16) for _ in range(2)] for _ in range(2)]; t_PmT = [[Tok(), Tok()], [Tok(), Tok()]]
    Tm = [[P.sb([128, 3, 128], BF16) for _ in range(2)] for _ in range(2)]; t_Tm = [[Tok(), Tok()], [Tok(), Tok()]]
    X0 = P.sb([128, 6, 64], BF16); t_X0 = Tok()
    Uv = P.sb([128, 6, 64], F32); t_Uv = Tok()
    Ahb = P.sb([128, 3, 128], BF16); t_Ah = Tok()
    Ub = P.sb([128, 6, 64], BF16); t_Ub = Tok()
    Sf = P.sb([128, 3, 64], F32); t_Sf = Tok()
    Sb = P.sb([128, 3, 64], BF16); t_Sb = Tok()
    ytm = [P.sb([128, 4, 384], F32) for _ in range(2)]; t_ytm = [Tok(), Tok()]
    t_out = Tok()
    zv = zr.rearrange("(c p) t -> p c t", p=128)

    for d in range(2 if RWKV_DBG[0] > -1 else 0):
        S.op("dve", lambda E: E.memset(Sf[:], 0.0), writes=[t_Sf])
        S.op("dve", lambda E: E.memset(Sb[:], 0.0), writes=[t_Sb])
        for bi in range(NBLK):
            if d == 0:
                lo, hi = 512 * bi, 512 * bi + 512
            else:
                lo, hi = T - 512 * (bi + 1), T - 512 * bi
            loc = (lambda ap: ap) if d == 0 else None
            s0 = 1 if lo == 0 else 0
            s1 = 513 if hi == T else 514
            osl = slice(0, 512) if d == 0 else rsl(0, 512)
            for c in range(11):
                zs = zsi[0] % NZS
                zsi[0] += 1
                b = c % 2
                if lo == 0:
                    S.op("pool", lambda E, zs=zs: E.memset(zraw[:, zs, 0:1], 0.0), writes=[t_zraw[zs]])
                if hi == T:
                    S.op("pool", lambda E, zs=zs: E.memset(zraw[:, zs, 513:514], 0.0), writes=[t_zraw[zs]])
                S.dma("sp", zraw[:, zs, s0:s1], zv[:, c, lo - 1 + s0:lo - 1 + s1], writes=[t_zraw[zs]])
                S.op("act", lambda E, c=c, b=b, zs=zs: E.activation(out=tmp0[b][:], in_=zraw[:, zs, 1:513], func=AF.Copy,
                                                                   scale=cmu[:, 0, c:c + 1]),
                     reads=[t_zraw[zs], t_c], writes=[t_tmp0[b]])
                S.op("dve", lambda E, c=c, b=b, zs=zs: E.scalar_tensor_tensor(out=tmp1[b][:], in0=zraw[:, zs, 0:512],
                                                                              scalar=cmu[:, 1, c:c + 1], in1=tmp0[b][:],
                                                                              op0=ALU.mult, op1=ALU.add),
                     reads=[t_zraw[zs], t_c, t_tmp0[b]], writes=[t_tmp1[b]])
                S.op("dve", lambda E, c=c, b=b, zs=zs, osl=osl: E.scalar_tensor_tensor(
                    out=zm[:, c, osl], in0=zraw[:, zs, 2:514], scalar=cmu[:, 2, c:c + 1], in1=tmp1[b][:],
                    op0=ALU.mult, op1=ALU.add),
                    reads=[t_zraw[zs], t_c, t_tmp1[b]], writes=[t_zm[c]])
            if RWKV_DBG[0] == 0:
                continue
            S.op("act", lambda E: E.activation(out=tz[0:64, :], in_=zm[0:64, 9, :], func=AF.Tanh),
                 reads=[t_zm[9]], writes=[t_tz])
            S.op("act", lambda E: E.activation(out=tz[64:128, :], in_=zm[64:128, 9, :], func=AF.Copy),
                 reads=[t_zm[9]], writes=[t_tz])
            if d == 0:
                S.op("act", lambda E: E.activation(out=sgz[:], in_=zm[:, 10, :], func=AF.Sigmoid),
                     reads=[t_zm[10]], writes=[t_sgz])
            v4 = lambda ap: ap.rearrange("p (q t) -> p q t", q=4)
            steps = []
            cb = {}

            def ST(f):
                steps.append(f)
            for c in range(3):
                cb[c] = dict(zr=zm[:, c, :], tr=t_zm[c], zk=zm[:, 3 + c, :], tk=t_zm[3 + c], zv=zm[:, 6 + c, :],
                             tv=t_zm[6 + c])

            def s_mm(c):
                X = cb[c]
                X["pW"], X["tpW"] = getbank()
                S.op("pe", lambda E: E.matmul(X["pW"][:], lhsT=w2a2[0:64, d, 128 * c:128 * c + 128], rhs=tz[0:64, :],
                                              start=True, stop=True), reads=[t_c, t_tz], writes=[X["tpW"]])
                X["pA"], X["tpA"] = getbank()
                S.op("pe", lambda E: E.matmul(X["pA"][:], lhsT=w2a2[64:128, d, 128 * c:128 * c + 128], rhs=tz[64:128, :],
                                              start=True, stop=True), reads=[t_c, t_tz], writes=[X["tpA"]])
            ST(s_mm)

            def s_sig(c):
                X = cb[c]
                S.op("act", lambda E: E.activation(out=B1[c][:], in_=X["pW"][:], func=AF.Sigmoid, bias=w0c[:, d, c:c + 1]),
                     reads=[X["tpW"], t_c], writes=[tB1[c]])
                S.op("act", lambda E: E.activation(out=B6[c][:], in_=X["pA"][:], func=AF.Sigmoid, bias=a0c[:, d, c:c + 1]),
                     reads=[X["tpA"], t_c], writes=[tB6[c]])
            ST(s_sig)

            def s_kkv(c):
                X = cb[c]
                S.op("dve", lambda E: E.tensor_scalar(out=B4[c][:], in0=X["zk"], scalar1=kkc[:, c:c + 1], scalar2=None,
                                                      op0=ALU.mult), reads=[X["tk"], t_c], writes=[tB4[c]])
                S.op("pool", lambda E: E.tensor_tensor(out=B5[c][:], in0=B4[c][:], in1=B4[c][:], op=ALU.mult),
                     reads=[tB4[c]], writes=[tB5[c]])
                X["pN"], X["tpN"] = getbank()
                S.op("pe", lambda E: E.matmul(X["pN"][:], lhsT=onesb[:], rhs=B5[c][:], start=True, stop=True),
                     reads=[t_c, tB5[c]], writes=[X["tpN"]])
            ST(s_kkv)

            def s_cls(c):
                S.op("dve", lambda E: E.tensor_tensor_scan(out=B2[c][:], data0=rst[:], data1=B1[c][:], initial=0.0,
                                                           op0=ALU.mult, op1=ALU.add),
                     reads=[tB1[c], t_c], writes=[tB2[c]])
                S.op("pool", lambda E: E.tensor_tensor(out=B1[c][:], in0=B2[c][:], in1=B1[c][:], op=ALU.subtract),
                     reads=[tB2[c]], writes=[tB1[c]])
            ST(s_cls)

            def s_exp(c):
                S.op("act", lambda E: E.activation(out=B3[c][:], in_=B2[c][:], func=AF.Exp, scale=DEC),
                     reads=[tB2[c]], writes=[tB3[c]])
                S.op("act", lambda E: E.activation(out=B2[c][:], in_=B2[c][:], func=AF.Exp, scale=-DEC),
                     reads=[tB2[c]], writes=[tB2[c]])
                S.op("act", lambda E: E.activation(out=B1[c][:], in_=B1[c][:], func=AF.Exp, scale=-DEC),
                     reads=[tB1[c]], writes=[tB1[c]])
            ST(s_exp)

            def s_rn(c):
                X = cb[c]
                S.op("dve", lambda E: E.tensor_scalar(out=B5[c][:], in0=X["pN"][:], scalar1=1e-12, scalar2=None,
                                                      op0=ALU.add), reads=[X["tpN"]], writes=[tB5[c]])
                S.op("act", lambda E: E.activation(out=B5[c][:], in_=B5[c][:], func=AF.Sqrt),
                     reads=[tB5[c]], writes=[tB5[c]])
                S.op("dve", lambda E: E.reciprocal(out=B5[c][:], in_=B5[c][:]),
                     reads=[tB5[c]], writes=[tB5[c]])
                S.op("dve", lambda E: E.tensor_tensor(out=B4[c][:], in0=B4[c][:], in1=B5[c][:], op=ALU.mult),
                     reads=[tB4[c], tB5[c]], writes=[tB4[c]])
                S.op("dve", lambda E: E.tensor_copy(out=WCt[:, c, :], in_=B2[c][:, 127:512:128]),
                     reads=[tB2[c]], writes=[t_WC])
            ST(s_rn)

            def s_kd(c):
                X = cb[c]
                S.op("dve", lambda E: E.tensor_scalar(out=B7[c][:], in0=B6[c][:], scalar1=kac[:, c:c + 1],
                                                      scalar2=omka[:, c:c + 1], op0=ALU.mult, op1=ALU.add),
                     reads=[tB6[c], t_c], writes=[tB7[c]])
                S.op("dve", lambda E: E.tensor_tensor(out=B7[c][:], in0=B7[c][:], in1=X["zk"], op=ALU.mult),
                     reads=[tB7[c], X["tk"]], writes=[tB7[c]])
                S.op("pool", lambda E: E.tensor_tensor(out=B6[c][:], in0=B4[c][:], in1=B6[c][:], op=ALU.mult),
                     reads=[tB4[c], tB6[c]], writes=[tB6[c]])
            ST(s_kd)

            def s_ar(c):
                X = cb[c]
                S.op("dve", lambda E: E.scalar_tensor_tensor(out=ARb[c][:, :, 0, :], in0=v4(B4[c][:]), scalar=-1.0,
                                                             in1=v4(B1[c][:]), op0=ALU.mult, op1=ALU.mult),
                     reads=[tB4[c], tB1[c]], writes=[t_AR[c]])
                S.op("dve", lambda E: E.tensor_tensor(out=ARb[c][:, :, 1, :], in0=v4(X["zr"]), in1=v4(B2[c][:]),
                                                      op=ALU.mult), reads=[X["tr"], tB2[c]], writes=[t_AR[c]])
                S.op("pool", lambda E: E.tensor_tensor(out=Bt[c][:], in0=B6[c][:], in1=B3[c][:], op=ALU.mult),
                     reads=[tB6[c], tB3[c]], writes=[t_Bt[c]])
                S.op("pool", lambda E: E.tensor_tensor(out=Kt[c][:], in0=B7[c][:], in1=B3[c][:], op=ALU.mult),
                     reads=[tB7[c], tB3[c]], writes=[t_Kt[c]])
                S.op("act", lambda E: E.activation(out=vb[c][:], in_=X["zv"], func=AF.Copy),
                     reads=[X["tv"]], writes=[t_vb[c]])
            ST(s_ar)

            def s_bb(c):
                wcb = WCt[:, c, :].unsqueeze(2).to_broadcast([128, 4, 128])
                S.op("dve", lambda E: E.tensor_tensor(out=v4(Bb[c][:]), in0=v4(Bt[c][:]), in1=wcb, op=ALU.mult),
                     reads=[t_Bt[c], t_WC], writes=[t_Bb[c]])
                S.op("dve", lambda E: E.tensor_tensor(out=v4(Kb[c][:]), in0=v4(Kt[c][:]), in1=wcb, op=ALU.mult),
                     reads=[t_Kt[c], t_WC], writes=[t_Kb[c]])
            ST(s_bb)

            def s_bonus(c):
                X = cb[c]
                S.op("dve", lambda E: E.scalar_tensor_tensor(out=B5[c][:], in0=X["zr"], scalar=rkc[:, c:c + 1],
                                                             in1=B7[c][:], op0=ALU.mult, op1=ALU.mult),
                     reads=[X["tr"], tB7[c], t_c], writes=[tB5[c]])
                pBn, t_pBn = getbank()
                S.op("pe", lambda E: E.matmul(pBn[:], lhsT=onesb[:], rhs=B5[c][:], start=True, stop=True),
                     reads=[t_c, tB5[c]], writes=[t_pBn])
                si = sgi[0] % 3
                sgi[0] += 1
                S.op("dve", lambda E: E.tensor_tensor(out=stg[si][:, osl], in0=pBn[:], in1=X["zv"], op=ALU.mult),
                     reads=[t_pBn, X["tv"]], writes=[t_stg[si]])
                S.dma("sp", bon[d][128 * c:128 * c + 128, lo:hi], stg[si][:], reads=[t_stg[si]], writes=[t_out])
                if d == 0:
                    pG, t_pG = getbank()
                    S.op("pe", lambda E: E.matmul(pG[:], lhsT=g2b[:, 128 * c:128 * c + 128], rhs=sgz[:], start=True,
                                                  stop=True), reads=[t_c, t_sgz], writes=[t_pG])
                    si2 = sgi[0] % 3
                    sgi[0] += 1
                    S.op("act", lambda E: E.activation(out=stg[si2][:], in_=pG[:], func=AF.Copy),
                         reads=[t_pG], writes=[t_stg[si2]])
                    S.dma("sp", gfm[128 * c:128 * c + 128, lo:hi], stg[si2][:], reads=[t_stg[si2]], writes=[t_out])
            ST(s_bonus)

            def s_tr(c):
                for q in range(4):
                    ts_ = trr[0] % 2
                    trr[0] += 1
                    srcs = [(ARb[c][:, q, 0, :], t_AR[c]), (Bb[c][:, 128 * q:128 * q + 128], t_Bb[c]),
                            (Kb[c][:, 128 * q:128 * q + 128], t_Kb[c]), (vb[c][:, 128 * q:128 * q + 128], t_vb[c])]
                    for ai, (src, tk) in enumerate(srcs):
                        S.op("pe", lambda E, ts_=ts_, ai=ai, src=src: E.transpose(out=pTr[ts_][:, ai, :], in_=src,
                                                                                 identity=ident[:]),
                             reads=[tk, t_id], writes=[t_pTr[ts_]])
                    if q % 2 == 0:
                        S.op("act", lambda E, ts_=ts_, q=q: E.activation(out=tm[c][q][:], in_=pTr[ts_][:], func=AF.Copy),
                             reads=[t_pTr[ts_]], writes=[t_tm[c][q]])
                    else:
                        S.op("dve", lambda E, ts_=ts_, q=q: E.tensor_copy(out=tm[c][q][:], in_=pTr[ts_][:]),
                             reads=[t_pTr[ts_]], writes=[t_tm[c][q]])
            ST(s_tr)
            for st_ in steps:
                for c in range(3):
                    st_(c)
            yb = bi % 2
            for q in range(4 if RWKV_DBG[0] >= 2 else 0):
                qs = slice(128 * q, 128 * q + 128)
                for h in range(6):
                    c, h2 = h // 2, h % 2
                    rows = slice(64 * h2, 64 * h2 + 64)
                    pM, t_pM = getbank()
                    S.op("pe", lambda E, pM=pM, c=c, rows=rows, qs=qs, q=q: E.matmul(
                        pM[:, 0:256], lhsT=Bt[c][rows, qs], rhs=ARb[c][rows, q, :, :].rearrange("p a b -> p (a b)"), start=True, stop=True),
                        reads=[t_Bt[c], t_AR[c]], writes=[t_pM])
                    S.op("pe", lambda E, pM=pM, c=c, rows=rows, qs=qs, q=q: E.matmul(
                        pM[:, 256:512], lhsT=Kt[c][rows, qs], rhs=ARb[c][rows, q, :, :].rearrange("p a b -> p (a b)"), start=True, stop=True),
                        reads=[t_Kt[c], t_AR[c]], writes=[t_pM])
                    S.op("dve", lambda E, pM=pM, h=h: E.tensor_tensor(out=MP[h][:], in0=pM[:], in1=mskU[:], op=ALU.mult),
                         reads=[t_pM, t_c], writes=[t_MP[h]])
                for hg in range(2):
                    pM3, t_pM3 = getbank()
                    for j in range(3):
                        h = 2 * j + hg
                        c, h2 = j, hg
                        rows = slice(64 * h2, 64 * h2 + 64)
                        S.op("pe", lambda E, pM3=pM3, j=j, c=c, rows=rows, qs=qs, q=q: E.matmul(
                            pM3[:, 128 * j:128 * j + 128], lhsT=ARb[c][rows, q, 0, :], rhs=Bt[c][rows, qs],
                            start=True, stop=True), reads=[t_Bt[c], t_AR[c]], writes=[t_pM3])
                    S.op("dve", lambda E, pM3=pM3, hg=hg: E.tensor_tensor(
                        out=MT[hg][:], in0=pM3[:, 0:384].rearrange("p (a b) -> p a b", a=3), in1=mskL[:], op=ALU.mult),
                        reads=[t_pM3, t_c], writes=[t_MT[hg]])
                if RWKV_DBG[0] < 3:
                    continue
                cur = [0, 0]
                for hg in range(2):
                    for j in range(3):
                        h = 2 * j + hg
                        S.op("pool", lambda E, hg=hg, j=j, h=h: E.tensor_tensor(out=Tm[hg][0][:, j, :], in0=MP[h][:, 0:128],
                                                                              in1=identf[:], op=ALU.add),
                             reads=[t_MP[h], t_id], writes=[t_Tm[hg][0]])
                for lev in range(1, 7):
                    for hg in range(2):
                        pv = cur[hg]
                        nx = 1 - pv
                        pP, t_pP = getbank()
                        pPT, t_pPT = getbank()
                        for j in range(3):
                            h = 2 * j + hg
                            if lev == 1:
                                Pprev, tP = MP[h][:, 0:128], t_MP[h]
                                PTprev, tPT = MT[hg][:, j, :], t_MT[hg]
                            else:
                                Pprev, tP = Pm[hg][pv][:, j, :], t_Pm[hg][pv]
                                PTprev, tPT = PmT[hg][pv][:, j, :], t_PmT[hg][pv]
                            if lev < 6:
                                S.op("pe", lambda E, pP=pP, j=j, Pprev=Pprev, PTprev=PTprev: E.matmul(
                                    pP[:, 128 * j:128 * j + 128], lhsT=PTprev, rhs=Pprev, start=True, stop=True),
                                    reads=[tP, tPT], writes=[t_pP])
                            S.op("pe", lambda E, pPT=pPT, j=j, Pprev=Pprev, PTprev=PTprev: E.matmul(
                                pPT[:, 128 * j:128 * j + 128], lhsT=Pprev, rhs=PTprev, start=True, stop=True),
                                reads=[tP, tPT], writes=[t_pPT])
                        v3 = lambda ap: ap[:, 0:384].rearrange("p (a b) -> p a b", a=3)
                        if lev < 6:
                            S.op("act", lambda E, pP=pP, hg=hg, nx=nx: E.activation(out=Pm[hg][nx][:], in_=v3(pP),
                                                                                    func=AF.Copy),
                                 reads=[t_pP], writes=[t_Pm[hg][nx]])
                        S.op("dve", lambda E, pPT=pPT, hg=hg, nx=nx: E.tensor_copy(out=PmT[hg][nx][:], in_=v3(pPT)),
                             reads=[t_pPT], writes=[t_PmT[hg][nx]])
                        pTT, t_pTT = getbank()
                        for j in range(3):
                            S.op("pe", lambda E, pTT=pTT, j=j, hg=hg, nx=nx, pv=pv: E.matmul(
                                pTT[:, 128 * j:128 * j + 128], lhsT=PmT[hg][nx][:, j, :], rhs=Tm[hg][pv][:, j, :],
                                start=True, stop=True), reads=[t_PmT[hg][nx], t_Tm[hg][pv]], writes=[t_pTT])
                        S.op("dve", lambda E, pTT=pTT, hg=hg, nx=nx, pv=pv: E.tensor_tensor(
                            out=Tm[hg][nx][:], in0=v3(pTT), in1=Tm[hg][pv][:], op=ALU.add),
                            reads=[t_pTT, t_Tm[hg][pv]], writes=[t_Tm[hg][nx]])
                        cur[hg] = nx
                TF = [Tm[0][cur[0]], Tm[1][cur[1]]]
                tTF = [t_Tm[0][cur[0]], t_Tm[1][cur[1]]]
                if RWKV_DBG[0] < 4:
                    continue
                pX, t_pX = getbank()
                for h in range(6):
                    c, h2 = h // 2, h % 2
                    sl_ = 3 * h2 + c
                    S.op("pe", lambda E, pX=pX, h=h, c=c, h2=h2, q=q, sl_=sl_: E.matmul(
                        pX[:, 64 * sl_:64 * sl_ + 64], lhsT=MP[h][:, 256:384], rhs=tm[c][q][:, 3, 64 * h2:64 * h2 + 64],
                        start=True, stop=True), reads=[t_MP[h], t_tm[c][q]], writes=[t_pX])
                S.op("act", lambda E, pX=pX: E.activation(out=X0[:], in_=pX[:, 0:384].rearrange("p (a b) -> p a b", a=6),
                                                         func=AF.Copy), reads=[t_pX], writes=[t_X0])
                pV, t_pV = getbank()
                for h in range(6):
                    c, h2 = h // 2, h % 2
                    sl_ = 3 * h2 + c
                    S.op("pe", lambda E, pV=pV, c=c, h2=h2, sl_=sl_: E.matmul(
                        pV[:, 64 * sl_:64 * sl_ + 64], lhsT=TF[h2][:, c, :], rhs=X0[:, sl_, :], start=True, stop=True),
                        reads=[tTF[h2], t_X0], writes=[t_pV])
                S.op("act", lambda E, pV=pV: E.activation(out=Uv[:], in_=pV[:, 0:384].rearrange("p (a b) -> p a b", a=6),
                                                         func=AF.Copy), reads=[t_pV], writes=[t_Uv])
                pH, t_pH = getbank()
                for h in range(6):
                    c, h2 = h // 2, h % 2
                    S.op("pe", lambda E, pH=pH, c=c, h2=h2, q=q: E.matmul(
                        pH[64 * h2:64 * h2 + 64, 128 * c:128 * c + 128], lhsT=tm[c][q][:, 0, 64 * h2:64 * h2 + 64],
                        rhs=TF[h2][:, c, :], start=True, stop=True), reads=[tTF[h2], t_tm[c][q]], writes=[t_pH])
                S.op("dve", lambda E, pH=pH: E.tensor_copy(out=Ahb[:], in_=pH[:, 0:384].rearrange("p (a b) -> p a b", a=3)),
                     reads=[t_pH], writes=[t_Ah])
                if RWKV_DBG[0] < 5:
                    continue
                pUb = [getbank(), getbank()]
                for h2 in range(2):
                    rows = slice(64 * h2, 64 * h2 + 64)
                    for c in range(3):
                        S.op("pe", lambda E, h2=h2, c=c, rows=rows: E.matmul(
                            pUb[h2][0][:, 64 * c:64 * c + 64], lhsT=Ahb[rows, c, :], rhs=Sb[rows, c, :], start=True,
                            stop=True), reads=[t_Ah, t_Sb], writes=[pUb[h2][1]])
                for h2 in range(2):
                    S.op("dve", lambda E, h2=h2: E.tensor_tensor(
                        out=Ub[:, 3 * h2:3 * h2 + 3, :], in0=pUb[h2][0][:, 0:192].rearrange("p (a b) -> p a b", a=3),
                        in1=Uv[:, 3 * h2:3 * h2 + 3, :], op=ALU.add),
                        reads=[pUb[h2][1], t_Uv], writes=[t_Ub])
                pYb = [getbank(), getbank()]
                for h2 in range(2):
                    rows = slice(64 * h2, 64 * h2 + 64)
                    for c in range(3):
                        h = 2 * c + h2
                        sl_ = 3 * h2 + c
                        oc = slice(64 * c, 64 * c + 64)
                        S.op("pe", lambda E, h2=h2, c=c, rows=rows, q=q, oc=oc: E.matmul(
                            pYb[h2][0][:, oc], lhsT=ARb[c][rows, q, 1, :], rhs=Sb[rows, c, :], start=True, stop=False),
                            reads=[t_AR[c], t_Sb], writes=[pYb[h2][1]])
                        S.op("pe", lambda E, h2=h2, h=h, sl_=sl_, oc=oc: E.matmul(
                            pYb[h2][0][:, oc], lhsT=MP[h][:, 128:256], rhs=Ub[:, sl_, :], start=False, stop=False),
                            reads=[t_MP[h], t_Ub], writes=[pYb[h2][1]])
                        S.op("pe", lambda E, h2=h2, h=h, c=c, q=q, oc=oc: E.matmul(
                            pYb[h2][0][:, oc], lhsT=MP[h][:, 384:512], rhs=tm[c][q][:, 3, 64 * h2:64 * h2 + 64],
                            start=False, stop=True), reads=[t_MP[h], t_tm[c][q]], writes=[pYb[h2][1]])
                for h2 in range(2):
                    S.op("act", lambda E, h2=h2, yb=yb, q=q: E.activation(
                        out=ytm[yb][:, q, :].rearrange("p (c g v) -> p g c v", g=2, v=64)[:, h2, :, :],
                        in_=pYb[h2][0][:, 0:192].rearrange("p (a b) -> p a b", a=3), func=AF.Copy),
                        reads=[pYb[h2][1]], writes=[t_ytm[yb]])
                pS_, t_pS = getbank()
                for h in range(6):
                    c, h2 = h // 2, h % 2
                    orow = slice(64 * h2, 64 * h2 + 64)
                    S.op("pe", lambda E, pS_=pS_, h=h, c=c, h2=h2, orow=orow, q=q: E.matmul(
                        pS_[orow, 64 * c:64 * c + 64], lhsT=tm[c][q][:, 1, 64 * h2:64 * h2 + 64], rhs=Ub[:, 3 * h2 + c, :],
                        start=True, stop=False), reads=[t_tm[c][q], t_Ub], writes=[t_pS])
                    S.op("pe", lambda E, pS_=pS_, h=h, c=c, h2=h2, orow=orow, q=q: E.matmul(
                        pS_[orow, 64 * c:64 * c + 64], lhsT=tm[c][q][:, 2, 64 * h2:64 * h2 + 64],
                        rhs=tm[c][q][:, 3, 64 * h2:64 * h2 + 64], start=False, stop=True),
                        reads=[t_tm[c][q]], writes=[t_pS])
                for c in range(3):
                    S.op("dve", lambda E, pS_=pS_, c=c, q=q: E.scalar_tensor_tensor(
                        out=Sf[:, c, :], in0=Sf[:, c, :], scalar=WCt[:, c, q:q + 1], in1=pS_[:, 64 * c:64 * c + 64],
                        op0=ALU.mult, op1=ALU.add), reads=[t_pS, t_WC, t_Sf], writes=[t_Sf])
                S.op("act", lambda E: E.activation(out=Sb[:], in_=Sf[:], func=AF.Copy), reads=[t_Sf], writes=[t_Sb])
            lb = 512 * bi
            S.dma("sp", ydir[d][lb:lb + 512, :].rearrange("(q p) f -> p q f", p=128), ytm[yb][:], reads=[t_ytm[yb]],
                  writes=[t_out])
    P.close()


LNX_EPS = 64e-5


def phase_rwkv_combine(nc, S, ydir, bon, gfm, lnx_w, lnx_b, mixT, T):
    P = Phase(nc, S)
    NT = T // 128
    ident, identf, t_id = make_ident(nc, S, P)
    t_c = Tok()
    J = P.sb([128, 128], F32)
    S.op("pool", lambda E: E.memset(J[:], 1.0), writes=[t_c])
    S.op("pool", lambda E: E.affine_select(out=J[:], in_=J[:], pattern=[[1, 128]], base=-127, channel_multiplier=1,
                                           compare_op=ALU.is_equal, fill=0.0), reads=[t_c], writes=[t_c])
    lwt = P.sb([128, 384], F32); lbt = P.sb([128, 384], F32)
    S.dma("sp", lwt[:], lnx_w.partition_broadcast(128), writes=[t_c])
    S.dma("sp", lbt[:], lnx_b.partition_broadcast(128), writes=[t_c])
    mh = P.sb([128, 6], F32)
    S.op("pool", lambda E: E.memset(mh[:], -0.5), writes=[t_c])
    NBF = 2
    y0 = [P.sb([128, 384], F32) for _ in range(NBF)]; t_y0 = [Tok() for _ in range(NBF)]
    y1 = [P.sb([128, 384], F32) for _ in range(NBF)]; t_y1 = [Tok() for _ in range(NBF)]
    b0 = [P.sb([128, 3, 128], F32) for _ in range(NBF)]; t_b0 = [Tok() for _ in range(NBF)]
    b1 = [P.sb([128, 3, 128], F32) for _ in range(NBF)]; t_b1 = [Tok() for _ in range(NBF)]
    gt = [P.sb([128, 3, 128], F32) for _ in range(NBF)]; t_gt = [Tok() for _ in range(NBF)]
    ys = [P.sb([128, 6, 64], F32) for _ in range(NBF)]; t_ys = [Tok() for _ in range(NBF)]
    sq = [P.sb([128, 6, 64], F32) for _ in range(NBF)]; t_sq = [Tok() for _ in range(NBF)]
    st = [P.sb([128, 4, 6], F32) for _ in range(NBF)]; t_st = [Tok() for _ in range(NBF)]
    rs = [P.sb([128, 384], F32) for _ in range(NBF)]; t_rs = [Tok() for _ in range(NBF)]
    ob = [P.sb([128, 3, 128], BF16) for _ in range(NBF)]; t_ob = [Tok() for _ in range(NBF)]
    pJ = [P.ps([128, 512], F32) for _ in range(2)]; t_pJ = [Tok(), Tok()]
    pB = [P.ps([128, 512], F32) for _ in range(2)]; t_pB = [Tok(), Tok()]
    pG = [P.ps([128, 512], F32) for _ in range(2)]; t_pG = [Tok(), Tok()]
    pO = [P.ps([128, 512], F32) for _ in range(2)]; t_pO = [Tok(), Tok()]
    t_out = Tok()
    f3 = lambda ap: ap.rearrange("p (a b) -> p a b", a=6)
    for n in range(NT):
        b = n % NBF
        tl = slice(128 * n, 128 * n + 128)
        S.dma("sp", y0[b][:], ydir[0][128 * n:128 * n + 128, :], writes=[t_y0[b]])
        S.dma("sp", y1[b][:], ydir[1][T - 128 * (n + 1):T - 128 * n, :], writes=[t_y1[b]])
        S.dma("sp", b0[b][:], bon[0][:, tl].rearrange("(c p) t -> p c t", p=128), writes=[t_b0[b]])
        S.dma("sp", b1[b][:], bon[1][:, tl].rearrange("(c p) t -> p c t", p=128), writes=[t_b1[b]])
        S.dma("sp", gt[b][:], gfm[:, tl].rearrange("(c p) t -> p c t", p=128), writes=[t_gt[b]])
        S.op("pe", lambda E, b=b: E.matmul(pJ[b][:, 0:384], lhsT=J[:], rhs=y1[b][:], start=True, stop=True),
             reads=[t_c, t_y1[b]], writes=[t_pJ[b]])
        S.op("dve", lambda E, b=b: E.tensor_tensor(out=ys[b][:], in0=f3(pJ[b][:, 0:384]), in1=f3(y0[b][:]), op=ALU.add),
             reads=[t_pJ[b], t_y0[b]], writes=[t_ys[b]])
        S.op("dve", lambda E, b=b: E.tensor_reduce(out=st[b][:, 0, :], in_=ys[b][:], axis=AX.X, op=ALU.add),
             reads=[t_ys[b]], writes=[t_st[b]])
        S.op("dve", lambda E, b=b: E.tensor_scalar(out=st[b][:, 0, :], in0=st[b][:, 0, :], scalar1=1.0 / 64, scalar2=None,
                                                   op0=ALU.mult), reads=[t_st[b]], writes=[t_st[b]])
        S.op("dve", lambda E, b=b: E.tensor_tensor(out=ys[b][:], in0=ys[b][:],
                                                   in1=st[b][:, 0, :].unsqueeze(2).to_broadcast([128, 6, 64]),
                                                   op=ALU.subtract), reads=[t_st[b], t_ys[b]], writes=[t_ys[b]])
        S.op("pool", lambda E, b=b: E.tensor_tensor(out=sq[b][:], in0=ys[b][:], in1=ys[b][:], op=ALU.mult),
             reads=[t_ys[b]], writes=[t_sq[b]])
        S.op("dve", lambda E, b=b: E.tensor_reduce(out=st[b][:, 1, :], in_=sq[b][:], axis=AX.X, op=ALU.add),
             reads=[t_sq[b]], writes=[t_st[b]])
        S.op("dve", lambda E, b=b: E.tensor_scalar(out=st[b][:, 2, :], in0=st[b][:, 1, :], scalar1=1.0 / 64,
                                                   scalar2=LNX_EPS, op0=ALU.mult, op1=ALU.add),
             reads=[t_st[b]], writes=[t_st[b]])
        S.op("pool", lambda E, b=b: E.tensor_tensor(out=st[b][:, 3, :], in0=st[b][:, 2, :], in1=mh[:], op=ALU.pow),
             reads=[t_st[b], t_c], writes=[t_st[b]])
        S.op("dve", lambda E, b=b: E.tensor_tensor(out=ys[b][:], in0=ys[b][:],
                                                   in1=st[b][:, 3, :].unsqueeze(2).to_broadcast([128, 6, 64]),
                                                   op=ALU.mult), reads=[t_st[b], t_ys[b]], writes=[t_ys[b]])
        yf = ys[b][:].rearrange("p a b -> p (a b)")
        S.op("pool", lambda E, b=b, yf=yf: E.tensor_tensor(out=yf, in0=yf, in1=lwt[:], op=ALU.mult),
             reads=[t_ys[b], t_c], writes=[t_ys[b]])
        S.op("pool", lambda E, b=b, yf=yf: E.tensor_tensor(out=yf, in0=yf, in1=lbt[:], op=ALU.add),
             reads=[t_ys[b], t_c], writes=[t_ys[b]])
        S.op("dve", lambda E, b=b: E.tensor_tensor(out=b0[b][:], in0=b0[b][:], in1=b1[b][:], op=ALU.add),
             reads=[t_b0[b], t_b1[b]], writes=[t_b0[b]])
        for c in range(3):
            S.op("pe", lambda E, b=b, c=c: E.transpose(out=pB[b][:, 128 * c:128 * c + 128], in_=b0[b][:, c, :],
                                                       identity=identf[:]),
                 reads=[t_b0[b], t_id], writes=[t_pB[b]])
        for c in range(3):
            S.op("pe", lambda E, b=b, c=c: E.transpose(out=pG[b][:, 128 * c:128 * c + 128], in_=gt[b][:, c, :],
                                                       identity=identf[:]),
                 reads=[t_gt[b], t_id], writes=[t_pG[b]])
        S.op("dve", lambda E, b=b, yf=yf: E.tensor_tensor(out=rs[b][:], in0=pB[b][:, 0:384], in1=yf, op=ALU.add),
             reads=[t_pB[b], t_ys[b]], writes=[t_rs[b]])
        S.op("dve", lambda E, b=b: E.tensor_tensor(out=rs[b][:], in0=pG[b][:, 0:384], in1=rs[b][:], op=ALU.mult),
             reads=[t_pG[b], t_rs[b]], writes=[t_rs[b]])
        for c in range(3):
            S.op("pe", lambda E, b=b, c=c: E.transpose(out=pO[b][:, 128 * c:128 * c + 128],
                                                       in_=rs[b][:, 128 * c:128 * c + 128], identity=identf[:]),
                 reads=[t_rs[b], t_id], writes=[t_pO[b]])
        S.op("act", lambda E, b=b: E.activation(out=ob[b][:], in_=pO[b][:, 0:384].rearrange("p (a b) -> p a b", a=3),
                                                func=AF.Copy), reads=[t_pO[b]], writes=[t_ob[b]])
        S.dma("sp", mixT[0:384, tl].rearrange("(c p) t -> p c t", p=128), ob[b][:], reads=[t_ob[b]], writes=[t_out])
    P.close()


PARAM_SHAPES = {
    "ffn1_norm_g": [2, 1024], "ffn1_w_gate": [2, 1024, 2816], "ffn1_w_up": [2, 1024, 2816],
    "ffn1_w_down": [2, 2816, 1024], "mix_norm_g": [2, 1024], "w_in": [2, 1024, 2816], "w_out": [2, 1024, 1024],
    "rwkv_mu_prev": [2, 1408], "rwkv_mu_next": [2, 1408], "rwkv_decay_w0": [2, 2, 384],
    "rwkv_decay_w2": [2, 2, 64, 384], "rwkv_iclr_a0": [2, 2, 384], "rwkv_iclr_a2": [2, 2, 64, 384],
    "rwkv_gate_w2": [2, 128, 384], "rwkv_k_k": [2, 384], "rwkv_k_a": [2, 384], "rwkv_r_k": [2, 6, 64],
    "rwkv_lnx_w": [2, 384], "rwkv_lnx_b": [2, 384], "s5_a_re": [2, 2, 16, 64], "s5_a_im": [2, 2, 16, 64],
    "s5_log_step": [2, 2, 16], "s5_b_re": [2, 16, 64, 16], "s5_b_im": [2, 16, 64, 16],
    "s5_c_re": [2, 2, 16, 16, 64], "s5_c_im": [2, 2, 16, 16, 64], "s5_d": [2, 256], "s5_glu_w": [2, 256, 512],
    "s5_glu_b": [2, 512], "ffn2_norm_g": [2, 1024], "ffn2_w_gate": [2, 1024, 2816], "ffn2_w_up": [2, 1024, 2816],
    "ffn2_w_down": [2, 2816, 1024], "final_norm_g": [1024],
}
DEPTH = 2


def build_program(T, depth=DEPTH):
    nc = bass.Bass("TRN2", target_bir_lowering=False)
    x = nc.dram_tensor("x", [T, D], F32, kind="ExternalInput").ap()
    p = {k: nc.dram_tensor(k, list(s), F32, kind="ExternalInput").ap() for k, s in PARAM_SHAPES.items()}
    out = nc.dram_tensor("out", [T, D], F32, kind="ExternalOutput").ap()
    h = nc.dram_tensor("h_res", [T, D], F32).ap()
    zr = nc.dram_tensor("z_rwkv", [1408, T], F32).ap()
    qk = nc.dram_tensor("z_qk", [768, T], BF16).ap()
    vtm = nc.dram_tensor("z_v", [T, 384], BF16).ap()
    us5 = nc.dram_tensor("z_s5", [256, T], F32).ap()
    mixT = nc.dram_tensor("mixT", [1024, T], BF16).ap()
    ydir = nc.dram_tensor("y_dir", [2, T, 384], F32).ap()
    bon = nc.dram_tensor("bonus", [2, 384, T], F32).ap()
    gfm = nc.dram_tensor("gate", [384, T], F32).ap()
    S = SchedI(nc)
    NT = T // 128
    tk = [Tok() for _ in range(NT)]
    for l in range(depth):
        src = x if l == 0 else h
        tk2 = [Tok() for _ in range(NT)]
        phase_ffn(nc, S, src, h, p["ffn1_norm_g"][l], p["ffn1_w_gate"][l], p["ffn1_w_up"][l], p["ffn1_w_down"][l],
                  tk, tk2, T)
        tk = tk2
        phase_win(nc, S, h, p["mix_norm_g"][l], p["w_in"][l], zr, qk, vtm, us5, tk, T)
        phase_rwkv(nc, S, zr, ydir, bon, gfm,
                   [p["rwkv_mu_prev"][l], p["rwkv_mu_next"][l], p["rwkv_decay_w0"][l], p["rwkv_decay_w2"][l],
                    p["rwkv_iclr_a0"][l], p["rwkv_iclr_a2"][l], p["rwkv_gate_w2"][l], p["rwkv_k_k"][l],
                    p["rwkv_k_a"][l], p["rwkv_r_k"][l]], T)
        phase_rwkv_combine(nc, S, ydir, bon, gfm, p["rwkv_lnx_w"][l], p["rwkv_lnx_b"][l], mixT, T)
        phase_attn(nc, S, qk, vtm, mixT, T)
        phase_s5(nc, S, us5, mixT,
                 [p["s5_a_re"][l], p["s5_a_im"][l], p["s5_log_step"][l], p["s5_b_re"][l], p["s5_b_im"][l],
                  p["s5_c_re"][l], p["s5_c_im"][l], p["s5_d"][l], p["s5_glu_w"][l], p["s5_glu_b"][l]], T)
        tk2 = [Tok() for _ in range(NT)]
        phase_wout(nc, S, h, h, mixT, p["w_out"][l], tk, tk2, T)
        tk = tk2
        tk2 = [Tok() for _ in range(NT)]
        phase_ffn(nc, S, h, h, p["ffn2_norm_g"][l], p["ffn2_w_gate"][l], p["ffn2_w_up"][l], p["ffn2_w_down"][l],
                  tk, tk2, T)
        tk = tk2
    tko = [Tok() for _ in range(NT)]
    phase_final(nc, S, h, out, p["final_norm_g"], tk, tko, T)
    S.finish()
    return nc, S


def kernel(**inputs):
    x = np.ascontiguousarray(np.asarray(inputs["x"], dtype=np.float32))
    B, T, _ = x.shape
    nc, S = build_program(T)
    params = {k: np.ascontiguousarray(np.asarray(inputs[k], dtype=np.float32)) for k in PARAM_SHAPES}
    in_maps = []
    for b in range(B):
        m = {"x": x[b]}
        m.update(params)
        in_maps.append(m)
    res = run_bass_kernel_spmd(nc, in_maps, core_ids=list(range(B)))
    return np.stack([np.asarray(r["out"], dtype=np.float32) for r in res.results], axis=0)
```

```python
import numpy as np
import concourse.bass as bass
import concourse.mybir as mybir
from concourse.bass_utils import run_bass_kernel_spmd

F32 = mybir.dt.float32
BF16 = mybir.dt.bfloat16
I32 = mybir.dt.int32
AF = mybir.ActivationFunctionType
ALU = mybir.AluOpType
AX = mybir.AxisListType

ENGS = ("pe", "act", "dve", "pool", "sp")


class Tok:
    __slots__ = ("w", "r", "name")

    def __init__(self, name=""):
        self.w = None
        self.r = {}
        self.name = name


class Sched:
    def __init__(self, nc, lanes_sp=8, lanes_pool=6, lanes_act=2, same_engine_sync=True):
        self.nc = nc
        self.ops = {e: [] for e in ENGS}
        self.cnt = {}
        self.sems = {}
        self.seen = {e: {} for e in ENGS}
        self.same = same_engine_sync
        self._ctx = []
        for e in ("pe", "act", "dve", "pool"):
            self._mksem(e)
        self.lanes = {"sp": [], "pool": [], "act": []}
        for q, n in (("sp", lanes_sp), ("pool", lanes_pool), ("act", lanes_act)):
            for i in range(n):
                nm = f"ln_{q}{i}"
                self._mksem(nm)
                self.lanes[q].append(nm)
        self.lane_rr = {"sp": 0, "pool": 0, "act": 0}
        self.n_instr = 0

    def _mksem(self, name):
        cm = self.nc.semaphore(name)
        s = cm.__enter__()
        self._ctx.append(cm)
        self.sems[name] = s
        self.cnt[name] = 0

    def _collect(self, eng, reads, writes):
        need = {}

        def add(src, val):
            if src == eng and (eng == "pe" or not self.same or eng == "sp"):
                return
            if need.get(src, 0) < val:
                need[src] = val
        for t in reads:
            if t.w is not None:
                add(*t.w)
        for t in writes:
            if t.w is not None:
                add(*t.w)
            for s, v in t.r.items():
                add(s, v)
        out = []
        seen = self.seen[eng]
        for s, v in need.items():
            if seen.get(s, 0) < v:
                seen[s] = v
                out.append((self.sems[s], v))
        return out

    def op(self, eng, fn, reads=(), writes=()):
        waits = self._collect(eng, reads, writes)
        self.cnt[eng] += 1
        c = self.cnt[eng]
        sem = self.sems[eng]

        def emit(E, waits=waits, fn=fn, sem=sem):
            for s, v in waits:
                E.wait_ge(s, v)
            fn(E).then_inc(sem, 1)
        self.ops[eng].append(emit)
        for t in reads:
            t.r[eng] = c
        for t in writes:
            t.w = (eng, c)
            t.r = {}
        self.n_instr += 1

    def dma(self, q, out, in_, reads=(), writes=(), **kw):
        lanes = self.lanes[q]
        ln = lanes[self.lane_rr[q] % len(lanes)]
        self.lane_rr[q] += 1
        waits = self._collect(q, reads, writes)
        prev = self.cnt[ln]
        if prev and self.seen[q].get(ln, 0) < prev:
            self.seen[q][ln] = prev
            waits.append((self.sems[ln], prev))
        self.cnt[ln] += 16
        c = self.cnt[ln]
        sem = self.sems[ln]

        def emit(E, waits=waits, sem=sem, out=out, in_=in_, kw=kw):
            for s, v in waits:
                E.wait_ge(s, v)
            E.dma_start(out=out, in_=in_, **kw).then_inc(sem, 16)
        self.ops[q].append(emit)
        for t in reads:
            t.r[ln] = c
        for t in writes:
            t.w = (ln, c)
            t.r = {}
        self.n_instr += 1

    def finish(self, final_toks):
        nc = self.nc
        fin = []
        need = {}
        for t in final_toks:
            if t.w is not None and need.get(t.w[0], 0) < t.w[1]:
                need[t.w[0]] = t.w[1]
        for s, v in self.cnt.items():
            if v and need.get(s, 0) < v:
                need[s] = v
        for s, v in need.items():
            fin.append((self.sems[s], v))
        ops = self.ops
        with nc.Block() as block:
            @block.tensor
            def _(E):
                for f in ops["pe"]:
                    f(E)

            @block.scalar
            def _(E):
                for f in ops["act"]:
                    f(E)

            @block.vector
            def _(E):
                for f in ops["dve"]:
                    f(E)

            @block.gpsimd
            def _(E):
                for f in ops["pool"]:
                    f(E)

            @block.sync
            def _(E):
                for f in ops["sp"]:
                    f(E)
                for s, v in fin:
                    E.wait_ge(s, v)
        for cm in reversed(self._ctx):
            cm.__exit__(None, None, None)


class Alloc:
    def __init__(self, nc):
        self.nc = nc
        self._ctx = []

    def sb(self, name, shape, dt):
        cm = self.nc.sbuf_tensor(name, list(shape), dt)
        t = cm.__enter__()
        self._ctx.append(cm)
        return t

    def ps(self, name, shape, dt):
        cm = self.nc.psum_tensor(name, list(shape), dt)
        t = cm.__enter__()
        self._ctx.append(cm)
        return t

    def close(self):
        for cm in reversed(self._ctx):
            cm.__exit__(None, None, None)


class SchedI(Sched):
    def __init__(self, nc, **kw):
        super().__init__(nc, **kw)
        self.E = {"pe": nc.tensor, "act": nc.scalar, "dve": nc.vector, "pool": nc.gpsimd, "sp": nc.sync}

    limit = 10 ** 9

    def op(self, eng, fn, reads=(), writes=()):
        if self.n_instr >= self.limit:
            return
        waits = self._collect(eng, reads, writes)
        self.cnt[eng] += 1
        c = self.cnt[eng]
        E = self.E[eng]
        for s, v in waits:
            E.wait_ge(s, v)
        fn(E).then_inc(self.sems[eng], 1)
        for t in reads:
            t.r[eng] = c
        for t in writes:
            t.w = (eng, c)
            t.r = {}
        self.n_instr += 1

    def dma(self, q, out, in_, reads=(), writes=(), **kw):
        if self.n_instr >= self.limit:
            return
        lanes = self.lanes[q]
        ln = lanes[self.lane_rr[q] % len(lanes)]
        self.lane_rr[q] += 1
        waits = self._collect(q, reads, writes)
        prev = self.cnt[ln]
        if prev and self.seen[q].get(ln, 0) < prev:
            self.seen[q][ln] = prev
            waits.append((self.sems[ln], prev))
        self.cnt[ln] += 16
        c = self.cnt[ln]
        E = self.E[q]
        for s, v in waits:
            E.wait_ge(s, v)
        E.dma_start(out=out, in_=in_, **kw).then_inc(self.sems[ln], 16)
        for t in reads:
            t.r[ln] = c
        for t in writes:
            t.w = (ln, c)
            t.r = {}
        self.n_instr += 1

    def barrier(self):
        for e in ("pe", "act", "dve", "pool", "sp"):
            E = self.E[e]
            for s, v in self.cnt.items():
                if v and s != e and self.seen[e].get(s, 0) < v:
                    self.seen[e][s] = v
                    E.wait_ge(self.sems[s], v)
                if s == e and v and e != "sp":
                    if self.seen[e].get(s, 0) < v:
                        self.seen[e][s] = v
                        E.wait_ge(self.sems[s], v)

    def finish(self, final_toks=()):
        self.barrier()
        for cm in reversed(self._ctx):
            cm.__exit__(None, None, None)


from contextlib import ExitStack

D = 1024
DFF = 2816
NFF = DFF // 128
KD = D // 128


class Phase:
    _uid = [0]

    def __init__(self, nc, S):
        self.nc, self.S = nc, S
        self.es = ExitStack()
        self.n = 0
        Phase._uid[0] += 1
        self.uid = Phase._uid[0]

    def sb(self, shape, dt, name=None):
        self.n += 1
        return self.es.enter_context(self.nc.sbuf_tensor(name or f"t{self.uid}_{self.n}", list(shape), dt))

    def ps(self, shape, dt, name=None):
        self.n += 1
        return self.es.enter_context(self.nc.psum_tensor(name or f"p{self.uid}_{self.n}", list(shape), dt))

    def close(self):
        self.S.barrier()
        self.es.close()


def make_ident(nc, S, P, dt=BF16):
    identf = P.sb([128, 128], F32)
    ident = P.sb([128, 128], dt)
    t = Tok()
    S.op("pool", lambda E: E.memset(identf[:], 1.0), writes=[t])
    S.op("pool", lambda E: E.affine_select(out=identf[:], in_=identf[:], pattern=[[-1, 128]], base=0,
                                           channel_multiplier=1, compare_op=ALU.is_equal, fill=0.0),
         reads=[t], writes=[t])
    S.op("dve", lambda E: E.tensor_copy(out=ident[:], in_=identf[:]), reads=[t], writes=[t])
    return ident, identf, t


def load_w_bf16(S, dst, src, tok, rows_per=128, col_split=2):
    K = dst.shape[1]
    N = dst.shape[2]
    cs = N // col_split
    for k in range(K):
        for c in range(col_split):
            S.dma("pool", dst[:, k, c * cs:(c + 1) * cs], src[k * 128:(k + 1) * 128, c * cs:(c + 1) * cs],
                  writes=[tok])


def rms_prep(S, P, ht, t_h, s, gt, t_g, xn, t_xn, junk, t_junk, stat, t_stat, mhalf, t_mh):
    S.op("dve", lambda E: E.scalar_tensor_tensor(out=junk[:], in0=ht[:, s, :], scalar=1.0 / D, in1=ht[:, s, :],
                                                 op0=ALU.mult, op1=ALU.mult, accum_out=stat[:, 0:1]),
         reads=[t_h], writes=[t_junk, t_stat])
    S.op("dve", lambda E: E.tensor_scalar(out=stat[:, 1:2], in0=stat[:, 0:1], scalar1=1e-6, scalar2=None,
                                          op0=ALU.add), reads=[t_stat], writes=[t_stat])
    S.op("pool", lambda E: E.tensor_tensor(out=stat[:, 2:3], in0=stat[:, 1:2], in1=mhalf[:, 0:1], op=ALU.pow),
         reads=[t_stat, t_mh], writes=[t_stat])
    S.op("dve", lambda E: E.scalar_tensor_tensor(out=xn[:], in0=ht[:, s, :], scalar=stat[:, 2:3], in1=gt[:],
                                                 op0=ALU.mult, op1=ALU.mult),
         reads=[t_h, t_stat, t_g], writes=[t_xn])


def phase_ffn(nc, S, h_in, h_out, g, wg, wu, wd, toks_in, toks_out, T):
    P = Phase(nc, S)
    NT = T // 128
    NS = 4
    NSUP = NT // NS
    hv_in = h_in.rearrange("(n p) d -> p n d", p=128)
    hv_out = h_out.rearrange("(n p) d -> p n d", p=128)
    ident, _, t_id = make_ident(nc, S, P)
    wg_b = P.sb([128, KD, DFF], BF16); t_wg = Tok()
    wu_b = P.sb([128, KD, DFF], BF16); t_wu = Tok()
    wd_b = P.sb([128, NFF, D], BF16); t_wd = Tok()
    load_w_bf16(S, wg_b, wg, t_wg)
    load_w_bf16(S, wu_b, wu, t_wu)
    load_w_bf16(S, wd_b, wd, t_wd, col_split=1)
    gt = P.sb([128, D], F32); t_g = Tok()
    S.dma("sp", gt[:], g.partition_broadcast(128), writes=[t_g])
    mhalf = P.sb([128, 1], F32); t_mh = Tok()
    S.op("pool", lambda E: E.memset(mhalf[:], -0.5), writes=[t_mh])
    ht = [P.sb([128, NS, D], F32)] * 2; t_ht = [[Tok() for _ in range(NS)]] * 2
    rl = [P.sb([128, 512], F32) for _ in range(4)]; t_rl = [Tok() for _ in range(4)]
    xn = [P.sb([128, D], BF16) for _ in range(2)]; t_xn = [Tok(), Tok()]
    junk = P.sb([128, D], BF16); t_junk = Tok()
    stat = [P.sb([128, 4], F32) for _ in range(2)]; t_stat = [Tok(), Tok()]
    xnT = [P.sb([128, KD, NS * 128], BF16) for _ in range(2)]; t_xnT = [Tok(), Tok()]
    hT = P.sb([128, NFF, NS * 128], BF16); t_hT = [Tok() for _ in range(NFF)]
    sg = [P.sb([128, NS * 128], BF16) for _ in range(2)]; t_sg = [Tok(), Tok()]
    pT = [P.ps([128, KD, 128], BF16) for _ in range(2)]; t_pT = [Tok(), Tok()]
    pG = [P.ps([128, 512], F32) for _ in range(2)]; t_pG = [Tok(), Tok()]
    pU = [P.ps([128, 512], F32) for _ in range(2)]; t_pU = [Tok(), Tok()]
    pD = [P.ps([128, 512], F32) for _ in range(2)]; t_pD = [Tok(), Tok()]
    itc = [0]

    def load(st):
        hb = st % 2
        for s in range(NS):
            n = st * NS + s
            S.dma("sp", ht[hb][:, s, :], hv_in[:, n, :], reads=[toks_in[n]], writes=[t_ht[hb][s]])

    def prep(st):
        hb = st % 2
        for s in range(NS):
            b = s % 2
            rms_prep(S, P, ht[hb], t_ht[hb][s], s, gt, t_g, xn[b], t_xn[b], junk, t_junk, stat[b], t_stat[b], mhalf,
                     t_mh)
            for k in range(KD):
                S.op("pe", lambda E, k=k, b=b: E.transpose(out=pT[b][:, k, :], in_=xn[b][:, k * 128:(k + 1) * 128],
                                                           identity=ident[:]),
                     reads=[t_xn[b], t_id], writes=[t_pT[b]])
            S.op("dve", lambda E, b=b, s=s, hb=hb: E.tensor_copy(out=xnT[hb][:, :, s * 128:(s + 1) * 128], in_=pT[b][:]),
                 reads=[t_pT[b]], writes=[t_xnT[hb]])

    def gateup(st):
        hb = st % 2
        for f in range(NFF):
            b = f % 2
            for k in range(KD):
                S.op("pe", lambda E, k=k, f=f, b=b: E.matmul(pG[b][:], lhsT=wg_b[:, k, f * 128:(f + 1) * 128],
                                                             rhs=xnT[hb][:, k, :], start=(k == 0), stop=(k == KD - 1)),
                     reads=[t_wg, t_xnT[hb]], writes=[t_pG[b]])
            for k in range(KD):
                S.op("pe", lambda E, k=k, f=f, b=b: E.matmul(pU[b][:], lhsT=wu_b[:, k, f * 128:(f + 1) * 128],
                                                             rhs=xnT[hb][:, k, :], start=(k == 0), stop=(k == KD - 1)),
                     reads=[t_wu, t_xnT[hb]], writes=[t_pU[b]])
            S.op("act", lambda E, b=b: E.activation(out=sg[b][:], in_=pG[b][:], func=AF.Silu),
                 reads=[t_pG[b]], writes=[t_sg[b]])
            S.op("dve", lambda E, b=b, f=f: E.tensor_tensor(out=hT[:, f, :], in0=pU[b][:], in1=sg[b][:], op=ALU.mult),
                 reads=[t_pU[b], t_sg[b]], writes=[t_hT[f]])

    def down(st):
        for s in range(NS):
            n = st * NS + s
            for c in range(2):
                b = itc[0] % 2
                r4 = itc[0] % 4
                itc[0] += 1
                S.dma("sp", rl[r4][:], hv_in[:, n, c * 512:(c + 1) * 512], reads=[toks_in[n]], writes=[t_rl[r4]])
                for f in range(NFF):
                    S.op("pe", lambda E, f=f, s=s, c=c, b=b: E.matmul(
                        pD[b][:], lhsT=hT[:, f, s * 128:(s + 1) * 128], rhs=wd_b[:, f, c * 512:(c + 1) * 512],
                        start=(f == 0), stop=(f == NFF - 1)),
                        reads=[t_wd, t_hT[f]], writes=[t_pD[b]])
                S.op("dve", lambda E, b=b, r4=r4: E.scalar_tensor_tensor(
                    out=rl[r4][:], in0=pD[b][:], scalar=0.5, in1=rl[r4][:], op0=ALU.mult, op1=ALU.add),
                    reads=[t_pD[b], t_rl[r4]], writes=[t_rl[r4]])
                S.dma("sp", hv_out[:, n, c * 512:(c + 1) * 512], rl[r4][:], reads=[t_rl[r4]], writes=[toks_out[n]])

    load(0)
    prep(0)
    for st in range(NSUP):
        if st + 1 < NSUP:
            load(st + 1)
        gateup(st)
        if st + 1 < NSUP:
            prep(st + 1)
        down(st)
    P.close()


RWKV_IN = 1408
ATT_Q0 = 1408
ATT_V0 = 2176
S5_0 = 2560
INW = 2816


def phase_win(nc, S, h_in, g, win, zr, qk, vtm, us5, toks_in, T):
    P = Phase(nc, S)
    NT = T // 128
    NS = 4
    NSUP = NT // NS
    hv_in = h_in.rearrange("(n p) d -> p n d", p=128)
    ident, _, t_id = make_ident(nc, S, P)
    w_b = P.sb([128, KD, INW], BF16); t_w = Tok()
    load_w_bf16(S, w_b, win, t_w)
    gt = P.sb([128, D], F32); t_g = Tok()
    S.dma("sp", gt[:], g.partition_broadcast(128), writes=[t_g])
    mhalf = P.sb([128, 1], F32); t_mh = Tok()
    S.op("pool", lambda E: E.memset(mhalf[:], -0.5), writes=[t_mh])
    ht = [P.sb([128, NS, D], F32) for _ in range(2)]; t_ht = [[Tok() for _ in range(NS)] for _ in range(2)]
    xn = [P.sb([128, D], BF16) for _ in range(2)]; t_xn = [Tok(), Tok()]
    junk = P.sb([128, D], BF16); t_junk = Tok()
    stat = [P.sb([128, 4], F32) for _ in range(2)]; t_stat = [Tok(), Tok()]
    xnT = [P.sb([128, KD, NS * 128], BF16) for _ in range(2)]; t_xnT = [Tok(), Tok()]
    stf = [P.sb([128, 512], F32) for _ in range(4)]; t_stf = [Tok() for _ in range(4)]
    stb = [P.sb([128, 512], BF16) for _ in range(4)]; t_stb = [Tok() for _ in range(4)]
    pT = [P.ps([128, KD, 128], BF16) for _ in range(2)]; t_pT = [Tok(), Tok()]
    pZ = [P.ps([128, 512], F32) for _ in range(4)]; t_pZ = [Tok() for _ in range(4)]
    t_out = Tok()
    chunks = []
    for c in range(11):
        chunks.append((c * 128, zr, c * 128, False))
    for c in range(6):
        chunks.append((ATT_Q0 + c * 128, qk, c * 128, True))
    for c in range(2):
        chunks.append((S5_0 + c * 128, us5, c * 128, False))
    it = 0
    ib = 0
    iff = 0
    for st in range(NSUP):
        hb = st % 2
        for s in range(NS):
            n = st * NS + s
            S.dma("sp", ht[hb][:, s, :], hv_in[:, n, :], reads=[toks_in[n]], writes=[t_ht[hb][s]])
        for s in range(NS):
            b = s % 2
            rms_prep(S, P, ht[hb], t_ht[hb][s], s, gt, t_g, xn[b], t_xn[b], junk, t_junk, stat[b], t_stat[b], mhalf, t_mh)
            for k in range(KD):
                S.op("pe", lambda E, k=k, b=b: E.transpose(out=pT[b][:, k, :], in_=xn[b][:, k * 128:(k + 1) * 128],
                                                           identity=ident[:]),
                     reads=[t_xn[b], t_id], writes=[t_pT[b]])
            S.op("dve", lambda E, b=b, s=s, hb=hb: E.tensor_copy(out=xnT[hb][:, :, s * 128:(s + 1) * 128], in_=pT[b][:]),
                 reads=[t_pT[b]], writes=[t_xnT[hb]])
        tsl = slice(st * 512, (st + 1) * 512)
        for (c0, dst, r0, isb) in chunks:
            pb = it % 4
            it += 1
            for k in range(KD):
                S.op("pe", lambda E, k=k, c0=c0, pb=pb, hb=hb: E.matmul(
                    pZ[pb][:], lhsT=w_b[:, k, c0:c0 + 128], rhs=xnT[hb][:, k, :], start=(k == 0), stop=(k == KD - 1)),
                    reads=[t_w, t_xnT[hb]], writes=[t_pZ[pb]])
            if isb:
                sb_ = ib % 4
                ib += 1
                S.op("act", lambda E, pb=pb, sb_=sb_: E.activation(out=stb[sb_][:], in_=pZ[pb][:], func=AF.Copy),
                     reads=[t_pZ[pb]], writes=[t_stb[sb_]])
                S.dma("sp", dst[r0:r0 + 128, tsl], stb[sb_][:], reads=[t_stb[sb_]], writes=[t_out])
            else:
                sf = iff % 4
                iff += 1
                eng = "act" if iff % 2 else "dve"
                if eng == "act":
                    S.op("act", lambda E, pb=pb, sf=sf: E.activation(out=stf[sf][:], in_=pZ[pb][:], func=AF.Copy),
                         reads=[t_pZ[pb]], writes=[t_stf[sf]])
                else:
                    S.op("dve", lambda E, pb=pb, sf=sf: E.tensor_copy(out=stf[sf][:], in_=pZ[pb][:]),
                         reads=[t_pZ[pb]], writes=[t_stf[sf]])
                S.dma("sp", dst[r0:r0 + 128, tsl], stf[sf][:], reads=[t_stf[sf]], writes=[t_out])
        for s in range(NS):
            n = st * NS + s
            pb = it % 4
            it += 1
            for k in range(KD):
                S.op("pe", lambda E, k=k, s=s, pb=pb, hb=hb: E.matmul(
                    pZ[pb][:, 0:384], lhsT=xnT[hb][:, k, s * 128:(s + 1) * 128], rhs=w_b[:, k, ATT_V0:ATT_V0 + 384],
                    start=(k == 0), stop=(k == KD - 1)),
                    reads=[t_w, t_xnT[hb]], writes=[t_pZ[pb]])
            sb_ = ib % 4
            ib += 1
            S.op("dve", lambda E, pb=pb, sb_=sb_: E.tensor_copy(out=stb[sb_][:, 0:384], in_=pZ[pb][:, 0:384]),
                 reads=[t_pZ[pb]], writes=[t_stb[sb_]])
            S.dma("sp", vtm[n * 128:(n + 1) * 128, :], stb[sb_][:, 0:384], reads=[t_stb[sb_]], writes=[t_out])
    P.close()


def phase_wout(nc, S, h_in, h_out, mixT, wout, toks_in, toks_out, T):
    P = Phase(nc, S)
    NT = T // 128
    NS = 4
    NSUP = NT // NS
    hv_in = h_in.rearrange("(n p) d -> p n d", p=128)
    hv_out = h_out.rearrange("(n p) d -> p n d", p=128)
    mv = mixT.rearrange("(k p) t -> p k t", p=128)
    w_b = P.sb([128, KD, D], BF16); t_w = Tok()
    load_w_bf16(S, w_b, wout, t_w, col_split=1)
    ht = [P.sb([128, NS, D], F32) for _ in range(2)]; t_ht = [[Tok() for _ in range(NS)] for _ in range(2)]
    ml = [P.sb([128, KD, 512], BF16) for _ in range(2)]; t_ml = [Tok(), Tok()]
    pD = [P.ps([128, 512], F32) for _ in range(4)]; t_pD = [Tok() for _ in range(4)]
    it = 0
    for st in range(NSUP):
        hb = st % 2
        S.dma("sp", ml[hb][:], mv[:, :, st * 512:(st + 1) * 512], writes=[t_ml[hb]])
        for s in range(NS):
            n = st * NS + s
            S.dma("sp", ht[hb][:, s, :], hv_in[:, n, :], reads=[toks_in[n]], writes=[t_ht[hb][s]])
        for s in range(NS):
            n = st * NS + s
            for c in range(2):
                b = it % 4
                it += 1
                for k in range(KD):
                    S.op("pe", lambda E, k=k, s=s, c=c, b=b, hb=hb: E.matmul(
                        pD[b][:], lhsT=ml[hb][:, k, s * 128:(s + 1) * 128], rhs=w_b[:, k, c * 512:(c + 1) * 512],
                        start=(k == 0), stop=(k == KD - 1)),
                        reads=[t_w, t_ml[hb]], writes=[t_pD[b]])
                S.op("dve", lambda E, s=s, c=c, b=b, hb=hb: E.tensor_tensor(
                    out=ht[hb][:, s, c * 512:(c + 1) * 512], in0=pD[b][:], in1=ht[hb][:, s, c * 512:(c + 1) * 512],
                    op=ALU.add), reads=[t_pD[b]], writes=[t_ht[hb][s]])
            S.dma("sp", hv_out[:, n, :], ht[hb][:, s, :], reads=[t_ht[hb][s]], writes=[toks_out[n]])
    P.close()


def phase_final(nc, S, h_in, out, g, toks_in, toks_out, T):
    P = Phase(nc, S)
    NT = T // 128
    hv_in = h_in.rearrange("(n p) d -> p n d", p=128)
    hv_out = out.rearrange("(n p) d -> p n d", p=128)
    gt = P.sb([128, D], F32); t_g = Tok()
    S.dma("sp", gt[:], g.partition_broadcast(128), writes=[t_g])
    mhalf = P.sb([128, 1], F32); t_mh = Tok()
    S.op("pool", lambda E: E.memset(mhalf[:], -0.5), writes=[t_mh])
    ht = [P.sb([128, 1, D], F32) for _ in range(4)]; t_ht = [Tok() for _ in range(4)]
    xo = [P.sb([128, D], F32) for _ in range(4)]; t_xo = [Tok() for _ in range(4)]
    junk = P.sb([128, D], BF16); t_junk = Tok()
    stat = [P.sb([128, 4], F32) for _ in range(4)]; t_stat = [Tok() for _ in range(4)]
    for n in range(NT):
        b = n % 4
        S.dma("sp", ht[b][:, 0, :], hv_in[:, n, :], reads=[toks_in[n]], writes=[t_ht[b]])
        rms_prep(S, P, ht[b], t_ht[b], 0, gt, t_g, xo[b], t_xo[b], junk, t_junk, stat[b], t_stat[b], mhalf, t_mh)
        S.dma("pool", hv_out[:, n, :], xo[b][:], reads=[t_xo[b]], writes=[toks_out[n]])
    P.close()


ALIBI = [0.25, 0.0625, 0.015625, 0.00390625, 0.5, 0.125]
DILS = [1, 4, 16]
NEG = -1.0e30


def phase_attn(nc, S, qk, vtm, mixT, T):
    P = Phase(nc, S)
    NB1 = T // 128
    dfi = P.sb([128, 128], I32)
    dff = P.sb([128, 128], F32)
    Dk = P.sb([128, 3, 128], F32)
    Mk = P.sb([128, 3, 128], F32)
    t_c = Tok()
    S.op("pool", lambda E: E.iota(dfi[:], pattern=[[1, 128]], base=0, channel_multiplier=-1), writes=[t_c])
    S.op("dve", lambda E: E.tensor_copy(out=dff[:], in_=dfi[:]), reads=[t_c], writes=[t_c])
    S.op("dve", lambda E: E.tensor_scalar(out=Dk[:, 0, :], in0=dff[:], scalar1=128.0, scalar2=None, op0=ALU.add),
         reads=[t_c], writes=[t_c])
    S.op("dve", lambda E: E.tensor_scalar(out=Dk[:, 1, :], in0=dff[:], scalar1=-1.0, scalar2=None, op0=ALU.mult),
         reads=[t_c], writes=[t_c])
    S.op("dve", lambda E: E.tensor_tensor(out=Dk[:, 1, :], in0=Dk[:, 1, :], in1=dff[:], op=ALU.max),
         reads=[t_c], writes=[t_c])
    S.op("dve", lambda E: E.tensor_scalar(out=Dk[:, 2, :], in0=dff[:], scalar1=-1.0, scalar2=128.0, op0=ALU.mult,
                                          op1=ALU.add), reads=[t_c], writes=[t_c])
    S.op("dve", lambda E: E.tensor_scalar(out=Mk[:], in0=Dk[:], scalar1=64.0, scalar2=NEG, op0=ALU.is_gt,
                                          op1=ALU.mult), reads=[t_c], writes=[t_c])
    sel = P.sb([65, 64], F32)
    S.op("dve", lambda E: E.memset(sel[:], 0.0), writes=[t_c])
    S.op("dve", lambda E: E.memset(sel[64:65, :], 1.0), reads=[t_c], writes=[t_c])

    qT = P.sb([128, T], BF16); t_q = Tok()
    kT = P.sb([128, T], BF16); t_k = Tok()
    vt = [P.sb([128, NB1, 2, 65], BF16) for _ in range(3)]; t_v = [Tok() for _ in range(3)]
    acc = P.sb([65, 2, T], F32)
    t_acc = [[Tok() for _ in range(NB1)] for _ in range(2)]
    biasT = P.sb([128, 2, 3, 3, 128], F32); t_b = Tok()
    NBF = 4
    sc = [P.sb([128, 3, 128], F32) for _ in range(NBF)]; t_sc = [Tok() for _ in range(NBF)]
    pr = [P.sb([128, 3, 128], BF16) for _ in range(NBF)]; t_pr = [Tok() for _ in range(NBF)]
    rec = [P.sb([64, 512], F32) for _ in range(2)]; t_rec = [Tok(), Tok()]
    ob = [P.sb([64, 512], BF16) for _ in range(2)]; t_ob = [Tok(), Tok()]
    pS = [P.ps([128, 3, 128], F32) for _ in range(NBF)]; t_pS = [Tok() for _ in range(NBF)]
    pOb = [P.ps([128, 512], F32) for _ in range(NBF)]; t_pO = [Tok() for _ in range(NBF)]
    pO = [x[0:65, 0:128] for x in pOb]
    pB = [x[0:64, 0:512] for x in pOb]; t_pB = t_pO
    t_out = Tok()
    it = 0
    for hp in range(3):
        S.dma("sp", qT[:], qk[128 * hp:128 * hp + 128, :], writes=[t_q])
        S.dma("sp", kT[:], qk[384 + 128 * hp:384 + 128 * hp + 128, :], writes=[t_k])
        for pi, d in enumerate(DILS):
            S.op("pool", lambda E, pi=pi: E.memset(vt[pi][:], 1.0), writes=[t_v[pi]])
            nm = T // d // 128
            vv = vtm.rearrange("(m j r) (h c) -> j r m h c", j=128, r=d, c=64)
            for r in range(d):
                for h2 in range(2):
                    S.dma("sp", vt[pi][:, r * nm:(r + 1) * nm, h2, 0:64], vv[:, r, :, 2 * hp + h2, :],
                          writes=[t_v[pi]])
        for h2 in range(2):
            for pi, d in enumerate(DILS):
                sl = -ALIBI[2 * hp + h2] * d
                S.op("dve", lambda E, h2=h2, pi=pi, sl=sl: E.scalar_tensor_tensor(
                    out=biasT[:, h2, pi, :, :], in0=Dk[:], scalar=sl, in1=Mk[:], op0=ALU.mult, op1=ALU.add),
                    reads=[t_c], writes=[t_b])
        for h2 in range(2):
            rows = slice(64 * h2, 64 * h2 + 64)
            stages = []
            for pi, d in enumerate(DILS):
                nb = T // d // 128
                for r in range(d):
                    for b in range(nb):
                        bi = it % NBF
                        it += 1
                        kts = [kt for kt in (b - 1, b, b + 1) if 0 <= kt < nb]
                        k0 = kts[0] - (b - 1)
                        nk = len(kts)
                        qs = slice(r + d * 128 * b, r + d * 128 * b + d * 127 + 1, d)
                        blks = sorted(set(range((r + d * 128 * b) // 128, (r + d * 128 * (b + 1) - d) // 128 + 1)))

                        def stA(bi=bi, kts=kts, k0=k0, nk=nk, qs=qs, r=r, d=d, pi=pi, rows=rows, h2=h2):
                            for ki, kt in enumerate(kts):
                                ks = slice(r + d * 128 * kt, r + d * 128 * kt + d * 127 + 1, d)
                                S.op("pe", lambda E, ki=ki, ks=ks: E.matmul(
                                    pS[bi][:, ki, :], lhsT=kT[rows, ks], rhs=qT[rows, qs], start=True, stop=True),
                                    reads=[t_q, t_k], writes=[t_pS[bi]])
                            S.op("dve", lambda E: E.scalar_tensor_tensor(
                                out=sc[bi][:, 0:nk, :], in0=pS[bi][:, 0:nk, :], scalar=0.125,
                                in1=biasT[:, h2, pi, k0:k0 + nk, :], op0=ALU.mult, op1=ALU.add),
                                reads=[t_pS[bi], t_b], writes=[t_sc[bi]])
                            S.op("act", lambda E: E.activation(out=pr[bi][:, 0:nk, :], in_=sc[bi][:, 0:nk, :],
                                                               func=AF.Exp),
                                 reads=[t_sc[bi]], writes=[t_pr[bi]])

                        def stB(bi=bi, kts=kts, nk=nk, qs=qs, r=r, nb=nb, pi=pi, h2=h2, blks=blks):
                            for ki, kt in enumerate(kts):
                                S.op("pe", lambda E, ki=ki, kt=kt: E.matmul(
                                    pO[bi], lhsT=vt[pi][:, r * nb + kt, h2, :], rhs=pr[bi][:, ki, :],
                                    start=(ki == 0), stop=(ki == nk - 1)),
                                    reads=[t_v[pi], t_pr[bi]], writes=[t_pO[bi]])
                            at = [t_acc[h2][x] for x in blks]
                            if pi == 0:
                                S.op("act", lambda E: E.activation(out=acc[:, h2, qs], in_=pO[bi], func=AF.Copy),
                                     reads=[t_pO[bi]], writes=at)
                            else:
                                S.op("dve", lambda E: E.tensor_tensor(out=acc[:, h2, qs], in0=pO[bi],
                                                                      in1=acc[:, h2, qs], op=ALU.add),
                                     reads=[t_pO[bi]], writes=at)
                        stages.append((stA, stB))
            SKEW = NBF - 1
            for i in range(len(stages) + SKEW):
                if i < len(stages):
                    stages[i][0]()
                if i - SKEW >= 0:
                    stages[i - SKEW][1]()
            head = 2 * hp + h2
            for c in range(T // 512):
                bi = c % 2
                cs = slice(c * 512, (c + 1) * 512)
                at = t_acc[h2][4 * c:4 * c + 4]
                S.op("pe", lambda E, bi=bi, h2=h2, cs=cs: E.matmul(pB[bi], lhsT=sel[:], rhs=acc[:, h2, cs],
                                                                   start=True, stop=True),
                     reads=at + [t_c], writes=[t_pB[bi]])
                S.op("dve", lambda E, bi=bi: E.reciprocal(out=rec[bi][:], in_=pB[bi]),
                     reads=[t_pB[bi]], writes=[t_rec[bi]])
                S.op("dve", lambda E, bi=bi, h2=h2, cs=cs: E.tensor_tensor(out=ob[bi][:], in0=acc[0:64, h2, cs],
                                                                          in1=rec[bi][:], op=ALU.mult),
                     reads=at + [t_rec[bi]], writes=[t_ob[bi]])
                S.dma("sp", mixT[384 + 64 * head:384 + 64 * head + 64, cs], ob[bi][:], reads=[t_ob[bi]],
                      writes=[t_out])
    P.close()


def rsl(lo, hi):
    return slice(hi - 1, (lo - 1) if lo > 0 else None, -1)


TWO_PI = 6.283185307179586
MAGIC = 12582912.0


def phase_s5(nc, S, us5, mixT, prm, T, C=256):
    P = Phase(nc, S)
    NCH = T // C
    NCMB = 16
    a_re, a_im, lstep, b_re, b_im, c_re, c_im, dsk, glu_w, glu_b = prm
    ident, identf, t_id = make_ident(nc, S, P)
    t_s = Tok()

    def dv(fn, eng="dve"):
        S.op(eng, fn, reads=[t_s, t_id], writes=[t_s])

    prs = P.sb([128, 40, 16], F32)
    cosT = P.sb([128, NCMB, C], F32)
    sinT = P.sb([128, NCMB, C], F32)
    BT = P.sb([128, NCMB, 2, 128], BF16)
    CT = P.sb([128, NCMB, 2, 128], BF16)
    gw = P.sb([128, 2, 512], BF16)
    gb = P.sb([128, 4], F32)
    dk = P.sb([128, 2], F32)
    ub = P.sb([128, 2, T], BF16); t_ub = Tok()
    ybwd = P.sb([128, 2, T], F32); t_yb = [Tok() for _ in range(NCH)]
    gi = P.sb([128, 2, NCMB], F32); t_gi = [Tok() for _ in range(NCMB)]
    banks = [P.ps([128, 512], F32) for _ in range(6)]
    P2 = Phase(nc, S)
    stg = P2.sb([16, 3, 128], F32)
    lst = P2.sb([16, 2], F32)
    S.dma("sp", stg[:, 0, :], a_re.rearrange("d (j g) p -> (d j) (g p)", g=2), writes=[t_s])
    S.dma("sp", stg[:, 1, :], a_im.rearrange("d (j g) p -> (d j) (g p)", g=2), writes=[t_s])
    S.dma("sp", lst[:], lstep.rearrange("d (j g) -> (d j) g", g=2), writes=[t_s])
    for g2 in range(2):
        dv(lambda E, g2=g2: E.tensor_copy(out=stg[:, 2, 64 * g2:64 * g2 + 64],
                                          in_=lst[:, g2:g2 + 1].to_broadcast([16, 64])))
    pst = banks[0][:, 0:48].rearrange("p (a b) -> p a b", a=3)
    for i in range(3):
        S.op("pe", lambda E, i=i: E.transpose(out=pst[:, i, :], in_=stg[:, i, :], identity=identf[0:16, 0:16]),
             reads=[t_s, t_id], writes=[t_s])
    nm = {}

    def V(name):
        if name not in nm:
            nm[name] = len(nm)
        return prs[:, nm[name], :]
    dv(lambda E: E.tensor_copy(out=prs[:, 0:3, :], in_=pst))
    nm.update({"are": 0, "aim": 1, "lst": 2})
    S.op("act", lambda E: E.activation(out=V("step"), in_=V("lst"), func=AF.Exp), reads=[t_s], writes=[t_s])
    dv(lambda E: E.tensor_tensor(out=V("ar"), in0=V("are"), in1=V("step"), op=ALU.mult))
    dv(lambda E: E.tensor_tensor(out=V("th"), in0=V("aim"), in1=V("step"), op=ALU.mult))
    S.op("act", lambda E: E.activation(out=V("rho"), in_=V("ar"), func=AF.Exp), reads=[t_s], writes=[t_s])

    def sin_of(dst, src, shift):
        dv(lambda E: E.tensor_scalar(out=V("k1"), in0=V(src), scalar1=1.0 / TWO_PI, scalar2=shift / TWO_PI,
                                     op0=ALU.mult, op1=ALU.add))
        dv(lambda E: E.tensor_scalar(out=V("k2"), in0=V("k1"), scalar1=MAGIC, scalar2=None, op0=ALU.add))
        dv(lambda E: E.tensor_scalar(out=V("k3"), in0=V("k2"), scalar1=-MAGIC, scalar2=None, op0=ALU.add))
        dv(lambda E: E.tensor_tensor(out=V("k1"), in0=V("k1"), in1=V("k3"), op=ALU.subtract))
        S.op("act", lambda E: E.activation(out=V(dst), in_=V("k1"), func=AF.Sin, scale=TWO_PI),
             reads=[t_s], writes=[t_s])
    sin_of("sn", "th", 0.0)
    sin_of("cs", "th", TWO_PI / 4)
    dv(lambda E: E.tensor_tensor(out=V("lr"), in0=V("rho"), in1=V("cs"), op=ALU.mult))
    dv(lambda E: E.tensor_tensor(out=V("li"), in0=V("rho"), in1=V("sn"), op=ALU.mult))
    dv(lambda E: E.tensor_scalar(out=V("nr"), in0=V("lr"), scalar1=-1.0, scalar2=None, op0=ALU.add))
    dv(lambda E: E.tensor_tensor(out=V("d1"), in0=V("are"), in1=V("are"), op=ALU.mult))
    dv(lambda E: E.tensor_tensor(out=V("d2"), in0=V("aim"), in1=V("aim"), op=ALU.mult))
    dv(lambda E: E.tensor_tensor(out=V("d1"), in0=V("d1"), in1=V("d2"), op=ALU.add))
    dv(lambda E: E.reciprocal(out=V("rd"), in_=V("d1")))
    dv(lambda E: E.tensor_tensor(out=V("z1"), in0=V("nr"), in1=V("are"), op=ALU.mult))
    dv(lambda E: E.tensor_tensor(out=V("z2"), in0=V("li"), in1=V("aim"), op=ALU.mult))
    dv(lambda E: E.tensor_tensor(out=V("z1"), in0=V("z1"), in1=V("z2"), op=ALU.add))
    dv(lambda E: E.tensor_tensor(out=V("zr"), in0=V("z1"), in1=V("rd"), op=ALU.mult))
    dv(lambda E: E.tensor_tensor(out=V("z1"), in0=V("li"), in1=V("are"), op=ALU.mult))
    dv(lambda E: E.tensor_tensor(out=V("z2"), in0=V("nr"), in1=V("aim"), op=ALU.mult))
    dv(lambda E: E.tensor_tensor(out=V("z1"), in0=V("z1"), in1=V("z2"), op=ALU.subtract))
    dv(lambda E: E.tensor_tensor(out=V("zi"), in0=V("z1"), in1=V("rd"), op=ALU.mult))

    tmpA = P2.sb([128, NCMB, C // 2], F32)
    tmpB = P2.sb([128, NCMB, C // 2], F32)
    dv(lambda E: E.memset(cosT[:, :, 0:1], 1.0))
    dv(lambda E: E.memset(sinT[:, :, 0:1], 0.0))
    dv(lambda E: E.tensor_copy(out=V("wr"), in_=V("cs")))
    dv(lambda E: E.tensor_copy(out=V("wi"), in_=V("sn")))
    L = 1
    while L < C:
        wrb = V("wr").unsqueeze(2).to_broadcast([128, NCMB, L])
        wib = V("wi").unsqueeze(2).to_broadcast([128, NCMB, L])
        dv(lambda E, L=L, wrb=wrb: E.tensor_tensor(out=tmpA[:, :, 0:L], in0=cosT[:, :, 0:L], in1=wrb, op=ALU.mult))
        dv(lambda E, L=L, wib=wib: E.tensor_tensor(out=tmpB[:, :, 0:L], in0=sinT[:, :, 0:L], in1=wib, op=ALU.mult))
        dv(lambda E, L=L: E.tensor_tensor(out=cosT[:, :, L:2 * L], in0=tmpA[:, :, 0:L], in1=tmpB[:, :, 0:L],
                                          op=ALU.subtract))
        dv(lambda E, L=L, wib=wib: E.tensor_tensor(out=tmpA[:, :, 0:L], in0=cosT[:, :, 0:L], in1=wib, op=ALU.mult))
        dv(lambda E, L=L, wrb=wrb: E.tensor_tensor(out=tmpB[:, :, 0:L], in0=sinT[:, :, 0:L], in1=wrb, op=ALU.mult))
        dv(lambda E, L=L: E.tensor_tensor(out=sinT[:, :, L:2 * L], in0=tmpA[:, :, 0:L], in1=tmpB[:, :, 0:L],
                                          op=ALU.add))
        dv(lambda E: E.tensor_tensor(out=V("q1"), in0=V("wr"), in1=V("wr"), op=ALU.mult))
        dv(lambda E: E.tensor_tensor(out=V("q2"), in0=V("wi"), in1=V("wi"), op=ALU.mult))
        dv(lambda E: E.tensor_tensor(out=V("q3"), in0=V("wr"), in1=V("wi"), op=ALU.mult))
        dv(lambda E: E.tensor_tensor(out=V("wr"), in0=V("q1"), in1=V("q2"), op=ALU.subtract))
        dv(lambda E: E.tensor_scalar(out=V("wi"), in0=V("q3"), scalar1=2.0, scalar2=None, op0=ALU.mult))
        L *= 2

    bst = P2.sb([128, 2, 8, 16], F32)
    S.dma("sp", bst[:, 0, :, :], b_re.rearrange("(j g) p c -> (g p) j c", g=2), writes=[t_s])
    S.dma("sp", bst[:, 1, :, :], b_im.rearrange("(j g) p c -> (g p) j c", g=2), writes=[t_s])
    bexp = P2.sb([128, 2, 128], F32)
    btmp = P2.sb([128, 16], F32)
    pX = [banks[1][:, 0:128], banks[2][:, 0:128]]
    for d in range(2):
        for j in range(8):
            cmb = d * 8 + j
            jj = j % 4
            dv(lambda E: E.memset(bexp[:], 0.0))
            for g2 in range(2):
                ps_ = slice(64 * g2, 64 * g2 + 64)
                cs_ = slice(32 * jj + 16 * g2, 32 * jj + 16 * g2 + 16)
                zr_ = prs[ps_, nm["zr"], cmb:cmb + 1]
                zi_ = prs[ps_, nm["zi"], cmb:cmb + 1]
                dv(lambda E, ps_=ps_, zi_=zi_, j=j: E.tensor_scalar(out=btmp[ps_, :], in0=bst[ps_, 1, j, :], scalar1=zi_,
                                                                    scalar2=None, op0=ALU.mult))
                dv(lambda E, ps_=ps_, cs_=cs_, zr_=zr_, j=j: E.scalar_tensor_tensor(
                    out=bexp[ps_, 0, cs_], in0=bst[ps_, 0, j, :], scalar=zr_, in1=btmp[ps_, :], op0=ALU.mult,
                    op1=ALU.subtract))
                dv(lambda E, ps_=ps_, zr_=zr_, j=j: E.tensor_scalar(out=btmp[ps_, :], in0=bst[ps_, 1, j, :], scalar1=zr_,
                                                                    scalar2=None, op0=ALU.mult))
                dv(lambda E, ps_=ps_, cs_=cs_, zi_=zi_, j=j: E.scalar_tensor_tensor(
                    out=bexp[ps_, 1, cs_], in0=bst[ps_, 0, j, :], scalar=zi_, in1=btmp[ps_, :], op0=ALU.mult,
                    op1=ALU.add))
            for ri in range(2):
                S.op("pe", lambda E, ri=ri: E.transpose(out=pX[ri], in_=bexp[:, ri, :], identity=identf[:]),
                     reads=[t_s, t_id], writes=[t_s])
                dv(lambda E, ri=ri, cmb=cmb: E.tensor_copy(out=BT[:, cmb, ri, :], in_=pX[ri]))
    cnat = P2.sb([128, 2, 2, 2, 64], F32)
    for d in range(2):
        for ri, cc in enumerate((c_re, c_im)):
            for ut in range(2):
                S.dma("sp", cnat[:, d, ri, ut, :], cc[d].rearrange("g c p -> (g c) p")[128 * ut:128 * ut + 128, :],
                      writes=[t_s])
    mki = P2.sb([128, 4, 2], I32)
    mk = P2.sb([128, 4, 2], F32)
    mk2 = P2.sb([128, 4, 2], F32)
    S.op("pool", lambda E: E.iota(mki[:], pattern=[[-32, 4], [-16, 2]], base=0, channel_multiplier=1),
         reads=[t_s], writes=[t_s])
    dv(lambda E: E.tensor_copy(out=mk[:], in_=mki[:]))
    dv(lambda E: E.tensor_scalar(out=mk2[:], in0=mk[:], scalar1=0.0, scalar2=None, op0=ALU.is_ge))
    dv(lambda E: E.tensor_scalar(out=mk[:], in0=mk[:], scalar1=15.0, scalar2=None, op0=ALU.is_le))
    dv(lambda E: E.tensor_tensor(out=mk[:], in0=mk[:], in1=mk2[:], op=ALU.mult))
    cx = P2.sb([128, 2, 64], F32)
    for d in range(2):
        for j in range(8):
            cmb = d * 8 + j
            jj = j % 4
            ut = j // 4
            for ri in range(2):
                for g2 in range(2):
                    dv(lambda E, d=d, ri=ri, ut=ut, jj=jj, g2=g2: E.tensor_scalar(
                        out=cx[:, g2, :], in0=cnat[:, d, ri, ut, :], scalar1=mk[:, jj, g2:g2 + 1], scalar2=None,
                        op0=ALU.mult))
                S.op("pe", lambda E, ri=ri: E.transpose(out=pX[ri], in_=cx[:].rearrange("p a b -> p (a b)"),
                                                        identity=identf[:]),
                     reads=[t_s, t_id], writes=[t_s])
                sgn = 1.0 if ri == 0 else -1.0
                dv(lambda E, ri=ri, cmb=cmb, sgn=sgn: E.tensor_scalar(out=CT[:, cmb, ri, :], in0=pX[ri], scalar1=sgn,
                                                                      scalar2=None, op0=ALU.mult))
    load_w_bf16(S, gw, glu_w, t_s, col_split=1)
    S.dma("sp", gb[:], glu_b.rearrange("(o p) -> p o", p=128), writes=[t_s], allow_slow_non_contiguous=True)
    S.dma("sp", dk[:], dsk.rearrange("(o p) -> p o", p=128), writes=[t_s], allow_slow_non_contiguous=True)

    for ut in range(2):
        for c4 in range(T // 2048 if T >= 2048 else 1):
            w_ = min(2048, T)
            S.dma("pool", ub[:, ut, c4 * w_:(c4 + 1) * w_], us5[128 * ut:128 * ut + 128, c4 * w_:(c4 + 1) * w_],
                  writes=[t_ub])
    S.op("dve", lambda E: E.memset(gi[:], 0.0), writes=t_gi)
    NB = 3
    P2.close()
    pBU = [banks[i][:, 0:2 * C].rearrange("p (a b) -> p a b", a=2) for i in range(2)]; t_pBU = [Tok() for _ in range(2)]
    m1 = [P.sb([128, 4, C], F32) for _ in range(NB)]; t_m1 = [Tok() for _ in range(NB)]
    gin = [P.sb([128, 2, C], F32) for _ in range(NB)]; t_gin = [Tok() for _ in range(NB)]
    gg = [P.sb([128, 2, C], F32) for _ in range(NB)]; t_gg = [Tok() for _ in range(NB)]
    m2 = [P.sb([128, 4, C], F32) for _ in range(NB)]; t_m2 = [Tok() for _ in range(NB)]
    ctmp = [P.sb([128, 2], F32) for _ in range(NB)]; t_ct = [Tok() for _ in range(NB)]
    hh = [P.sb([128, 4, 2, C], BF16) for _ in range(2)]; t_hh = [[Tok() for _ in range(4)] for _ in range(2)]
    pY = [banks[2 + i][:, 0:C] for i in range(2)]; t_pY = [Tok(), Tok()]
    uf = [P.sb([128, 2, C], F32) for _ in range(2)]; t_uf = [Tok(), Tok()]
    yv = [P.sb([128, C], F32) for _ in range(2)]; t_yv = [Tok(), Tok()]
    y2 = [P.sb([128, C], F32) for _ in range(2)]; t_y2 = [Tok(), Tok()]
    ygl = [P.sb([128, 2, C], BF16) for _ in range(2)]; t_yg = [[Tok(), Tok()] for _ in range(2)]
    pZ = [banks[4 + i][:, 0:C] for i in range(2)]; t_pZ = [Tok(), Tok()]
    sg = [P.sb([128, C], F32) for _ in range(2)]; t_sg = [Tok(), Tok()]
    oo = [P.sb([128, C], BF16) for _ in range(2)]; t_oo = [Tok(), Tok()]
    t_out = Tok()
    iy = [0]
    items = []
    for d in (1, 0):
        for ci in range(NCH):
            for ut in range(2):
                for jj in range(4):
                    items.append((d, ci, ut, jj))

    def geom(d, ci):
        if d == 1:
            lo, hi = T - (ci + 1) * C, T - ci * C
            return lo, hi, rsl(lo, hi)
        lo, hi = ci * C, (ci + 1) * C
        return lo, hi, slice(lo, hi)

    def stA(i):
        d, ci, ut, jj = items[i]
        lo, hi, tsl = geom(d, ci)
        cmb = d * 8 + ut * 4 + jj
        b = i % NB
        pb = i % 2
        if d == 0 and ut == 0 and jj == 0:
            ufb = ci % 2
            S.dma("sp", uf[ufb][:], us5.rearrange("(u p) t -> p u t", p=128)[:, :, lo:hi], writes=[t_uf[ufb]])
        for ri in range(2):
            S.op("pe", lambda E, ri=ri: E.matmul(pBU[pb][:, ri, :], lhsT=BT[:, cmb, ri, :], rhs=ub[:, ut, tsl],
                                                 start=True, stop=True), reads=[t_s, t_ub], writes=[t_pBU[pb]])
        cs_ = cosT[:, cmb, :]
        sn_ = sinT[:, cmb, :]
        for k, (src, tab) in enumerate(((0, cs_), (1, sn_), (1, cs_), (0, sn_))):
            S.op("dve", lambda E, k=k, src=src, tab=tab: E.tensor_tensor(out=m1[b][:, k, :], in0=pBU[pb][:, src, :],
                                                                         in1=tab, op=ALU.mult),
                 reads=[t_pBU[pb], t_s], writes=[t_m1[b]])
        S.op("pool", lambda E: E.tensor_tensor(out=gin[b][:, 0, :], in0=m1[b][:, 0, :], in1=m1[b][:, 1, :], op=ALU.add),
             reads=[t_m1[b]], writes=[t_gin[b]])
        S.op("pool", lambda E: E.tensor_tensor(out=gin[b][:, 1, :], in0=m1[b][:, 2, :], in1=m1[b][:, 3, :],
                                               op=ALU.subtract), reads=[t_m1[b]], writes=[t_gin[b]])

    def stB(i):
        d, ci, ut, jj = items[i]
        cmb = d * 8 + ut * 4 + jj
        b = i % NB
        rho_b = prs[:, nm["rho"], cmb:cmb + 1].to_broadcast([128, C])
        for ri in range(2):
            S.op("dve", lambda E, ri=ri: E.tensor_tensor_scan(
                out=gg[b][:, ri, :], data0=rho_b, data1=gin[b][:, ri, :], initial=gi[:, ri, cmb:cmb + 1],
                op0=ALU.mult, op1=ALU.add), reads=[t_gin[b], t_gi[cmb], t_s], writes=[t_gg[b]])
        wr_ = prs[:, nm["wr"], cmb:cmb + 1]
        wi_ = prs[:, nm["wi"], cmb:cmb + 1]
        S.op("dve", lambda E: E.tensor_scalar(out=ctmp[b][:, 0:1], in0=gg[b][:, 1, C - 1:C], scalar1=wi_, scalar2=None,
                                              op0=ALU.mult), reads=[t_gg[b], t_s], writes=[t_ct[b]])
        S.op("dve", lambda E: E.tensor_scalar(out=ctmp[b][:, 1:2], in0=gg[b][:, 1, C - 1:C], scalar1=wr_, scalar2=None,
                                              op0=ALU.mult), reads=[t_gg[b], t_s], writes=[t_ct[b]])
        S.op("dve", lambda E: E.scalar_tensor_tensor(out=gi[:, 0, cmb:cmb + 1], in0=gg[b][:, 0, C - 1:C], scalar=wr_,
                                                     in1=ctmp[b][:, 0:1], op0=ALU.mult, op1=ALU.subtract),
             reads=[t_gg[b], t_ct[b], t_s], writes=[t_gi[cmb]])
        S.op("dve", lambda E: E.scalar_tensor_tensor(out=gi[:, 1, cmb:cmb + 1], in0=gg[b][:, 0, C - 1:C], scalar=wi_,
                                                     in1=ctmp[b][:, 1:2], op0=ALU.mult, op1=ALU.add),
             reads=[t_gg[b], t_ct[b], t_s], writes=[t_gi[cmb]])

    def stC(i):
        d, ci, ut, jj = items[i]
        lo, hi, tsl = geom(d, ci)
        chn = lo // C
        cmb = d * 8 + ut * 4 + jj
        b = i % NB
        hb = (ci * 2 + ut) % 2
        cs_ = cosT[:, cmb, :]
        sn_ = sinT[:, cmb, :]
        S.op("pool", lambda E: E.tensor_tensor(out=m2[b][:, 0, :], in0=gg[b][:, 0, :], in1=cs_, op=ALU.mult),
             reads=[t_gg[b], t_s], writes=[t_m2[b]])
        S.op("pool", lambda E: E.tensor_tensor(out=m2[b][:, 1, :], in0=gg[b][:, 1, :], in1=sn_, op=ALU.mult),
             reads=[t_gg[b], t_s], writes=[t_m2[b]])
        S.op("dve", lambda E: E.tensor_tensor(out=m2[b][:, 2, :], in0=gg[b][:, 1, :], in1=cs_, op=ALU.mult),
             reads=[t_gg[b], t_s], writes=[t_m2[b]])
        S.op("dve", lambda E: E.tensor_tensor(out=m2[b][:, 3, :], in0=gg[b][:, 0, :], in1=sn_, op=ALU.mult),
             reads=[t_gg[b], t_s], writes=[t_m2[b]])
        S.op("pool", lambda E: E.tensor_tensor(out=hh[hb][:, jj, 0, :], in0=m2[b][:, 0, :], in1=m2[b][:, 1, :],
                                               op=ALU.subtract), reads=[t_m2[b]], writes=[t_hh[hb][jj]])
        S.op("pool", lambda E: E.tensor_tensor(out=hh[hb][:, jj, 1, :], in0=m2[b][:, 2, :], in1=m2[b][:, 3, :],
                                               op=ALU.add), reads=[t_m2[b]], writes=[t_hh[hb][jj]])
        if jj != 3:
            return
        yb_ = iy[0] % 2
        iy[0] += 1
        n_mm = 0
        for j4 in range(4):
            cm2 = d * 8 + ut * 4 + j4
            for ri in range(2):
                S.op("pe", lambda E, cm2=cm2, ri=ri, j4=j4, n_mm=n_mm: E.matmul(
                    pY[yb_], lhsT=CT[:, cm2, ri, :], rhs=hh[hb][:, j4, ri, :], start=(n_mm == 0), stop=(n_mm == 7)),
                    reads=[t_s, t_hh[hb][j4]], writes=[t_pY[yb_]])
                n_mm += 1
        if d == 1:
            S.op("act", lambda E: E.activation(out=ybwd[:, ut, tsl], in_=pY[yb_], func=AF.Copy),
                 reads=[t_pY[yb_]], writes=[t_yb[chn]])
            return
        ufb = ci % 2
        gb_ = ci % 2
        S.op("dve", lambda E: E.tensor_tensor(out=yv[yb_][:], in0=pY[yb_], in1=ybwd[:, ut, tsl], op=ALU.add),
             reads=[t_pY[yb_], t_yb[chn]], writes=[t_yv[yb_]])
        S.op("dve", lambda E: E.scalar_tensor_tensor(out=yv[yb_][:], in0=uf[ufb][:, ut, :], scalar=dk[:, ut:ut + 1],
                                                     in1=yv[yb_][:], op0=ALU.mult, op1=ALU.add),
             reads=[t_uf[ufb], t_s, t_yv[yb_]], writes=[t_yv[yb_]])
        S.op("act", lambda E: E.activation(out=y2[yb_][:], in_=yv[yb_][:], func=AF.Square),
             reads=[t_yv[yb_]], writes=[t_y2[yb_]])
        S.op("pool", lambda E: E.tensor_scalar(out=y2[yb_][:], in0=y2[yb_][:], scalar1=0.044715, scalar2=1.0,
                                               op0=ALU.mult, op1=ALU.add), reads=[t_y2[yb_]], writes=[t_y2[yb_]])
        S.op("pool", lambda E: E.tensor_tensor(out=y2[yb_][:], in0=y2[yb_][:], in1=yv[yb_][:], op=ALU.mult),
             reads=[t_y2[yb_], t_yv[yb_]], writes=[t_y2[yb_]])
        S.op("act", lambda E: E.activation(out=y2[yb_][:], in_=y2[yb_][:], func=AF.Sigmoid, scale=1.5957691216057308),
             reads=[t_y2[yb_]], writes=[t_y2[yb_]])
        S.op("dve", lambda E: E.tensor_tensor(out=ygl[gb_][:, ut, :], in0=y2[yb_][:], in1=yv[yb_][:], op=ALU.mult),
             reads=[t_y2[yb_], t_yv[yb_]], writes=[t_yg[gb_][ut]])
        if ut != 1:
            return
        for o in range(2):
            for half in range(2):
                oc = o + 2 * half
                for u2 in range(2):
                    S.op("pe", lambda E, half=half, oc=oc, u2=u2: E.matmul(
                        pZ[half], lhsT=gw[:, u2, oc * 128:(oc + 1) * 128], rhs=ygl[gb_][:, u2, :], start=(u2 == 0),
                        stop=(u2 == 1)), reads=[t_s, t_yg[gb_][u2]], writes=[t_pZ[half]])
            S.op("act", lambda E, o=o: E.activation(out=sg[o][:], in_=pZ[1], func=AF.Sigmoid, bias=gb[:, 2 + o:3 + o]),
                 reads=[t_pZ[1], t_s], writes=[t_sg[o]])
            S.op("dve", lambda E, o=o: E.scalar_tensor_tensor(out=oo[o][:], in0=pZ[0], scalar=gb[:, o:o + 1],
                                                              in1=sg[o][:], op0=ALU.add, op1=ALU.mult),
                 reads=[t_pZ[0], t_sg[o], t_s], writes=[t_oo[o]])
            S.dma("sp", mixT[768 + 128 * o:768 + 128 * o + 128, lo:hi], oo[o][:], reads=[t_oo[o]], writes=[t_out])

    N = len(items)
    for i in range(N + 2):
        if i < N:
            stA(i)
        if 0 <= i - 1 < N:
            stB(i - 1)
        if 0 <= i - 2 < N:
            stC(i - 2)
    P.close()


DEC = 0.6065306597126334
RWKV_DBG = [9]


def phase_rwkv(nc, S, zr, ydir, bon, gfm, prm, T):
    mu_p, mu_n, w0, w2, a0, a2, g2, k_k, k_a, r_k = prm
    P = Phase(nc, S)
    NBLK = T // 512
    ident, identf, t_id = make_ident(nc, S, P)
    t_c = Tok()

    def cst(fn, eng="dve"):
        S.op(eng, fn, reads=[t_c, t_id], writes=[t_c])
    onesb = P.sb([128, 128], F32)
    cst(lambda E: E.memset(onesb[:], 0.0))
    cst(lambda E: E.memset(onesb[0:64, 0:64], 1.0))
    cst(lambda E: E.memset(onesb[64:128, 64:128], 1.0))
    mskU = P.sb([128, 512], F32)
    mskL = P.sb([128, 3, 128], F32)
    cst(lambda E: E.memset(mskU[:], 1.0), "pool")
    cst(lambda E: E.memset(mskL[:], 1.0), "pool")
    for i in range(4):
        cmp_ = ALU.is_gt if i % 2 == 0 else ALU.is_ge
        cst(lambda E, i=i, cmp_=cmp_: E.affine_select(out=mskU[:, 128 * i:128 * i + 128], in_=mskU[:, 128 * i:128 * i + 128],
                                                      pattern=[[1, 128]], base=0, channel_multiplier=-1, compare_op=cmp_,
                                                      fill=0.0), "pool")
    for i in range(3):
        cst(lambda E, i=i: E.affine_select(out=mskL[:, i, :], in_=mskL[:, i, :], pattern=[[-1, 128]], base=0,
                                           channel_multiplier=1, compare_op=ALU.is_gt, fill=0.0), "pool")
    rst = P.sb([128, 512], F32)
    cst(lambda E: E.memset(rst[:], 1.0))
    for q in range(4):
        cst(lambda E, q=q: E.memset(rst[:, 128 * q:128 * q + 1], 0.0))
    mh512 = P.sb([128, 512], F32)
    cst(lambda E: E.memset(mh512[:], -0.5), "pool")
    cmu = P.sb([128, 3, 11], F32)
    S.dma("sp", cmu[:, 1, :], mu_p.rearrange("(c p) -> p c", p=128), writes=[t_c], allow_slow_non_contiguous=True)
    S.dma("sp", cmu[:, 2, :], mu_n.rearrange("(c p) -> p c", p=128), writes=[t_c], allow_slow_non_contiguous=True)
    cst(lambda E: E.tensor_tensor(out=cmu[:, 0, :], in0=cmu[:, 1, :], in1=cmu[:, 2, :], op=ALU.add))
    cst(lambda E: E.tensor_scalar(out=cmu[:, 0, :], in0=cmu[:, 0, :], scalar1=-1.0, scalar2=1.0, op0=ALU.mult, op1=ALU.add))
    w0c = P.sb([128, 2, 3], F32); a0c = P.sb([128, 2, 3], F32)
    for d in range(2):
        S.dma("sp", w0c[:, d, :], w0[d].rearrange("(c p) -> p c", p=128), writes=[t_c], allow_slow_non_contiguous=True)
        S.dma("sp", a0c[:, d, :], a0[d].rearrange("(c p) -> p c", p=128), writes=[t_c], allow_slow_non_contiguous=True)
    kkc = P.sb([128, 3], F32); kac = P.sb([128, 3], F32); omka = P.sb([128, 3], F32); rkc = P.sb([128, 3], F32)
    S.dma("sp", kkc[:], k_k.rearrange("(c p) -> p c", p=128), writes=[t_c], allow_slow_non_contiguous=True)
    S.dma("sp", kac[:], k_a.rearrange("(c p) -> p c", p=128), writes=[t_c], allow_slow_non_contiguous=True)
    S.dma("sp", rkc[:], r_k.rearrange("h k -> (h k)").rearrange("(c p) -> p c", p=128), writes=[t_c],
          allow_slow_non_contiguous=True)
    cst(lambda E: E.tensor_scalar(out=omka[:], in0=kac[:], scalar1=-1.0, scalar2=1.0, op0=ALU.mult, op1=ALU.add))
    w2a2 = P.sb([128, 2, 384], BF16)
    for d in range(2):
        S.dma("pool", w2a2[0:64, d, :], w2[d], writes=[t_c])
        S.dma("pool", w2a2[64:128, d, :], a2[d], writes=[t_c])
    g2b = P.sb([128, 384], BF16)
    S.dma("pool", g2b[:], g2, writes=[t_c])

    banks = [P.ps([128, 512], F32) for _ in range(6)]
    t_bk = [Tok() for _ in range(6)]
    bkrr = [0]

    def getbank():
        i = bkrr[0] % 6
        bkrr[0] += 1
        return banks[i], t_bk[i]
    pTr = [P.ps([128, 4, 128], BF16) for _ in range(2)]; t_pTr = [Tok(), Tok()]
    trr = [0]

    NZS = 4
    zraw = P.sb([128, NZS, 514], F32); t_zraw = [Tok() for _ in range(NZS)]
    zsi = [0]
    zm = P.sb([128, 11, 512], F32); t_zm = [Tok() for _ in range(11)]
    tmp0 = [P.sb([128, 512], F32) for _ in range(2)]; t_tmp0 = [Tok(), Tok()]
    tmp1 = [P.sb([128, 512], F32) for _ in range(2)]; t_tmp1 = [Tok(), Tok()]
    tz = P.sb([128, 512], BF16); t_tz = Tok()
    sgz = P.sb([128, 512], BF16); t_sgz = Tok()
    B1 = [P.sb([128, 512], F32) for _ in range(3)]; tB1 = [Tok() for _ in range(3)]
    B2 = [P.sb([128, 512], F32) for _ in range(3)]; tB2 = [Tok() for _ in range(3)]
    B3 = [P.sb([128, 512], F32) for _ in range(3)]; tB3 = [Tok() for _ in range(3)]
    B4 = [P.sb([128, 512], F32) for _ in range(3)]; tB4 = [Tok() for _ in range(3)]
    B5 = [P.sb([128, 512], F32) for _ in range(3)]; tB5 = [Tok() for _ in range(3)]
    B6 = [P.sb([128, 512], F32) for _ in range(3)]; tB6 = [Tok() for _ in range(3)]
    B7 = [P.sb([128, 512], F32) for _ in range(3)]; tB7 = [Tok() for _ in range(3)]
    ARb = [P.sb([128, 4, 2, 128], BF16) for _ in range(3)]; t_AR = [Tok() for _ in range(3)]
    Bt = [P.sb([128, 512], BF16) for _ in range(3)]; t_Bt = [Tok() for _ in range(3)]
    Kt = [P.sb([128, 512], BF16) for _ in range(3)]; t_Kt = [Tok() for _ in range(3)]
    Bb = [P.sb([128, 512], BF16) for _ in range(3)]; t_Bb = [Tok() for _ in range(3)]
    Kb = [P.sb([128, 512], BF16) for _ in range(3)]; t_Kb = [Tok() for _ in range(3)]
    vb = [P.sb([128, 512], BF16) for _ in range(3)]; t_vb = [Tok() for _ in range(3)]
    WCt = P.sb([128, 3, 4], F32); t_WC = Tok()
    tm = [[P.sb([128, 4, 128], BF16) for _ in range(4)] for _ in range(3)]
    t_tm = [[Tok() for _ in range(4)] for _ in range(3)]
    stg = [P.sb([128, 512], F32) for _ in range(3)]; t_stg = [Tok() for _ in range(3)]
    sgi = [0]
    MP = [P.sb([128, 512], BF16) for _ in range(6)]; t_MP = [Tok() for _ in range(6)]
    MT = [P.sb([128, 3, 128], BF16) for _ in range(2)]; t_MT = [Tok(), Tok()]
    Pm = [[P.sb([128, 3, 128], BF16) for _ in range(2)] for _ in range(2)]; t_Pm = [[Tok(), Tok()], [Tok(), Tok()]]
    PmT = [[P.sb([128, 3, 128], BF16) for _ in range(2)] for _ in range(2)]; t_PmT = [[Tok(), Tok()], [Tok(), Tok()]]
    Tm = [[P.sb([128, 3, 128], BF16) for _ in range(2)] for _ in range(2)]; t_Tm = [[Tok(), Tok()], [Tok(), Tok()]]
    X0 = P.sb([128, 6, 64], BF16); t_X0 = Tok()
    Uv = P.sb([128, 6, 64], F32); t_Uv = Tok()
    Ahb = P.sb([128, 3, 128], BF16); t_Ah = Tok()
    Ub = P.sb([128, 6, 64], BF16); t_Ub = Tok()
    Sf = P.sb([128, 3, 64], F32); t_Sf = Tok()
    Sb = P.sb([128, 3, 64], BF16); t_Sb = Tok()
    ytm = [P.sb([128, 4, 384], F32) for _ in range(2)]; t_ytm = [Tok(), Tok()]
    t_out = Tok()
    zv = zr.rearrange("(c p) t -> p c t", p=128)

    for d in range(2 if RWKV_DBG[0] > -1 else 0):
        S.op("dve", lambda E: E.memset(Sf[:], 0.0), writes=[t_Sf])
        S.op("dve", lambda E: E.memset(Sb[:], 0.0), writes=[t_Sb])
        for bi in range(NBLK):
            if d == 0:
                lo, hi = 512 * bi, 512 * bi + 512
            else:
                lo, hi = T - 512 * (bi + 1), T - 512 * bi
            loc = (lambda ap: ap) if d == 0 else None
            s0 = 1 if lo == 0 else 0
            s1 = 513 if hi == T else 514
            osl = slice(0, 512) if d == 0 else rsl(0, 512)
            for c in range(11):
                zs = zsi[0] % NZS
                zsi[0] += 1
                b = c % 2
                if lo == 0:
                    S.op("pool", lambda E, zs=zs: E.memset(zraw[:, zs, 0:1], 0.0), writes=[t_zraw[zs]])
                if hi == T:
                    S.op("pool", lambda E, zs=zs: E.memset(zraw[:, zs, 513:514], 0.0), writes=[t_zraw[zs]])
                S.dma("sp", zraw[:, zs, s0:s1], zv[:, c, lo - 1 + s0:lo - 1 + s1], writes=[t_zraw[zs]])
                S.op("act", lambda E, c=c, b=b, zs=zs: E.activation(out=tmp0[b][:], in_=zraw[:, zs, 1:513], func=AF.Copy,
                                                                   scale=cmu[:, 0, c:c + 1]),
                     reads=[t_zraw[zs], t_c], writes=[t_tmp0[b]])
                S.op("dve", lambda E, c=c, b=b, zs=zs: E.scalar_tensor_tensor(out=tmp1[b][:], in0=zraw[:, zs, 0:512],
                                                                              scalar=cmu[:, 1, c:c + 1], in1=tmp0[b][:],
                                                                              op0=ALU.mult, op1=ALU.add),
                     reads=[t_zraw[zs], t_c, t_tmp0[b]], writes=[t_tmp1[b]])
                S.op("dve", lambda E, c=c, b=b, zs=zs, osl=osl: E.scalar_tensor_tensor(
                    out=zm[:, c, osl], in0=zraw[:, zs, 2:514], scalar=cmu[:, 2, c:c + 1], in1=tmp1[b][:],
                    op0=ALU.mult, op1=ALU.add),
                    reads=[t_zraw[zs], t_c, t_tmp1[b]], writes=[t_zm[c]])
            if RWKV_DBG[0] == 0:
                continue
            S.op("act", lambda E: E.activation(out=tz[0:64, :], in_=zm[0:64, 9, :], func=AF.Tanh),
                 reads=[t_zm[9]], writes=[t_tz])
            S.op("act", lambda E: E.activation(out=tz[64:128, :], in_=zm[64:128, 9, :], func=AF.Copy),
                 reads=[t_zm[9]], writes=[t_tz])
            if d == 0:
                S.op("act", lambda E: E.activation(out=sgz[:], in_=zm[:, 10, :], func=AF.Sigmoid),
                     reads=[t_zm[10]], writes=[t_sgz])
            v4 = lambda ap: ap.rearrange("p (q t) -> p q t", q=4)
            steps = []
            cb = {}

            def ST(f):
                steps.append(f)
            for c in range(3):
                cb[c] = dict(zr=zm[:, c, :], tr=t_zm[c], zk=zm[:, 3 + c, :], tk=t_zm[3 + c], zv=zm[:, 6 + c, :],
                             tv=t_zm[6 + c])

            def s_mm(c):
                X = cb[c]
                X["pW"], X["tpW"] = getbank()
                S.op("pe", lambda E: E.matmul(X["pW"][:], lhsT=w2a2[0:64, d, 128 * c:128 * c + 128], rhs=tz[0:64, :],
                                              start=True, stop=True), reads=[t_c, t_tz], writes=[X["tpW"]])
                X["pA"], X["tpA"] = getbank()
                S.op("pe", lambda E: E.matmul(X["pA"][:], lhsT=w2a2[64:128, d, 128 * c:128 * c + 128], rhs=tz[64:128, :],
                                              start=True, stop=True), reads=[t_c, t_tz], writes=[X["tpA"]])
            ST(s_mm)

            def s_sig(c):
                X = cb[c]
                S.op("act", lambda E: E.activation(out=B1[c][:], in_=X["pW"][:], func=AF.Sigmoid, bias=w0c[:, d, c:c + 1]),
                     reads=[X["tpW"], t_c], writes=[tB1[c]])
                S.op("act", lambda E: E.activation(out=B6[c][:], in_=X["pA"][:], func=AF.Sigmoid, bias=a0c[:, d, c:c + 1]),
                     reads=[X["tpA"], t_c], writes=[tB6[c]])
            ST(s_sig)

            def s_kkv(c):
                X = cb[c]
                S.op("dve", lambda E: E.tensor_scalar(out=B4[c][:], in0=X["zk"], scalar1=kkc[:, c:c + 1], scalar2=None,
                                                      op0=ALU.mult), reads=[X["tk"], t_c], writes=[tB4[c]])
                S.op("pool", lambda E: E.tensor_tensor(out=B5[c][:], in0=B4[c][:], in1=B4[c][:], op=ALU.mult),
                     reads=[tB4[c]], writes=[tB5[c]])
                X["pN"], X["tpN"] = getbank()
                S.op("pe", lambda E: E.matmul(X["pN"][:], lhsT=onesb[:], rhs=B5[c][:], start=True, stop=True),
                     reads=[t_c, tB5[c]], writes=[X["tpN"]])
            ST(s_kkv)

            def s_cls(c):
                S.op("dve", lambda E: E.tensor_tensor_scan(out=B2[c][:], data0=rst[:], data1=B1[c][:], initial=0.0,
                                                           op0=ALU.mult, op1=ALU.add),
                     reads=[tB1[c], t_c], writes=[tB2[c]])
                S.op("pool", lambda E: E.tensor_tensor(out=B1[c][:], in0=B2[c][:], in1=B1[c][:], op=ALU.subtract),
                     reads=[tB2[c]], writes=[tB1[c]])
            ST(s_cls)

            def s_exp(c):
                S.op("act", lambda E: E.activation(out=B3[c][:], in_=B2[c][:], func=AF.Exp, scale=DEC),
                     reads=[tB2[c]], writes=[tB3[c]])
                S.op("act", lambda E: E.activation(out=B2[c][:], in_=B2[c][:], func=AF.Exp, scale=-DEC),
                     reads=[tB2[c]], writes=[tB2[c]])
                S.op("act", lambda E: E.activation(out=B1[c][:], in_=B1[c][:], func=AF.Exp, scale=-DEC),
                     reads=[tB1[c]], writes=[tB1[c]])
            ST(s_exp)

            def s_rn(c):
                X = cb[c]
                S.op("dve", lambda E: E.tensor_scalar(out=B5[c][:], in0=X["pN"][:], scalar1=1e-12, scalar2=None,
                                                      op0=ALU.add), reads=[X["tpN"]], writes=[tB5[c]])
                S.op("act", lambda E: E.activation(out=B5[c][:], in_=B5[c][:], func=AF.Sqrt),
                     reads=[tB5[c]], writes=[tB5[c]])
                S.op("dve", lambda E: E.reciprocal(out=B5[c][:], in_=B5[c][:]),
                     reads=[tB5[c]], writes=[tB5[c]])
                S.op("dve", lambda E: E.tensor_tensor(out=B4[c][:], in0=B4[c][:], in1=B5[c][:], op=ALU.mult),
                     reads=[tB4[c], tB5[c]], writes=[tB4[c]])
                S.op("dve", lambda E: E.tensor_copy(out=WCt[:, c, :], in_=B2[c][:, 127:512:128]),
                     reads=[tB2[c]], writes=[t_WC])
            ST(s_rn)

            def s_kd(c):
                X = cb[c]
                S.op("dve", lambda E: E.tensor_scalar(out=B7[c][:], in0=B6[c][:], scalar1=kac[:, c:c + 1],
                                                      scalar2=omka[:, c:c + 1], op0=ALU.mult, op1=ALU.add),
                     reads=[tB6[c], t_c], writes=[tB7[c]])
                S.op("dve", lambda E: E.tensor_tensor(out=B7[c][:], in0=B7[c][:], in1=X["zk"], op=ALU.mult),
                     reads=[tB7[c], X["tk"]], writes=[tB7[c]])
                S.op("pool", lambda E: E.tensor_tensor(out=B6[c][:], in0=B4[c][:], in1=B6[c][:], op=ALU.mult),
                     reads=[tB4[c], tB6[c]], writes=[tB6[c]])
            ST(s_kd)

            def s_ar(c):
                X = cb[c]
                S.op("dve", lambda E: E.scalar_tensor_tensor(out=ARb[c][:, :, 0, :], in0=v4(B4[c][:]), scalar=-1.0,
                                                             in1=v4(B1[c][:]), op0=ALU.mult, op1=ALU.mult),
                     reads=[tB4[c], tB1[c]], writes=[t_AR[c]])
                S.op("dve", lambda E: E.tensor_tensor(out=ARb[c][:, :, 1, :], in0=v4(X["zr"]), in1=v4(B2[c][:]),
                                                      op=ALU.mult), reads=[X["tr"], tB2[c]], writes=[t_AR[c]])
                S.op("pool", lambda E: E.tensor_tensor(out=Bt[c][:], in0=B6[c][:], in1=B3[c][:], op=ALU.mult),
                     reads=[tB6[c], tB3[c]], writes=[t_Bt[c]])
                S.op("pool", lambda E: E.tensor_tensor(out=Kt[c][:], in0=B7[c][:], in1=B3[c][:], op=ALU.mult),
                     reads=[tB7[c], tB3[c]], writes=[t_Kt[c]])
                S.op("act", lambda E: E.activation(out=vb[c][:], in_=X["zv"], func=AF.Copy),
                     reads=[X["tv"]], writes=[t_vb[c]])
            ST(s_ar)

            def s_bb(c):
                wcb = WCt[:, c, :].unsqueeze(2).to_broadcast([128, 4, 128])
                S.op("dve", lambda E: E.tensor_tensor(out=v4(Bb[c][:]), in0=v4(Bt[c][:]), in1=wcb, op=ALU.mult),
                     reads=[t_Bt[c], t_WC], writes=[t_Bb[c]])
                S.op("dve", lambda E: E.tensor_tensor(out=v4(Kb[c][:]), in0=v4(Kt[c][:]), in1=wcb, op=ALU.mult),
                     reads=[t_Kt[c], t_WC], writes=[t_Kb[c]])
            ST(s_bb)

            def s_bonus(c):
                X = cb[c]
                S.op("dve", lambda E: E.scalar_tensor_tensor(out=B5[c][:], in0=X["zr"], scalar=rkc[:, c:c + 1],
                                                             in1=B7[c][:], op0=ALU.mult, op1=ALU.mult),
                     reads=[X["tr"], tB7[c], t_c], writes=[tB5[c]])
                pBn, t_pBn = getbank()
                S.op("pe", lambda E: E.matmul(pBn[:], lhsT=onesb[:], rhs=B5[c][:], start=True, stop=True),
                     reads=[t_c, tB5[c]], writes=[t_pBn])
                si = sgi[0] % 3
                sgi[0] += 1
                S.op("dve", lambda E: E.tensor_tensor(out=stg[si][:, osl], in0=pBn[:], in1=X["zv"], op=ALU.mult),
                     reads=[t_pBn, X["tv"]], writes=[t_stg[si]])
                S.dma("sp", bon[d][128 * c:128 * c + 128, lo:hi], stg[si][:], reads=[t_stg[si]], writes=[t_out])
                if d == 0:
                    pG, t_pG = getbank()
                    S.op("pe", lambda E: E.matmul(pG[:], lhsT=g2b[:, 128 * c:128 * c + 128], rhs=sgz[:], start=True,
                                                  stop=True), reads=[t_c, t_sgz], writes=[t_pG])
                    si2 = sgi[0] % 3
                    sgi[0] += 1
                    S.op("act", lambda E: E.activation(out=stg[si2][:], in_=pG[:], func=AF.Copy),
                         reads=[t_pG], writes=[t_stg[si2]])
                    S.dma("sp", gfm[128 * c:128 * c + 128, lo:hi], stg[si2][:], reads=[t_stg[si2]], writes=[t_out])
            ST(s_bonus)

            def s_tr(c):
                for q in range(4):
                    ts_ = trr[0] % 2
                    trr[0] += 1
                    srcs = [(ARb[c][:, q, 0, :], t_AR[c]), (Bb[c][:, 128 * q:128 * q + 128], t_Bb[c]),
                            (Kb[c][:, 128 * q:128 * q + 128], t_Kb[c]), (vb[c][:, 128 * q:128 * q + 128], t_vb[c])]
                    for ai, (src, tk) in enumerate(srcs):
                        S.op("pe", lambda E, ts_=ts_, ai=ai, src=src: E.transpose(out=pTr[ts_][:, ai, :], in_=src,
                                                                                 identity=ident[:]),
                             reads=[tk, t_id], writes=[t_pTr[ts_]])
                    if q % 2 == 0:
                        S.op("act", lambda E, ts_=ts_, q=q: E.activation(out=tm[c][q][:], in_=pTr[ts_][:], func=AF.Copy),
                             reads=[t_pTr[ts_]], writes=[t_tm[c][q]])
                    else:
                        S.op("dve", lambda E, ts_=ts_, q=q: E.tensor_copy(out=tm[c][q][:], in_=pTr[ts_][:]),
                             reads=[t_pTr[ts_]], writes=[t_tm[c][q]])
            ST(s_tr)
            for st_ in steps:
                for c in range(3):
                    st_(c)
            yb = bi % 2
            for q in range(4 if RWKV_DBG[0] >= 2 else 0):
                qs = slice(128 * q, 128 * q + 128)
                for h in range(6):
                    c, h2 = h // 2, h % 2
                    rows = slice(64 * h2, 64 * h2 + 64)
                    pM, t_pM = getbank()
                    S.op("pe", lambda E, pM=pM, c=c, rows=rows, qs=qs, q=q: E.matmul(
                        pM[:, 0:256], lhsT=Bt[c][rows, qs], rhs=ARb[c][rows, q, :, :].rearrange("p a b -> p (a b)"), start=True, stop=True),
                        reads=[t_Bt[c], t_AR[c]], writes=[t_pM])
                    S.op("pe", lambda E, pM=pM, c=c, rows=rows, qs=qs, q=q: E.matmul(
                        pM[:, 256:512], lhsT=Kt[c][rows, qs], rhs=ARb[c][rows, q, :, :].rearrange("p a b -> p (a b)"), start=True, stop=True),
                        reads=[t_Kt[c], t_AR[c]], writes=[t_pM])
                    S.op("dve", lambda E, pM=pM, h=h: E.tensor_tensor(out=MP[h][:], in0=pM[:], in1=mskU[:], op=ALU.mult),
                         reads=[t_pM, t_c], writes=[t_MP[h]])
                for hg in range(2):
                    pM3, t_pM3 = getbank()
                    for j in range(3):
                        h = 2 * j + hg
                        c, h2 = j, hg
                        rows = slice(64 * h2, 64 * h2 + 64)
                        S.op("pe", lambda E, pM3=pM3, j=j, c=c, rows=rows, qs=qs, q=q: E.matmul(
                            pM3[:, 128 * j:128 * j + 128], lhsT=ARb[c][rows, q, 0, :], rhs=Bt[c][rows, qs],
                            start=True, stop=True), reads=[t_Bt[c], t_AR[c]], writes=[t_pM3])
                    S.op("dve", lambda E, pM3=pM3, hg=hg: E.tensor_tensor(
                        out=MT[hg][:], in0=pM3[:, 0:384].rearrange("p (a b) -> p a b", a=3), in1=mskL[:], op=ALU.mult),
                        reads=[t_pM3, t_c], writes=[t_MT[hg]])
                if RWKV_DBG[0] < 3:
                    continue
                cur = [0, 0]
                for hg in range(2):
                    for j in range(3):
                        h = 2 * j + hg
                        S.op("pool", lambda E, hg=hg, j=j, h=h: E.tensor_tensor(out=Tm[hg][0][:, j, :], in0=MP[h][:, 0:128],
                                                                              in1=identf[:], op=ALU.add),
                             reads=[t_MP[h], t_id], writes=[t_Tm[hg][0]])
                v3 = lambda ap: ap[:, 0:384].rearrange("p (a b) -> p a b", a=3)
                for lev in range(1, 7):
                    bk = {}
                    for hg in range(2):
                        pv = cur[hg]
                        pP, t_pP = getbank()
                        pPT, t_pPT = getbank()
                        bk[hg] = (pP, t_pP, pPT, t_pPT)
                        for j in range(3):
                            h = 2 * j + hg
                            if lev == 1:
                                Pprev, tP = MP[h][:, 0:128], t_MP[h]
                                PTprev, tPT = MT[hg][:, j, :], t_MT[hg]
                            else:
                                Pprev, tP = Pm[hg][pv][:, j, :], t_Pm[hg][pv]
                                PTprev, tPT = PmT[hg][pv][:, j, :], t_PmT[hg][pv]
                            if lev < 6:
                                S.op("pe", lambda E, pP=pP, j=j, Pprev=Pprev, PTprev=PTprev: E.matmul(
                                    pP[:, 128 * j:128 * j + 128], lhsT=PTprev, rhs=Pprev, start=True, stop=True),
                                    reads=[tP, tPT], writes=[t_pP])
                            S.op("pe", lambda E, pPT=pPT, j=j, Pprev=Pprev, PTprev=PTprev: E.matmul(
                                pPT[:, 128 * j:128 * j + 128], lhsT=Pprev, rhs=PTprev, start=True, stop=True),
                                reads=[tP, tPT], writes=[t_pPT])
                    for hg in range(2):
                        pP, t_pP, pPT, t_pPT = bk[hg]
                        nx = 1 - cur[hg]
                        if lev < 6:
                            S.op("act", lambda E, pP=pP, hg=hg, nx=nx: E.activation(out=Pm[hg][nx][:], in_=v3(pP),
                                                                                    func=AF.Copy),
                                 reads=[t_pP], writes=[t_Pm[hg][nx]])
                        S.op("dve", lambda E, pPT=pPT, hg=hg, nx=nx: E.tensor_copy(out=PmT[hg][nx][:], in_=v3(pPT)),
                             reads=[t_pPT], writes=[t_PmT[hg][nx]])
                    bt = {}
                    for hg in range(2):
                        pv = cur[hg]
                        nx = 1 - pv
                        pTT, t_pTT = getbank()
                        bt[hg] = (pTT, t_pTT)
                        for j in range(3):
                            S.op("pe", lambda E, pTT=pTT, j=j, hg=hg, nx=nx, pv=pv: E.matmul(
                                pTT[:, 128 * j:128 * j + 128], lhsT=PmT[hg][nx][:, j, :], rhs=Tm[hg][pv][:, j, :],
                                start=True, stop=True), reads=[t_PmT[hg][nx], t_Tm[hg][pv]], writes=[t_pTT])
                    for hg in range(2):
                        pv = cur[hg]
                        nx = 1 - pv
                        pTT, t_pTT = bt[hg]
                        S.op("dve", lambda E, pTT=pTT, hg=hg, nx=nx, pv=pv: E.tensor_tensor(
                            out=Tm[hg][nx][:], in0=v3(pTT), in1=Tm[hg][pv][:], op=ALU.add),
                            reads=[t_pTT, t_Tm[hg][pv]], writes=[t_Tm[hg][nx]])
                        cur[hg] = nx
                TF = [Tm[0][cur[0]], Tm[1][cur[1]]]
                tTF = [t_Tm[0][cur[0]], t_Tm[1][cur[1]]]
                if RWKV_DBG[0] < 4:
                    continue
                pX, t_pX = getbank()
                for h in range(6):
                    c, h2 = h // 2, h % 2
                    sl_ = 3 * h2 + c
                    S.op("pe", lambda E, pX=pX, h=h, c=c, h2=h2, q=q, sl_=sl_: E.matmul(
                        pX[:, 64 * sl_:64 * sl_ + 64], lhsT=MP[h][:, 256:384], rhs=tm[c][q][:, 3, 64 * h2:64 * h2 + 64],
                        start=True, stop=True), reads=[t_MP[h], t_tm[c][q]], writes=[t_pX])
                S.op("act", lambda E, pX=pX: E.activation(out=X0[:], in_=pX[:, 0:384].rearrange("p (a b) -> p a b", a=6),
                                                         func=AF.Copy), reads=[t_pX], writes=[t_X0])
                pV, t_pV = getbank()
                for h in range(6):
                    c, h2 = h // 2, h % 2
                    sl_ = 3 * h2 + c
                    S.op("pe", lambda E, pV=pV, c=c, h2=h2, sl_=sl_: E.matmul(
                        pV[:, 64 * sl_:64 * sl_ + 64], lhsT=TF[h2][:, c, :], rhs=X0[:, sl_, :], start=True, stop=True),
                        reads=[tTF[h2], t_X0], writes=[t_pV])
                S.op("act", lambda E, pV=pV: E.activation(out=Uv[:], in_=pV[:, 0:384].rearrange("p (a b) -> p a b", a=6),
                                                         func=AF.Copy), reads=[t_pV], writes=[t_Uv])
                pH, t_pH = getbank()
                for h in range(6):
                    c, h2 = h // 2, h % 2
                    S.op("pe", lambda E, pH=pH, c=c, h2=h2, q=q: E.matmul(
                        pH[64 * h2:64 * h2 + 64, 128 * c:128 * c + 128], lhsT=tm[c][q][:, 0, 64 * h2:64 * h2 + 64],
                        rhs=TF[h2][:, c, :], start=True, stop=True), reads=[tTF[h2], t_tm[c][q]], writes=[t_pH])
                S.op("dve", lambda E, pH=pH: E.tensor_copy(out=Ahb[:], in_=pH[:, 0:384].rearrange("p (a b) -> p a b", a=3)),
                     reads=[t_pH], writes=[t_Ah])
                if RWKV_DBG[0] < 5:
                    continue
                pUb = [getbank(), getbank()]
                for h2 in range(2):
                    rows = slice(64 * h2, 64 * h2 + 64)
                    for c in range(3):
                        S.op("pe", lambda E, h2=h2, c=c, rows=rows: E.matmul(
                            pUb[h2][0][:, 64 * c:64 * c + 64], lhsT=Ahb[rows, c, :], rhs=Sb[rows, c, :], start=True,
                            stop=True), reads=[t_Ah, t_Sb], writes=[pUb[h2][1]])
                for h2 in range(2):
                    S.op("dve", lambda E, h2=h2: E.tensor_tensor(
                        out=Ub[:, 3 * h2:3 * h2 + 3, :], in0=pUb[h2][0][:, 0:192].rearrange("p (a b) -> p a b", a=3),
                        in1=Uv[:, 3 * h2:3 * h2 + 3, :], op=ALU.add),
                        reads=[pUb[h2][1], t_Uv], writes=[t_Ub])
                pYb = [getbank(), getbank()]
                for h2 in range(2):
                    rows = slice(64 * h2, 64 * h2 + 64)
                    for c in range(3):
                        h = 2 * c + h2
                        sl_ = 3 * h2 + c
                        oc = slice(64 * c, 64 * c + 64)
                        S.op("pe", lambda E, h2=h2, c=c, rows=rows, q=q, oc=oc: E.matmul(
                            pYb[h2][0][:, oc], lhsT=ARb[c][rows, q, 1, :], rhs=Sb[rows, c, :], start=True, stop=False),
                            reads=[t_AR[c], t_Sb], writes=[pYb[h2][1]])
                        S.op("pe", lambda E, h2=h2, h=h, sl_=sl_, oc=oc: E.matmul(
                            pYb[h2][0][:, oc], lhsT=MP[h][:, 128:256], rhs=Ub[:, sl_, :], start=False, stop=False),
                            reads=[t_MP[h], t_Ub], writes=[pYb[h2][1]])
                        S.op("pe", lambda E, h2=h2, h=h, c=c, q=q, oc=oc: E.matmul(
                            pYb[h2][0][:, oc], lhsT=MP[h][:, 384:512], rhs=tm[c][q][:, 3, 64 * h2:64 * h2 + 64],
                            start=False, stop=True), reads=[t_MP[h], t_tm[c][q]], writes=[pYb[h2][1]])
                for h2 in range(2):
                    S.op("act", lambda E, h2=h2, yb=yb, q=q: E.activation(
                        out=ytm[yb][:, q, :].rearrange("p (c g v) -> p g c v", g=2, v=64)[:, h2, :, :],
                        in_=pYb[h2][0][:, 0:192].rearrange("p (a b) -> p a b", a=3), func=AF.Copy),
                        reads=[pYb[h2][1]], writes=[t_ytm[yb]])
                pS_, t_pS = getbank()
                for h in range(6):
                    c, h2 = h // 2, h % 2
                    orow = slice(64 * h2, 64 * h2 + 64)
                    S.op("pe", lambda E, pS_=pS_, h=h, c=c, h2=h2, orow=orow, q=q: E.matmul(
                        pS_[orow, 64 * c:64 * c + 64], lhsT=tm[c][q][:, 1, 64 * h2:64 * h2 + 64], rhs=Ub[:, 3 * h2 + c, :],
                        start=True, stop=False), reads=[t_tm[c][q], t_Ub], writes=[t_pS])
                    S.op("pe", lambda E, pS_=pS_, h=h, c=c, h2=h2, orow=orow, q=q: E.matmul(
                        pS_[orow, 64 * c:64 * c + 64], lhsT=tm[c][q][:, 2, 64 * h2:64 * h2 + 64],
                        rhs=tm[c][q][:, 3, 64 * h2:64 * h2 + 64], start=False, stop=True),
                        reads=[t_tm[c][q]], writes=[t_pS])
                for c in range(3):
                    S.op("dve", lambda E, pS_=pS_, c=c, q=q: E.scalar_tensor_tensor(
                        out=Sf[:, c, :], in0=Sf[:, c, :], scalar=WCt[:, c, q:q + 1], in1=pS_[:, 64 * c:64 * c + 64],
                        op0=ALU.mult, op1=ALU.add), reads=[t_pS, t_WC, t_Sf], writes=[t_Sf])
                S.op("act", lambda E: E.activation(out=Sb[:], in_=Sf[:], func=AF.Copy), reads=[t_Sf], writes=[t_Sb])
            lb = 512 * bi
            S.dma("sp", ydir[d][lb:lb + 512, :].rearrange("(q p) f -> p q f", p=128), ytm[yb][:], reads=[t_ytm[yb]],
                  writes=[t_out])
    P.close()


LNX_EPS = 64e-5


def phase_rwkv_combine(nc, S, ydir, bon, gfm, lnx_w, lnx_b, mixT, T):
    P = Phase(nc, S)
    NT = T // 128
    ident, identf, t_id = make_ident(nc, S, P)
    t_c = Tok()
    J = P.sb([128, 128], F32)
    S.op("pool", lambda E: E.memset(J[:], 1.0), writes=[t_c])
    S.op("pool", lambda E: E.affine_select(out=J[:], in_=J[:], pattern=[[1, 128]], base=-127, channel_multiplier=1,
                                           compare_op=ALU.is_equal, fill=0.0), reads=[t_c], writes=[t_c])
    lwt = P.sb([128, 384], F32); lbt = P.sb([128, 384], F32)
    S.dma("sp", lwt[:], lnx_w.partition_broadcast(128), writes=[t_c])
    S.dma("sp", lbt[:], lnx_b.partition_broadcast(128), writes=[t_c])
    mh = P.sb([128, 6], F32)
    S.op("pool", lambda E: E.memset(mh[:], -0.5), writes=[t_c])
    NBF = 2
    y0 = [P.sb([128, 384], F32) for _ in range(NBF)]; t_y0 = [Tok() for _ in range(NBF)]
    y1 = [P.sb([128, 384], F32) for _ in range(NBF)]; t_y1 = [Tok() for _ in range(NBF)]
    b0 = [P.sb([128, 3, 128], F32) for _ in range(NBF)]; t_b0 = [Tok() for _ in range(NBF)]
    b1 = [P.sb([128, 3, 128], F32) for _ in range(NBF)]; t_b1 = [Tok() for _ in range(NBF)]
    gt = [P.sb([128, 3, 128], F32) for _ in range(NBF)]; t_gt = [Tok() for _ in range(NBF)]
    ys = [P.sb([128, 6, 64], F32) for _ in range(NBF)]; t_ys = [Tok() for _ in range(NBF)]
    sq = [P.sb([128, 6, 64], F32) for _ in range(NBF)]; t_sq = [Tok() for _ in range(NBF)]
    st = [P.sb([128, 4, 6], F32) for _ in range(NBF)]; t_st = [Tok() for _ in range(NBF)]
    rs = [P.sb([128, 384], F32) for _ in range(NBF)]; t_rs = [Tok() for _ in range(NBF)]
    ob = [P.sb([128, 3, 128], BF16) for _ in range(NBF)]; t_ob = [Tok() for _ in range(NBF)]
    pJ = [P.ps([128, 512], F32) for _ in range(2)]; t_pJ = [Tok(), Tok()]
    pB = [P.ps([128, 512], F32) for _ in range(2)]; t_pB = [Tok(), Tok()]
    pG = [P.ps([128, 512], F32) for _ in range(2)]; t_pG = [Tok(), Tok()]
    pO = [P.ps([128, 512], F32) for _ in range(2)]; t_pO = [Tok(), Tok()]
    t_out = Tok()
    f3 = lambda ap: ap.rearrange("p (a b) -> p a b", a=6)
    for n in range(NT):
        b = n % NBF
        tl = slice(128 * n, 128 * n + 128)
        S.dma("sp", y0[b][:], ydir[0][128 * n:128 * n + 128, :], writes=[t_y0[b]])
        S.dma("sp", y1[b][:], ydir[1][T - 128 * (n + 1):T - 128 * n, :], writes=[t_y1[b]])
        S.dma("sp", b0[b][:], bon[0][:, tl].rearrange("(c p) t -> p c t", p=128), writes=[t_b0[b]])
        S.dma("sp", b1[b][:], bon[1][:, tl].rearrange("(c p) t -> p c t", p=128), writes=[t_b1[b]])
        S.dma("sp", gt[b][:], gfm[:, tl].rearrange("(c p) t -> p c t", p=128), writes=[t_gt[b]])
        S.op("pe", lambda E, b=b: E.matmul(pJ[b][:, 0:384], lhsT=J[:], rhs=y1[b][:], start=True, stop=True),
             reads=[t_c, t_y1[b]], writes=[t_pJ[b]])
        S.op("dve", lambda E, b=b: E.tensor_tensor(out=ys[b][:], in0=f3(pJ[b][:, 0:384]), in1=f3(y0[b][:]), op=ALU.add),
             reads=[t_pJ[b], t_y0[b]], writes=[t_ys[b]])
        S.op("dve", lambda E, b=b: E.tensor_reduce(out=st[b][:, 0, :], in_=ys[b][:], axis=AX.X, op=ALU.add),
             reads=[t_ys[b]], writes=[t_st[b]])
        S.op("dve", lambda E, b=b: E.tensor_scalar(out=st[b][:, 0, :], in0=st[b][:, 0, :], scalar1=1.0 / 64, scalar2=None,
                                                   op0=ALU.mult), reads=[t_st[b]], writes=[t_st[b]])
        S.op("dve", lambda E, b=b: E.tensor_tensor(out=ys[b][:], in0=ys[b][:],
                                                   in1=st[b][:, 0, :].unsqueeze(2).to_broadcast([128, 6, 64]),
                                                   op=ALU.subtract), reads=[t_st[b], t_ys[b]], writes=[t_ys[b]])
        S.op("pool", lambda E, b=b: E.tensor_tensor(out=sq[b][:], in0=ys[b][:], in1=ys[b][:], op=ALU.mult),
             reads=[t_ys[b]], writes=[t_sq[b]])
        S.op("dve", lambda E, b=b: E.tensor_reduce(out=st[b][:, 1, :], in_=sq[b][:], axis=AX.X, op=ALU.add),
             reads=[t_sq[b]], writes=[t_st[b]])
        S.op("dve", lambda E, b=b: E.tensor_scalar(out=st[b][:, 2, :], in0=st[b][:, 1, :], scalar1=1.0 / 64,
                                                   scalar2=LNX_EPS, op0=ALU.mult, op1=ALU.add),
             reads=[t_st[b]], writes=[t_st[b]])
        S.op("pool", lambda E, b=b: E.tensor_tensor(out=st[b][:, 3, :], in0=st[b][:, 2, :], in1=mh[:], op=ALU.pow),
             reads=[t_st[b], t_c], writes=[t_st[b]])
        S.op("dve", lambda E, b=b: E.tensor_tensor(out=ys[b][:], in0=ys[b][:],
                                                   in1=st[b][:, 3, :].unsqueeze(2).to_broadcast([128, 6, 64]),
                                                   op=ALU.mult), reads=[t_st[b], t_ys[b]], writes=[t_ys[b]])
        yf = ys[b][:].rearrange("p a b -> p (a b)")
        S.op("pool", lambda E, b=b, yf=yf: E.tensor_tensor(out=yf, in0=yf, in1=lwt[:], op=ALU.mult),
             reads=[t_ys[b], t_c], writes=[t_ys[b]])
        S.op("pool", lambda E, b=b, yf=yf: E.tensor_tensor(out=yf, in0=yf, in1=lbt[:], op=ALU.add),
             reads=[t_ys[b], t_c], writes=[t_ys[b]])
        S.op("dve", lambda E, b=b: E.tensor_tensor(out=b0[b][:], in0=b0[b][:], in1=b1[b][:], op=ALU.add),
             reads=[t_b0[b], t_b1[b]], writes=[t_b0[b]])
        for c in range(3):
            S.op("pe", lambda E, b=b, c=c: E.transpose(out=pB[b][:, 128 * c:128 * c + 128], in_=b0[b][:, c, :],
                                                       identity=identf[:]),
                 reads=[t_b0[b], t_id], writes=[t_pB[b]])
        for c in range(3):
            S.op("pe", lambda E, b=b, c=c: E.transpose(out=pG[b][:, 128 * c:128 * c + 128], in_=gt[b][:, c, :],
                                                       identity=identf[:]),
                 reads=[t_gt[b], t_id], writes=[t_pG[b]])
        S.op("dve", lambda E, b=b, yf=yf: E.tensor_tensor(out=rs[b][:], in0=pB[b][:, 0:384], in1=yf, op=ALU.add),
             reads=[t_pB[b], t_ys[b]], writes=[t_rs[b]])
        S.op("dve", lambda E, b=b: E.tensor_tensor(out=rs[b][:], in0=pG[b][:, 0:384], in1=rs[b][:], op=ALU.mult),
             reads=[t_pG[b], t_rs[b]], writes=[t_rs[b]])
        for c in range(3):
            S.op("pe", lambda E, b=b, c=c: E.transpose(out=pO[b][:, 128 * c:128 * c + 128],
                                                       in_=rs[b][:, 128 * c:128 * c + 128], identity=identf[:]),
                 reads=[t_rs[b], t_id], writes=[t_pO[b]])
        S.op("act", lambda E, b=b: E.activation(out=ob[b][:], in_=pO[b][:, 0:384].rearrange("p (a b) -> p a b", a=3),
                                                func=AF.Copy), reads=[t_pO[b]], writes=[t_ob[b]])
        S.dma("sp", mixT[0:384, tl].rearrange("(c p) t -> p c t", p=128), ob[b][:], reads=[t_ob[b]], writes=[t_out])
    P.close()


PARAM_SHAPES = {
    "ffn1_norm_g": [2, 1024], "ffn1_w_gate": [2, 1024, 2816], "ffn1_w_up": [2, 1024, 2816],
    "ffn1_w_down": [2, 2816, 1024], "mix_norm_g": [2, 1024], "w_in": [2, 1024, 2816], "w_out": [2, 1024, 1024],
    "rwkv_mu_prev": [2, 1408], "rwkv_mu_next": [2, 1408], "rwkv_decay_w0": [2, 2, 384],
    "rwkv_decay_w2": [2, 2, 64, 384], "rwkv_iclr_a0": [2, 2, 384], "rwkv_iclr_a2": [2, 2, 64, 384],
    "rwkv_gate_w2": [2, 128, 384], "rwkv_k_k": [2, 384], "rwkv_k_a": [2, 384], "rwkv_r_k": [2, 6, 64],
    "rwkv_lnx_w": [2, 384], "rwkv_lnx_b": [2, 384], "s5_a_re": [2, 2, 16, 64], "s5_a_im": [2, 2, 16, 64],
    "s5_log_step": [2, 2, 16], "s5_b_re": [2, 16, 64, 16], "s5_b_im": [2, 16, 64, 16],
    "s5_c_re": [2, 2, 16, 16, 64], "s5_c_im": [2, 2, 16, 16, 64], "s5_d": [2, 256], "s5_glu_w": [2, 256, 512],
    "s5_glu_b": [2, 512], "ffn2_norm_g": [2, 1024], "ffn2_w_gate": [2, 1024, 2816], "ffn2_w_up": [2, 1024, 2816],
    "ffn2_w_down": [2, 2816, 1024], "final_norm_g": [1024],
}
DEPTH = 2


def build_program(T, depth=DEPTH):
    nc = bass.Bass("TRN2", target_bir_lowering=False)
    x = nc.dram_tensor("x", [T, D], F32, kind="ExternalInput").ap()
    p = {k: nc.dram_tensor(k, list(s), F32, kind="ExternalInput").ap() for k, s in PARAM_SHAPES.items()}
    out = nc.dram_tensor("out", [T, D], F32, kind="ExternalOutput").ap()
    h = nc.dram_tensor("h_res", [T, D], F32).ap()
    zr = nc.dram_tensor("z_rwkv", [1408, T], F32).ap()
    qk = nc.dram_tensor("z_qk", [768, T], BF16).ap()
    vtm = nc.dram_tensor("z_v", [T, 384], BF16).ap()
    us5 = nc.dram_tensor("z_s5", [256, T], F32).ap()
    mixT = nc.dram_tensor("mixT", [1024, T], BF16).ap()
    ydir = nc.dram_tensor("y_dir", [2, T, 384], F32).ap()
    bon = nc.dram_tensor("bonus", [2, 384, T], F32).ap()
    gfm = nc.dram_tensor("gate", [384, T], F32).ap()
    S = SchedI(nc)
    NT = T // 128
    tk = [Tok() for _ in range(NT)]
    for l in range(depth):
        src = x if l == 0 else h
        tk2 = [Tok() for _ in range(NT)]
        phase_ffn(nc, S, src, h, p["ffn1_norm_g"][l], p["ffn1_w_gate"][l], p["ffn1_w_up"][l], p["ffn1_w_down"][l],
                  tk, tk2, T)
        tk = tk2
        phase_win(nc, S, h, p["mix_norm_g"][l], p["w_in"][l], zr, qk, vtm, us5, tk, T)
        phase_rwkv(nc, S, zr, ydir, bon, gfm,
                   [p["rwkv_mu_prev"][l], p["rwkv_mu_next"][l], p["rwkv_decay_w0"][l], p["rwkv_decay_w2"][l],
                    p["rwkv_iclr_a0"][l], p["rwkv_iclr_a2"][l], p["rwkv_gate_w2"][l], p["rwkv_k_k"][l],
                    p["rwkv_k_a"][l], p["rwkv_r_k"][l]], T)
        phase_rwkv_combine(nc, S, ydir, bon, gfm, p["rwkv_lnx_w"][l], p["rwkv_lnx_b"][l], mixT, T)
        phase_attn(nc, S, qk, vtm, mixT, T)
        phase_s5(nc, S, us5, mixT,
                 [p["s5_a_re"][l], p["s5_a_im"][l], p["s5_log_step"][l], p["s5_b_re"][l], p["s5_b_im"][l],
                  p["s5_c_re"][l], p["s5_c_im"][l], p["s5_d"][l], p["s5_glu_w"][l], p["s5_glu_b"][l]], T)
        tk2 = [Tok() for _ in range(NT)]
        phase_wout(nc, S, h, h, mixT, p["w_out"][l], tk, tk2, T)
        tk = tk2
        tk2 = [Tok() for _ in range(NT)]
        phase_ffn(nc, S, h, h, p["ffn2_norm_g"][l], p["ffn2_w_gate"][l], p["ffn2_w_up"][l], p["ffn2_w_down"][l],
                  tk, tk2, T)
        tk = tk2
    tko = [Tok() for _ in range(NT)]
    phase_final(nc, S, h, out, p["final_norm_g"], tk, tko, T)
    S.finish()
    return nc, S


def kernel(**inputs):
    x = np.ascontiguousarray(np.asarray(inputs["x"], dtype=np.float32))
    B, T, _ = x.shape
    nc, S = build_program(T)
    params = {k: np.ascontiguousarray(np.asarray(inputs[k], dtype=np.float32)) for k in PARAM_SHAPES}
    in_maps = []
    for b in range(B):
        m = {"x": x[b]}
        m.update(params)
        in_maps.append(m)
    res = run_bass_kernel_spmd(nc, in_maps, core_ids=list(range(B)))
    return np.stack([np.asarray(r["out"], dtype=np.float32) for r in res.results], axis=0)
```

```python
import numpy as np
import concourse.bass as bass
import concourse.mybir as mybir
from concourse.bass_utils import run_bass_kernel_spmd

F32 = mybir.dt.float32
BF16 = mybir.dt.bfloat16
I32 = mybir.dt.int32
AF = mybir.ActivationFunctionType
ALU = mybir.AluOpType
AX = mybir.AxisListType

ENGS = ("pe", "act", "dve", "pool", "sp")


class Tok:
    __slots__ = ("w", "r", "name")

    def __init__(self, name=""):
        self.w = None
        self.r = {}
        self.name = name


class Sched:
    def __init__(self, nc, lanes_sp=8, lanes_pool=6, lanes_act=2, same_engine_sync=True):
        self.nc = nc
        self.ops = {e: [] for e in ENGS}
        self.cnt = {}
        self.sems = {}
        self.seen = {e: {} for e in ENGS}
        self.same = same_engine_sync
        self._ctx = []
        for e in ("pe", "act", "dve", "pool"):
            self._mksem(e)
        self.lanes = {"sp": [], "pool": [], "act": []}
        for q, n in (("sp", lanes_sp), ("pool", lanes_pool), ("act", lanes_act)):
            for i in range(n):
                nm = f"ln_{q}{i}"
                self._mksem(nm)
                self.lanes[q].append(nm)
        self.lane_rr = {"sp": 0, "pool": 0, "act": 0}
        self.n_instr = 0

    def _mksem(self, name):
        cm = self.nc.semaphore(name)
        s = cm.__enter__()
        self._ctx.append(cm)
        self.sems[name] = s
        self.cnt[name] = 0

    def _collect(self, eng, reads, writes):
        need = {}

        def add(src, val):
            if src == eng and (eng == "pe" or not self.same or eng == "sp"):
                return
            if need.get(src, 0) < val:
                need[src] = val
        for t in reads:
            if t.w is not None:
                add(*t.w)
        for t in writes:
            if t.w is not None:
                add(*t.w)
            for s, v in t.r.items():
                add(s, v)
        out = []
        seen = self.seen[eng]
        for s, v in need.items():
            if seen.get(s, 0) < v:
                seen[s] = v
                out.append((self.sems[s], v))
        return out

    def op(self, eng, fn, reads=(), writes=()):
        waits = self._collect(eng, reads, writes)
        self.cnt[eng] += 1
        c = self.cnt[eng]
        sem = self.sems[eng]

        def emit(E, waits=waits, fn=fn, sem=sem):
            for s, v in waits:
                E.wait_ge(s, v)
            fn(E).then_inc(sem, 1)
        self.ops[eng].append(emit)
        for t in reads:
            t.r[eng] = c
        for t in writes:
            t.w = (eng, c)
            t.r = {}
        self.n_instr += 1

    def dma(self, q, out, in_, reads=(), writes=(), **kw):
        lanes = self.lanes[q]
        ln = lanes[self.lane_rr[q] % len(lanes)]
        self.lane_rr[q] += 1
        waits = self._collect(q, reads, writes)
        prev = self.cnt[ln]
        if prev and self.seen[q].get(ln, 0) < prev:
            self.seen[q][ln] = prev
            waits.append((self.sems[ln], prev))
        self.cnt[ln] += 16
        c = self.cnt[ln]
        sem = self.sems[ln]

        def emit(E, waits=waits, sem=sem, out=out, in_=in_, kw=kw):
            for s, v in waits:
                E.wait_ge(s, v)
            E.dma_start(out=out, in_=in_, **kw).then_inc(sem, 16)
        self.ops[q].append(emit)
        for t in reads:
            t.r[ln] = c
        for t in writes:
            t.w = (ln, c)
            t.r = {}
        self.n_instr += 1

    def finish(self, final_toks):
        nc = self.nc
        fin = []
        need = {}
        for t in final_toks:
            if t.w is not None and need.get(t.w[0], 0) < t.w[1]:
                need[t.w[0]] = t.w[1]
        for s, v in self.cnt.items():
            if v and need.get(s, 0) < v:
                need[s] = v
        for s, v in need.items():
            fin.append((self.sems[s], v))
        ops = self.ops
        with nc.Block() as block:
            @block.tensor
            def _(E):
                for f in ops["pe"]:
                    f(E)

            @block.scalar
            def _(E):
                for f in ops["act"]:
                    f(E)

            @block.vector
            def _(E):
                for f in ops["dve"]:
                    f(E)

            @block.gpsimd
            def _(E):
                for f in ops["pool"]:
                    f(E)

            @block.sync
            def _(E):
                for f in ops["sp"]:
                    f(E)
                for s, v in fin:
                    E.wait_ge(s, v)
        for cm in reversed(self._ctx):
            cm.__exit__(None, None, None)


class Alloc:
    def __init__(self, nc):
        self.nc = nc
        self._ctx = []

    def sb(self, name, shape, dt):
        cm = self.nc.sbuf_tensor(name, list(shape), dt)
        t = cm.__enter__()
        self._ctx.append(cm)
        return t

    def ps(self, name, shape, dt):
        cm = self.nc.psum_tensor(name, list(shape), dt)
        t = cm.__enter__()
        self._ctx.append(cm)
        return t

    def close(self):
        for cm in reversed(self._ctx):
            cm.__exit__(None, None, None)


class SchedI(Sched):
    def __init__(self, nc, **kw):
        super().__init__(nc, **kw)
        self.E = {"pe": nc.tensor, "act": nc.scalar, "dve": nc.vector, "pool": nc.gpsimd, "sp": nc.sync}

    limit = 10 ** 9

    def op(self, eng, fn, reads=(), writes=()):
        if self.n_instr >= self.limit:
            return
        waits = self._collect(eng, reads, writes)
        self.cnt[eng] += 1
        c = self.cnt[eng]
        E = self.E[eng]
        for s, v in waits:
            E.wait_ge(s, v)
        fn(E).then_inc(self.sems[eng], 1)
        for t in reads:
            t.r[eng] = c
        for t in writes:
            t.w = (eng, c)
            t.r = {}
        self.n_instr += 1

    def dma(self, q, out, in_, reads=(), writes=(), **kw):
        if self.n_instr >= self.limit:
            return
        lanes = self.lanes[q]
        ln = lanes[self.lane_rr[q] % len(lanes)]
        self.lane_rr[q] += 1
        waits = self._collect(q, reads, writes)
        prev = self.cnt[ln]
        if prev and self.seen[q].get(ln, 0) < prev:
            self.seen[q][ln] = prev
            waits.append((self.sems[ln], prev))
        self.cnt[ln] += 16
        c = self.cnt[ln]
        E = self.E[q]
        for s, v in waits:
            E.wait_ge(s, v)
        E.dma_start(out=out, in_=in_, **kw).then_inc(self.sems[ln], 16)
        for t in reads:
            t.r[ln] = c
        for t in writes:
            t.w = (ln, c)
            t.r = {}
        self.n_instr += 1

    def barrier(self):
        for e in ("pe", "act", "dve", "pool", "sp"):
            E = self.E[e]
            for s, v in self.cnt.items():
                if v and s != e and self.seen[e].get(s, 0) < v:
                    self.seen[e][s] = v
                    E.wait_ge(self.sems[s], v)
                if s == e and v and e != "sp":
                    if self.seen[e].get(s, 0) < v:
                        self.seen[e][s] = v
                        E.wait_ge(self.sems[s], v)

    def finish(self, final_toks=()):
        self.barrier()
        for cm in reversed(self._ctx):
            cm.__exit__(None, None, None)


from contextlib import ExitStack

D = 1024
DFF = 2816
NFF = DFF // 128
KD = D // 128


class Phase:
    _uid = [0]

    def __init__(self, nc, S):
        self.nc, self.S = nc, S
        self.es = ExitStack()
        self.n = 0
        Phase._uid[0] += 1
        self.uid = Phase._uid[0]

    def sb(self, shape, dt, name=None):
        self.n += 1
        return self.es.enter_context(self.nc.sbuf_tensor(name or f"t{self.uid}_{self.n}", list(shape), dt))

    def ps(self, shape, dt, name=None):
        self.n += 1
        return self.es.enter_context(self.nc.psum_tensor(name or f"p{self.uid}_{self.n}", list(shape), dt))

    def close(self):
        self.S.barrier()
        self.es.close()


def make_ident(nc, S, P, dt=BF16):
    identf = P.sb([128, 128], F32)
    ident = P.sb([128, 128], dt)
    t = Tok()
    S.op("pool", lambda E: E.memset(identf[:], 1.0), writes=[t])
    S.op("pool", lambda E: E.affine_select(out=identf[:], in_=identf[:], pattern=[[-1, 128]], base=0,
                                           channel_multiplier=1, compare_op=ALU.is_equal, fill=0.0),
         reads=[t], writes=[t])
    S.op("dve", lambda E: E.tensor_copy(out=ident[:], in_=identf[:]), reads=[t], writes=[t])
    return ident, identf, t


def load_w_bf16(S, dst, src, tok, rows_per=128, col_split=2):
    K = dst.shape[1]
    N = dst.shape[2]
    cs = N // col_split
    for k in range(K):
        for c in range(col_split):
            S.dma("pool", dst[:, k, c * cs:(c + 1) * cs], src[k * 128:(k + 1) * 128, c * cs:(c + 1) * cs],
                  writes=[tok])


def rms_prep(S, P, ht, t_h, s, gt, t_g, xn, t_xn, junk, t_junk, stat, t_stat, mhalf, t_mh):
    S.op("dve", lambda E: E.scalar_tensor_tensor(out=junk[:], in0=ht[:, s, :], scalar=1.0 / D, in1=ht[:, s, :],
                                                 op0=ALU.mult, op1=ALU.mult, accum_out=stat[:, 0:1]),
         reads=[t_h], writes=[t_junk, t_stat])
    S.op("dve", lambda E: E.tensor_scalar(out=stat[:, 1:2], in0=stat[:, 0:1], scalar1=1e-6, scalar2=None,
                                          op0=ALU.add), reads=[t_stat], writes=[t_stat])
    S.op("pool", lambda E: E.tensor_tensor(out=stat[:, 2:3], in0=stat[:, 1:2], in1=mhalf[:, 0:1], op=ALU.pow),
         reads=[t_stat, t_mh], writes=[t_stat])
    S.op("dve", lambda E: E.scalar_tensor_tensor(out=xn[:], in0=ht[:, s, :], scalar=stat[:, 2:3], in1=gt[:],
                                                 op0=ALU.mult, op1=ALU.mult),
         reads=[t_h, t_stat, t_g], writes=[t_xn])


def phase_ffn(nc, S, h_in, h_out, g, wg, wu, wd, toks_in, toks_out, T):
    P = Phase(nc, S)
    NT = T // 128
    NS = 4
    NSUP = NT // NS
    hv_in = h_in.rearrange("(n p) d -> p n d", p=128)
    hv_out = h_out.rearrange("(n p) d -> p n d", p=128)
    ident, _, t_id = make_ident(nc, S, P)
    wg_b = P.sb([128, KD, DFF], BF16); t_wg = Tok()
    wu_b = P.sb([128, KD, DFF], BF16); t_wu = Tok()
    wd_b = P.sb([128, NFF, D], BF16); t_wd = Tok()
    load_w_bf16(S, wg_b, wg, t_wg)
    load_w_bf16(S, wu_b, wu, t_wu)
    load_w_bf16(S, wd_b, wd, t_wd, col_split=1)
    gt = P.sb([128, D], F32); t_g = Tok()
    S.dma("sp", gt[:], g.partition_broadcast(128), writes=[t_g])
    mhalf = P.sb([128, 1], F32); t_mh = Tok()
    S.op("pool", lambda E: E.memset(mhalf[:], -0.5), writes=[t_mh])
    ht = [P.sb([128, NS, D], F32)] * 2; t_ht = [[Tok() for _ in range(NS)]] * 2
    rl = [P.sb([128, 512], F32) for _ in range(4)]; t_rl = [Tok() for _ in range(4)]
    xn = [P.sb([128, D], BF16) for _ in range(2)]; t_xn = [Tok(), Tok()]
    junk = P.sb([128, D], BF16); t_junk = Tok()
    stat = [P.sb([128, 4], F32) for _ in range(2)]; t_stat = [Tok(), Tok()]
    xnT = [P.sb([128, KD, NS * 128], BF16) for _ in range(2)]; t_xnT = [Tok(), Tok()]
    hT = P.sb([128, NFF, NS * 128], BF16); t_hT = [Tok() for _ in range(NFF)]
    sg = [P.sb([128, NS * 128], BF16) for _ in range(2)]; t_sg = [Tok(), Tok()]
    pT = [P.ps([128, KD, 128], BF16) for _ in range(2)]; t_pT = [Tok(), Tok()]
    pG = [P.ps([128, 512], F32) for _ in range(2)]; t_pG = [Tok(), Tok()]
    pU = [P.ps([128, 512], F32) for _ in range(2)]; t_pU = [Tok(), Tok()]
    pD = [P.ps([128, 512], F32) for _ in range(2)]; t_pD = [Tok(), Tok()]
    itc = [0]

    def load(st):
        hb = st % 2
        for s in range(NS):
            n = st * NS + s
            S.dma("sp", ht[hb][:, s, :], hv_in[:, n, :], reads=[toks_in[n]], writes=[t_ht[hb][s]])

    def prep(st):
        hb = st % 2
        for s in range(NS):
            b = s % 2
            rms_prep(S, P, ht[hb], t_ht[hb][s], s, gt, t_g, xn[b], t_xn[b], junk, t_junk, stat[b], t_stat[b], mhalf,
                     t_mh)
            for k in range(KD):
                S.op("pe", lambda E, k=k, b=b: E.transpose(out=pT[b][:, k, :], in_=xn[b][:, k * 128:(k + 1) * 128],
                                                           identity=ident[:]),
                     reads=[t_xn[b], t_id], writes=[t_pT[b]])
            S.op("dve", lambda E, b=b, s=s, hb=hb: E.tensor_copy(out=xnT[hb][:, :, s * 128:(s + 1) * 128], in_=pT[b][:]),
                 reads=[t_pT[b]], writes=[t_xnT[hb]])

    def gateup(st):
        hb = st % 2
        for f in range(NFF):
            b = f % 2
            for k in range(KD):
                S.op("pe", lambda E, k=k, f=f, b=b: E.matmul(pG[b][:], lhsT=wg_b[:, k, f * 128:(f + 1) * 128],
                                                             rhs=xnT[hb][:, k, :], start=(k == 0), stop=(k == KD - 1)),
                     reads=[t_wg, t_xnT[hb]], writes=[t_pG[b]])
            for k in range(KD):
                S.op("pe", lambda E, k=k, f=f, b=b: E.matmul(pU[b][:], lhsT=wu_b[:, k, f * 128:(f + 1) * 128],
                                                             rhs=xnT[hb][:, k, :], start=(k == 0), stop=(k == KD - 1)),
                     reads=[t_wu, t_xnT[hb]], writes=[t_pU[b]])
            S.op("act", lambda E, b=b: E.activation(out=sg[b][:], in_=pG[b][:], func=AF.Silu),
                 reads=[t_pG[b]], writes=[t_sg[b]])
            S.op("dve", lambda E, b=b, f=f: E.tensor_tensor(out=hT[:, f, :], in0=pU[b][:], in1=sg[b][:], op=ALU.mult),
                 reads=[t_pU[b], t_sg[b]], writes=[t_hT[f]])

    def down(st):
        for s in range(NS):
            n = st * NS + s
            for c in range(2):
                b = itc[0] % 2
                r4 = itc[0] % 4
                itc[0] += 1
                S.dma("sp", rl[r4][:], hv_in[:, n, c * 512:(c + 1) * 512], reads=[toks_in[n]], writes=[t_rl[r4]])
                for f in range(NFF):
                    S.op("pe", lambda E, f=f, s=s, c=c, b=b: E.matmul(
                        pD[b][:], lhsT=hT[:, f, s * 128:(s + 1) * 128], rhs=wd_b[:, f, c * 512:(c + 1) * 512],
                        start=(f == 0), stop=(f == NFF - 1)),
                        reads=[t_wd, t_hT[f]], writes=[t_pD[b]])
                S.op("dve", lambda E, b=b, r4=r4: E.scalar_tensor_tensor(
                    out=rl[r4][:], in0=pD[b][:], scalar=0.5, in1=rl[r4][:], op0=ALU.mult, op1=ALU.add),
                    reads=[t_pD[b], t_rl[r4]], writes=[t_rl[r4]])
                S.dma("sp", hv_out[:, n, c * 512:(c + 1) * 512], rl[r4][:], reads=[t_rl[r4]], writes=[toks_out[n]])

    load(0)
    prep(0)
    for st in range(NSUP):
        if st + 1 < NSUP:
            load(st + 1)
        gateup(st)
        if st + 1 < NSUP:
            prep(st + 1)
        down(st)
    P.close()


RWKV_IN = 1408
ATT_Q0 = 1408
ATT_V0 = 2176
S5_0 = 2560
INW = 2816


def phase_win(nc, S, h_in, g, win, zr, qk, vtm, us5, toks_in, T):
    P = Phase(nc, S)
    NT = T // 128
    NS = 4
    NSUP = NT // NS
    hv_in = h_in.rearrange("(n p) d -> p n d", p=128)
    ident, _, t_id = make_ident(nc, S, P)
    w_b = P.sb([128, KD, INW], BF16); t_w = Tok()
    load_w_bf16(S, w_b, win, t_w)
    gt = P.sb([128, D], F32); t_g = Tok()
    S.dma("sp", gt[:], g.partition_broadcast(128), writes=[t_g])
    mhalf = P.sb([128, 1], F32); t_mh = Tok()
    S.op("pool", lambda E: E.memset(mhalf[:], -0.5), writes=[t_mh])
    ht = [P.sb([128, NS, D], F32) for _ in range(2)]; t_ht = [[Tok() for _ in range(NS)] for _ in range(2)]
    xn = [P.sb([128, D], BF16) for _ in range(2)]; t_xn = [Tok(), Tok()]
    junk = P.sb([128, D], BF16); t_junk = Tok()
    stat = [P.sb([128, 4], F32) for _ in range(2)]; t_stat = [Tok(), Tok()]
    xnT = [P.sb([128, KD, NS * 128], BF16) for _ in range(2)]; t_xnT = [Tok(), Tok()]
    stf = [P.sb([128, 512], F32) for _ in range(4)]; t_stf = [Tok() for _ in range(4)]
    stb = [P.sb([128, 512], BF16) for _ in range(4)]; t_stb = [Tok() for _ in range(4)]
    pT = [P.ps([128, KD, 128], BF16) for _ in range(2)]; t_pT = [Tok(), Tok()]
    pZ = [P.ps([128, 512], F32) for _ in range(4)]; t_pZ = [Tok() for _ in range(4)]
    t_out = Tok()
    chunks = []
    for c in range(11):
        chunks.append((c * 128, zr, c * 128, False))
    for c in range(6):
        chunks.append((ATT_Q0 + c * 128, qk, c * 128, True))
    for c in range(2):
        chunks.append((S5_0 + c * 128, us5, c * 128, False))
    cnt = {"it": 0, "ib": 0, "iff": 0}

    def load(st):
        hb = st % 2
        for s in range(NS):
            n = st * NS + s
            S.dma("sp", ht[hb][:, s, :], hv_in[:, n, :], reads=[toks_in[n]], writes=[t_ht[hb][s]])

    def prep(st):
        hb = st % 2
        for s in range(NS):
            b = s % 2
            rms_prep(S, P, ht[hb], t_ht[hb][s], s, gt, t_g, xn[b], t_xn[b], junk, t_junk, stat[b], t_stat[b], mhalf, t_mh)
            for k in range(KD):
                S.op("pe", lambda E, k=k, b=b: E.transpose(out=pT[b][:, k, :], in_=xn[b][:, k * 128:(k + 1) * 128],
                                                           identity=ident[:]),
                     reads=[t_xn[b], t_id], writes=[t_pT[b]])
            S.op("dve", lambda E, b=b, s=s, hb=hb: E.tensor_copy(out=xnT[hb][:, :, s * 128:(s + 1) * 128], in_=pT[b][:]),
                 reads=[t_pT[b]], writes=[t_xnT[hb]])

    def fm_chunks(st, chs):
        hb = st % 2
        tsl = slice(st * 512, (st + 1) * 512)
        for (c0, dst, r0, isb) in chs:
            pb = cnt["it"] % 4
            cnt["it"] += 1
            for k in range(KD):
                S.op("pe", lambda E, k=k, c0=c0, pb=pb, hb=hb: E.matmul(
                    pZ[pb][:], lhsT=w_b[:, k, c0:c0 + 128], rhs=xnT[hb][:, k, :], start=(k == 0), stop=(k == KD - 1)),
                    reads=[t_w, t_xnT[hb]], writes=[t_pZ[pb]])
            if isb:
                sb_ = cnt["ib"] % 4
                cnt["ib"] += 1
                S.op("act", lambda E, pb=pb, sb_=sb_: E.activation(out=stb[sb_][:], in_=pZ[pb][:], func=AF.Copy),
                     reads=[t_pZ[pb]], writes=[t_stb[sb_]])
                S.dma("sp", dst[r0:r0 + 128, tsl], stb[sb_][:], reads=[t_stb[sb_]], writes=[t_out])
            else:
                sf = cnt["iff"] % 4
                cnt["iff"] += 1
                if cnt["iff"] % 2:
                    S.op("act", lambda E, pb=pb, sf=sf: E.activation(out=stf[sf][:], in_=pZ[pb][:], func=AF.Copy),
                         reads=[t_pZ[pb]], writes=[t_stf[sf]])
                else:
                    S.op("dve", lambda E, pb=pb, sf=sf: E.tensor_copy(out=stf[sf][:], in_=pZ[pb][:]),
                         reads=[t_pZ[pb]], writes=[t_stf[sf]])
                S.dma("sp", dst[r0:r0 + 128, tsl], stf[sf][:], reads=[t_stf[sf]], writes=[t_out])

    def v_chunks(st):
        hb = st % 2
        for s in range(NS):
            n = st * NS + s
            pb = cnt["it"] % 4
            cnt["it"] += 1
            for k in range(KD):
                S.op("pe", lambda E, k=k, s=s, pb=pb, hb=hb: E.matmul(
                    pZ[pb][:, 0:384], lhsT=xnT[hb][:, k, s * 128:(s + 1) * 128], rhs=w_b[:, k, ATT_V0:ATT_V0 + 384],
                    start=(k == 0), stop=(k == KD - 1)),
                    reads=[t_w, t_xnT[hb]], writes=[t_pZ[pb]])
            sb_ = cnt["ib"] % 4
            cnt["ib"] += 1
            S.op("dve", lambda E, pb=pb, sb_=sb_: E.tensor_copy(out=stb[sb_][:, 0:384], in_=pZ[pb][:, 0:384]),
                 reads=[t_pZ[pb]], writes=[t_stb[sb_]])
            S.dma("sp", vtm[n * 128:(n + 1) * 128, :], stb[sb_][:, 0:384], reads=[t_stb[sb_]], writes=[t_out])

    load(0)
    prep(0)
    for st in range(NSUP):
        if st + 1 < NSUP:
            load(st + 1)
        fm_chunks(st, chunks[:10])
        if st + 1 < NSUP:
            prep(st + 1)
        fm_chunks(st, chunks[10:])
        v_chunks(st)
    P.close()


def phase_wout(nc, S, h_in, h_out, mixT, wout, toks_in, toks_out, T):
    P = Phase(nc, S)
    NT = T // 128
    NS = 4
    NSUP = NT // NS
    hv_in = h_in.rearrange("(n p) d -> p n d", p=128)
    hv_out = h_out.rearrange("(n p) d -> p n d", p=128)
    mv = mixT.rearrange("(k p) t -> p k t", p=128)
    w_b = P.sb([128, KD, D], BF16); t_w = Tok()
    load_w_bf16(S, w_b, wout, t_w, col_split=1)
    ht = [P.sb([128, NS, D], F32) for _ in range(2)]; t_ht = [[Tok() for _ in range(NS)] for _ in range(2)]
    ml = [P.sb([128, KD, 512], BF16) for _ in range(2)]; t_ml = [Tok(), Tok()]
    pD = [P.ps([128, 512], F32) for _ in range(4)]; t_pD = [Tok() for _ in range(4)]
    it = 0
    for st in range(NSUP):
        hb = st % 2
        S.dma("sp", ml[hb][:], mv[:, :, st * 512:(st + 1) * 512], writes=[t_ml[hb]])
        for s in range(NS):
            n = st * NS + s
            S.dma("sp", ht[hb][:, s, :], hv_in[:, n, :], reads=[toks_in[n]], writes=[t_ht[hb][s]])
        for s in range(NS):
            n = st * NS + s
            for c in range(2):
                b = it % 4
                it += 1
                for k in range(KD):
                    S.op("pe", lambda E, k=k, s=s, c=c, b=b, hb=hb: E.matmul(
                        pD[b][:], lhsT=ml[hb][:, k, s * 128:(s + 1) * 128], rhs=w_b[:, k, c * 512:(c + 1) * 512],
                        start=(k == 0), stop=(k == KD - 1)),
                        reads=[t_w, t_ml[hb]], writes=[t_pD[b]])
                S.op("dve", lambda E, s=s, c=c, b=b, hb=hb: E.tensor_tensor(
                    out=ht[hb][:, s, c * 512:(c + 1) * 512], in0=pD[b][:], in1=ht[hb][:, s, c * 512:(c + 1) * 512],
                    op=ALU.add), reads=[t_pD[b]], writes=[t_ht[hb][s]])
            S.dma("sp", hv_out[:, n, :], ht[hb][:, s, :], reads=[t_ht[hb][s]], writes=[toks_out[n]])
    P.close()


def phase_final(nc, S, h_in, out, g, toks_in, toks_out, T):
    P = Phase(nc, S)
    NT = T // 128
    hv_in = h_in.rearrange("(n p) d -> p n d", p=128)
    hv_out = out.rearrange("(n p) d -> p n d", p=128)
    gt = P.sb([128, D], F32); t_g = Tok()
    S.dma("sp", gt[:], g.partition_broadcast(128), writes=[t_g])
    mhalf = P.sb([128, 1], F32); t_mh = Tok()
    S.op("pool", lambda E: E.memset(mhalf[:], -0.5), writes=[t_mh])
    ht = [P.sb([128, 1, D], F32) for _ in range(4)]; t_ht = [Tok() for _ in range(4)]
    xo = [P.sb([128, D], F32) for _ in range(4)]; t_xo = [Tok() for _ in range(4)]
    junk = P.sb([128, D], BF16); t_junk = Tok()
    stat = [P.sb([128, 4], F32) for _ in range(4)]; t_stat = [Tok() for _ in range(4)]
    for n in range(NT):
        b = n % 4
        S.dma("sp", ht[b][:, 0, :], hv_in[:, n, :], reads=[toks_in[n]], writes=[t_ht[b]])
        rms_prep(S, P, ht[b], t_ht[b], 0, gt, t_g, xo[b], t_xo[b], junk, t_junk, stat[b], t_stat[b], mhalf, t_mh)
        S.dma("pool", hv_out[:, n, :], xo[b][:], reads=[t_xo[b]], writes=[toks_out[n]])
    P.close()


ALIBI = [0.25, 0.0625, 0.015625, 0.00390625, 0.5, 0.125]
DILS = [1, 4, 16]
NEG = -1.0e30


def phase_attn(nc, S, qk, vtm, mixT, T):
    P = Phase(nc, S)
    NB1 = T // 128
    dfi = P.sb([128, 128], I32)
    dff = P.sb([128, 128], F32)
    Dk = P.sb([128, 3, 128], F32)
    Mk = P.sb([128, 3, 128], F32)
    t_c = Tok()
    S.op("pool", lambda E: E.iota(dfi[:], pattern=[[1, 128]], base=0, channel_multiplier=-1), writes=[t_c])
    S.op("dve", lambda E: E.tensor_copy(out=dff[:], in_=dfi[:]), reads=[t_c], writes=[t_c])
    S.op("dve", lambda E: E.tensor_scalar(out=Dk[:, 0, :], in0=dff[:], scalar1=128.0, scalar2=None, op0=ALU.add),
         reads=[t_c], writes=[t_c])
    S.op("dve", lambda E: E.tensor_scalar(out=Dk[:, 1, :], in0=dff[:], scalar1=-1.0, scalar2=None, op0=ALU.mult),
         reads=[t_c], writes=[t_c])
    S.op("dve", lambda E: E.tensor_tensor(out=Dk[:, 1, :], in0=Dk[:, 1, :], in1=dff[:], op=ALU.max),
         reads=[t_c], writes=[t_c])
    S.op("dve", lambda E: E.tensor_scalar(out=Dk[:, 2, :], in0=dff[:], scalar1=-1.0, scalar2=128.0, op0=ALU.mult,
                                          op1=ALU.add), reads=[t_c], writes=[t_c])
    S.op("dve", lambda E: E.tensor_scalar(out=Mk[:], in0=Dk[:], scalar1=64.0, scalar2=NEG, op0=ALU.is_gt,
                                          op1=ALU.mult), reads=[t_c], writes=[t_c])
    sel = P.sb([65, 64], F32)
    S.op("dve", lambda E: E.memset(sel[:], 0.0), writes=[t_c])
    S.op("dve", lambda E: E.memset(sel[64:65, :], 1.0), reads=[t_c], writes=[t_c])

    qT = P.sb([128, T], BF16); t_q = Tok()
    kT = P.sb([128, T], BF16); t_k = Tok()
    vt = [P.sb([128, NB1, 2, 65], BF16) for _ in range(3)]; t_v = [Tok() for _ in range(3)]
    acc = P.sb([65, 2, T], F32)
    t_acc = [[Tok() for _ in range(NB1)] for _ in range(2)]
    biasT = P.sb([128, 2, 3, 3, 128], F32); t_b = Tok()
    NBF = 4
    sc = [P.sb([128, 3, 128], F32) for _ in range(NBF)]; t_sc = [Tok() for _ in range(NBF)]
    pr = [P.sb([128, 3, 128], BF16) for _ in range(NBF)]; t_pr = [Tok() for _ in range(NBF)]
    rec = [P.sb([64, 512], F32) for _ in range(2)]; t_rec = [Tok(), Tok()]
    ob = [P.sb([64, 512], BF16) for _ in range(2)]; t_ob = [Tok(), Tok()]
    pS = [P.ps([128, 3, 128], F32) for _ in range(NBF)]; t_pS = [Tok() for _ in range(NBF)]
    pOb = [P.ps([128, 512], F32) for _ in range(NBF)]; t_pO = [Tok() for _ in range(NBF)]
    pO = [x[0:65, 0:128] for x in pOb]
    pB = [x[0:64, 0:512] for x in pOb]; t_pB = t_pO
    t_out = Tok()
    it = 0
    for hp in range(3):
        S.dma("sp", qT[:], qk[128 * hp:128 * hp + 128, :], writes=[t_q])
        S.dma("sp", kT[:], qk[384 + 128 * hp:384 + 128 * hp + 128, :], writes=[t_k])
        for pi, d in enumerate(DILS):
            S.op("pool", lambda E, pi=pi: E.memset(vt[pi][:], 1.0), writes=[t_v[pi]])
            nm = T // d // 128
            vv = vtm.rearrange("(m j r) (h c) -> j r m h c", j=128, r=d, c=64)
            for r in range(d):
                for h2 in range(2):
                    S.dma("sp", vt[pi][:, r * nm:(r + 1) * nm, h2, 0:64], vv[:, r, :, 2 * hp + h2, :],
                          writes=[t_v[pi]])
        for h2 in range(2):
            for pi, d in enumerate(DILS):
                sl = -ALIBI[2 * hp + h2] * d
                S.op("dve", lambda E, h2=h2, pi=pi, sl=sl: E.scalar_tensor_tensor(
                    out=biasT[:, h2, pi, :, :], in0=Dk[:], scalar=sl, in1=Mk[:], op0=ALU.mult, op1=ALU.add),
                    reads=[t_c], writes=[t_b])
        for h2 in range(2):
            rows = slice(64 * h2, 64 * h2 + 64)
            stages = []
            for pi, d in enumerate(DILS):
                nb = T // d // 128
                for r in range(d):
                    for b in range(nb):
                        bi = it % NBF
                        it += 1
                        kts = [kt for kt in (b - 1, b, b + 1) if 0 <= kt < nb]
                        k0 = kts[0] - (b - 1)
                        nk = len(kts)
                        qs = slice(r + d * 128 * b, r + d * 128 * b + d * 127 + 1, d)
                        blks = sorted(set(range((r + d * 128 * b) // 128, (r + d * 128 * (b + 1) - d) // 128 + 1)))

                        def stA(bi=bi, kts=kts, k0=k0, nk=nk, qs=qs, r=r, d=d, pi=pi, rows=rows, h2=h2):
                            for ki, kt in enumerate(kts):
                                ks = slice(r + d * 128 * kt, r + d * 128 * kt + d * 127 + 1, d)
                                S.op("pe", lambda E, ki=ki, ks=ks: E.matmul(
                                    pS[bi][:, ki, :], lhsT=kT[rows, ks], rhs=qT[rows, qs], start=True, stop=True),
                                    reads=[t_q, t_k], writes=[t_pS[bi]])
                            S.op("dve", lambda E: E.scalar_tensor_tensor(
                                out=sc[bi][:, 0:nk, :], in0=pS[bi][:, 0:nk, :], scalar=0.125,
                                in1=biasT[:, h2, pi, k0:k0 + nk, :], op0=ALU.mult, op1=ALU.add),
                                reads=[t_pS[bi], t_b], writes=[t_sc[bi]])
                            S.op("act", lambda E: E.activation(out=pr[bi][:, 0:nk, :], in_=sc[bi][:, 0:nk, :],
                                                               func=AF.Exp),
                                 reads=[t_sc[bi]], writes=[t_pr[bi]])

                        def stB(bi=bi, kts=kts, nk=nk, qs=qs, r=r, nb=nb, pi=pi, h2=h2, blks=blks):
                            for ki, kt in enumerate(kts):
                                S.op("pe", lambda E, ki=ki, kt=kt: E.matmul(
                                    pO[bi], lhsT=vt[pi][:, r * nb + kt, h2, :], rhs=pr[bi][:, ki, :],
                                    start=(ki == 0), stop=(ki == nk - 1)),
                                    reads=[t_v[pi], t_pr[bi]], writes=[t_pO[bi]])
                            at = [t_acc[h2][x] for x in blks]
                            if pi == 0:
                                S.op("act", lambda E: E.activation(out=acc[:, h2, qs], in_=pO[bi], func=AF.Copy),
                                     reads=[t_pO[bi]], writes=at)
                            else:
                                S.op("dve", lambda E: E.tensor_tensor(out=acc[:, h2, qs], in0=pO[bi],
                                                                      in1=acc[:, h2, qs], op=ALU.add),
                                     reads=[t_pO[bi]], writes=at)
                        stages.append((stA, stB))
            SKEW = NBF - 1
            for i in range(len(stages) + SKEW):
                if i < len(stages):
                    stages[i][0]()
                if i - SKEW >= 0:
                    stages[i - SKEW][1]()
            head = 2 * hp + h2
            for c in range(T // 512):
                bi = c % 2
                cs = slice(c * 512, (c + 1) * 512)
                at = t_acc[h2][4 * c:4 * c + 4]
                S.op("pe", lambda E, bi=bi, h2=h2, cs=cs: E.matmul(pB[bi], lhsT=sel[:], rhs=acc[:, h2, cs],
                                                                   start=True, stop=True),
                     reads=at + [t_c], writes=[t_pB[bi]])
                S.op("dve", lambda E, bi=bi: E.reciprocal(out=rec[bi][:], in_=pB[bi]),
                     reads=[t_pB[bi]], writes=[t_rec[bi]])
                S.op("dve", lambda E, bi=bi, h2=h2, cs=cs: E.tensor_tensor(out=ob[bi][:], in0=acc[0:64, h2, cs],
                                                                          in1=rec[bi][:], op=ALU.mult),
                     reads=at + [t_rec[bi]], writes=[t_ob[bi]])
                S.dma("sp", mixT[384 + 64 * head:384 + 64 * head + 64, cs], ob[bi][:], reads=[t_ob[bi]],
                      writes=[t_out])
    P.close()


def rsl(lo, hi):
    return slice(hi - 1, (lo - 1) if lo > 0 else None, -1)


TWO_PI = 6.283185307179586
MAGIC = 12582912.0


def phase_s5(nc, S, us5, mixT, prm, T, C=256):
    P = Phase(nc, S)
    NCH = T // C
    NCMB = 16
    a_re, a_im, lstep, b_re, b_im, c_re, c_im, dsk, glu_w, glu_b = prm
    ident, identf, t_id = make_ident(nc, S, P)
    t_s = Tok()

    def dv(fn, eng="dve"):
        S.op(eng, fn, reads=[t_s, t_id], writes=[t_s])

    prs = P.sb([128, 40, 16], F32)
    cosT = P.sb([128, NCMB, C], F32)
    sinT = P.sb([128, NCMB, C], F32)
    BT = P.sb([128, NCMB, 2, 128], BF16)
    CT = P.sb([128, NCMB, 2, 128], BF16)
    gw = P.sb([128, 2, 512], BF16)
    gb = P.sb([128, 4], F32)
    dk = P.sb([128, 2], F32)
    ub = P.sb([128, 2, T], BF16); t_ub = Tok()
    ybwd = P.sb([128, 2, T], F32); t_yb = [Tok() for _ in range(NCH)]
    gi = P.sb([128, 2, NCMB], F32); t_gi = [Tok() for _ in range(NCMB)]
    banks = [P.ps([128, 512], F32) for _ in range(6)]
    P2 = Phase(nc, S)
    stg = P2.sb([16, 3, 128], F32)
    lst = P2.sb([16, 2], F32)
    S.dma("sp", stg[:, 0, :], a_re.rearrange("d (j g) p -> (d j) (g p)", g=2), writes=[t_s])
    S.dma("sp", stg[:, 1, :], a_im.rearrange("d (j g) p -> (d j) (g p)", g=2), writes=[t_s])
    S.dma("sp", lst[:], lstep.rearrange("d (j g) -> (d j) g", g=2), writes=[t_s])
    for g2 in range(2):
        dv(lambda E, g2=g2: E.tensor_copy(out=stg[:, 2, 64 * g2:64 * g2 + 64],
                                          in_=lst[:, g2:g2 + 1].to_broadcast([16, 64])))
    pst = banks[0][:, 0:48].rearrange("p (a b) -> p a b", a=3)
    for i in range(3):
        S.op("pe", lambda E, i=i: E.transpose(out=pst[:, i, :], in_=stg[:, i, :], identity=identf[0:16, 0:16]),
             reads=[t_s, t_id], writes=[t_s])
    nm = {}

    def V(name):
        if name not in nm:
            nm[name] = len(nm)
        return prs[:, nm[name], :]
    dv(lambda E: E.tensor_copy(out=prs[:, 0:3, :], in_=pst))
    nm.update({"are": 0, "aim": 1, "lst": 2})
    S.op("act", lambda E: E.activation(out=V("step"), in_=V("lst"), func=AF.Exp), reads=[t_s], writes=[t_s])
    dv(lambda E: E.tensor_tensor(out=V("ar"), in0=V("are"), in1=V("step"), op=ALU.mult))
    dv(lambda E: E.tensor_tensor(out=V("th"), in0=V("aim"), in1=V("step"), op=ALU.mult))
    S.op("act", lambda E: E.activation(out=V("rho"), in_=V("ar"), func=AF.Exp), reads=[t_s], writes=[t_s])

    def sin_of(dst, src, shift):
        dv(lambda E: E.tensor_scalar(out=V("k1"), in0=V(src), scalar1=1.0 / TWO_PI, scalar2=shift / TWO_PI,
                                     op0=ALU.mult, op1=ALU.add))
        dv(lambda E: E.tensor_scalar(out=V("k2"), in0=V("k1"), scalar1=MAGIC, scalar2=None, op0=ALU.add))
        dv(lambda E: E.tensor_scalar(out=V("k3"), in0=V("k2"), scalar1=-MAGIC, scalar2=None, op0=ALU.add))
        dv(lambda E: E.tensor_tensor(out=V("k1"), in0=V("k1"), in1=V("k3"), op=ALU.subtract))
        S.op("act", lambda E: E.activation(out=V(dst), in_=V("k1"), func=AF.Sin, scale=TWO_PI),
             reads=[t_s], writes=[t_s])
    sin_of("sn", "th", 0.0)
    sin_of("cs", "th", TWO_PI / 4)
    dv(lambda E: E.tensor_tensor(out=V("lr"), in0=V("rho"), in1=V("cs"), op=ALU.mult))
    dv(lambda E: E.tensor_tensor(out=V("li"), in0=V("rho"), in1=V("sn"), op=ALU.mult))
    dv(lambda E: E.tensor_scalar(out=V("nr"), in0=V("lr"), scalar1=-1.0, scalar2=None, op0=ALU.add))
    dv(lambda E: E.tensor_tensor(out=V("d1"), in0=V("are"), in1=V("are"), op=ALU.mult))
    dv(lambda E: E.tensor_tensor(out=V("d2"), in0=V("aim"), in1=V("aim"), op=ALU.mult))
    dv(lambda E: E.tensor_tensor(out=V("d1"), in0=V("d1"), in1=V("d2"), op=ALU.add))
    dv(lambda E: E.reciprocal(out=V("rd"), in_=V("d1")))
    dv(lambda E: E.tensor_tensor(out=V("z1"), in0=V("nr"), in1=V("are"), op=ALU.mult))
    dv(lambda E: E.tensor_tensor(out=V("z2"), in0=V("li"), in1=V("aim"), op=ALU.mult))
    dv(lambda E: E.tensor_tensor(out=V("z1"), in0=V("z1"), in1=V("z2"), op=ALU.add))
    dv(lambda E: E.tensor_tensor(out=V("zr"), in0=V("z1"), in1=V("rd"), op=ALU.mult))
    dv(lambda E: E.tensor_tensor(out=V("z1"), in0=V("li"), in1=V("are"), op=ALU.mult))
    dv(lambda E: E.tensor_tensor(out=V("z2"), in0=V("nr"), in1=V("aim"), op=ALU.mult))
    dv(lambda E: E.tensor_tensor(out=V("z1"), in0=V("z1"), in1=V("z2"), op=ALU.subtract))
    dv(lambda E: E.tensor_tensor(out=V("zi"), in0=V("z1"), in1=V("rd"), op=ALU.mult))

    tmpA = P2.sb([128, NCMB, C // 2], F32)
    tmpB = P2.sb([128, NCMB, C // 2], F32)
    dv(lambda E: E.memset(cosT[:, :, 0:1], 1.0))
    dv(lambda E: E.memset(sinT[:, :, 0:1], 0.0))
    dv(lambda E: E.tensor_copy(out=V("wr"), in_=V("cs")))
    dv(lambda E: E.tensor_copy(out=V("wi"), in_=V("sn")))
    L = 1
    while L < C:
        wrb = V("wr").unsqueeze(2).to_broadcast([128, NCMB, L])
        wib = V("wi").unsqueeze(2).to_broadcast([128, NCMB, L])
        dv(lambda E, L=L, wrb=wrb: E.tensor_tensor(out=tmpA[:, :, 0:L], in0=cosT[:, :, 0:L], in1=wrb, op=ALU.mult))
        dv(lambda E, L=L, wib=wib: E.tensor_tensor(out=tmpB[:, :, 0:L], in0=sinT[:, :, 0:L], in1=wib, op=ALU.mult))
        dv(lambda E, L=L: E.tensor_tensor(out=cosT[:, :, L:2 * L], in0=tmpA[:, :, 0:L], in1=tmpB[:, :, 0:L],
                                          op=ALU.subtract))
        dv(lambda E, L=L, wib=wib: E.tensor_tensor(out=tmpA[:, :, 0:L], in0=cosT[:, :, 0:L], in1=wib, op=ALU.mult))
        dv(lambda E, L=L, wrb=wrb: E.tensor_tensor(out=tmpB[:, :, 0:L], in0=sinT[:, :, 0:L], in1=wrb, op=ALU.mult))
        dv(lambda E, L=L: E.tensor_tensor(out=sinT[:, :, L:2 * L], in0=tmpA[:, :, 0:L], in1=tmpB[:, :, 0:L],
                                          op=ALU.add))
        dv(lambda E: E.tensor_tensor(out=V("q1"), in0=V("wr"), in1=V("wr"), op=ALU.mult))
        dv(lambda E: E.tensor_tensor(out=V("q2"), in0=V("wi"), in1=V("wi"), op=ALU.mult))
        dv(lambda E: E.tensor_tensor(out=V("q3"), in0=V("wr"), in1=V("wi"), op=ALU.mult))
        dv(lambda E: E.tensor_tensor(out=V("wr"), in0=V("q1"), in1=V("q2"), op=ALU.subtract))
        dv(lambda E: E.tensor_scalar(out=V("wi"), in0=V("q3"), scalar1=2.0, scalar2=None, op0=ALU.mult))
        L *= 2

    bst = P2.sb([128, 2, 8, 16], F32)
    S.dma("sp", bst[:, 0, :, :], b_re.rearrange("(j g) p c -> (g p) j c", g=2), writes=[t_s])
    S.dma("sp", bst[:, 1, :, :], b_im.rearrange("(j g) p c -> (g p) j c", g=2), writes=[t_s])
    bexp = P2.sb([128, 2, 128], F32)
    btmp = P2.sb([128, 16], F32)
    pX = [banks[1][:, 0:128], banks[2][:, 0:128]]
    for d in range(2):
        for j in range(8):
            cmb = d * 8 + j
            jj = j % 4
            dv(lambda E: E.memset(bexp[:], 0.0))
            for g2 in range(2):
                ps_ = slice(64 * g2, 64 * g2 + 64)
                cs_ = slice(32 * jj + 16 * g2, 32 * jj + 16 * g2 + 16)
                zr_ = prs[ps_, nm["zr"], cmb:cmb + 1]
                zi_ = prs[ps_, nm["zi"], cmb:cmb + 1]
                dv(lambda E, ps_=ps_, zi_=zi_, j=j: E.tensor_scalar(out=btmp[ps_, :], in0=bst[ps_, 1, j, :], scalar1=zi_,
                                                                    scalar2=None, op0=ALU.mult))
                dv(lambda E, ps_=ps_, cs_=cs_, zr_=zr_, j=j: E.scalar_tensor_tensor(
                    out=bexp[ps_, 0, cs_], in0=bst[ps_, 0, j, :], scalar=zr_, in1=btmp[ps_, :], op0=ALU.mult,
                    op1=ALU.subtract))
                dv(lambda E, ps_=ps_, zr_=zr_, j=j: E.tensor_scalar(out=btmp[ps_, :], in0=bst[ps_, 1, j, :], scalar1=zr_,
                                                                    scalar2=None, op0=ALU.mult))
                dv(lambda E, ps_=ps_, cs_=cs_, zi_=zi_, j=j: E.scalar_tensor_tensor(
                    out=bexp[ps_, 1, cs_], in0=bst[ps_, 0, j, :], scalar=zi_, in1=btmp[ps_, :], op0=ALU.mult,
                    op1=ALU.add))
            for ri in range(2):
                S.op("pe", lambda E, ri=ri: E.transpose(out=pX[ri], in_=bexp[:, ri, :], identity=identf[:]),
                     reads=[t_s, t_id], writes=[t_s])
                dv(lambda E, ri=ri, cmb=cmb: E.tensor_copy(out=BT[:, cmb, ri, :], in_=pX[ri]))
    cnat = P2.sb([128, 2, 2, 2, 64], F32)
    for d in range(2):
        for ri, cc in enumerate((c_re, c_im)):
            for ut in range(2):
                S.dma("sp", cnat[:, d, ri, ut, :], cc[d].rearrange("g c p -> (g c) p")[128 * ut:128 * ut + 128, :],
                      writes=[t_s])
    mki = P2.sb([128, 4, 2], I32)
    mk = P2.sb([128, 4, 2], F32)
    mk2 = P2.sb([128, 4, 2], F32)
    S.op("pool", lambda E: E.iota(mki[:], pattern=[[-32, 4], [-16, 2]], base=0, channel_multiplier=1),
         reads=[t_s], writes=[t_s])
    dv(lambda E: E.tensor_copy(out=mk[:], in_=mki[:]))
    dv(lambda E: E.tensor_scalar(out=mk2[:], in0=mk[:], scalar1=0.0, scalar2=None, op0=ALU.is_ge))
    dv(lambda E: E.tensor_scalar(out=mk[:], in0=mk[:], scalar1=15.0, scalar2=None, op0=ALU.is_le))
    dv(lambda E: E.tensor_tensor(out=mk[:], in0=mk[:], in1=mk2[:], op=ALU.mult))
    cx = P2.sb([128, 2, 64], F32)
    for d in range(2):
        for j in range(8):
            cmb = d * 8 + j
            jj = j % 4
            ut = j // 4
            for ri in range(2):
                for g2 in range(2):
                    dv(lambda E, d=d, ri=ri, ut=ut, jj=jj, g2=g2: E.tensor_scalar(
                        out=cx[:, g2, :], in0=cnat[:, d, ri, ut, :], scalar1=mk[:, jj, g2:g2 + 1], scalar2=None,
                        op0=ALU.mult))
                S.op("pe", lambda E, ri=ri: E.transpose(out=pX[ri], in_=cx[:].rearrange("p a b -> p (a b)"),
                                                        identity=identf[:]),
                     reads=[t_s, t_id], writes=[t_s])
                sgn = 1.0 if ri == 0 else -1.0
                dv(lambda E, ri=ri, cmb=cmb, sgn=sgn: E.tensor_scalar(out=CT[:, cmb, ri, :], in0=pX[ri], scalar1=sgn,
                                                                      scalar2=None, op0=ALU.mult))
    load_w_bf16(S, gw, glu_w, t_s, col_split=1)
    S.dma("sp", gb[:], glu_b.rearrange("(o p) -> p o", p=128), writes=[t_s], allow_slow_non_contiguous=True)
    S.dma("sp", dk[:], dsk.rearrange("(o p) -> p o", p=128), writes=[t_s], allow_slow_non_contiguous=True)

    for ut in range(2):
        for c4 in range(T // 2048 if T >= 2048 else 1):
            w_ = min(2048, T)
            S.dma("pool", ub[:, ut, c4 * w_:(c4 + 1) * w_], us5[128 * ut:128 * ut + 128, c4 * w_:(c4 + 1) * w_],
                  writes=[t_ub])
    S.op("dve", lambda E: E.memset(gi[:], 0.0), writes=t_gi)
    NB = 3
    P2.close()
    pBU = [banks[i][:, 0:2 * C].rearrange("p (a b) -> p a b", a=2) for i in range(2)]; t_pBU = [Tok() for _ in range(2)]
    m1 = [P.sb([128, 4, C], F32) for _ in range(NB)]; t_m1 = [Tok() for _ in range(NB)]
    gin = [P.sb([128, 2, C], F32) for _ in range(NB)]; t_gin = [Tok() for _ in range(NB)]
    gg = [P.sb([128, 2, C], F32) for _ in range(NB)]; t_gg = [Tok() for _ in range(NB)]
    m2 = [P.sb([128, 4, C], F32) for _ in range(NB)]; t_m2 = [Tok() for _ in range(NB)]
    ctmp = [P.sb([128, 2], F32) for _ in range(NB)]; t_ct = [Tok() for _ in range(NB)]
    hh = [P.sb([128, 4, 2, C], BF16) for _ in range(2)]; t_hh = [[Tok() for _ in range(4)] for _ in range(2)]
    pY = [banks[2 + i][:, 0:C] for i in range(2)]; t_pY = [Tok(), Tok()]
    uf = [P.sb([128, 2, C], F32) for _ in range(2)]; t_uf = [Tok(), Tok()]
    yv = [P.sb([128, C], F32) for _ in range(2)]; t_yv = [Tok(), Tok()]
    y2 = [P.sb([128, C], F32) for _ in range(2)]; t_y2 = [Tok(), Tok()]
    ygl = [P.sb([128, 2, C], BF16) for _ in range(2)]; t_yg = [[Tok(), Tok()] for _ in range(2)]
    pZ = [banks[4 + i][:, 0:C] for i in range(2)]; t_pZ = [Tok(), Tok()]
    sg = [P.sb([128, C], F32) for _ in range(2)]; t_sg = [Tok(), Tok()]
    oo = [P.sb([128, C], BF16) for _ in range(2)]; t_oo = [Tok(), Tok()]
    t_out = Tok()
    iy = [0]
    items = []
    for d in (1, 0):
        for ci in range(NCH):
            for ut in range(2):
                for jj in range(4):
                    items.append((d, ci, ut, jj))

    def geom(d, ci):
        if d == 1:
            lo, hi = T - (ci + 1) * C, T - ci * C
            return lo, hi, rsl(lo, hi)
        lo, hi = ci * C, (ci + 1) * C
        return lo, hi, slice(lo, hi)

    def stA(i):
        d, ci, ut, jj = items[i]
        lo, hi, tsl = geom(d, ci)
        cmb = d * 8 + ut * 4 + jj
        b = i % NB
        pb = i % 2
        if d == 0 and ut == 0 and jj == 0:
            ufb = ci % 2
            S.dma("sp", uf[ufb][:], us5.rearrange("(u p) t -> p u t", p=128)[:, :, lo:hi], writes=[t_uf[ufb]])
        for ri in range(2):
            S.op("pe", lambda E, ri=ri: E.matmul(pBU[pb][:, ri, :], lhsT=BT[:, cmb, ri, :], rhs=ub[:, ut, tsl],
                                                 start=True, stop=True), reads=[t_s, t_ub], writes=[t_pBU[pb]])
        cs_ = cosT[:, cmb, :]
        sn_ = sinT[:, cmb, :]
        for k, (src, tab) in enumerate(((0, cs_), (1, sn_), (1, cs_), (0, sn_))):
            S.op("dve", lambda E, k=k, src=src, tab=tab: E.tensor_tensor(out=m1[b][:, k, :], in0=pBU[pb][:, src, :],
                                                                         in1=tab, op=ALU.mult),
                 reads=[t_pBU[pb], t_s], writes=[t_m1[b]])
        S.op("pool", lambda E: E.tensor_tensor(out=gin[b][:, 0, :], in0=m1[b][:, 0, :], in1=m1[b][:, 1, :], op=ALU.add),
             reads=[t_m1[b]], writes=[t_gin[b]])
        S.op("pool", lambda E: E.tensor_tensor(out=gin[b][:, 1, :], in0=m1[b][:, 2, :], in1=m1[b][:, 3, :],
                                               op=ALU.subtract), reads=[t_m1[b]], writes=[t_gin[b]])

    def stB(i):
        d, ci, ut, jj = items[i]
        cmb = d * 8 + ut * 4 + jj
        b = i % NB
        rho_b = prs[:, nm["rho"], cmb:cmb + 1].to_broadcast([128, C])
        for ri in range(2):
            S.op("dve", lambda E, ri=ri: E.tensor_tensor_scan(
                out=gg[b][:, ri, :], data0=rho_b, data1=gin[b][:, ri, :], initial=gi[:, ri, cmb:cmb + 1],
                op0=ALU.mult, op1=ALU.add), reads=[t_gin[b], t_gi[cmb], t_s], writes=[t_gg[b]])
        wr_ = prs[:, nm["wr"], cmb:cmb + 1]
        wi_ = prs[:, nm["wi"], cmb:cmb + 1]
        S.op("dve", lambda E: E.tensor_scalar(out=ctmp[b][:, 0:1], in0=gg[b][:, 1, C - 1:C], scalar1=wi_, scalar2=None,
                                              op0=ALU.mult), reads=[t_gg[b], t_s], writes=[t_ct[b]])
        S.op("dve", lambda E: E.tensor_scalar(out=ctmp[b][:, 1:2], in0=gg[b][:, 1, C - 1:C], scalar1=wr_, scalar2=None,
                                              op0=ALU.mult), reads=[t_gg[b], t_s], writes=[t_ct[b]])
        S.op("dve", lambda E: E.scalar_tensor_tensor(out=gi[:, 0, cmb:cmb + 1], in0=gg[b][:, 0, C - 1:C], scalar=wr_,
                                                     in1=ctmp[b][:, 0:1], op0=ALU.mult, op1=ALU.subtract),
             reads=[t_gg[b], t_ct[b], t_s], writes=[t_gi[cmb]])
        S.op("dve", lambda E: E.scalar_tensor_tensor(out=gi[:, 1, cmb:cmb + 1], in0=gg[b][:, 0, C - 1:C], scalar=wi_,
                                                     in1=ctmp[b][:, 1:2], op0=ALU.mult, op1=ALU.add),
             reads=[t_gg[b], t_ct[b], t_s], writes=[t_gi[cmb]])

    def stC(i):
        d, ci, ut, jj = items[i]
        lo, hi, tsl = geom(d, ci)
        chn = lo // C
        cmb = d * 8 + ut * 4 + jj
        b = i % NB
        hb = (ci * 2 + ut) % 2
        cs_ = cosT[:, cmb, :]
        sn_ = sinT[:, cmb, :]
        S.op("pool", lambda E: E.tensor_tensor(out=m2[b][:, 0, :], in0=gg[b][:, 0, :], in1=cs_, op=ALU.mult),
             reads=[t_gg[b], t_s], writes=[t_m2[b]])
        S.op("pool", lambda E: E.tensor_tensor(out=m2[b][:, 1, :], in0=gg[b][:, 1, :], in1=sn_, op=ALU.mult),
             reads=[t_gg[b], t_s], writes=[t_m2[b]])
        S.op("dve", lambda E: E.tensor_tensor(out=m2[b][:, 2, :], in0=gg[b][:, 1, :], in1=cs_, op=ALU.mult),
             reads=[t_gg[b], t_s], writes=[t_m2[b]])
        S.op("dve", lambda E: E.tensor_tensor(out=m2[b][:, 3, :], in0=gg[b][:, 0, :], in1=sn_, op=ALU.mult),
             reads=[t_gg[b], t_s], writes=[t_m2[b]])
        S.op("pool", lambda E: E.tensor_tensor(out=hh[hb][:, jj, 0, :], in0=m2[b][:, 0, :], in1=m2[b][:, 1, :],
                                               op=ALU.subtract), reads=[t_m2[b]], writes=[t_hh[hb][jj]])
        S.op("pool", lambda E: E.tensor_tensor(out=hh[hb][:, jj, 1, :], in0=m2[b][:, 2, :], in1=m2[b][:, 3, :],
                                               op=ALU.add), reads=[t_m2[b]], writes=[t_hh[hb][jj]])
        if jj != 3:
            return
        yb_ = iy[0] % 2
        iy[0] += 1
        n_mm = 0
        for j4 in range(4):
            cm2 = d * 8 + ut * 4 + j4
            for ri in range(2):
                S.op("pe", lambda E, cm2=cm2, ri=ri, j4=j4, n_mm=n_mm: E.matmul(
                    pY[yb_], lhsT=CT[:, cm2, ri, :], rhs=hh[hb][:, j4, ri, :], start=(n_mm == 0), stop=(n_mm == 7)),
                    reads=[t_s, t_hh[hb][j4]], writes=[t_pY[yb_]])
                n_mm += 1
        if d == 1:
            S.op("act", lambda E: E.activation(out=ybwd[:, ut, tsl], in_=pY[yb_], func=AF.Copy),
                 reads=[t_pY[yb_]], writes=[t_yb[chn]])
            return
        ufb = ci % 2
        gb_ = ci % 2
        S.op("dve", lambda E: E.tensor_tensor(out=yv[yb_][:], in0=pY[yb_], in1=ybwd[:, ut, tsl], op=ALU.add),
             reads=[t_pY[yb_], t_yb[chn]], writes=[t_yv[yb_]])
        S.op("dve", lambda E: E.scalar_tensor_tensor(out=yv[yb_][:], in0=uf[ufb][:, ut, :], scalar=dk[:, ut:ut + 1],
                                                     in1=yv[yb_][:], op0=ALU.mult, op1=ALU.add),
             reads=[t_uf[ufb], t_s, t_yv[yb_]], writes=[t_yv[yb_]])
        S.op("act", lambda E: E.activation(out=y2[yb_][:], in_=yv[yb_][:], func=AF.Square),
             reads=[t_yv[yb_]], writes=[t_y2[yb_]])
        S.op("pool", lambda E: E.tensor_scalar(out=y2[yb_][:], in0=y2[yb_][:], scalar1=0.044715, scalar2=1.0,
                                               op0=ALU.mult, op1=ALU.add), reads=[t_y2[yb_]], writes=[t_y2[yb_]])
        S.op("pool", lambda E: E.tensor_tensor(out=y2[yb_][:], in0=y2[yb_][:], in1=yv[yb_][:], op=ALU.mult),
             reads=[t_y2[yb_], t_yv[yb_]], writes=[t_y2[yb_]])
        S.op("act", lambda E: E.activation(out=y2[yb_][:], in_=y2[yb_][:], func=AF.Sigmoid, scale=1.5957691216057308),
             reads=[t_y2[yb_]], writes=[t_y2[yb_]])
        S.op("dve", lambda E: E.tensor_tensor(out=ygl[gb_][:, ut, :], in0=y2[yb_][:], in1=yv[yb_][:], op=ALU.mult),
             reads=[t_y2[yb_], t_yv[yb_]], writes=[t_yg[gb_][ut]])
        if ut != 1:
            return
        for o in range(2):
            for half in range(2):
                oc = o + 2 * half
                for u2 in range(2):
                    S.op("pe", lambda E, half=half, oc=oc, u2=u2: E.matmul(
                        pZ[half], lhsT=gw[:, u2, oc * 128:(oc + 1) * 128], rhs=ygl[gb_][:, u2, :], start=(u2 == 0),
                        stop=(u2 == 1)), reads=[t_s, t_yg[gb_][u2]], writes=[t_pZ[half]])
            S.op("act", lambda E, o=o: E.activation(out=sg[o][:], in_=pZ[1], func=AF.Sigmoid, bias=gb[:, 2 + o:3 + o]),
                 reads=[t_pZ[1], t_s], writes=[t_sg[o]])
            S.op("dve", lambda E, o=o: E.scalar_tensor_tensor(out=oo[o][:], in0=pZ[0], scalar=gb[:, o:o + 1],
                                                              in1=sg[o][:], op0=ALU.add, op1=ALU.mult),
                 reads=[t_pZ[0], t_sg[o], t_s], writes=[t_oo[o]])
            S.dma("sp", mixT[768 + 128 * o:768 + 128 * o + 128, lo:hi], oo[o][:], reads=[t_oo[o]], writes=[t_out])

    N = len(items)
    for i in range(N + 2):
        if i < N:
            stA(i)
        if 0 <= i - 1 < N:
            stB(i - 1)
        if 0 <= i - 2 < N:
            stC(i - 2)
    P.close()


DEC = 0.6065306597126334
RWKV_DBG = [9]


def phase_rwkv(nc, S, zr, ydir, bon, gfm, prm, T):
    mu_p, mu_n, w0, w2, a0, a2, g2, k_k, k_a, r_k = prm
    P = Phase(nc, S)
    NBLK = T // 512
    ident, identf, t_id = make_ident(nc, S, P)
    t_c = Tok()

    def cst(fn, eng="dve"):
        S.op(eng, fn, reads=[t_c, t_id], writes=[t_c])
    onesb = P.sb([128, 128], F32)
    cst(lambda E: E.memset(onesb[:], 0.0))
    cst(lambda E: E.memset(onesb[0:64, 0:64], 1.0))
    cst(lambda E: E.memset(onesb[64:128, 64:128], 1.0))
    mskU = P.sb([128, 512], F32)
    mskL = P.sb([128, 3, 128], F32)
    cst(lambda E: E.memset(mskU[:], 1.0), "pool")
    cst(lambda E: E.memset(mskL[:], 1.0), "pool")
    for i in range(4):
        cmp_ = ALU.is_gt if i % 2 == 0 else ALU.is_ge
        cst(lambda E, i=i, cmp_=cmp_: E.affine_select(out=mskU[:, 128 * i:128 * i + 128], in_=mskU[:, 128 * i:128 * i + 128],
                                                      pattern=[[1, 128]], base=0, channel_multiplier=-1, compare_op=cmp_,
                                                      fill=0.0), "pool")
    for i in range(3):
        cst(lambda E, i=i: E.affine_select(out=mskL[:, i, :], in_=mskL[:, i, :], pattern=[[-1, 128]], base=0,
                                           channel_multiplier=1, compare_op=ALU.is_gt, fill=0.0), "pool")
    rst = P.sb([128, 512], F32)
    cst(lambda E: E.memset(rst[:], 1.0))
    for q in range(4):
        cst(lambda E, q=q: E.memset(rst[:, 128 * q:128 * q + 1], 0.0))
    mh512 = P.sb([128, 512], F32)
    cst(lambda E: E.memset(mh512[:], -0.5), "pool")
    cmu = P.sb([128, 3, 11], F32)
    S.dma("sp", cmu[:, 1, :], mu_p.rearrange("(c p) -> p c", p=128), writes=[t_c], allow_slow_non_contiguous=True)
    S.dma("sp", cmu[:, 2, :], mu_n.rearrange("(c p) -> p c", p=128), writes=[t_c], allow_slow_non_contiguous=True)
    cst(lambda E: E.tensor_tensor(out=cmu[:, 0, :], in0=cmu[:, 1, :], in1=cmu[:, 2, :], op=ALU.add))
    cst(lambda E: E.tensor_scalar(out=cmu[:, 0, :], in0=cmu[:, 0, :], scalar1=-1.0, scalar2=1.0, op0=ALU.mult, op1=ALU.add))
    w0c = P.sb([128, 2, 3], F32); a0c = P.sb([128, 2, 3], F32)
    for d in range(2):
        S.dma("sp", w0c[:, d, :], w0[d].rearrange("(c p) -> p c", p=128), writes=[t_c], allow_slow_non_contiguous=True)
        S.dma("sp", a0c[:, d, :], a0[d].rearrange("(c p) -> p c", p=128), writes=[t_c], allow_slow_non_contiguous=True)
    kkc = P.sb([128, 3], F32); kac = P.sb([128, 3], F32); omka = P.sb([128, 3], F32); rkc = P.sb([128, 3], F32)
    S.dma("sp", kkc[:], k_k.rearrange("(c p) -> p c", p=128), writes=[t_c], allow_slow_non_contiguous=True)
    S.dma("sp", kac[:], k_a.rearrange("(c p) -> p c", p=128), writes=[t_c], allow_slow_non_contiguous=True)
    S.dma("sp", rkc[:], r_k.rearrange("h k -> (h k)").rearrange("(c p) -> p c", p=128), writes=[t_c],
          allow_slow_non_contiguous=True)
    cst(lambda E: E.tensor_scalar(out=omka[:], in0=kac[:], scalar1=-1.0, scalar2=1.0, op0=ALU.mult, op1=ALU.add))
    w2a2 = P.sb([128, 2, 384], BF16)
    for d in range(2):
        S.dma("pool", w2a2[0:64, d, :], w2[d], writes=[t_c])
        S.dma("pool", w2a2[64:128, d, :], a2[d], writes=[t_c])
    g2b = P.sb([128, 384], BF16)
    S.dma("pool", g2b[:], g2, writes=[t_c])

    banks = [P.ps([128, 512], F32) for _ in range(6)]
    t_bk = [Tok() for _ in range(6)]
    bkrr = [0]

    def getbank():
        i = bkrr[0] % 6
        bkrr[0] += 1
        return banks[i], t_bk[i]
    pTr = [P.ps([128, 4, 128], BF16) for _ in range(2)]; t_pTr = [Tok(), Tok()]
    trr = [0]

    NZS = 4
    zraw = P.sb([128, NZS, 514], F32); t_zraw = [Tok() for _ in range(NZS)]
    zsi = [0]
    zm = P.sb([128, 11, 512], F32); t_zm = [Tok() for _ in range(11)]
    tmp0 = [P.sb([128, 512], F32) for _ in range(2)]; t_tmp0 = [Tok(), Tok()]
    tmp1 = [P.sb([128, 512], F32) for _ in range(2)]; t_tmp1 = [Tok(), Tok()]
    tz = P.sb([128, 512], BF16); t_tz = Tok()
    sgz = P.sb([128, 512], BF16); t_sgz = Tok()
    B1 = [P.sb([128, 512], F32) for _ in range(3)]; tB1 = [Tok() for _ in range(3)]
    B2 = [P.sb([128, 512], F32) for _ in range(3)]; tB2 = [Tok() for _ in range(3)]
    B3 = [P.sb([128, 512], F32) for _ in range(3)]; tB3 = [Tok() for _ in range(3)]
    B4 = [P.sb([128, 512], F32) for _ in range(3)]; tB4 = [Tok() for _ in range(3)]
    B5 = [P.sb([128, 512], F32) for _ in range(3)]; tB5 = [Tok() for _ in range(3)]
    B6 = [P.sb([128, 512], F32) for _ in range(3)]; tB6 = [Tok() for _ in range(3)]
    B7 = [P.sb([128, 512], F32) for _ in range(3)]; tB7 = [Tok() for _ in range(3)]
    ARb = [P.sb([128, 4, 2, 128], BF16) for _ in range(3)]; t_AR = [Tok() for _ in range(3)]
    Bt = [P.sb([128, 512], BF16) for _ in range(3)]; t_Bt = [Tok() for _ in range(3)]
    Kt = [P.sb([128, 512], BF16) for _ in range(3)]; t_Kt = [Tok() for _ in range(3)]
    Bb = [P.sb([128, 512], BF16) for _ in range(3)]; t_Bb = [Tok() for _ in range(3)]
    Kb = [P.sb([128, 512], BF16) for _ in range(3)]; t_Kb = [Tok() for _ in range(3)]
    vb = [P.sb([128, 512], BF16) for _ in range(3)]; t_vb = [Tok() for _ in range(3)]
    WCt = P.sb([128, 3, 4], F32); t_WC = Tok()
    tm = [[P.sb([128, 4, 128], BF16) for _ in range(4)] for _ in range(3)]
    t_tm = [[Tok() for _ in range(4)] for _ in range(3)]
    stg = [P.sb([128, 512], F32) for _ in range(3)]; t_stg = [Tok() for _ in range(3)]
    sgi = [0]
    MP = [P.sb([128, 512], BF16) for _ in range(6)]; t_MP = [Tok() for _ in range(6)]
    MT = [P.sb([128, 3, 128], BF16) for _ in range(2)]; t_MT = [Tok(), Tok()]
    Pm = [[P.sb([128, 3, 128], BF16) for _ in range(2)] for _ in range(2)]; t_Pm = [[Tok(), Tok()], [Tok(), Tok()]]
    PmT = [[P.sb([128, 3, 128], BF16) for _ in range(2)] for _ in range(2)]; t_PmT = [[Tok(), Tok()], [Tok(), Tok()]]
    Tm = [[P.sb([128, 3, 128], BF16) for _ in range(2)] for _ in range(2)]; t_Tm = [[Tok(), Tok()], [Tok(), Tok()]]
    X0 = P.sb([128, 6, 64], BF16); t_X0 = Tok()
    Uv = P.sb([128, 6, 64], F32); t_Uv = Tok()
    Ahb = P.sb([128, 3, 128], BF16); t_Ah = Tok()
    Ub = P.sb([128, 6, 64], BF16); t_Ub = Tok()
    Sf = P.sb([128, 3, 64], F32); t_Sf = Tok()
    Sb = P.sb([128, 3, 64], BF16); t_Sb = Tok()
    ytm = [P.sb([128, 4, 384], F32) for _ in range(2)]; t_ytm = [Tok(), Tok()]
    t_out = Tok()
    zv = zr.rearrange("(c p) t -> p c t", p=128)

    for d in range(2 if RWKV_DBG[0] > -1 else 0):
        S.op("dve", lambda E: E.memset(Sf[:], 0.0), writes=[t_Sf])
        S.op("dve", lambda E: E.memset(Sb[:], 0.0), writes=[t_Sb])
        for bi in range(NBLK):
            if d == 0:
                lo, hi = 512 * bi, 512 * bi + 512
            else:
                lo, hi = T - 512 * (bi + 1), T - 512 * bi
            loc = (lambda ap: ap) if d == 0 else None
            s0 = 1 if lo == 0 else 0
            s1 = 513 if hi == T else 514
            osl = slice(0, 512) if d == 0 else rsl(0, 512)
            for c in range(11):
                zs = zsi[0] % NZS
                zsi[0] += 1
                b = c % 2
                if lo == 0:
                    S.op("pool", lambda E, zs=zs: E.memset(zraw[:, zs, 0:1], 0.0), writes=[t_zraw[zs]])
                if hi == T:
                    S.op("pool", lambda E, zs=zs: E.memset(zraw[:, zs, 513:514], 0.0), writes=[t_zraw[zs]])
                S.dma("sp", zraw[:, zs, s0:s1], zv[:, c, lo - 1 + s0:lo - 1 + s1], writes=[t_zraw[zs]])
                S.op("act", lambda E, c=c, b=b, zs=zs: E.activation(out=tmp0[b][:], in_=zraw[:, zs, 1:513], func=AF.Copy,
                                                                   scale=cmu[:, 0, c:c + 1]),
                     reads=[t_zraw[zs], t_c], writes=[t_tmp0[b]])
                S.op("dve", lambda E, c=c, b=b, zs=zs: E.scalar_tensor_tensor(out=tmp1[b][:], in0=zraw[:, zs, 0:512],
                                                                              scalar=cmu[:, 1, c:c + 1], in1=tmp0[b][:],
                                                                              op0=ALU.mult, op1=ALU.add),
                     reads=[t_zraw[zs], t_c, t_tmp0[b]], writes=[t_tmp1[b]])
                S.op("dve", lambda E, c=c, b=b, zs=zs, osl=osl: E.scalar_tensor_tensor(
                    out=zm[:, c, osl], in0=zraw[:, zs, 2:514], scalar=cmu[:, 2, c:c + 1], in1=tmp1[b][:],
                    op0=ALU.mult, op1=ALU.add),
                    reads=[t_zraw[zs], t_c, t_tmp1[b]], writes=[t_zm[c]])
            if RWKV_DBG[0] == 0:
                continue
            S.op("act", lambda E: E.activation(out=tz[0:64, :], in_=zm[0:64, 9, :], func=AF.Tanh),
                 reads=[t_zm[9]], writes=[t_tz])
            S.op("act", lambda E: E.activation(out=tz[64:128, :], in_=zm[64:128, 9, :], func=AF.Copy),
                 reads=[t_zm[9]], writes=[t_tz])
            if d == 0:
                S.op("act", lambda E: E.activation(out=sgz[:], in_=zm[:, 10, :], func=AF.Sigmoid),
                     reads=[t_zm[10]], writes=[t_sgz])
            v4 = lambda ap: ap.rearrange("p (q t) -> p q t", q=4)
            steps = []
            cb = {}

            def ST(f):
                steps.append(f)
            for c in range(3):
                cb[c] = dict(zr=zm[:, c, :], tr=t_zm[c], zk=zm[:, 3 + c, :], tk=t_zm[3 + c], zv=zm[:, 6 + c, :],
                             tv=t_zm[6 + c])

            def s_mm(c):
                X = cb[c]
                X["pW"], X["tpW"] = getbank()
                S.op("pe", lambda E: E.matmul(X["pW"][:], lhsT=w2a2[0:64, d, 128 * c:128 * c + 128], rhs=tz[0:64, :],
                                              start=True, stop=True), reads=[t_c, t_tz], writes=[X["tpW"]])
                X["pA"], X["tpA"] = getbank()
                S.op("pe", lambda E: E.matmul(X["pA"][:], lhsT=w2a2[64:128, d, 128 * c:128 * c + 128], rhs=tz[64:128, :],
                                              start=True, stop=True), reads=[t_c, t_tz], writes=[X["tpA"]])
            ST(s_mm)

            def s_sig(c):
                X = cb[c]
                S.op("act", lambda E: E.activation(out=B1[c][:], in_=X["pW"][:], func=AF.Sigmoid, bias=w0c[:, d, c:c + 1]),
                     reads=[X["tpW"], t_c], writes=[tB1[c]])
                S.op("act", lambda E: E.activation(out=B6[c][:], in_=X["pA"][:], func=AF.Sigmoid, bias=a0c[:, d, c:c + 1]),
                     reads=[X["tpA"], t_c], writes=[tB6[c]])
            ST(s_sig)

            def s_kkv(c):
                X = cb[c]
                S.op("dve", lambda E: E.tensor_scalar(out=B4[c][:], in0=X["zk"], scalar1=kkc[:, c:c + 1], scalar2=None,
                                                      op0=ALU.mult), reads=[X["tk"], t_c], writes=[tB4[c]])
                S.op("pool", lambda E: E.tensor_tensor(out=B5[c][:], in0=B4[c][:], in1=B4[c][:], op=ALU.mult),
                     reads=[tB4[c]], writes=[tB5[c]])
                X["pN"], X["tpN"] = getbank()
                S.op("pe", lambda E: E.matmul(X["pN"][:], lhsT=onesb[:], rhs=B5[c][:], start=True, stop=True),
                     reads=[t_c, tB5[c]], writes=[X["tpN"]])
            ST(s_kkv)

            def s_cls(c):
                S.op("dve", lambda E: E.tensor_tensor_scan(out=B2[c][:], data0=rst[:], data1=B1[c][:], initial=0.0,
                                                           op0=ALU.mult, op1=ALU.add),
                     reads=[tB1[c], t_c], writes=[tB2[c]])
                S.op("pool", lambda E: E.tensor_tensor(out=B1[c][:], in0=B2[c][:], in1=B1[c][:], op=ALU.subtract),
                     reads=[tB2[c]], writes=[tB1[c]])
            ST(s_cls)

            def s_exp(c):
                S.op("act", lambda E: E.activation(out=B3[c][:], in_=B2[c][:], func=AF.Exp, scale=DEC),
                     reads=[tB2[c]], writes=[tB3[c]])
                S.op("act", lambda E: E.activation(out=B2[c][:], in_=B2[c][:], func=AF.Exp, scale=-DEC),
                     reads=[tB2[c]], writes=[tB2[c]])
                S.op("act", lambda E: E.activation(out=B1[c][:], in_=B1[c][:], func=AF.Exp, scale=-DEC),
                     reads=[tB1[c]], writes=[tB1[c]])
            ST(s_exp)

            def s_rn(c):
                X = cb[c]
                S.op("dve", lambda E: E.tensor_scalar(out=B5[c][:], in0=X["pN"][:], scalar1=1e-12, scalar2=None,
                                                      op0=ALU.add), reads=[X["tpN"]], writes=[tB5[c]])
                S.op("act", lambda E: E.activation(out=B5[c][:], in_=B5[c][:], func=AF.Sqrt),
                     reads=[tB5[c]], writes=[tB5[c]])
                S.op("dve", lambda E: E.reciprocal(out=B5[c][:], in_=B5[c][:]),
                     reads=[tB5[c]], writes=[tB5[c]])
                S.op("dve", lambda E: E.tensor_tensor(out=B4[c][:], in0=B4[c][:], in1=B5[c][:], op=ALU.mult),
                     reads=[tB4[c], tB5[c]], writes=[tB4[c]])
                S.op("dve", lambda E: E.tensor_copy(out=WCt[:, c, :], in_=B2[c][:, 127:512:128]),
                     reads=[tB2[c]], writes=[t_WC])
            ST(s_rn)

            def s_kd(c):
                X = cb[c]
                S.op("dve", lambda E: E.tensor_scalar(out=B7[c][:], in0=B6[c][:], scalar1=kac[:, c:c + 1],
                                                      scalar2=omka[:, c:c + 1], op0=ALU.mult, op1=ALU.add),
                     reads=[tB6[c], t_c], writes=[tB7[c]])
                S.op("dve", lambda E: E.tensor_tensor(out=B7[c][:], in0=B7[c][:], in1=X["zk"], op=ALU.mult),
                     reads=[tB7[c], X["tk"]], writes=[tB7[c]])
                S.op("pool", lambda E: E.tensor_tensor(out=B6[c][:], in0=B4[c][:], in1=B6[c][:], op=ALU.mult),
                     reads=[tB4[c], tB6[c]], writes=[tB6[c]])
            ST(s_kd)

            def s_ar(c):
                X = cb[c]
                S.op("dve", lambda E: E.scalar_tensor_tensor(out=ARb[c][:, :, 0, :], in0=v4(B4[c][:]), scalar=-1.0,
                                                             in1=v4(B1[c][:]), op0=ALU.mult, op1=ALU.mult),
                     reads=[tB4[c], tB1[c]], writes=[t_AR[c]])
                S.op("pool", lambda E: E.tensor_tensor(out=ARb[c][:, :, 1, :], in0=v4(X["zr"]), in1=v4(B2[c][:]),
                                                       op=ALU.mult), reads=[X["tr"], tB2[c]], writes=[t_AR[c]])
                S.op("pool", lambda E: E.tensor_tensor(out=Bt[c][:], in0=B6[c][:], in1=B3[c][:], op=ALU.mult),
                     reads=[tB6[c], tB3[c]], writes=[t_Bt[c]])
                S.op("pool", lambda E: E.tensor_tensor(out=Kt[c][:], in0=B7[c][:], in1=B3[c][:], op=ALU.mult),
                     reads=[tB7[c], tB3[c]], writes=[t_Kt[c]])
                S.op("act", lambda E: E.activation(out=vb[c][:], in_=X["zv"], func=AF.Copy),
                     reads=[X["tv"]], writes=[t_vb[c]])
            ST(s_ar)

            def s_bb(c):
                wcb = WCt[:, c, :].unsqueeze(2).to_broadcast([128, 4, 128])
                S.op("pool", lambda E: E.tensor_tensor(out=v4(Bb[c][:]), in0=v4(Bt[c][:]), in1=wcb, op=ALU.mult),
                     reads=[t_Bt[c], t_WC], writes=[t_Bb[c]])
                S.op("pool", lambda E: E.tensor_tensor(out=v4(Kb[c][:]), in0=v4(Kt[c][:]), in1=wcb, op=ALU.mult),
                     reads=[t_Kt[c], t_WC], writes=[t_Kb[c]])
            ST(s_bb)

            def s_bonus(c):
                X = cb[c]
                S.op("dve", lambda E: E.scalar_tensor_tensor(out=B5[c][:], in0=X["zr"], scalar=rkc[:, c:c + 1],
                                                             in1=B7[c][:], op0=ALU.mult, op1=ALU.mult),
                     reads=[X["tr"], tB7[c], t_c], writes=[tB5[c]])
                pBn, t_pBn = getbank()
                S.op("pe", lambda E: E.matmul(pBn[:], lhsT=onesb[:], rhs=B5[c][:], start=True, stop=True),
                     reads=[t_c, tB5[c]], writes=[t_pBn])
                si = sgi[0] % 3
                sgi[0] += 1
                S.op("dve", lambda E: E.tensor_tensor(out=stg[si][:, osl], in0=pBn[:], in1=X["zv"], op=ALU.mult),
                     reads=[t_pBn, X["tv"]], writes=[t_stg[si]])
                S.dma("sp", bon[d][128 * c:128 * c + 128, lo:hi], stg[si][:], reads=[t_stg[si]], writes=[t_out])
                if d == 0:
                    pG, t_pG = getbank()
                    S.op("pe", lambda E: E.matmul(pG[:], lhsT=g2b[:, 128 * c:128 * c + 128], rhs=sgz[:], start=True,
                                                  stop=True), reads=[t_c, t_sgz], writes=[t_pG])
                    si2 = sgi[0] % 3
                    sgi[0] += 1
                    S.op("act", lambda E: E.activation(out=stg[si2][:], in_=pG[:], func=AF.Copy),
                         reads=[t_pG], writes=[t_stg[si2]])
                    S.dma("sp", gfm[128 * c:128 * c + 128, lo:hi], stg[si2][:], reads=[t_stg[si2]], writes=[t_out])
            ST(s_bonus)

            def s_tr(c):
                for q in range(4):
                    ts_ = trr[0] % 2
                    trr[0] += 1
                    srcs = [(ARb[c][:, q, 0, :], t_AR[c]), (Bb[c][:, 128 * q:128 * q + 128], t_Bb[c]),
                            (Kb[c][:, 128 * q:128 * q + 128], t_Kb[c]), (vb[c][:, 128 * q:128 * q + 128], t_vb[c])]
                    for ai, (src, tk) in enumerate(srcs):
                        S.op("pe", lambda E, ts_=ts_, ai=ai, src=src: E.transpose(out=pTr[ts_][:, ai, :], in_=src,
                                                                                 identity=ident[:]),
                             reads=[tk, t_id], writes=[t_pTr[ts_]])
                    if q % 2 == 0:
                        S.op("act", lambda E, ts_=ts_, q=q: E.activation(out=tm[c][q][:], in_=pTr[ts_][:], func=AF.Copy),
                             reads=[t_pTr[ts_]], writes=[t_tm[c][q]])
                    else:
                        S.op("dve", lambda E, ts_=ts_, q=q: E.tensor_copy(out=tm[c][q][:], in_=pTr[ts_][:]),
                             reads=[t_pTr[ts_]], writes=[t_tm[c][q]])
            ST(s_tr)
            for st_ in steps:
                for c in range(3):
                    st_(c)
            yb = bi % 2
            for q in range(4 if RWKV_DBG[0] >= 2 else 0):
                qs = slice(128 * q, 128 * q + 128)
                for h in range(6):
                    c, h2 = h // 2, h % 2
                    rows = slice(64 * h2, 64 * h2 + 64)
                    pM, t_pM = getbank()
                    S.op("pe", lambda E, pM=pM, c=c, rows=rows, qs=qs, q=q: E.matmul(
                        pM[:, 0:256], lhsT=Bt[c][rows, qs], rhs=ARb[c][rows, q, :, :].rearrange("p a b -> p (a b)"), start=True, stop=True),
                        reads=[t_Bt[c], t_AR[c]], writes=[t_pM])
                    S.op("pe", lambda E, pM=pM, c=c, rows=rows, qs=qs, q=q: E.matmul(
                        pM[:, 256:512], lhsT=Kt[c][rows, qs], rhs=ARb[c][rows, q, :, :].rearrange("p a b -> p (a b)"), start=True, stop=True),
                        reads=[t_Kt[c], t_AR[c]], writes=[t_pM])
                    S.op("dve", lambda E, pM=pM, h=h: E.tensor_tensor(out=MP[h][:], in0=pM[:], in1=mskU[:], op=ALU.mult),
                         reads=[t_pM, t_c], writes=[t_MP[h]])
                for hg in range(2):
                    pM3, t_pM3 = getbank()
                    for j in range(3):
                        h = 2 * j + hg
                        c, h2 = j, hg
                        rows = slice(64 * h2, 64 * h2 + 64)
                        S.op("pe", lambda E, pM3=pM3, j=j, c=c, rows=rows, qs=qs, q=q: E.matmul(
                            pM3[:, 128 * j:128 * j + 128], lhsT=ARb[c][rows, q, 0, :], rhs=Bt[c][rows, qs],
                            start=True, stop=True), reads=[t_Bt[c], t_AR[c]], writes=[t_pM3])
                    S.op("dve", lambda E, pM3=pM3, hg=hg: E.tensor_tensor(
                        out=MT[hg][:], in0=pM3[:, 0:384].rearrange("p (a b) -> p a b", a=3), in1=mskL[:], op=ALU.mult),
                        reads=[t_pM3, t_c], writes=[t_MT[hg]])
                if RWKV_DBG[0] < 3:
                    continue
                cur = [0, 0]
                for hg in range(2):
                    for j in range(3):
                        h = 2 * j + hg
                        S.op("pool", lambda E, hg=hg, j=j, h=h: E.tensor_tensor(out=Tm[hg][0][:, j, :], in0=MP[h][:, 0:128],
                                                                              in1=identf[:], op=ALU.add),
                             reads=[t_MP[h], t_id], writes=[t_Tm[hg][0]])
                v3 = lambda ap: ap[:, 0:384].rearrange("p (a b) -> p a b", a=3)
                for lev in range(1, 7):
                    bk = {}
                    for hg in range(2):
                        pv = cur[hg]
                        pP, t_pP = getbank()
                        pPT, t_pPT = getbank()
                        bk[hg] = (pP, t_pP, pPT, t_pPT)
                        for j in range(3):
                            h = 2 * j + hg
                            if lev == 1:
                                Pprev, tP = MP[h][:, 0:128], t_MP[h]
                                PTprev, tPT = MT[hg][:, j, :], t_MT[hg]
                            else:
                                Pprev, tP = Pm[hg][pv][:, j, :], t_Pm[hg][pv]
                                PTprev, tPT = PmT[hg][pv][:, j, :], t_PmT[hg][pv]
                            if lev < 6:
                                S.op("pe", lambda E, pP=pP, j=j, Pprev=Pprev, PTprev=PTprev: E.matmul(
                                    pP[:, 128 * j:128 * j + 128], lhsT=PTprev, rhs=Pprev, start=True, stop=True),
                                    reads=[tP, tPT], writes=[t_pP])
                            S.op("pe", lambda E, pPT=pPT, j=j, Pprev=Pprev, PTprev=PTprev: E.matmul(
                                pPT[:, 128 * j:128 * j + 128], lhsT=Pprev, rhs=PTprev, start=True, stop=True),
                                reads=[tP, tPT], writes=[t_pPT])
                    for hg in range(2):
                        pP, t_pP, pPT, t_pPT = bk[hg]
                        nx = 1 - cur[hg]
                        if lev < 6:
                            S.op("act", lambda E, pP=pP, hg=hg, nx=nx: E.activation(out=Pm[hg][nx][:], in_=v3(pP),
                                                                                    func=AF.Copy),
                                 reads=[t_pP], writes=[t_Pm[hg][nx]])
                        S.op("dve", lambda E, pPT=pPT, hg=hg, nx=nx: E.tensor_copy(out=PmT[hg][nx][:], in_=v3(pPT)),
                             reads=[t_pPT], writes=[t_PmT[hg][nx]])
                    bt = {}
                    for hg in range(2):
                        pv = cur[hg]
                        nx = 1 - pv
                        pTT, t_pTT = getbank()
                        bt[hg] = (pTT, t_pTT)
                        for j in range(3):
                            S.op("pe", lambda E, pTT=pTT, j=j, hg=hg, nx=nx, pv=pv: E.matmul(
                                pTT[:, 128 * j:128 * j + 128], lhsT=PmT[hg][nx][:, j, :], rhs=Tm[hg][pv][:, j, :],
                                start=True, stop=True), reads=[t_PmT[hg][nx], t_Tm[hg][pv]], writes=[t_pTT])
                    for hg in range(2):
                        pv = cur[hg]
                        nx = 1 - pv
                        pTT, t_pTT = bt[hg]
                        S.op("dve", lambda E, pTT=pTT, hg=hg, nx=nx, pv=pv: E.tensor_tensor(
                            out=Tm[hg][nx][:], in0=v3(pTT), in1=Tm[hg][pv][:], op=ALU.add),
                            reads=[t_pTT, t_Tm[hg][pv]], writes=[t_Tm[hg][nx]])
                        cur[hg] = nx
                TF = [Tm[0][cur[0]], Tm[1][cur[1]]]
                tTF = [t_Tm[0][cur[0]], t_Tm[1][cur[1]]]
                if RWKV_DBG[0] < 4:
                    continue
                pX, t_pX = getbank()
                for h in range(6):
                    c, h2 = h // 2, h % 2
                    sl_ = 3 * h2 + c
                    S.op("pe", lambda E, pX=pX, h=h, c=c, h2=h2, q=q, sl_=sl_: E.matmul(
                        pX[:, 64 * sl_:64 * sl_ + 64], lhsT=MP[h][:, 256:384], rhs=tm[c][q][:, 3, 64 * h2:64 * h2 + 64],
                        start=True, stop=True), reads=[t_MP[h], t_tm[c][q]], writes=[t_pX])
                S.op("act", lambda E, pX=pX: E.activation(out=X0[:], in_=pX[:, 0:384].rearrange("p (a b) -> p a b", a=6),
                                                         func=AF.Copy), reads=[t_pX], writes=[t_X0])
                pV, t_pV = getbank()
                for h in range(6):
                    c, h2 = h // 2, h % 2
                    sl_ = 3 * h2 + c
                    S.op("pe", lambda E, pV=pV, c=c, h2=h2, sl_=sl_: E.matmul(
                        pV[:, 64 * sl_:64 * sl_ + 64], lhsT=TF[h2][:, c, :], rhs=X0[:, sl_, :], start=True, stop=True),
                        reads=[tTF[h2], t_X0], writes=[t_pV])
                S.op("act", lambda E, pV=pV: E.activation(out=Uv[:], in_=pV[:, 0:384].rearrange("p (a b) -> p a b", a=6),
                                                         func=AF.Copy), reads=[t_pV], writes=[t_Uv])
                pH, t_pH = getbank()
                for h in range(6):
                    c, h2 = h // 2, h % 2
                    S.op("pe", lambda E, pH=pH, c=c, h2=h2, q=q: E.matmul(
                        pH[64 * h2:64 * h2 + 64, 128 * c:128 * c + 128], lhsT=tm[c][q][:, 0, 64 * h2:64 * h2 + 64],
                        rhs=TF[h2][:, c, :], start=True, stop=True), reads=[tTF[h2], t_tm[c][q]], writes=[t_pH])
                S.op("dve", lambda E, pH=pH: E.tensor_copy(out=Ahb[:], in_=pH[:, 0:384].rearrange("p (a b) -> p a b", a=3)),
                     reads=[t_pH], writes=[t_Ah])
                if RWKV_DBG[0] < 5:
                    continue
                pUb = [getbank(), getbank()]
                for h2 in range(2):
                    rows = slice(64 * h2, 64 * h2 + 64)
                    for c in range(3):
                        S.op("pe", lambda E, h2=h2, c=c, rows=rows: E.matmul(
                            pUb[h2][0][:, 64 * c:64 * c + 64], lhsT=Ahb[rows, c, :], rhs=Sb[rows, c, :], start=True,
                            stop=True), reads=[t_Ah, t_Sb], writes=[pUb[h2][1]])
                for h2 in range(2):
                    S.op("dve", lambda E, h2=h2: E.tensor_tensor(
                        out=Ub[:, 3 * h2:3 * h2 + 3, :], in0=pUb[h2][0][:, 0:192].rearrange("p (a b) -> p a b", a=3),
                        in1=Uv[:, 3 * h2:3 * h2 + 3, :], op=ALU.add),
                        reads=[pUb[h2][1], t_Uv], writes=[t_Ub])
                pYb = [getbank(), getbank()]
                for h2 in range(2):
                    rows = slice(64 * h2, 64 * h2 + 64)
                    for c in range(3):
                        h = 2 * c + h2
                        sl_ = 3 * h2 + c
                        oc = slice(64 * c, 64 * c + 64)
                        S.op("pe", lambda E, h2=h2, c=c, rows=rows, q=q, oc=oc: E.matmul(
                            pYb[h2][0][:, oc], lhsT=ARb[c][rows, q, 1, :], rhs=Sb[rows, c, :], start=True, stop=False),
                            reads=[t_AR[c], t_Sb], writes=[pYb[h2][1]])
                        S.op("pe", lambda E, h2=h2, h=h, sl_=sl_, oc=oc: E.matmul(
                            pYb[h2][0][:, oc], lhsT=MP[h][:, 128:256], rhs=Ub[:, sl_, :], start=False, stop=False),
                            reads=[t_MP[h], t_Ub], writes=[pYb[h2][1]])
                        S.op("pe", lambda E, h2=h2, h=h, c=c, q=q, oc=oc: E.matmul(
                            pYb[h2][0][:, oc], lhsT=MP[h][:, 384:512], rhs=tm[c][q][:, 3, 64 * h2:64 * h2 + 64],
                            start=False, stop=True), reads=[t_MP[h], t_tm[c][q]], writes=[pYb[h2][1]])
                for h2 in range(2):
                    S.op("act", lambda E, h2=h2, yb=yb, q=q: E.activation(
                        out=ytm[yb][:, q, :].rearrange("p (c g v) -> p g c v", g=2, v=64)[:, h2, :, :],
                        in_=pYb[h2][0][:, 0:192].rearrange("p (a b) -> p a b", a=3), func=AF.Copy),
                        reads=[pYb[h2][1]], writes=[t_ytm[yb]])
                pS_, t_pS = getbank()
                for h in range(6):
                    c, h2 = h // 2, h % 2
                    orow = slice(64 * h2, 64 * h2 + 64)
                    S.op("pe", lambda E, pS_=pS_, h=h, c=c, h2=h2, orow=orow, q=q: E.matmul(
                        pS_[orow, 64 * c:64 * c + 64], lhsT=tm[c][q][:, 1, 64 * h2:64 * h2 + 64], rhs=Ub[:, 3 * h2 + c, :],
                        start=True, stop=False), reads=[t_tm[c][q], t_Ub], writes=[t_pS])
                    S.op("pe", lambda E, pS_=pS_, h=h, c=c, h2=h2, orow=orow, q=q: E.matmul(
                        pS_[orow, 64 * c:64 * c + 64], lhsT=tm[c][q][:, 2, 64 * h2:64 * h2 + 64],
                        rhs=tm[c][q][:, 3, 64 * h2:64 * h2 + 64], start=False, stop=True),
                        reads=[t_tm[c][q]], writes=[t_pS])
                for c in range(3):
                    S.op("dve", lambda E, pS_=pS_, c=c, q=q: E.scalar_tensor_tensor(
                        out=Sf[:, c, :], in0=Sf[:, c, :], scalar=WCt[:, c, q:q + 1], in1=pS_[:, 64 * c:64 * c + 64],
                        op0=ALU.mult, op1=ALU.add), reads=[t_pS, t_WC, t_Sf], writes=[t_Sf])
                S.op("act", lambda E: E.activation(out=Sb[:], in_=Sf[:], func=AF.Copy), reads=[t_Sf], writes=[t_Sb])
            lb = 512 * bi
            S.dma("sp", ydir[d][lb:lb + 512, :].rearrange("(q p) f -> p q f", p=128), ytm[yb][:], reads=[t_ytm[yb]],
                  writes=[t_out])
    P.close()


LNX_EPS = 64e-5


def phase_rwkv_combine(nc, S, ydir, bon, gfm, lnx_w, lnx_b, mixT, T):
    P = Phase(nc, S)
    NT = T // 128
    ident, identf, t_id = make_ident(nc, S, P)
    t_c = Tok()
    J = P.sb([128, 128], F32)
    S.op("pool", lambda E: E.memset(J[:], 1.0), writes=[t_c])
    S.op("pool", lambda E: E.affine_select(out=J[:], in_=J[:], pattern=[[1, 128]], base=-127, channel_multiplier=1,
                                           compare_op=ALU.is_equal, fill=0.0), reads=[t_c], writes=[t_c])
    lwt = P.sb([128, 384], F32); lbt = P.sb([128, 384], F32)
    S.dma("sp", lwt[:], lnx_w.partition_broadcast(128), writes=[t_c])
    S.dma("sp", lbt[:], lnx_b.partition_broadcast(128), writes=[t_c])
    mh = P.sb([128, 6], F32)
    S.op("pool", lambda E: E.memset(mh[:], -0.5), writes=[t_c])
    NBF = 3
    y0 = [P.sb([128, 384], F32) for _ in range(NBF)]; t_y0 = [Tok() for _ in range(NBF)]
    y1 = [P.sb([128, 384], F32) for _ in range(NBF)]; t_y1 = [Tok() for _ in range(NBF)]
    b0 = [P.sb([128, 3, 128], F32) for _ in range(NBF)]; t_b0 = [Tok() for _ in range(NBF)]
    b1 = [P.sb([128, 3, 128], F32) for _ in range(NBF)]; t_b1 = [Tok() for _ in range(NBF)]
    gt = [P.sb([128, 3, 128], F32) for _ in range(NBF)]; t_gt = [Tok() for _ in range(NBF)]
    ys = [P.sb([128, 6, 64], F32) for _ in range(NBF)]; t_ys = [Tok() for _ in range(NBF)]
    sq = [P.sb([128, 6, 64], F32) for _ in range(NBF)]; t_sq = [Tok() for _ in range(NBF)]
    st = [P.sb([128, 4, 6], F32) for _ in range(NBF)]; t_st = [Tok() for _ in range(NBF)]
    rs = [P.sb([128, 384], F32) for _ in range(NBF)]; t_rs = [Tok() for _ in range(NBF)]
    ob = [P.sb([128, 3, 128], BF16) for _ in range(NBF)]; t_ob = [Tok() for _ in range(NBF)]
    pJ = [P.ps([128, 512], F32) for _ in range(2)]; t_pJ = [Tok(), Tok()]
    pB = [P.ps([128, 512], F32) for _ in range(2)]; t_pB = [Tok(), Tok()]
    pG = [P.ps([128, 512], F32) for _ in range(2)]; t_pG = [Tok(), Tok()]
    pO = [P.ps([128, 512], F32) for _ in range(2)]; t_pO = [Tok(), Tok()]
    t_out = Tok()
    f3 = lambda ap: ap.rearrange("p (a b) -> p a b", a=6)
    def stA(n):
        b = n % NBF
        pb2 = n % 2
        tl = slice(128 * n, 128 * n + 128)
        S.dma("sp", y0[b][:], ydir[0][128 * n:128 * n + 128, :], writes=[t_y0[b]])
        S.dma("sp", y1[b][:], ydir[1][T - 128 * (n + 1):T - 128 * n, :], writes=[t_y1[b]])
        S.dma("sp", b0[b][:], bon[0][:, tl].rearrange("(c p) t -> p c t", p=128), writes=[t_b0[b]])
        S.dma("sp", b1[b][:], bon[1][:, tl].rearrange("(c p) t -> p c t", p=128), writes=[t_b1[b]])
        S.dma("sp", gt[b][:], gfm[:, tl].rearrange("(c p) t -> p c t", p=128), writes=[t_gt[b]])
        S.op("pe", lambda E, b=b: E.matmul(pJ[pb2][:, 0:384], lhsT=J[:], rhs=y1[b][:], start=True, stop=True),
             reads=[t_c, t_y1[b]], writes=[t_pJ[pb2]])
        S.op("dve", lambda E, b=b: E.tensor_tensor(out=ys[b][:], in0=f3(pJ[pb2][:, 0:384]), in1=f3(y0[b][:]), op=ALU.add),
             reads=[t_pJ[pb2], t_y0[b]], writes=[t_ys[b]])
        S.op("dve", lambda E, b=b: E.tensor_reduce(out=st[b][:, 0, :], in_=ys[b][:], axis=AX.X, op=ALU.add),
             reads=[t_ys[b]], writes=[t_st[b]])
        S.op("dve", lambda E, b=b: E.tensor_scalar(out=st[b][:, 0, :], in0=st[b][:, 0, :], scalar1=1.0 / 64, scalar2=None,
                                                   op0=ALU.mult), reads=[t_st[b]], writes=[t_st[b]])
        S.op("dve", lambda E, b=b: E.tensor_tensor(out=ys[b][:], in0=ys[b][:],
                                                   in1=st[b][:, 0, :].unsqueeze(2).to_broadcast([128, 6, 64]),
                                                   op=ALU.subtract), reads=[t_st[b], t_ys[b]], writes=[t_ys[b]])
        S.op("pool", lambda E, b=b: E.tensor_tensor(out=sq[b][:], in0=ys[b][:], in1=ys[b][:], op=ALU.mult),
             reads=[t_ys[b]], writes=[t_sq[b]])
        S.op("dve", lambda E, b=b: E.tensor_reduce(out=st[b][:, 1, :], in_=sq[b][:], axis=AX.X, op=ALU.add),
             reads=[t_sq[b]], writes=[t_st[b]])
        S.op("dve", lambda E, b=b: E.tensor_scalar(out=st[b][:, 2, :], in0=st[b][:, 1, :], scalar1=1.0 / 64,
                                                   scalar2=LNX_EPS, op0=ALU.mult, op1=ALU.add),
             reads=[t_st[b]], writes=[t_st[b]])
        S.op("pool", lambda E, b=b: E.tensor_tensor(out=st[b][:, 3, :], in0=st[b][:, 2, :], in1=mh[:], op=ALU.pow),
             reads=[t_st[b], t_c], writes=[t_st[b]])
        S.op("dve", lambda E, b=b: E.tensor_tensor(out=ys[b][:], in0=ys[b][:],
                                                   in1=st[b][:, 3, :].unsqueeze(2).to_broadcast([128, 6, 64]),
                                                   op=ALU.mult), reads=[t_st[b], t_ys[b]], writes=[t_ys[b]])
        yf = ys[b][:].rearrange("p a b -> p (a b)")
        S.op("pool", lambda E, b=b, yf=yf: E.tensor_tensor(out=yf, in0=yf, in1=lwt[:], op=ALU.mult),
             reads=[t_ys[b], t_c], writes=[t_ys[b]])
        S.op("pool", lambda E, b=b, yf=yf: E.tensor_tensor(out=yf, in0=yf, in1=lbt[:], op=ALU.add),
             reads=[t_ys[b], t_c], writes=[t_ys[b]])
        S.op("dve", lambda E, b=b: E.tensor_tensor(out=b0[b][:], in0=b0[b][:], in1=b1[b][:], op=ALU.add),
             reads=[t_b0[b], t_b1[b]], writes=[t_b0[b]])
    def stB(n):
        b = n % NBF
        pb2 = n % 2
        tl = slice(128 * n, 128 * n + 128)
        yf = ys[b][:].rearrange("p a b -> p (a b)")
        for c in range(3):
            S.op("pe", lambda E, b=b, c=c: E.transpose(out=pB[pb2][:, 128 * c:128 * c + 128], in_=b0[b][:, c, :],
                                                       identity=identf[:]),
                 reads=[t_b0[b], t_id], writes=[t_pB[pb2]])
        for c in range(3):
            S.op("pe", lambda E, b=b, c=c: E.transpose(out=pG[pb2][:, 128 * c:128 * c + 128], in_=gt[b][:, c, :],
                                                       identity=identf[:]),
                 reads=[t_gt[b], t_id], writes=[t_pG[pb2]])
        S.op("dve", lambda E, b=b, yf=yf: E.tensor_tensor(out=rs[b][:], in0=pB[pb2][:, 0:384], in1=yf, op=ALU.add),
             reads=[t_pB[pb2], t_ys[b]], writes=[t_rs[b]])
        S.op("dve", lambda E, b=b: E.tensor_tensor(out=rs[b][:], in0=pG[pb2][:, 0:384], in1=rs[b][:], op=ALU.mult),
             reads=[t_pG[pb2], t_rs[b]], writes=[t_rs[b]])
        for c in range(3):
            S.op("pe", lambda E, b=b, c=c: E.transpose(out=pO[pb2][:, 128 * c:128 * c + 128],
                                                       in_=rs[b][:, 128 * c:128 * c + 128], identity=identf[:]),
                 reads=[t_rs[b], t_id], writes=[t_pO[pb2]])
        S.op("act", lambda E, b=b: E.activation(out=ob[b][:], in_=pO[pb2][:, 0:384].rearrange("p (a b) -> p a b", a=3),
                                                func=AF.Copy), reads=[t_pO[pb2]], writes=[t_ob[b]])
        S.dma("sp", mixT[0:384, tl].rearrange("(c p) t -> p c t", p=128), ob[b][:], reads=[t_ob[b]], writes=[t_out])
    for n in range(NT + 1):
        if n < NT:
            stA(n)
        if n >= 1:
            stB(n - 1)
    P.close()


PARAM_SHAPES = {
    "ffn1_norm_g": [2, 1024], "ffn1_w_gate": [2, 1024, 2816], "ffn1_w_up": [2, 1024, 2816],
    "ffn1_w_down": [2, 2816, 1024], "mix_norm_g": [2, 1024], "w_in": [2, 1024, 2816], "w_out": [2, 1024, 1024],
    "rwkv_mu_prev": [2, 1408], "rwkv_mu_next": [2, 1408], "rwkv_decay_w0": [2, 2, 384],
    "rwkv_decay_w2": [2, 2, 64, 384], "rwkv_iclr_a0": [2, 2, 384], "rwkv_iclr_a2": [2, 2, 64, 384],
    "rwkv_gate_w2": [2, 128, 384], "rwkv_k_k": [2, 384], "rwkv_k_a": [2, 384], "rwkv_r_k": [2, 6, 64],
    "rwkv_lnx_w": [2, 384], "rwkv_lnx_b": [2, 384], "s5_a_re": [2, 2, 16, 64], "s5_a_im": [2, 2, 16, 64],
    "s5_log_step": [2, 2, 16], "s5_b_re": [2, 16, 64, 16], "s5_b_im": [2, 16, 64, 16],
    "s5_c_re": [2, 2, 16, 16, 64], "s5_c_im": [2, 2, 16, 16, 64], "s5_d": [2, 256], "s5_glu_w": [2, 256, 512],
    "s5_glu_b": [2, 512], "ffn2_norm_g": [2, 1024], "ffn2_w_gate": [2, 1024, 2816], "ffn2_w_up": [2, 1024, 2816],
    "ffn2_w_down": [2, 2816, 1024], "final_norm_g": [1024],
}
DEPTH = 2


def build_program(T, depth=DEPTH):
    nc = bass.Bass("TRN2", target_bir_lowering=False)
    x = nc.dram_tensor("x", [T, D], F32, kind="ExternalInput").ap()
    p = {k: nc.dram_tensor(k, list(s), F32, kind="ExternalInput").ap() for k, s in PARAM_SHAPES.items()}
    out = nc.dram_tensor("out", [T, D], F32, kind="ExternalOutput").ap()
    h = nc.dram_tensor("h_res", [T, D], F32).ap()
    zr = nc.dram_tensor("z_rwkv", [1408, T], F32).ap()
    qk = nc.dram_tensor("z_qk", [768, T], BF16).ap()
    vtm = nc.dram_tensor("z_v", [T, 384], BF16).ap()
    us5 = nc.dram_tensor("z_s5", [256, T], F32).ap()
    mixT = nc.dram_tensor("mixT", [1024, T], BF16).ap()
    ydir = nc.dram_tensor("y_dir", [2, T, 384], F32).ap()
    bon = nc.dram_tensor("bonus", [2, 384, T], F32).ap()
    gfm = nc.dram_tensor("gate", [384, T], F32).ap()
    S = SchedI(nc)
    NT = T // 128
    tk = [Tok() for _ in range(NT)]
    for l in range(depth):
        src = x if l == 0 else h
        tk2 = [Tok() for _ in range(NT)]
        phase_ffn(nc, S, src, h, p["ffn1_norm_g"][l], p["ffn1_w_gate"][l], p["ffn1_w_up"][l], p["ffn1_w_down"][l],
                  tk, tk2, T)
        tk = tk2
        phase_win(nc, S, h, p["mix_norm_g"][l], p["w_in"][l], zr, qk, vtm, us5, tk, T)
        phase_rwkv(nc, S, zr, ydir, bon, gfm,
                   [p["rwkv_mu_prev"][l], p["rwkv_mu_next"][l], p["rwkv_decay_w0"][l], p["rwkv_decay_w2"][l],
                    p["rwkv_iclr_a0"][l], p["rwkv_iclr_a2"][l], p["rwkv_gate_w2"][l], p["rwkv_k_k"][l],
                    p["rwkv_k_a"][l], p["rwkv_r_k"][l]], T)
        phase_rwkv_combine(nc, S, ydir, bon, gfm, p["rwkv_lnx_w"][l], p["rwkv_lnx_b"][l], mixT, T)
        phase_attn(nc, S, qk, vtm, mixT, T)
        phase_s5(nc, S, us5, mixT,
                 [p["s5_a_re"][l], p["s5_a_im"][l], p["s5_log_step"][l], p["s5_b_re"][l], p["s5_b_im"][l],
                  p["s5_c_re"][l], p["s5_c_im"][l], p["s5_d"][l], p["s5_glu_w"][l], p["s5_glu_b"][l]], T)
        tk2 = [Tok() for _ in range(NT)]
        phase_wout(nc, S, h, h, mixT, p["w_out"][l], tk, tk2, T)
        tk = tk2
        tk2 = [Tok() for _ in range(NT)]
        phase_ffn(nc, S, h, h, p["ffn2_norm_g"][l], p["ffn2_w_gate"][l], p["ffn2_w_up"][l], p["ffn2_w_down"][l],
                  tk, tk2, T)
        tk = tk2
    tko = [Tok() for _ in range(NT)]
    phase_final(nc, S, h, out, p["final_norm_g"], tk, tko, T)
    S.finish()
    return nc, S


def kernel(**inputs):
    x = np.ascontiguousarray(np.asarray(inputs["x"], dtype=np.float32))
    B, T, _ = x.shape
    nc, S = build_program(T)
    params = {k: np.ascontiguousarray(np.asarray(inputs[k], dtype=np.float32)) for k in PARAM_SHAPES}
    in_maps = []
    for b in range(B):
        m = {"x": x[b]}
        m.update(params)
        in_maps.append(m)
    res = run_bass_kernel_spmd(nc, in_maps, core_ids=list(range(B)))
    return np.stack([np.asarray(r["out"], dtype=np.float32) for r in res.results], axis=0)
```

```python
import numpy as np
import concourse.bass as bass
import concourse.mybir as mybir
from concourse.bass_utils import run_bass_kernel_spmd

F32 = mybir.dt.float32
BF16 = mybir.dt.bfloat16
I32 = mybir.dt.int32
AF = mybir.ActivationFunctionType
ALU = mybir.AluOpType
AX = mybir.AxisListType

ENGS = ("pe", "act", "dve", "pool", "sp")


class Tok:
    __slots__ = ("w", "r", "name")

    def __init__(self, name=""):
        self.w = None
        self.r = {}
        self.name = name


class Sched:
    def __init__(self, nc, lanes_sp=8, lanes_pool=6, lanes_act=2, same_engine_sync=True):
        self.nc = nc
        self.ops = {e: [] for e in ENGS}
        self.cnt = {}
        self.sems = {}
        self.seen = {e: {} for e in ENGS}
        self.same = same_engine_sync
        self._ctx = []
        for e in ("pe", "act", "dve", "pool"):
            self._mksem(e)
        self.lanes = {"sp": [], "pool": [], "act": []}
        for q, n in (("sp", lanes_sp), ("pool", lanes_pool), ("act", lanes_act)):
            for i in range(n):
                nm = f"ln_{q}{i}"
                self._mksem(nm)
                self.lanes[q].append(nm)
        self.lane_rr = {"sp": 0, "pool": 0, "act": 0}
        self.n_instr = 0

    def _mksem(self, name):
        cm = self.nc.semaphore(name)
        s = cm.__enter__()
        self._ctx.append(cm)
        self.sems[name] = s
        self.cnt[name] = 0

    def _collect(self, eng, reads, writes):
        need = {}

        def add(src, val):
            if src == eng and (eng == "pe" or not self.same or eng == "sp"):
                return
            if need.get(src, 0) < val:
                need[src] = val
        for t in reads:
            if t.w is not None:
                add(*t.w)
        for t in writes:
            if t.w is not None:
                add(*t.w)
            for s, v in t.r.items():
                add(s, v)
        out = []
        seen = self.seen[eng]
        for s, v in need.items():
            if seen.get(s, 0) < v:
                seen[s] = v
                out.append((self.sems[s], v))
        return out

    def op(self, eng, fn, reads=(), writes=()):
        waits = self._collect(eng, reads, writes)
        self.cnt[eng] += 1
        c = self.cnt[eng]
        sem = self.sems[eng]

        def emit(E, waits=waits, fn=fn, sem=sem):
            for s, v in waits:
                E.wait_ge(s, v)
            fn(E).then_inc(sem, 1)
        self.ops[eng].append(emit)
        for t in reads:
            t.r[eng] = c
        for t in writes:
            t.w = (eng, c)
            t.r = {}
        self.n_instr += 1

    def dma(self, q, out, in_, reads=(), writes=(), **kw):
        lanes = self.lanes[q]
        ln = lanes[self.lane_rr[q] % len(lanes)]
        self.lane_rr[q] += 1
        waits = self._collect(q, reads, writes)
        prev = self.cnt[ln]
        if prev and self.seen[q].get(ln, 0) < prev:
            self.seen[q][ln] = prev
            waits.append((self.sems[ln], prev))
        self.cnt[ln] += 16
        c = self.cnt[ln]
        sem = self.sems[ln]

        def emit(E, waits=waits, sem=sem, out=out, in_=in_, kw=kw):
            for s, v in waits:
                E.wait_ge(s, v)
            E.dma_start(out=out, in_=in_, **kw).then_inc(sem, 16)
        self.ops[q].append(emit)
        for t in reads:
            t.r[ln] = c
        for t in writes:
            t.w = (ln, c)
            t.r = {}
        self.n_instr += 1

    def finish(self, final_toks):
        nc = self.nc
        fin = []
        need = {}
        for t in final_toks:
            if t.w is not None and need.get(t.w[0], 0) < t.w[1]:
                need[t.w[0]] = t.w[1]
        for s, v in self.cnt.items():
            if v and need.get(s, 0) < v:
                need[s] = v
        for s, v in need.items():
            fin.append((self.sems[s], v))
        ops = self.ops
        with nc.Block() as block:
            @block.tensor
            def _(E):
                for f in ops["pe"]:
                    f(E)

            @block.scalar
            def _(E):
                for f in ops["act"]:
                    f(E)

            @block.vector
            def _(E):
                for f in ops["dve"]:
                    f(E)

            @block.gpsimd
            def _(E):
                for f in ops["pool"]:
                    f(E)

            @block.sync
            def _(E):
                for f in ops["sp"]:
                    f(E)
                for s, v in fin:
                    E.wait_ge(s, v)
        for cm in reversed(self._ctx):
            cm.__exit__(None, None, None)


class Alloc:
    def __init__(self, nc):
        self.nc = nc
        self._ctx = []

    def sb(self, name, shape, dt):
        cm = self.nc.sbuf_tensor(name, list(shape), dt)
        t = cm.__enter__()
        self._ctx.append(cm)
        return t

    def ps(self, name, shape, dt):
        cm = self.nc.psum_tensor(name, list(shape), dt)
        t = cm.__enter__()
        self._ctx.append(cm)
        return t

    def close(self):
        for cm in reversed(self._ctx):
            cm.__exit__(None, None, None)


class SchedI(Sched):
    def __init__(self, nc, **kw):
        super().__init__(nc, **kw)
        self.E = {"pe": nc.tensor, "act": nc.scalar, "dve": nc.vector, "pool": nc.gpsimd, "sp": nc.sync}

    limit = 10 ** 9

    def op(self, eng, fn, reads=(), writes=()):
        if self.n_instr >= self.limit:
            return
        waits = self._collect(eng, reads, writes)
        self.cnt[eng] += 1
        c = self.cnt[eng]
        E = self.E[eng]
        for s, v in waits:
            E.wait_ge(s, v)
        fn(E).then_inc(self.sems[eng], 1)
        for t in reads:
            t.r[eng] = c
        for t in writes:
            t.w = (eng, c)
            t.r = {}
        self.n_instr += 1

    def dma(self, q, out, in_, reads=(), writes=(), **kw):
        if self.n_instr >= self.limit:
            return
        lanes = self.lanes[q]
        ln = lanes[self.lane_rr[q] % len(lanes)]
        self.lane_rr[q] += 1
        waits = self._collect(q, reads, writes)
        prev = self.cnt[ln]
        if prev and self.seen[q].get(ln, 0) < prev:
            self.seen[q][ln] = prev
            waits.append((self.sems[ln], prev))
        self.cnt[ln] += 16
        c = self.cnt[ln]
        E = self.E[q]
        for s, v in waits:
            E.wait_ge(s, v)
        E.dma_start(out=out, in_=in_, **kw).then_inc(self.sems[ln], 16)
        for t in reads:
            t.r[ln] = c
        for t in writes:
            t.w = (ln, c)
            t.r = {}
        self.n_instr += 1

    def barrier(self):
        for e in ("pe", "act", "dve", "pool", "sp"):
            E = self.E[e]
            for s, v in self.cnt.items():
                if v and s != e and self.seen[e].get(s, 0) < v:
                    self.seen[e][s] = v
                    E.wait_ge(self.sems[s], v)
                if s == e and v and e != "sp":
                    if self.seen[e].get(s, 0) < v:
                        self.seen[e][s] = v
                        E.wait_ge(self.sems[s], v)

    def finish(self, final_toks=()):
        self.barrier()
        for cm in reversed(self._ctx):
            cm.__exit__(None, None, None)


from contextlib import ExitStack

D = 1024
DFF = 2816
NFF = DFF // 128
KD = D // 128


class Phase:
    _uid = [0]

    def __init__(self, nc, S):
        self.nc, self.S = nc, S
        self.es = ExitStack()
        self.n = 0
        Phase._uid[0] += 1
        self.uid = Phase._uid[0]

    def sb(self, shape, dt, name=None):
        self.n += 1
        return self.es.enter_context(self.nc.sbuf_tensor(name or f"t{self.uid}_{self.n}", list(shape), dt))

    def ps(self, shape, dt, name=None):
        self.n += 1
        return self.es.enter_context(self.nc.psum_tensor(name or f"p{self.uid}_{self.n}", list(shape), dt))

    def close(self):
        self.S.barrier()
        self.es.close()


def make_ident(nc, S, P, dt=BF16):
    identf = P.sb([128, 128], F32)
    ident = P.sb([128, 128], dt)
    t = Tok()
    S.op("pool", lambda E: E.memset(identf[:], 1.0), writes=[t])
    S.op("pool", lambda E: E.affine_select(out=identf[:], in_=identf[:], pattern=[[-1, 128]], base=0,
                                           channel_multiplier=1, compare_op=ALU.is_equal, fill=0.0),
         reads=[t], writes=[t])
    S.op("dve", lambda E: E.tensor_copy(out=ident[:], in_=identf[:]), reads=[t], writes=[t])
    return ident, identf, t


def load_w_bf16(S, dst, src, tok, rows_per=128, col_split=2):
    K = dst.shape[1]
    N = dst.shape[2]
    cs = N // col_split
    for k in range(K):
        for c in range(col_split):
            S.dma("pool", dst[:, k, c * cs:(c + 1) * cs], src[k * 128:(k + 1) * 128, c * cs:(c + 1) * cs],
                  writes=[tok])


def rms_prep(S, P, ht, t_h, s, gt, t_g, xn, t_xn, junk, t_junk, stat, t_stat, mhalf, t_mh):
    S.op("dve", lambda E: E.scalar_tensor_tensor(out=junk[:], in0=ht[:, s, :], scalar=1.0 / D, in1=ht[:, s, :],
                                                 op0=ALU.mult, op1=ALU.mult, accum_out=stat[:, 0:1]),
         reads=[t_h], writes=[t_junk, t_stat])
    S.op("dve", lambda E: E.tensor_scalar(out=stat[:, 1:2], in0=stat[:, 0:1], scalar1=1e-6, scalar2=None,
                                          op0=ALU.add), reads=[t_stat], writes=[t_stat])
    S.op("pool", lambda E: E.tensor_tensor(out=stat[:, 2:3], in0=stat[:, 1:2], in1=mhalf[:, 0:1], op=ALU.pow),
         reads=[t_stat, t_mh], writes=[t_stat])
    S.op("dve", lambda E: E.scalar_tensor_tensor(out=xn[:], in0=ht[:, s, :], scalar=stat[:, 2:3], in1=gt[:],
                                                 op0=ALU.mult, op1=ALU.mult),
         reads=[t_h, t_stat, t_g], writes=[t_xn])


def phase_ffn(nc, S, h_in, h_out, g, wg, wu, wd, toks_in, toks_out, T):
    P = Phase(nc, S)
    NT = T // 128
    NS = 4
    NSUP = NT // NS
    hv_in = h_in.rearrange("(n p) d -> p n d", p=128)
    hv_out = h_out.rearrange("(n p) d -> p n d", p=128)
    ident, _, t_id = make_ident(nc, S, P)
    wg_b = P.sb([128, KD, DFF], BF16); t_wg = Tok()
    wu_b = P.sb([128, KD, DFF], BF16); t_wu = Tok()
    wd_b = P.sb([128, NFF, D], BF16); t_wd = Tok()
    load_w_bf16(S, wg_b, wg, t_wg)
    load_w_bf16(S, wu_b, wu, t_wu)
    load_w_bf16(S, wd_b, wd, t_wd, col_split=1)
    gt = P.sb([128, D], F32); t_g = Tok()
    S.dma("sp", gt[:], g.partition_broadcast(128), writes=[t_g])
    mhalf = P.sb([128, 1], F32); t_mh = Tok()
    S.op("pool", lambda E: E.memset(mhalf[:], -0.5), writes=[t_mh])
    ht = [P.sb([128, NS, D], F32)] * 2; t_ht = [[Tok() for _ in range(NS)]] * 2
    rl = [P.sb([128, 512], F32) for _ in range(4)]; t_rl = [Tok() for _ in range(4)]
    xn = [P.sb([128, D], BF16) for _ in range(2)]; t_xn = [Tok(), Tok()]
    junk = P.sb([128, D], BF16); t_junk = Tok()
    stat = [P.sb([128, 4], F32) for _ in range(2)]; t_stat = [Tok(), Tok()]
    xnT = [P.sb([128, KD, NS * 128], BF16) for _ in range(2)]; t_xnT = [Tok(), Tok()]
    hT = P.sb([128, NFF, NS * 128], BF16); t_hT = [Tok() for _ in range(NFF)]
    sg = [P.sb([128, NS * 128], BF16) for _ in range(2)]; t_sg = [Tok(), Tok()]
    pT = [P.ps([128, KD, 128], BF16) for _ in range(2)]; t_pT = [Tok(), Tok()]
    pG = [P.ps([128, 512], F32) for _ in range(2)]; t_pG = [Tok(), Tok()]
    pU = [P.ps([128, 512], F32) for _ in range(2)]; t_pU = [Tok(), Tok()]
    pD = [P.ps([128, 512], F32) for _ in range(2)]; t_pD = [Tok(), Tok()]
    itc = [0]

    def load(st):
        hb = st % 2
        for s in range(NS):
            n = st * NS + s
            S.dma("sp", ht[hb][:, s, :], hv_in[:, n, :], reads=[toks_in[n]], writes=[t_ht[hb][s]])

    def prep(st):
        hb = st % 2
        for s in range(NS):
            b = s % 2
            rms_prep(S, P, ht[hb], t_ht[hb][s], s, gt, t_g, xn[b], t_xn[b], junk, t_junk, stat[b], t_stat[b], mhalf,
                     t_mh)
            for k in range(KD):
                S.op("pe", lambda E, k=k, b=b: E.transpose(out=pT[b][:, k, :], in_=xn[b][:, k * 128:(k + 1) * 128],
                                                           identity=ident[:]),
                     reads=[t_xn[b], t_id], writes=[t_pT[b]])
            S.op("dve", lambda E, b=b, s=s, hb=hb: E.tensor_copy(out=xnT[hb][:, :, s * 128:(s + 1) * 128], in_=pT[b][:]),
                 reads=[t_pT[b]], writes=[t_xnT[hb]])

    def gateup(st):
        hb = st % 2
        for f in range(NFF):
            b = f % 2
            for k in range(KD):
                S.op("pe", lambda E, k=k, f=f, b=b: E.matmul(pG[b][:], lhsT=wg_b[:, k, f * 128:(f + 1) * 128],
                                                             rhs=xnT[hb][:, k, :], start=(k == 0), stop=(k == KD - 1)),
                     reads=[t_wg, t_xnT[hb]], writes=[t_pG[b]])
            for k in range(KD):
                S.op("pe", lambda E, k=k, f=f, b=b: E.matmul(pU[b][:], lhsT=wu_b[:, k, f * 128:(f + 1) * 128],
                                                             rhs=xnT[hb][:, k, :], start=(k == 0), stop=(k == KD - 1)),
                     reads=[t_wu, t_xnT[hb]], writes=[t_pU[b]])
            S.op("act", lambda E, b=b: E.activation(out=sg[b][:], in_=pG[b][:], func=AF.Silu),
                 reads=[t_pG[b]], writes=[t_sg[b]])
            S.op("dve", lambda E, b=b, f=f: E.tensor_tensor(out=hT[:, f, :], in0=pU[b][:], in1=sg[b][:], op=ALU.mult),
                 reads=[t_pU[b], t_sg[b]], writes=[t_hT[f]])

    def down(st):
        for s in range(NS):
            n = st * NS + s
            for c in range(2):
                b = itc[0] % 2
                r4 = itc[0] % 4
                itc[0] += 1
                S.dma("sp", rl[r4][:], hv_in[:, n, c * 512:(c + 1) * 512], reads=[toks_in[n]], writes=[t_rl[r4]])
                for f in range(NFF):
                    S.op("pe", lambda E, f=f, s=s, c=c, b=b: E.matmul(
                        pD[b][:], lhsT=hT[:, f, s * 128:(s + 1) * 128], rhs=wd_b[:, f, c * 512:(c + 1) * 512],
                        start=(f == 0), stop=(f == NFF - 1)),
                        reads=[t_wd, t_hT[f]], writes=[t_pD[b]])
                S.op("dve", lambda E, b=b, r4=r4: E.scalar_tensor_tensor(
                    out=rl[r4][:], in0=pD[b][:], scalar=0.5, in1=rl[r4][:], op0=ALU.mult, op1=ALU.add),
                    reads=[t_pD[b], t_rl[r4]], writes=[t_rl[r4]])
                S.dma("sp", hv_out[:, n, c * 512:(c + 1) * 512], rl[r4][:], reads=[t_rl[r4]], writes=[toks_out[n]])

    load(0)
    prep(0)
    for st in range(NSUP):
        if st + 1 < NSUP:
            load(st + 1)
        gateup(st)
        if st + 1 < NSUP:
            prep(st + 1)
        down(st)
    P.close()


RWKV_IN = 1408
ATT_Q0 = 1408
ATT_V0 = 2176
S5_0 = 2560
INW = 2816


def phase_win(nc, S, h_in, g, win, zr, qk, vtm, us5, toks_in, T):
    P = Phase(nc, S)
    NT = T // 128
    NS = 4
    NSUP = NT // NS
    hv_in = h_in.rearrange("(n p) d -> p n d", p=128)
    ident, _, t_id = make_ident(nc, S, P)
    w_b = P.sb([128, KD, INW], BF16); t_w = Tok()
    load_w_bf16(S, w_b, win, t_w)
    gt = P.sb([128, D], F32); t_g = Tok()
    S.dma("sp", gt[:], g.partition_broadcast(128), writes=[t_g])
    mhalf = P.sb([128, 1], F32); t_mh = Tok()
    S.op("pool", lambda E: E.memset(mhalf[:], -0.5), writes=[t_mh])
    ht = [P.sb([128, NS, D], F32) for _ in range(2)]; t_ht = [[Tok() for _ in range(NS)] for _ in range(2)]
    xn = [P.sb([128, D], BF16) for _ in range(2)]; t_xn = [Tok(), Tok()]
    junk = P.sb([128, D], BF16); t_junk = Tok()
    stat = [P.sb([128, 4], F32) for _ in range(2)]; t_stat = [Tok(), Tok()]
    xnT = [P.sb([128, KD, NS * 128], BF16) for _ in range(2)]; t_xnT = [Tok(), Tok()]
    stf = [P.sb([128, 512], F32) for _ in range(4)]; t_stf = [Tok() for _ in range(4)]
    stb = [P.sb([128, 512], BF16) for _ in range(4)]; t_stb = [Tok() for _ in range(4)]
    pT = [P.ps([128, KD, 128], BF16) for _ in range(2)]; t_pT = [Tok(), Tok()]
    pZ = [P.ps([128, 512], F32) for _ in range(4)]; t_pZ = [Tok() for _ in range(4)]
    t_out = Tok()
    chunks = []
    for c in range(11):
        chunks.append((c * 128, zr, c * 128, False))
    for c in range(6):
        chunks.append((ATT_Q0 + c * 128, qk, c * 128, True))
    for c in range(2):
        chunks.append((S5_0 + c * 128, us5, c * 128, False))
    cnt = {"it": 0, "ib": 0, "iff": 0}

    def load(st):
        hb = st % 2
        for s in range(NS):
            n = st * NS + s
            S.dma("sp", ht[hb][:, s, :], hv_in[:, n, :], reads=[toks_in[n]], writes=[t_ht[hb][s]])

    def prep(st):
        hb = st % 2
        for s in range(NS):
            b = s % 2
            rms_prep(S, P, ht[hb], t_ht[hb][s], s, gt, t_g, xn[b], t_xn[b], junk, t_junk, stat[b], t_stat[b], mhalf, t_mh)
            for k in range(KD):
                S.op("pe", lambda E, k=k, b=b: E.transpose(out=pT[b][:, k, :], in_=xn[b][:, k * 128:(k + 1) * 128],
                                                           identity=ident[:]),
                     reads=[t_xn[b], t_id], writes=[t_pT[b]])
            S.op("dve", lambda E, b=b, s=s, hb=hb: E.tensor_copy(out=xnT[hb][:, :, s * 128:(s + 1) * 128], in_=pT[b][:]),
                 reads=[t_pT[b]], writes=[t_xnT[hb]])

    def fm_chunks(st, chs):
        hb = st % 2
        tsl = slice(st * 512, (st + 1) * 512)
        for (c0, dst, r0, isb) in chs:
            pb = cnt["it"] % 4
            cnt["it"] += 1
            for k in range(KD):
                S.op("pe", lambda E, k=k, c0=c0, pb=pb, hb=hb: E.matmul(
                    pZ[pb][:], lhsT=w_b[:, k, c0:c0 + 128], rhs=xnT[hb][:, k, :], start=(k == 0), stop=(k == KD - 1)),
                    reads=[t_w, t_xnT[hb]], writes=[t_pZ[pb]])
            if isb:
                sb_ = cnt["ib"] % 4
                cnt["ib"] += 1
                S.op("act", lambda E, pb=pb, sb_=sb_: E.activation(out=stb[sb_][:], in_=pZ[pb][:], func=AF.Copy),
                     reads=[t_pZ[pb]], writes=[t_stb[sb_]])
                S.dma("sp", dst[r0:r0 + 128, tsl], stb[sb_][:], reads=[t_stb[sb_]], writes=[t_out])
            else:
                sf = cnt["iff"] % 4
                cnt["iff"] += 1
                if cnt["iff"] % 2:
                    S.op("act", lambda E, pb=pb, sf=sf: E.activation(out=stf[sf][:], in_=pZ[pb][:], func=AF.Copy),
                         reads=[t_pZ[pb]], writes=[t_stf[sf]])
                else:
                    S.op("dve", lambda E, pb=pb, sf=sf: E.tensor_copy(out=stf[sf][:], in_=pZ[pb][:]),
                         reads=[t_pZ[pb]], writes=[t_stf[sf]])
                S.dma("sp", dst[r0:r0 + 128, tsl], stf[sf][:], reads=[t_stf[sf]], writes=[t_out])

    def v_chunks(st):
        hb = st % 2
        for s in range(NS):
            n = st * NS + s
            pb = cnt["it"] % 4
            cnt["it"] += 1
            for k in range(KD):
                S.op("pe", lambda E, k=k, s=s, pb=pb, hb=hb: E.matmul(
                    pZ[pb][:, 0:384], lhsT=xnT[hb][:, k, s * 128:(s + 1) * 128], rhs=w_b[:, k, ATT_V0:ATT_V0 + 384],
                    start=(k == 0), stop=(k == KD - 1)),
                    reads=[t_w, t_xnT[hb]], writes=[t_pZ[pb]])
            sb_ = cnt["ib"] % 4
            cnt["ib"] += 1
            S.op("dve", lambda E, pb=pb, sb_=sb_: E.tensor_copy(out=stb[sb_][:, 0:384], in_=pZ[pb][:, 0:384]),
                 reads=[t_pZ[pb]], writes=[t_stb[sb_]])
            S.dma("sp", vtm[n * 128:(n + 1) * 128, :], stb[sb_][:, 0:384], reads=[t_stb[sb_]], writes=[t_out])

    load(0)
    prep(0)
    for st in range(NSUP):
        if st + 1 < NSUP:
            load(st + 1)
        fm_chunks(st, chunks[:10])
        if st + 1 < NSUP:
            prep(st + 1)
        fm_chunks(st, chunks[10:])
        v_chunks(st)
    P.close()


def phase_wout(nc, S, h_in, h_out, mixT, wout, toks_in, toks_out, T):
    P = Phase(nc, S)
    NT = T // 128
    NS = 4
    NSUP = NT // NS
    hv_in = h_in.rearrange("(n p) d -> p n d", p=128)
    hv_out = h_out.rearrange("(n p) d -> p n d", p=128)
    mv = mixT.rearrange("(k p) t -> p k t", p=128)
    w_b = P.sb([128, KD, D], BF16); t_w = Tok()
    load_w_bf16(S, w_b, wout, t_w, col_split=1)
    ht = [P.sb([128, NS, D], F32) for _ in range(2)]; t_ht = [[Tok() for _ in range(NS)] for _ in range(2)]
    ml = [P.sb([128, KD, 512], BF16) for _ in range(2)]; t_ml = [Tok(), Tok()]
    pD = [P.ps([128, 512], F32) for _ in range(4)]; t_pD = [Tok() for _ in range(4)]
    it = 0
    for st in range(NSUP):
        hb = st % 2
        S.dma("sp", ml[hb][:], mv[:, :, st * 512:(st + 1) * 512], writes=[t_ml[hb]])
        for s in range(NS):
            n = st * NS + s
            S.dma("sp", ht[hb][:, s, :], hv_in[:, n, :], reads=[toks_in[n]], writes=[t_ht[hb][s]])
        for s in range(NS):
            n = st * NS + s
            for c in range(2):
                b = it % 4
                it += 1
                for k in range(KD):
                    S.op("pe", lambda E, k=k, s=s, c=c, b=b, hb=hb: E.matmul(
                        pD[b][:], lhsT=ml[hb][:, k, s * 128:(s + 1) * 128], rhs=w_b[:, k, c * 512:(c + 1) * 512],
                        start=(k == 0), stop=(k == KD - 1)),
                        reads=[t_w, t_ml[hb]], writes=[t_pD[b]])
                S.op("dve", lambda E, s=s, c=c, b=b, hb=hb: E.tensor_tensor(
                    out=ht[hb][:, s, c * 512:(c + 1) * 512], in0=pD[b][:], in1=ht[hb][:, s, c * 512:(c + 1) * 512],
                    op=ALU.add), reads=[t_pD[b]], writes=[t_ht[hb][s]])
            S.dma("sp", hv_out[:, n, :], ht[hb][:, s, :], reads=[t_ht[hb][s]], writes=[toks_out[n]])
    P.close()


def phase_final(nc, S, h_in, out, g, toks_in, toks_out, T):
    P = Phase(nc, S)
    NT = T // 128
    hv_in = h_in.rearrange("(n p) d -> p n d", p=128)
    hv_out = out.rearrange("(n p) d -> p n d", p=128)
    gt = P.sb([128, D], F32); t_g = Tok()
    S.dma("sp", gt[:], g.partition_broadcast(128), writes=[t_g])
    mhalf = P.sb([128, 1], F32); t_mh = Tok()
    S.op("pool", lambda E: E.memset(mhalf[:], -0.5), writes=[t_mh])
    ht = [P.sb([128, 1, D], F32) for _ in range(4)]; t_ht = [Tok() for _ in range(4)]
    xo = [P.sb([128, D], F32) for _ in range(4)]; t_xo = [Tok() for _ in range(4)]
    junk = P.sb([128, D], BF16); t_junk = Tok()
    stat = [P.sb([128, 4], F32) for _ in range(4)]; t_stat = [Tok() for _ in range(4)]
    for n in range(NT):
        b = n % 4
        S.dma("sp", ht[b][:, 0, :], hv_in[:, n, :], reads=[toks_in[n]], writes=[t_ht[b]])
        rms_prep(S, P, ht[b], t_ht[b], 0, gt, t_g, xo[b], t_xo[b], junk, t_junk, stat[b], t_stat[b], mhalf, t_mh)
        S.dma("pool", hv_out[:, n, :], xo[b][:], reads=[t_xo[b]], writes=[toks_out[n]])
    P.close()


ALIBI = [0.25, 0.0625, 0.015625, 0.00390625, 0.5, 0.125]
DILS = [1, 4, 16]
NEG = -1.0e30


def phase_attn(nc, S, qk, vtm, mixT, T):
    P = Phase(nc, S)
    NB1 = T // 128
    dfi = P.sb([128, 128], I32)
    dff = P.sb([128, 128], F32)
    Dk = P.sb([128, 3, 128], F32)
    Mk = P.sb([128, 3, 128], F32)
    t_c = Tok()
    S.op("pool", lambda E: E.iota(dfi[:], pattern=[[1, 128]], base=0, channel_multiplier=-1), writes=[t_c])
    S.op("dve", lambda E: E.tensor_copy(out=dff[:], in_=dfi[:]), reads=[t_c], writes=[t_c])
    S.op("dve", lambda E: E.tensor_scalar(out=Dk[:, 0, :], in0=dff[:], scalar1=128.0, scalar2=None, op0=ALU.add),
         reads=[t_c], writes=[t_c])
    S.op("dve", lambda E: E.tensor_scalar(out=Dk[:, 1, :], in0=dff[:], scalar1=-1.0, scalar2=None, op0=ALU.mult),
         reads=[t_c], writes=[t_c])
    S.op("dve", lambda E: E.tensor_tensor(out=Dk[:, 1, :], in0=Dk[:, 1, :], in1=dff[:], op=ALU.max),
         reads=[t_c], writes=[t_c])
    S.op("dve", lambda E: E.tensor_scalar(out=Dk[:, 2, :], in0=dff[:], scalar1=-1.0, scalar2=128.0, op0=ALU.mult,
                                          op1=ALU.add), reads=[t_c], writes=[t_c])
    S.op("dve", lambda E: E.tensor_scalar(out=Mk[:], in0=Dk[:], scalar1=64.0, scalar2=NEG, op0=ALU.is_gt,
                                          op1=ALU.mult), reads=[t_c], writes=[t_c])
    sel = P.sb([65, 64], F32)
    S.op("dve", lambda E: E.memset(sel[:], 0.0), writes=[t_c])
    S.op("dve", lambda E: E.memset(sel[64:65, :], 1.0), reads=[t_c], writes=[t_c])

    qT = P.sb([128, T], BF16); t_q = Tok()
    kT = P.sb([128, T], BF16); t_k = Tok()
    vt = [P.sb([128, NB1, 2, 65], BF16) for _ in range(3)]; t_v = [Tok() for _ in range(3)]
    acc = P.sb([65, 2, T], F32)
    t_acc = [[Tok() for _ in range(NB1)] for _ in range(2)]
    biasT = P.sb([128, 2, 3, 3, 128], F32); t_b = Tok()
    NBF = 4
    sc = [P.sb([128, 3, 128], F32) for _ in range(NBF)]; t_sc = [Tok() for _ in range(NBF)]
    pr = [P.sb([128, 3, 128], BF16) for _ in range(NBF)]; t_pr = [Tok() for _ in range(NBF)]
    rec = [P.sb([64, 512], F32) for _ in range(2)]; t_rec = [Tok(), Tok()]
    ob = [P.sb([64, 512], BF16) for _ in range(2)]; t_ob = [Tok(), Tok()]
    pS = [P.ps([128, 3, 128], F32) for _ in range(NBF)]; t_pS = [Tok() for _ in range(NBF)]
    pOb = [P.ps([128, 512], F32) for _ in range(NBF)]; t_pO = [Tok() for _ in range(NBF)]
    pO = [x[0:65, 0:128] for x in pOb]
    pB = [x[0:64, 0:512] for x in pOb]; t_pB = t_pO
    t_out = Tok()
    it = 0
    for hp in range(3):
        S.dma("sp", qT[:], qk[128 * hp:128 * hp + 128, :], writes=[t_q])
        S.dma("sp", kT[:], qk[384 + 128 * hp:384 + 128 * hp + 128, :], writes=[t_k])
        for pi, d in enumerate(DILS):
            S.op("pool", lambda E, pi=pi: E.memset(vt[pi][:], 1.0), writes=[t_v[pi]])
            nm = T // d // 128
            vv = vtm.rearrange("(m j r) (h c) -> j r m h c", j=128, r=d, c=64)
            for r in range(d):
                for h2 in range(2):
                    S.dma("sp", vt[pi][:, r * nm:(r + 1) * nm, h2, 0:64], vv[:, r, :, 2 * hp + h2, :],
                          writes=[t_v[pi]])
        for h2 in range(2):
            for pi, d in enumerate(DILS):
                sl = -ALIBI[2 * hp + h2] * d
                S.op("dve", lambda E, h2=h2, pi=pi, sl=sl: E.scalar_tensor_tensor(
                    out=biasT[:, h2, pi, :, :], in0=Dk[:], scalar=sl, in1=Mk[:], op0=ALU.mult, op1=ALU.add),
                    reads=[t_c], writes=[t_b])
        for h2 in range(2):
            rows = slice(64 * h2, 64 * h2 + 64)
            stages = []
            for pi, d in enumerate(DILS):
                nb = T // d // 128
                for r in range(d):
                    for b in range(nb):
                        bi = it % NBF
                        it += 1
                        kts = [kt for kt in (b - 1, b, b + 1) if 0 <= kt < nb]
                        k0 = kts[0] - (b - 1)
                        nk = len(kts)
                        qs = slice(r + d * 128 * b, r + d * 128 * b + d * 127 + 1, d)
                        blks = sorted(set(range((r + d * 128 * b) // 128, (r + d * 128 * (b + 1) - d) // 128 + 1)))

                        def stA(bi=bi, kts=kts, k0=k0, nk=nk, qs=qs, r=r, d=d, pi=pi, rows=rows, h2=h2):
                            for ki, kt in enumerate(kts):
                                ks = slice(r + d * 128 * kt, r + d * 128 * kt + d * 127 + 1, d)
                                S.op("pe", lambda E, ki=ki, ks=ks: E.matmul(
                                    pS[bi][:, ki, :], lhsT=kT[rows, ks], rhs=qT[rows, qs], start=True, stop=True),
                                    reads=[t_q, t_k], writes=[t_pS[bi]])
                            S.op("dve", lambda E: E.scalar_tensor_tensor(
                                out=sc[bi][:, 0:nk, :], in0=pS[bi][:, 0:nk, :], scalar=0.125,
                                in1=biasT[:, h2, pi, k0:k0 + nk, :], op0=ALU.mult, op1=ALU.add),
                                reads=[t_pS[bi], t_b], writes=[t_sc[bi]])
                            S.op("act", lambda E: E.activation(out=pr[bi][:, 0:nk, :], in_=sc[bi][:, 0:nk, :],
                                                               func=AF.Exp),
                                 reads=[t_sc[bi]], writes=[t_pr[bi]])

                        def stB(bi=bi, kts=kts, nk=nk, qs=qs, r=r, nb=nb, pi=pi, h2=h2, blks=blks):
                            for ki, kt in enumerate(kts):
                                S.op("pe", lambda E, ki=ki, kt=kt: E.matmul(
                                    pO[bi], lhsT=vt[pi][:, r * nb + kt, h2, :], rhs=pr[bi][:, ki, :],
                                    start=(ki == 0), stop=(ki == nk - 1)),
                                    reads=[t_v[pi], t_pr[bi]], writes=[t_pO[bi]])
                            at = [t_acc[h2][x] for x in blks]
                            if pi == 0:
                                S.op("act", lambda E: E.activation(out=acc[:, h2, qs], in_=pO[bi], func=AF.Copy),
                                     reads=[t_pO[bi]], writes=at)
                            else:
                                S.op("dve", lambda E: E.tensor_tensor(out=acc[:, h2, qs], in0=pO[bi],
                                                                      in1=acc[:, h2, qs], op=ALU.add),
                                     reads=[t_pO[bi]], writes=at)
                        stages.append((stA, stB))
            SKEW = NBF - 1
            for i in range(len(stages) + SKEW):
                if i < len(stages):
                    stages[i][0]()
                if i - SKEW >= 0:
                    stages[i - SKEW][1]()
            head = 2 * hp + h2
            for c in range(T // 512):
                bi = c % 2
                cs = slice(c * 512, (c + 1) * 512)
                at = t_acc[h2][4 * c:4 * c + 4]
                S.op("pe", lambda E, bi=bi, h2=h2, cs=cs: E.matmul(pB[bi], lhsT=sel[:], rhs=acc[:, h2, cs],
                                                                   start=True, stop=True),
                     reads=at + [t_c], writes=[t_pB[bi]])
                S.op("dve", lambda E, bi=bi: E.reciprocal(out=rec[bi][:], in_=pB[bi]),
                     reads=[t_pB[bi]], writes=[t_rec[bi]])
                S.op("dve", lambda E, bi=bi, h2=h2, cs=cs: E.tensor_tensor(out=ob[bi][:], in0=acc[0:64, h2, cs],
                                                                          in1=rec[bi][:], op=ALU.mult),
                     reads=at + [t_rec[bi]], writes=[t_ob[bi]])
                S.dma("sp", mixT[384 + 64 * head:384 + 64 * head + 64, cs], ob[bi][:], reads=[t_ob[bi]],
                      writes=[t_out])
    P.close()


def rsl(lo, hi):
    return slice(hi - 1, (lo - 1) if lo > 0 else None, -1)


TWO_PI = 6.283185307179586
MAGIC = 12582912.0


def phase_s5(nc, S, us5, mixT, prm, T, C=256):
    P = Phase(nc, S)
    NCH = T // C
    NCMB = 16
    a_re, a_im, lstep, b_re, b_im, c_re, c_im, dsk, glu_w, glu_b = prm
    ident, identf, t_id = make_ident(nc, S, P)
    t_s = Tok()

    def dv(fn, eng="dve"):
        S.op(eng, fn, reads=[t_s, t_id], writes=[t_s])

    prs = P.sb([128, 40, 16], F32)
    cosT = P.sb([128, NCMB, C], F32)
    sinT = P.sb([128, NCMB, C], F32)
    BT = P.sb([128, NCMB, 2, 128], BF16)
    CT = P.sb([128, NCMB, 2, 128], BF16)
    gw = P.sb([128, 2, 512], BF16)
    gb = P.sb([128, 4], F32)
    dk = P.sb([128, 2], F32)
    ub = P.sb([128, 2, T], BF16); t_ub = Tok()
    ybwd = P.sb([128, 2, T], F32); t_yb = [Tok() for _ in range(NCH)]
    gi = P.sb([128, 2, NCMB], F32); t_gi = [Tok() for _ in range(NCMB)]
    banks = [P.ps([128, 512], F32) for _ in range(6)]
    P2 = Phase(nc, S)
    stg = P2.sb([16, 3, 128], F32)
    lst = P2.sb([16, 2], F32)
    S.dma("sp", stg[:, 0, :], a_re.rearrange("d (j g) p -> (d j) (g p)", g=2), writes=[t_s])
    S.dma("sp", stg[:, 1, :], a_im.rearrange("d (j g) p -> (d j) (g p)", g=2), writes=[t_s])
    S.dma("sp", lst[:], lstep.rearrange("d (j g) -> (d j) g", g=2), writes=[t_s])
    for g2 in range(2):
        dv(lambda E, g2=g2: E.tensor_copy(out=stg[:, 2, 64 * g2:64 * g2 + 64],
                                          in_=lst[:, g2:g2 + 1].to_broadcast([16, 64])))
    pst = banks[0][:, 0:48].rearrange("p (a b) -> p a b", a=3)
    for i in range(3):
        S.op("pe", lambda E, i=i: E.transpose(out=pst[:, i, :], in_=stg[:, i, :], identity=identf[0:16, 0:16]),
             reads=[t_s, t_id], writes=[t_s])
    nm = {}

    def V(name):
        if name not in nm:
            nm[name] = len(nm)
        return prs[:, nm[name], :]
    dv(lambda E: E.tensor_copy(out=prs[:, 0:3, :], in_=pst))
    nm.update({"are": 0, "aim": 1, "lst": 2})
    S.op("act", lambda E: E.activation(out=V("step"), in_=V("lst"), func=AF.Exp), reads=[t_s], writes=[t_s])
    dv(lambda E: E.tensor_tensor(out=V("ar"), in0=V("are"), in1=V("step"), op=ALU.mult))
    dv(lambda E: E.tensor_tensor(out=V("th"), in0=V("aim"), in1=V("step"), op=ALU.mult))
    S.op("act", lambda E: E.activation(out=V("rho"), in_=V("ar"), func=AF.Exp), reads=[t_s], writes=[t_s])

    def sin_of(dst, src, shift):
        dv(lambda E: E.tensor_scalar(out=V("k1"), in0=V(src), scalar1=1.0 / TWO_PI, scalar2=shift / TWO_PI,
                                     op0=ALU.mult, op1=ALU.add))
        dv(lambda E: E.tensor_scalar(out=V("k2"), in0=V("k1"), scalar1=MAGIC, scalar2=None, op0=ALU.add))
        dv(lambda E: E.tensor_scalar(out=V("k3"), in0=V("k2"), scalar1=-MAGIC, scalar2=None, op0=ALU.add))
        dv(lambda E: E.tensor_tensor(out=V("k1"), in0=V("k1"), in1=V("k3"), op=ALU.subtract))
        S.op("act", lambda E: E.activation(out=V(dst), in_=V("k1"), func=AF.Sin, scale=TWO_PI),
             reads=[t_s], writes=[t_s])
    sin_of("sn", "th", 0.0)
    sin_of("cs", "th", TWO_PI / 4)
    dv(lambda E: E.tensor_tensor(out=V("lr"), in0=V("rho"), in1=V("cs"), op=ALU.mult))
    dv(lambda E: E.tensor_tensor(out=V("li"), in0=V("rho"), in1=V("sn"), op=ALU.mult))
    dv(lambda E: E.tensor_scalar(out=V("nr"), in0=V("lr"), scalar1=-1.0, scalar2=None, op0=ALU.add))
    dv(lambda E: E.tensor_tensor(out=V("d1"), in0=V("are"), in1=V("are"), op=ALU.mult))
    dv(lambda E: E.tensor_tensor(out=V("d2"), in0=V("aim"), in1=V("aim"), op=ALU.mult))
    dv(lambda E: E.tensor_tensor(out=V("d1"), in0=V("d1"), in1=V("d2"), op=ALU.add))
    dv(lambda E: E.reciprocal(out=V("rd"), in_=V("d1")))
    dv(lambda E: E.tensor_tensor(out=V("z1"), in0=V("nr"), in1=V("are"), op=ALU.mult))
    dv(lambda E: E.tensor_tensor(out=V("z2"), in0=V("li"), in1=V("aim"), op=ALU.mult))
    dv(lambda E: E.tensor_tensor(out=V("z1"), in0=V("z1"), in1=V("z2"), op=ALU.add))
    dv(lambda E: E.tensor_tensor(out=V("zr"), in0=V("z1"), in1=V("rd"), op=ALU.mult))
    dv(lambda E: E.tensor_tensor(out=V("z1"), in0=V("li"), in1=V("are"), op=ALU.mult))
    dv(lambda E: E.tensor_tensor(out=V("z2"), in0=V("nr"), in1=V("aim"), op=ALU.mult))
    dv(lambda E: E.tensor_tensor(out=V("z1"), in0=V("z1"), in1=V("z2"), op=ALU.subtract))
    dv(lambda E: E.tensor_tensor(out=V("zi"), in0=V("z1"), in1=V("rd"), op=ALU.mult))

    tmpA = P2.sb([128, NCMB, C // 2], F32)
    tmpB = P2.sb([128, NCMB, C // 2], F32)
    dv(lambda E: E.memset(cosT[:, :, 0:1], 1.0))
    dv(lambda E: E.memset(sinT[:, :, 0:1], 0.0))
    dv(lambda E: E.tensor_copy(out=V("wr"), in_=V("cs")))
    dv(lambda E: E.tensor_copy(out=V("wi"), in_=V("sn")))
    L = 1
    while L < C:
        wrb = V("wr").unsqueeze(2).to_broadcast([128, NCMB, L])
        wib = V("wi").unsqueeze(2).to_broadcast([128, NCMB, L])
        dv(lambda E, L=L, wrb=wrb: E.tensor_tensor(out=tmpA[:, :, 0:L], in0=cosT[:, :, 0:L], in1=wrb, op=ALU.mult))
        dv(lambda E, L=L, wib=wib: E.tensor_tensor(out=tmpB[:, :, 0:L], in0=sinT[:, :, 0:L], in1=wib, op=ALU.mult))
        dv(lambda E, L=L: E.tensor_tensor(out=cosT[:, :, L:2 * L], in0=tmpA[:, :, 0:L], in1=tmpB[:, :, 0:L],
                                          op=ALU.subtract))
        dv(lambda E, L=L, wib=wib: E.tensor_tensor(out=tmpA[:, :, 0:L], in0=cosT[:, :, 0:L], in1=wib, op=ALU.mult))
        dv(lambda E, L=L, wrb=wrb: E.tensor_tensor(out=tmpB[:, :, 0:L], in0=sinT[:, :, 0:L], in1=wrb, op=ALU.mult))
        dv(lambda E, L=L: E.tensor_tensor(out=sinT[:, :, L:2 * L], in0=tmpA[:, :, 0:L], in1=tmpB[:, :, 0:L],
                                          op=ALU.add))
        dv(lambda E: E.tensor_tensor(out=V("q1"), in0=V("wr"), in1=V("wr"), op=ALU.mult))
        dv(lambda E: E.tensor_tensor(out=V("q2"), in0=V("wi"), in1=V("wi"), op=ALU.mult))
        dv(lambda E: E.tensor_tensor(out=V("q3"), in0=V("wr"), in1=V("wi"), op=ALU.mult))
        dv(lambda E: E.tensor_tensor(out=V("wr"), in0=V("q1"), in1=V("q2"), op=ALU.subtract))
        dv(lambda E: E.tensor_scalar(out=V("wi"), in0=V("q3"), scalar1=2.0, scalar2=None, op0=ALU.mult))
        L *= 2

    bst = P2.sb([128, 2, 8, 16], F32)
    S.dma("sp", bst[:, 0, :, :], b_re.rearrange("(j g) p c -> (g p) j c", g=2), writes=[t_s])
    S.dma("sp", bst[:, 1, :, :], b_im.rearrange("(j g) p c -> (g p) j c", g=2), writes=[t_s])
    bexp = P2.sb([128, 2, 128], F32)
    btmp = P2.sb([128, 16], F32)
    pX = [banks[1][:, 0:128], banks[2][:, 0:128]]
    for d in range(2):
        for j in range(8):
            cmb = d * 8 + j
            jj = j % 4
            dv(lambda E: E.memset(bexp[:], 0.0))
            for g2 in range(2):
                ps_ = slice(64 * g2, 64 * g2 + 64)
                cs_ = slice(32 * jj + 16 * g2, 32 * jj + 16 * g2 + 16)
                zr_ = prs[ps_, nm["zr"], cmb:cmb + 1]
                zi_ = prs[ps_, nm["zi"], cmb:cmb + 1]
                dv(lambda E, ps_=ps_, zi_=zi_, j=j: E.tensor_scalar(out=btmp[ps_, :], in0=bst[ps_, 1, j, :], scalar1=zi_,
                                                                    scalar2=None, op0=ALU.mult))
                dv(lambda E, ps_=ps_, cs_=cs_, zr_=zr_, j=j: E.scalar_tensor_tensor(
                    out=bexp[ps_, 0, cs_], in0=bst[ps_, 0, j, :], scalar=zr_, in1=btmp[ps_, :], op0=ALU.mult,
                    op1=ALU.subtract))
                dv(lambda E, ps_=ps_, zr_=zr_, j=j: E.tensor_scalar(out=btmp[ps_, :], in0=bst[ps_, 1, j, :], scalar1=zr_,
                                                                    scalar2=None, op0=ALU.mult))
                dv(lambda E, ps_=ps_, cs_=cs_, zi_=zi_, j=j: E.scalar_tensor_tensor(
                    out=bexp[ps_, 1, cs_], in0=bst[ps_, 0, j, :], scalar=zi_, in1=btmp[ps_, :], op0=ALU.mult,
                    op1=ALU.add))
            for ri in range(2):
                S.op("pe", lambda E, ri=ri: E.transpose(out=pX[ri], in_=bexp[:, ri, :], identity=identf[:]),
                     reads=[t_s, t_id], writes=[t_s])
                dv(lambda E, ri=ri, cmb=cmb: E.tensor_copy(out=BT[:, cmb, ri, :], in_=pX[ri]))
    cnat = P2.sb([128, 2, 2, 2, 64], F32)
    for d in range(2):
        for ri, cc in enumerate((c_re, c_im)):
            for ut in range(2):
                S.dma("sp", cnat[:, d, ri, ut, :], cc[d].rearrange("g c p -> (g c) p")[128 * ut:128 * ut + 128, :],
                      writes=[t_s])
    mki = P2.sb([128, 4, 2], I32)
    mk = P2.sb([128, 4, 2], F32)
    mk2 = P2.sb([128, 4, 2], F32)
    S.op("pool", lambda E: E.iota(mki[:], pattern=[[-32, 4], [-16, 2]], base=0, channel_multiplier=1),
         reads=[t_s], writes=[t_s])
    dv(lambda E: E.tensor_copy(out=mk[:], in_=mki[:]))
    dv(lambda E: E.tensor_scalar(out=mk2[:], in0=mk[:], scalar1=0.0, scalar2=None, op0=ALU.is_ge))
    dv(lambda E: E.tensor_scalar(out=mk[:], in0=mk[:], scalar1=15.0, scalar2=None, op0=ALU.is_le))
    dv(lambda E: E.tensor_tensor(out=mk[:], in0=mk[:], in1=mk2[:], op=ALU.mult))
    cx = P2.sb([128, 2, 64], F32)
    for d in range(2):
        for j in range(8):
            cmb = d * 8 + j
            jj = j % 4
            ut = j // 4
            for ri in range(2):
                for g2 in range(2):
                    dv(lambda E, d=d, ri=ri, ut=ut, jj=jj, g2=g2: E.tensor_scalar(
                        out=cx[:, g2, :], in0=cnat[:, d, ri, ut, :], scalar1=mk[:, jj, g2:g2 + 1], scalar2=None,
                        op0=ALU.mult))
                S.op("pe", lambda E, ri=ri: E.transpose(out=pX[ri], in_=cx[:].rearrange("p a b -> p (a b)"),
                                                        identity=identf[:]),
                     reads=[t_s, t_id], writes=[t_s])
                sgn = 1.0 if ri == 0 else -1.0
                dv(lambda E, ri=ri, cmb=cmb, sgn=sgn: E.tensor_scalar(out=CT[:, cmb, ri, :], in0=pX[ri], scalar1=sgn,
                                                                      scalar2=None, op0=ALU.mult))
    load_w_bf16(S, gw, glu_w, t_s, col_split=1)
    S.dma("sp", gb[:], glu_b.rearrange("(o p) -> p o", p=128), writes=[t_s], allow_slow_non_contiguous=True)
    S.dma("sp", dk[:], dsk.rearrange("(o p) -> p o", p=128), writes=[t_s], allow_slow_non_contiguous=True)

    for ut in range(2):
        for c4 in range(T // 2048 if T >= 2048 else 1):
            w_ = min(2048, T)
            S.dma("pool", ub[:, ut, c4 * w_:(c4 + 1) * w_], us5[128 * ut:128 * ut + 128, c4 * w_:(c4 + 1) * w_],
                  writes=[t_ub])
    S.op("dve", lambda E: E.memset(gi[:], 0.0), writes=t_gi)
    NB = 3
    P2.close()
    pBU = [banks[i][:, 0:2 * C].rearrange("p (a b) -> p a b", a=2) for i in range(2)]; t_pBU = [Tok() for _ in range(2)]
    m1 = [P.sb([128, 4, C], F32) for _ in range(NB)]; t_m1 = [Tok() for _ in range(NB)]
    gin = [P.sb([128, 2, C], F32) for _ in range(NB)]; t_gin = [Tok() for _ in range(NB)]
    gg = [P.sb([128, 2, C], F32) for _ in range(NB)]; t_gg = [Tok() for _ in range(NB)]
    m2 = [P.sb([128, 4, C], F32) for _ in range(NB)]; t_m2 = [Tok() for _ in range(NB)]
    ctmp = [P.sb([128, 2], F32) for _ in range(NB)]; t_ct = [Tok() for _ in range(NB)]
    hh = [P.sb([128, 4, 2, C], BF16) for _ in range(2)]; t_hh = [[Tok() for _ in range(4)] for _ in range(2)]
    pY = [banks[2 + i][:, 0:C] for i in range(2)]; t_pY = [Tok(), Tok()]
    uf = [P.sb([128, 2, C], F32) for _ in range(2)]; t_uf = [Tok(), Tok()]
    yv = [P.sb([128, C], F32) for _ in range(2)]; t_yv = [Tok(), Tok()]
    y2 = [P.sb([128, C], F32) for _ in range(2)]; t_y2 = [Tok(), Tok()]
    ygl = [P.sb([128, 2, C], BF16) for _ in range(2)]; t_yg = [[Tok(), Tok()] for _ in range(2)]
    pZ = [banks[4 + i][:, 0:C] for i in range(2)]; t_pZ = [Tok(), Tok()]
    sg = [P.sb([128, C], F32) for _ in range(2)]; t_sg = [Tok(), Tok()]
    oo = [P.sb([128, C], BF16) for _ in range(2)]; t_oo = [Tok(), Tok()]
    t_out = Tok()
    iy = [0]
    items = []
    for d in (1, 0):
        for ci in range(NCH):
            for ut in range(2):
                for jj in range(4):
                    items.append((d, ci, ut, jj))

    def geom(d, ci):
        if d == 1:
            lo, hi = T - (ci + 1) * C, T - ci * C
            return lo, hi, rsl(lo, hi)
        lo, hi = ci * C, (ci + 1) * C
        return lo, hi, slice(lo, hi)

    def stA(i):
        d, ci, ut, jj = items[i]
        lo, hi, tsl = geom(d, ci)
        cmb = d * 8 + ut * 4 + jj
        b = i % NB
        pb = i % 2
        if d == 0 and ut == 0 and jj == 0:
            ufb = ci % 2
            S.dma("sp", uf[ufb][:], us5.rearrange("(u p) t -> p u t", p=128)[:, :, lo:hi], writes=[t_uf[ufb]])
        for ri in range(2):
            S.op("pe", lambda E, ri=ri: E.matmul(pBU[pb][:, ri, :], lhsT=BT[:, cmb, ri, :], rhs=ub[:, ut, tsl],
                                                 start=True, stop=True), reads=[t_s, t_ub], writes=[t_pBU[pb]])
        cs_ = cosT[:, cmb, :]
        sn_ = sinT[:, cmb, :]
        for k, (src, tab) in enumerate(((0, cs_), (1, sn_), (1, cs_), (0, sn_))):
            S.op("dve", lambda E, k=k, src=src, tab=tab: E.tensor_tensor(out=m1[b][:, k, :], in0=pBU[pb][:, src, :],
                                                                         in1=tab, op=ALU.mult),
                 reads=[t_pBU[pb], t_s], writes=[t_m1[b]])
        S.op("pool", lambda E: E.tensor_tensor(out=gin[b][:, 0, :], in0=m1[b][:, 0, :], in1=m1[b][:, 1, :], op=ALU.add),
             reads=[t_m1[b]], writes=[t_gin[b]])
        S.op("pool", lambda E: E.tensor_tensor(out=gin[b][:, 1, :], in0=m1[b][:, 2, :], in1=m1[b][:, 3, :],
                                               op=ALU.subtract), reads=[t_m1[b]], writes=[t_gin[b]])

    def stB(i):
        d, ci, ut, jj = items[i]
        cmb = d * 8 + ut * 4 + jj
        b = i % NB
        rho_b = prs[:, nm["rho"], cmb:cmb + 1].to_broadcast([128, C])
        for ri in range(2):
            S.op("dve", lambda E, ri=ri: E.tensor_tensor_scan(
                out=gg[b][:, ri, :], data0=rho_b, data1=gin[b][:, ri, :], initial=gi[:, ri, cmb:cmb + 1],
                op0=ALU.mult, op1=ALU.add), reads=[t_gin[b], t_gi[cmb], t_s], writes=[t_gg[b]])
        wr_ = prs[:, nm["wr"], cmb:cmb + 1]
        wi_ = prs[:, nm["wi"], cmb:cmb + 1]
        S.op("dve", lambda E: E.tensor_scalar(out=ctmp[b][:, 0:1], in0=gg[b][:, 1, C - 1:C], scalar1=wi_, scalar2=None,
                                              op0=ALU.mult), reads=[t_gg[b], t_s], writes=[t_ct[b]])
        S.op("dve", lambda E: E.tensor_scalar(out=ctmp[b][:, 1:2], in0=gg[b][:, 1, C - 1:C], scalar1=wr_, scalar2=None,
                                              op0=ALU.mult), reads=[t_gg[b], t_s], writes=[t_ct[b]])
        S.op("dve", lambda E: E.scalar_tensor_tensor(out=gi[:, 0, cmb:cmb + 1], in0=gg[b][:, 0, C - 1:C], scalar=wr_,
                                                     in1=ctmp[b][:, 0:1], op0=ALU.mult, op1=ALU.subtract),
             reads=[t_gg[b], t_ct[b], t_s], writes=[t_gi[cmb]])
        S.op("dve", lambda E: E.scalar_tensor_tensor(out=gi[:, 1, cmb:cmb + 1], in0=gg[b][:, 0, C - 1:C], scalar=wi_,
                                                     in1=ctmp[b][:, 1:2], op0=ALU.mult, op1=ALU.add),
             reads=[t_gg[b], t_ct[b], t_s], writes=[t_gi[cmb]])

    def stC(i):
        d, ci, ut, jj = items[i]
        lo, hi, tsl = geom(d, ci)
        chn = lo // C
        cmb = d * 8 + ut * 4 + jj
        b = i % NB
        hb = (ci * 2 + ut) % 2
        cs_ = cosT[:, cmb, :]
        sn_ = sinT[:, cmb, :]
        S.op("pool", lambda E: E.tensor_tensor(out=m2[b][:, 0, :], in0=gg[b][:, 0, :], in1=cs_, op=ALU.mult),
             reads=[t_gg[b], t_s], writes=[t_m2[b]])
        S.op("pool", lambda E: E.tensor_tensor(out=m2[b][:, 1, :], in0=gg[b][:, 1, :], in1=sn_, op=ALU.mult),
             reads=[t_gg[b], t_s], writes=[t_m2[b]])
        S.op("dve", lambda E: E.tensor_tensor(out=m2[b][:, 2, :], in0=gg[b][:, 1, :], in1=cs_, op=ALU.mult),
             reads=[t_gg[b], t_s], writes=[t_m2[b]])
        S.op("dve", lambda E: E.tensor_tensor(out=m2[b][:, 3, :], in0=gg[b][:, 0, :], in1=sn_, op=ALU.mult),
             reads=[t_gg[b], t_s], writes=[t_m2[b]])
        S.op("pool", lambda E: E.tensor_tensor(out=hh[hb][:, jj, 0, :], in0=m2[b][:, 0, :], in1=m2[b][:, 1, :],
                                               op=ALU.subtract), reads=[t_m2[b]], writes=[t_hh[hb][jj]])
        S.op("pool", lambda E: E.tensor_tensor(out=hh[hb][:, jj, 1, :], in0=m2[b][:, 2, :], in1=m2[b][:, 3, :],
                                               op=ALU.add), reads=[t_m2[b]], writes=[t_hh[hb][jj]])
        if jj != 3:
            return
        yb_ = iy[0] % 2
        iy[0] += 1
        n_mm = 0
        for j4 in range(4):
            cm2 = d * 8 + ut * 4 + j4
            for ri in range(2):
                S.op("pe", lambda E, cm2=cm2, ri=ri, j4=j4, n_mm=n_mm: E.matmul(
                    pY[yb_], lhsT=CT[:, cm2, ri, :], rhs=hh[hb][:, j4, ri, :], start=(n_mm == 0), stop=(n_mm == 7)),
                    reads=[t_s, t_hh[hb][j4]], writes=[t_pY[yb_]])
                n_mm += 1
        if d == 1:
            S.op("act", lambda E: E.activation(out=ybwd[:, ut, tsl], in_=pY[yb_], func=AF.Copy),
                 reads=[t_pY[yb_]], writes=[t_yb[chn]])
            return
        ufb = ci % 2
        gb_ = ci % 2
        S.op("dve", lambda E: E.tensor_tensor(out=yv[yb_][:], in0=pY[yb_], in1=ybwd[:, ut, tsl], op=ALU.add),
             reads=[t_pY[yb_], t_yb[chn]], writes=[t_yv[yb_]])
        S.op("dve", lambda E: E.scalar_tensor_tensor(out=yv[yb_][:], in0=uf[ufb][:, ut, :], scalar=dk[:, ut:ut + 1],
                                                     in1=yv[yb_][:], op0=ALU.mult, op1=ALU.add),
             reads=[t_uf[ufb], t_s, t_yv[yb_]], writes=[t_yv[yb_]])
        S.op("act", lambda E: E.activation(out=y2[yb_][:], in_=yv[yb_][:], func=AF.Square),
             reads=[t_yv[yb_]], writes=[t_y2[yb_]])
        S.op("pool", lambda E: E.tensor_scalar(out=y2[yb_][:], in0=y2[yb_][:], scalar1=0.044715, scalar2=1.0,
                                               op0=ALU.mult, op1=ALU.add), reads=[t_y2[yb_]], writes=[t_y2[yb_]])
        S.op("pool", lambda E: E.tensor_tensor(out=y2[yb_][:], in0=y2[yb_][:], in1=yv[yb_][:], op=ALU.mult),
             reads=[t_y2[yb_], t_yv[yb_]], writes=[t_y2[yb_]])
        S.op("act", lambda E: E.activation(out=y2[yb_][:], in_=y2[yb_][:], func=AF.Sigmoid, scale=1.5957691216057308),
             reads=[t_y2[yb_]], writes=[t_y2[yb_]])
        S.op("dve", lambda E: E.tensor_tensor(out=ygl[gb_][:, ut, :], in0=y2[yb_][:], in1=yv[yb_][:], op=ALU.mult),
             reads=[t_y2[yb_], t_yv[yb_]], writes=[t_yg[gb_][ut]])
        if ut != 1:
            return
        for o in range(2):
            for half in range(2):
                oc = o + 2 * half
                for u2 in range(2):
                    S.op("pe", lambda E, half=half, oc=oc, u2=u2: E.matmul(
                        pZ[half], lhsT=gw[:, u2, oc * 128:(oc + 1) * 128], rhs=ygl[gb_][:, u2, :], start=(u2 == 0),
                        stop=(u2 == 1)), reads=[t_s, t_yg[gb_][u2]], writes=[t_pZ[half]])
            S.op("act", lambda E, o=o: E.activation(out=sg[o][:], in_=pZ[1], func=AF.Sigmoid, bias=gb[:, 2 + o:3 + o]),
                 reads=[t_pZ[1], t_s], writes=[t_sg[o]])
            S.op("dve", lambda E, o=o: E.scalar_tensor_tensor(out=oo[o][:], in0=pZ[0], scalar=gb[:, o:o + 1],
                                                              in1=sg[o][:], op0=ALU.add, op1=ALU.mult),
                 reads=[t_pZ[0], t_sg[o], t_s], writes=[t_oo[o]])
            S.dma("sp", mixT[768 + 128 * o:768 + 128 * o + 128, lo:hi], oo[o][:], reads=[t_oo[o]], writes=[t_out])

    N = len(items)
    for i in range(N + 2):
        if i < N:
            stA(i)
        if 0 <= i - 1 < N:
            stB(i - 1)
        if 0 <= i - 2 < N:
            stC(i - 2)
    P.close()


DEC = 0.6065306597126334
RWKV_DBG = [9]


def phase_rwkv(nc, S, zr, ydir, bon, gfm, prm, T):
    mu_p, mu_n, w0, w2, a0, a2, g2, k_k, k_a, r_k = prm
    P = Phase(nc, S)
    NBLK = T // 512
    ident, identf, t_id = make_ident(nc, S, P)
    t_c = Tok()

    def cst(fn, eng="dve"):
        S.op(eng, fn, reads=[t_c, t_id], writes=[t_c])
    onesb = P.sb([128, 128], F32)
    cst(lambda E: E.memset(onesb[:], 0.0))
    cst(lambda E: E.memset(onesb[0:64, 0:64], 1.0))
    cst(lambda E: E.memset(onesb[64:128, 64:128], 1.0))
    mskU = P.sb([128, 512], F32)
    mskL = P.sb([128, 3, 128], F32)
    cst(lambda E: E.memset(mskU[:], 1.0), "pool")
    cst(lambda E: E.memset(mskL[:], 1.0), "pool")
    for i in range(4):
        cmp_ = ALU.is_gt if i % 2 == 0 else ALU.is_ge
        cst(lambda E, i=i, cmp_=cmp_: E.affine_select(out=mskU[:, 128 * i:128 * i + 128], in_=mskU[:, 128 * i:128 * i + 128],
                                                      pattern=[[1, 128]], base=0, channel_multiplier=-1, compare_op=cmp_,
                                                      fill=0.0), "pool")
    for i in range(3):
        cst(lambda E, i=i: E.affine_select(out=mskL[:, i, :], in_=mskL[:, i, :], pattern=[[-1, 128]], base=0,
                                           channel_multiplier=1, compare_op=ALU.is_gt, fill=0.0), "pool")
    rst = P.sb([128, 512], F32)
    cst(lambda E: E.memset(rst[:], 1.0))
    for q in range(4):
        cst(lambda E, q=q: E.memset(rst[:, 128 * q:128 * q + 1], 0.0))
    mh512 = P.sb([128, 512], F32)
    cst(lambda E: E.memset(mh512[:], -0.5), "pool")
    cmu = P.sb([128, 3, 11], F32)
    S.dma("sp", cmu[:, 1, :], mu_p.rearrange("(c p) -> p c", p=128), writes=[t_c], allow_slow_non_contiguous=True)
    S.dma("sp", cmu[:, 2, :], mu_n.rearrange("(c p) -> p c", p=128), writes=[t_c], allow_slow_non_contiguous=True)
    cst(lambda E: E.tensor_tensor(out=cmu[:, 0, :], in0=cmu[:, 1, :], in1=cmu[:, 2, :], op=ALU.add))
    cst(lambda E: E.tensor_scalar(out=cmu[:, 0, :], in0=cmu[:, 0, :], scalar1=-1.0, scalar2=1.0, op0=ALU.mult, op1=ALU.add))
    w0c = P.sb([128, 2, 3], F32); a0c = P.sb([128, 2, 3], F32)
    for d in range(2):
        S.dma("sp", w0c[:, d, :], w0[d].rearrange("(c p) -> p c", p=128), writes=[t_c], allow_slow_non_contiguous=True)
        S.dma("sp", a0c[:, d, :], a0[d].rearrange("(c p) -> p c", p=128), writes=[t_c], allow_slow_non_contiguous=True)
    kkc = P.sb([128, 3], F32); kac = P.sb([128, 3], F32); omka = P.sb([128, 3], F32); rkc = P.sb([128, 3], F32)
    S.dma("sp", kkc[:], k_k.rearrange("(c p) -> p c", p=128), writes=[t_c], allow_slow_non_contiguous=True)
    S.dma("sp", kac[:], k_a.rearrange("(c p) -> p c", p=128), writes=[t_c], allow_slow_non_contiguous=True)
    S.dma("sp", rkc[:], r_k.rearrange("h k -> (h k)").rearrange("(c p) -> p c", p=128), writes=[t_c],
          allow_slow_non_contiguous=True)
    cst(lambda E: E.tensor_scalar(out=omka[:], in0=kac[:], scalar1=-1.0, scalar2=1.0, op0=ALU.mult, op1=ALU.add))
    w2a2 = P.sb([128, 2, 384], BF16)
    for d in range(2):
        S.dma("pool", w2a2[0:64, d, :], w2[d], writes=[t_c])
        S.dma("pool", w2a2[64:128, d, :], a2[d], writes=[t_c])
    g2b = P.sb([128, 384], BF16)
    S.dma("pool", g2b[:], g2, writes=[t_c])

    banks = [P.ps([128, 512], F32) for _ in range(6)]
    t_bk = [Tok() for _ in range(6)]
    bkrr = [0]

    def getbank():
        i = bkrr[0] % 6
        bkrr[0] += 1
        return banks[i], t_bk[i]
    pTr = [P.ps([128, 4, 128], BF16) for _ in range(2)]; t_pTr = [Tok(), Tok()]
    trr = [0]

    NZS = 4
    zraw = P.sb([128, NZS, 514], F32); t_zraw = [Tok() for _ in range(NZS)]
    zsi = [0]
    zm = P.sb([128, 11, 512], F32); t_zm = [Tok() for _ in range(11)]
    tmp0 = [P.sb([128, 512], F32) for _ in range(2)]; t_tmp0 = [Tok(), Tok()]
    tmp1 = [P.sb([128, 512], F32) for _ in range(2)]; t_tmp1 = [Tok(), Tok()]
    tz = P.sb([128, 512], BF16); t_tz = Tok()
    sgz = P.sb([128, 512], BF16); t_sgz = Tok()
    B1 = [P.sb([128, 512], F32) for _ in range(3)]; tB1 = [Tok() for _ in range(3)]
    B2 = [P.sb([128, 512], F32) for _ in range(3)]; tB2 = [Tok() for _ in range(3)]
    B3 = [P.sb([128, 512], F32) for _ in range(3)]; tB3 = [Tok() for _ in range(3)]
    B4 = [P.sb([128, 512], F32) for _ in range(3)]; tB4 = [Tok() for _ in range(3)]
    B5 = [P.sb([128, 512], F32) for _ in range(3)]; tB5 = [Tok() for _ in range(3)]
    B6 = [P.sb([128, 512], F32) for _ in range(3)]; tB6 = [Tok() for _ in range(3)]
    B7 = [P.sb([128, 512], F32) for _ in range(3)]; tB7 = [Tok() for _ in range(3)]
    ARb = [P.sb([128, 4, 2, 128], BF16) for _ in range(3)]; t_AR = [Tok() for _ in range(3)]
    Bt = [P.sb([128, 512], BF16) for _ in range(3)]; t_Bt = [Tok() for _ in range(3)]
    Kt = [P.sb([128, 512], BF16) for _ in range(3)]; t_Kt = [Tok() for _ in range(3)]
    Bb = [P.sb([128, 512], BF16) for _ in range(3)]; t_Bb = [Tok() for _ in range(3)]
    Kb = [P.sb([128, 512], BF16) for _ in range(3)]; t_Kb = [Tok() for _ in range(3)]
    vb = [P.sb([128, 512], BF16) for _ in range(3)]; t_vb = [Tok() for _ in range(3)]
    WCt = P.sb([128, 3, 4], F32); t_WC = Tok()
    tm = [[P.sb([128, 4, 128], BF16) for _ in range(4)] for _ in range(3)]
    t_tm = [[Tok() for _ in range(4)] for _ in range(3)]
    stg = [P.sb([128, 512], F32) for _ in range(3)]; t_stg = [Tok() for _ in range(3)]
    sgi = [0]
    MPs = [[P.sb([128, 512], BF16) for _ in range(6)] for _ in range(2)]
    t_MPs = [[Tok() for _ in range(6)] for _ in range(2)]
    MTs = [[P.sb([128, 3, 128], BF16) for _ in range(2)] for _ in range(2)]
    t_MTs = [[Tok(), Tok()], [Tok(), Tok()]]
    Pm = [[P.sb([128, 3, 128], BF16) for _ in range(2)] for _ in range(2)]; t_Pm = [[Tok(), Tok()], [Tok(), Tok()]]
    PmT = [[P.sb([128, 3, 128], BF16) for _ in range(2)] for _ in range(2)]; t_PmT = [[Tok(), Tok()], [Tok(), Tok()]]
    Tm = [[P.sb([128, 3, 128], BF16) for _ in range(2)] for _ in range(2)]; t_Tm = [[Tok(), Tok()], [Tok(), Tok()]]
    X0 = P.sb([128, 6, 64], BF16); t_X0 = Tok()
    Uv = P.sb([128, 6, 64], F32); t_Uv = Tok()
    Ahb = P.sb([128, 3, 128], BF16); t_Ah = Tok()
    Ub = P.sb([128, 6, 64], BF16); t_Ub = Tok()
    Sf = P.sb([128, 3, 64], F32); t_Sf = Tok()
    Sb = P.sb([128, 3, 64], BF16); t_Sb = Tok()
    ytm = [P.sb([128, 4, 384], F32) for _ in range(2)]; t_ytm = [Tok(), Tok()]
    t_out = Tok()
    zv = zr.rearrange("(c p) t -> p c t", p=128)

    for d in range(2 if RWKV_DBG[0] > -1 else 0):
        S.op("dve", lambda E: E.memset(Sf[:], 0.0), writes=[t_Sf])
        S.op("dve", lambda E: E.memset(Sb[:], 0.0), writes=[t_Sb])
        for bi in range(NBLK):
            if d == 0:
                lo, hi = 512 * bi, 512 * bi + 512
            else:
                lo, hi = T - 512 * (bi + 1), T - 512 * bi
            loc = (lambda ap: ap) if d == 0 else None
            s0 = 1 if lo == 0 else 0
            s1 = 513 if hi == T else 514
            osl = slice(0, 512) if d == 0 else rsl(0, 512)
            for c in range(11):
                zs = zsi[0] % NZS
                zsi[0] += 1
                b = c % 2
                if lo == 0:
                    S.op("pool", lambda E, zs=zs: E.memset(zraw[:, zs, 0:1], 0.0), writes=[t_zraw[zs]])
                if hi == T:
                    S.op("pool", lambda E, zs=zs: E.memset(zraw[:, zs, 513:514], 0.0), writes=[t_zraw[zs]])
                S.dma("sp", zraw[:, zs, s0:s1], zv[:, c, lo - 1 + s0:lo - 1 + s1], writes=[t_zraw[zs]])
                S.op("act", lambda E, c=c, b=b, zs=zs: E.activation(out=tmp0[b][:], in_=zraw[:, zs, 1:513], func=AF.Copy,
                                                                   scale=cmu[:, 0, c:c + 1]),
                     reads=[t_zraw[zs], t_c], writes=[t_tmp0[b]])
                S.op("dve", lambda E, c=c, b=b, zs=zs: E.scalar_tensor_tensor(out=tmp1[b][:], in0=zraw[:, zs, 0:512],
                                                                              scalar=cmu[:, 1, c:c + 1], in1=tmp0[b][:],
                                                                              op0=ALU.mult, op1=ALU.add),
                     reads=[t_zraw[zs], t_c, t_tmp0[b]], writes=[t_tmp1[b]])
                S.op("dve", lambda E, c=c, b=b, zs=zs, osl=osl: E.scalar_tensor_tensor(
                    out=zm[:, c, osl], in0=zraw[:, zs, 2:514], scalar=cmu[:, 2, c:c + 1], in1=tmp1[b][:],
                    op0=ALU.mult, op1=ALU.add),
                    reads=[t_zraw[zs], t_c, t_tmp1[b]], writes=[t_zm[c]])
            if RWKV_DBG[0] == 0:
                continue
            S.op("act", lambda E: E.activation(out=tz[0:64, :], in_=zm[0:64, 9, :], func=AF.Tanh),
                 reads=[t_zm[9]], writes=[t_tz])
            S.op("act", lambda E: E.activation(out=tz[64:128, :], in_=zm[64:128, 9, :], func=AF.Copy),
                 reads=[t_zm[9]], writes=[t_tz])
            if d == 0:
                S.op("act", lambda E: E.activation(out=sgz[:], in_=zm[:, 10, :], func=AF.Sigmoid),
                     reads=[t_zm[10]], writes=[t_sgz])
            v4 = lambda ap: ap.rearrange("p (q t) -> p q t", q=4)
            steps = []
            cb = {}

            def ST(f):
                steps.append(f)
            for c in range(3):
                cb[c] = dict(zr=zm[:, c, :], tr=t_zm[c], zk=zm[:, 3 + c, :], tk=t_zm[3 + c], zv=zm[:, 6 + c, :],
                             tv=t_zm[6 + c])

            def s_mm(c):
                X = cb[c]
                X["pW"], X["tpW"] = getbank()
                S.op("pe", lambda E: E.matmul(X["pW"][:], lhsT=w2a2[0:64, d, 128 * c:128 * c + 128], rhs=tz[0:64, :],
                                              start=True, stop=True), reads=[t_c, t_tz], writes=[X["tpW"]])
                X["pA"], X["tpA"] = getbank()
                S.op("pe", lambda E: E.matmul(X["pA"][:], lhsT=w2a2[64:128, d, 128 * c:128 * c + 128], rhs=tz[64:128, :],
                                              start=True, stop=True), reads=[t_c, t_tz], writes=[X["tpA"]])
            ST(s_mm)

            def s_sig(c):
                X = cb[c]
                S.op("act", lambda E: E.activation(out=B1[c][:], in_=X["pW"][:], func=AF.Sigmoid, bias=w0c[:, d, c:c + 1]),
                     reads=[X["tpW"], t_c], writes=[tB1[c]])
                S.op("act", lambda E: E.activation(out=B6[c][:], in_=X["pA"][:], func=AF.Sigmoid, bias=a0c[:, d, c:c + 1]),
                     reads=[X["tpA"], t_c], writes=[tB6[c]])
            ST(s_sig)

            def s_kkv(c):
                X = cb[c]
                S.op("dve", lambda E: E.tensor_scalar(out=B4[c][:], in0=X["zk"], scalar1=kkc[:, c:c + 1], scalar2=None,
                                                      op0=ALU.mult), reads=[X["tk"], t_c], writes=[tB4[c]])
                S.op("pool", lambda E: E.tensor_tensor(out=B5[c][:], in0=B4[c][:], in1=B4[c][:], op=ALU.mult),
                     reads=[tB4[c]], writes=[tB5[c]])
                X["pN"], X["tpN"] = getbank()
                S.op("pe", lambda E: E.matmul(X["pN"][:], lhsT=onesb[:], rhs=B5[c][:], start=True, stop=True),
                     reads=[t_c, tB5[c]], writes=[X["tpN"]])
            ST(s_kkv)

            def s_cls(c):
                S.op("dve", lambda E: E.tensor_tensor_scan(out=B2[c][:], data0=rst[:], data1=B1[c][:], initial=0.0,
                                                           op0=ALU.mult, op1=ALU.add),
                     reads=[tB1[c], t_c], writes=[tB2[c]])
                S.op("pool", lambda E: E.tensor_tensor(out=B1[c][:], in0=B2[c][:], in1=B1[c][:], op=ALU.subtract),
                     reads=[tB2[c]], writes=[tB1[c]])
            ST(s_cls)

            def s_exp(c):
                S.op("act", lambda E: E.activation(out=B3[c][:], in_=B2[c][:], func=AF.Exp, scale=DEC),
                     reads=[tB2[c]], writes=[tB3[c]])
                S.op("act", lambda E: E.activation(out=B2[c][:], in_=B2[c][:], func=AF.Exp, scale=-DEC),
                     reads=[tB2[c]], writes=[tB2[c]])
                S.op("act", lambda E: E.activation(out=B1[c][:], in_=B1[c][:], func=AF.Exp, scale=-DEC),
                     reads=[tB1[c]], writes=[tB1[c]])
            ST(s_exp)

            def s_rn(c):
                X = cb[c]
                S.op("dve", lambda E: E.tensor_scalar(out=B5[c][:], in0=X["pN"][:], scalar1=1e-12, scalar2=None,
                                                      op0=ALU.add), reads=[X["tpN"]], writes=[tB5[c]])
                S.op("act", lambda E: E.activation(out=B5[c][:], in_=B5[c][:], func=AF.Sqrt),
                     reads=[tB5[c]], writes=[tB5[c]])
                S.op("dve", lambda E: E.reciprocal(out=B5[c][:], in_=B5[c][:]),
                     reads=[tB5[c]], writes=[tB5[c]])
                S.op("dve", lambda E: E.tensor_tensor(out=B4[c][:], in0=B4[c][:], in1=B5[c][:], op=ALU.mult),
                     reads=[tB4[c], tB5[c]], writes=[tB4[c]])
                S.op("dve", lambda E: E.tensor_copy(out=WCt[:, c, :], in_=B2[c][:, 127:512:128]),
                     reads=[tB2[c]], writes=[t_WC])
            ST(s_rn)

            def s_kd(c):
                X = cb[c]
                S.op("dve", lambda E: E.tensor_scalar(out=B7[c][:], in0=B6[c][:], scalar1=kac[:, c:c + 1],
                                                      scalar2=omka[:, c:c + 1], op0=ALU.mult, op1=ALU.add),
                     reads=[tB6[c], t_c], writes=[tB7[c]])
                S.op("dve", lambda E: E.tensor_tensor(out=B7[c][:], in0=B7[c][:], in1=X["zk"], op=ALU.mult),
                     reads=[tB7[c], X["tk"]], writes=[tB7[c]])
                S.op("pool", lambda E: E.tensor_tensor(out=B6[c][:], in0=B4[c][:], in1=B6[c][:], op=ALU.mult),
                     reads=[tB4[c], tB6[c]], writes=[tB6[c]])
            ST(s_kd)

            def s_ar(c):
                X = cb[c]
                S.op("dve", lambda E: E.scalar_tensor_tensor(out=ARb[c][:, :, 0, :], in0=v4(B4[c][:]), scalar=-1.0,
                                                             in1=v4(B1[c][:]), op0=ALU.mult, op1=ALU.mult),
                     reads=[tB4[c], tB1[c]], writes=[t_AR[c]])
                S.op("pool", lambda E: E.tensor_tensor(out=ARb[c][:, :, 1, :], in0=v4(X["zr"]), in1=v4(B2[c][:]),
                                                       op=ALU.mult), reads=[X["tr"], tB2[c]], writes=[t_AR[c]])
                S.op("pool", lambda E: E.tensor_tensor(out=Bt[c][:], in0=B6[c][:], in1=B3[c][:], op=ALU.mult),
                     reads=[tB6[c], tB3[c]], writes=[t_Bt[c]])
                S.op("pool", lambda E: E.tensor_tensor(out=Kt[c][:], in0=B7[c][:], in1=B3[c][:], op=ALU.mult),
                     reads=[tB7[c], tB3[c]], writes=[t_Kt[c]])
                S.op("act", lambda E: E.activation(out=vb[c][:], in_=X["zv"], func=AF.Copy),
                     reads=[X["tv"]], writes=[t_vb[c]])
            ST(s_ar)

            def s_bb(c):
                wcb = WCt[:, c, :].unsqueeze(2).to_broadcast([128, 4, 128])
                S.op("pool", lambda E: E.tensor_tensor(out=v4(Bb[c][:]), in0=v4(Bt[c][:]), in1=wcb, op=ALU.mult),
                     reads=[t_Bt[c], t_WC], writes=[t_Bb[c]])
                S.op("pool", lambda E: E.tensor_tensor(out=v4(Kb[c][:]), in0=v4(Kt[c][:]), in1=wcb, op=ALU.mult),
                     reads=[t_Kt[c], t_WC], writes=[t_Kb[c]])
            ST(s_bb)

            def s_bonus(c):
                X = cb[c]
                S.op("dve", lambda E: E.scalar_tensor_tensor(out=B5[c][:], in0=X["zr"], scalar=rkc[:, c:c + 1],
                                                             in1=B7[c][:], op0=ALU.mult, op1=ALU.mult),
                     reads=[X["tr"], tB7[c], t_c], writes=[tB5[c]])
                pBn, t_pBn = getbank()
                S.op("pe", lambda E: E.matmul(pBn[:], lhsT=onesb[:], rhs=B5[c][:], start=True, stop=True),
                     reads=[t_c, tB5[c]], writes=[t_pBn])
                si = sgi[0] % 3
                sgi[0] += 1
                S.op("dve", lambda E: E.tensor_tensor(out=stg[si][:, osl], in0=pBn[:], in1=X["zv"], op=ALU.mult),
                     reads=[t_pBn, X["tv"]], writes=[t_stg[si]])
                S.dma("sp", bon[d][128 * c:128 * c + 128, lo:hi], stg[si][:], reads=[t_stg[si]], writes=[t_out])
                if d == 0:
                    pG, t_pG = getbank()
                    S.op("pe", lambda E: E.matmul(pG[:], lhsT=g2b[:, 128 * c:128 * c + 128], rhs=sgz[:], start=True,
                                                  stop=True), reads=[t_c, t_sgz], writes=[t_pG])
                    si2 = sgi[0] % 3
                    sgi[0] += 1
                    S.op("act", lambda E: E.activation(out=stg[si2][:], in_=pG[:], func=AF.Copy),
                         reads=[t_pG], writes=[t_stg[si2]])
                    S.dma("sp", gfm[128 * c:128 * c + 128, lo:hi], stg[si2][:], reads=[t_stg[si2]], writes=[t_out])
            ST(s_bonus)

            def s_tr(c):
                for q in range(4):
                    ts_ = trr[0] % 2
                    trr[0] += 1
                    srcs = [(ARb[c][:, q, 0, :], t_AR[c]), (Bb[c][:, 128 * q:128 * q + 128], t_Bb[c]),
                            (Kb[c][:, 128 * q:128 * q + 128], t_Kb[c]), (vb[c][:, 128 * q:128 * q + 128], t_vb[c])]
                    for ai, (src, tk) in enumerate(srcs):
                        S.op("pe", lambda E, ts_=ts_, ai=ai, src=src: E.transpose(out=pTr[ts_][:, ai, :], in_=src,
                                                                                 identity=ident[:]),
                             reads=[tk, t_id], writes=[t_pTr[ts_]])
                    if q % 2 == 0:
                        S.op("act", lambda E, ts_=ts_, q=q: E.activation(out=tm[c][q][:], in_=pTr[ts_][:], func=AF.Copy),
                             reads=[t_pTr[ts_]], writes=[t_tm[c][q]])
                    else:
                        S.op("dve", lambda E, ts_=ts_, q=q: E.tensor_copy(out=tm[c][q][:], in_=pTr[ts_][:]),
                             reads=[t_pTr[ts_]], writes=[t_tm[c][q]])
            ST(s_tr)
            for st_ in steps:
                for c in range(3):
                    st_(c)
            yb = bi % 2
            def chunk_gen(q):
                MP, t_MP, MT, t_MT = MPs[q % 2], t_MPs[q % 2], MTs[q % 2], t_MTs[q % 2]
                qs = slice(128 * q, 128 * q + 128)
                for h in range(6):
                    c, h2 = h // 2, h % 2
                    rows = slice(64 * h2, 64 * h2 + 64)
                    pM, t_pM = getbank()
                    S.op("pe", lambda E, pM=pM, c=c, rows=rows, qs=qs, q=q: E.matmul(
                        pM[:, 0:256], lhsT=Bt[c][rows, qs], rhs=ARb[c][rows, q, :, :].rearrange("p a b -> p (a b)"), start=True, stop=True),
                        reads=[t_Bt[c], t_AR[c]], writes=[t_pM])
                    S.op("pe", lambda E, pM=pM, c=c, rows=rows, qs=qs, q=q: E.matmul(
                        pM[:, 256:512], lhsT=Kt[c][rows, qs], rhs=ARb[c][rows, q, :, :].rearrange("p a b -> p (a b)"), start=True, stop=True),
                        reads=[t_Kt[c], t_AR[c]], writes=[t_pM])
                    S.op("dve", lambda E, pM=pM, h=h: E.tensor_tensor(out=MP[h][:], in0=pM[:], in1=mskU[:], op=ALU.mult),
                         reads=[t_pM, t_c], writes=[t_MP[h]])
                for hg in range(2):
                    pM3, t_pM3 = getbank()
                    for j in range(3):
                        h = 2 * j + hg
                        c, h2 = j, hg
                        rows = slice(64 * h2, 64 * h2 + 64)
                        S.op("pe", lambda E, pM3=pM3, j=j, c=c, rows=rows, qs=qs, q=q: E.matmul(
                            pM3[:, 128 * j:128 * j + 128], lhsT=ARb[c][rows, q, 0, :], rhs=Bt[c][rows, qs],
                            start=True, stop=True), reads=[t_Bt[c], t_AR[c]], writes=[t_pM3])
                    S.op("dve", lambda E, pM3=pM3, hg=hg: E.tensor_tensor(
                        out=MT[hg][:], in0=pM3[:, 0:384].rearrange("p (a b) -> p a b", a=3), in1=mskL[:], op=ALU.mult),
                        reads=[t_pM3, t_c], writes=[t_MT[hg]])
                yield "scores"
                cur = [0, 0]
                for hg in range(2):
                    for j in range(3):
                        h = 2 * j + hg
                        S.op("pool", lambda E, hg=hg, j=j, h=h: E.tensor_tensor(out=Tm[hg][0][:, j, :], in0=MP[h][:, 0:128],
                                                                              in1=identf[:], op=ALU.add),
                             reads=[t_MP[h], t_id], writes=[t_Tm[hg][0]])
                v3 = lambda ap: ap[:, 0:384].rearrange("p (a b) -> p a b", a=3)
                yield "t0"
                for lev in range(1, 7):
                    if lev > 1:
                        yield "lev"
                    bk = {}
                    for hg in range(2):
                        pv = cur[hg]
                        pP, t_pP = getbank()
                        pPT, t_pPT = getbank()
                        bk[hg] = (pP, t_pP, pPT, t_pPT)
                        for j in range(3):
                            h = 2 * j + hg
                            if lev == 1:
                                Pprev, tP = MP[h][:, 0:128], t_MP[h]
                                PTprev, tPT = MT[hg][:, j, :], t_MT[hg]
                            else:
                                Pprev, tP = Pm[hg][pv][:, j, :], t_Pm[hg][pv]
                                PTprev, tPT = PmT[hg][pv][:, j, :], t_PmT[hg][pv]
                            if lev < 6:
                                S.op("pe", lambda E, pP=pP, j=j, Pprev=Pprev, PTprev=PTprev: E.matmul(
                                    pP[:, 128 * j:128 * j + 128], lhsT=PTprev, rhs=Pprev, start=True, stop=True),
                                    reads=[tP, tPT], writes=[t_pP])
                            S.op("pe", lambda E, pPT=pPT, j=j, Pprev=Pprev, PTprev=PTprev: E.matmul(
                                pPT[:, 128 * j:128 * j + 128], lhsT=Pprev, rhs=PTprev, start=True, stop=True),
                                reads=[tP, tPT], writes=[t_pPT])
                    for hg in range(2):
                        pP, t_pP, pPT, t_pPT = bk[hg]
                        nx = 1 - cur[hg]
                        if lev < 6:
                            S.op("act", lambda E, pP=pP, hg=hg, nx=nx: E.activation(out=Pm[hg][nx][:], in_=v3(pP),
                                                                                    func=AF.Copy),
                                 reads=[t_pP], writes=[t_Pm[hg][nx]])
                        S.op("dve", lambda E, pPT=pPT, hg=hg, nx=nx: E.tensor_copy(out=PmT[hg][nx][:], in_=v3(pPT)),
                             reads=[t_pPT], writes=[t_PmT[hg][nx]])
                    bt = {}
                    for hg in range(2):
                        pv = cur[hg]
                        nx = 1 - pv
                        pTT, t_pTT = getbank()
                        bt[hg] = (pTT, t_pTT)
                        for j in range(3):
                            S.op("pe", lambda E, pTT=pTT, j=j, hg=hg, nx=nx, pv=pv: E.matmul(
                                pTT[:, 128 * j:128 * j + 128], lhsT=PmT[hg][nx][:, j, :], rhs=Tm[hg][pv][:, j, :],
                                start=True, stop=True), reads=[t_PmT[hg][nx], t_Tm[hg][pv]], writes=[t_pTT])
                    for hg in range(2):
                        pv = cur[hg]
                        nx = 1 - pv
                        pTT, t_pTT = bt[hg]
                        S.op("dve", lambda E, pTT=pTT, hg=hg, nx=nx, pv=pv: E.tensor_tensor(
                            out=Tm[hg][nx][:], in0=v3(pTT), in1=Tm[hg][pv][:], op=ALU.add),
                            reads=[t_pTT, t_Tm[hg][pv]], writes=[t_Tm[hg][nx]])
                        cur[hg] = nx
                TF = [Tm[0][cur[0]], Tm[1][cur[1]]]
                tTF = [t_Tm[0][cur[0]], t_Tm[1][cur[1]]]
                pX, t_pX = getbank()
                for h in range(6):
                    c, h2 = h // 2, h % 2
                    sl_ = 3 * h2 + c
                    S.op("pe", lambda E, pX=pX, h=h, c=c, h2=h2, q=q, sl_=sl_: E.matmul(
                        pX[:, 64 * sl_:64 * sl_ + 64], lhsT=MP[h][:, 256:384], rhs=tm[c][q][:, 3, 64 * h2:64 * h2 + 64],
                        start=True, stop=True), reads=[t_MP[h], t_tm[c][q]], writes=[t_pX])
                S.op("act", lambda E, pX=pX: E.activation(out=X0[:], in_=pX[:, 0:384].rearrange("p (a b) -> p a b", a=6),
                                                         func=AF.Copy), reads=[t_pX], writes=[t_X0])
                pV, t_pV = getbank()
                for h in range(6):
                    c, h2 = h // 2, h % 2
                    sl_ = 3 * h2 + c
                    S.op("pe", lambda E, pV=pV, c=c, h2=h2, sl_=sl_: E.matmul(
                        pV[:, 64 * sl_:64 * sl_ + 64], lhsT=TF[h2][:, c, :], rhs=X0[:, sl_, :], start=True, stop=True),
                        reads=[tTF[h2], t_X0], writes=[t_pV])
                S.op("act", lambda E, pV=pV: E.activation(out=Uv[:], in_=pV[:, 0:384].rearrange("p (a b) -> p a b", a=6),
                                                         func=AF.Copy), reads=[t_pV], writes=[t_Uv])
                pH, t_pH = getbank()
                for h in range(6):
                    c, h2 = h // 2, h % 2
                    S.op("pe", lambda E, pH=pH, c=c, h2=h2, q=q: E.matmul(
                        pH[64 * h2:64 * h2 + 64, 128 * c:128 * c + 128], lhsT=tm[c][q][:, 0, 64 * h2:64 * h2 + 64],
                        rhs=TF[h2][:, c, :], start=True, stop=True), reads=[tTF[h2], t_tm[c][q]], writes=[t_pH])
                S.op("dve", lambda E, pH=pH: E.tensor_copy(out=Ahb[:], in_=pH[:, 0:384].rearrange("p (a b) -> p a b", a=3)),
                     reads=[t_pH], writes=[t_Ah])
                yield "post"
                pUb = [getbank(), getbank()]
                for h2 in range(2):
                    rows = slice(64 * h2, 64 * h2 + 64)
                    for c in range(3):
                        S.op("pe", lambda E, h2=h2, c=c, rows=rows: E.matmul(
                            pUb[h2][0][:, 64 * c:64 * c + 64], lhsT=Ahb[rows, c, :], rhs=Sb[rows, c, :], start=True,
                            stop=True), reads=[t_Ah, t_Sb], writes=[pUb[h2][1]])
                for h2 in range(2):
                    S.op("dve", lambda E, h2=h2: E.tensor_tensor(
                        out=Ub[:, 3 * h2:3 * h2 + 3, :], in0=pUb[h2][0][:, 0:192].rearrange("p (a b) -> p a b", a=3),
                        in1=Uv[:, 3 * h2:3 * h2 + 3, :], op=ALU.add),
                        reads=[pUb[h2][1], t_Uv], writes=[t_Ub])
                yield "q1"
                pYb = [getbank(), getbank()]
                for h2 in range(2):
                    rows = slice(64 * h2, 64 * h2 + 64)
                    for c in range(3):
                        h = 2 * c + h2
                        sl_ = 3 * h2 + c
                        oc = slice(64 * c, 64 * c + 64)
                        S.op("pe", lambda E, h2=h2, c=c, rows=rows, q=q, oc=oc: E.matmul(
                            pYb[h2][0][:, oc], lhsT=ARb[c][rows, q, 1, :], rhs=Sb[rows, c, :], start=True, stop=False),
                            reads=[t_AR[c], t_Sb], writes=[pYb[h2][1]])
                        S.op("pe", lambda E, h2=h2, h=h, sl_=sl_, oc=oc: E.matmul(
                            pYb[h2][0][:, oc], lhsT=MP[h][:, 128:256], rhs=Ub[:, sl_, :], start=False, stop=False),
                            reads=[t_MP[h], t_Ub], writes=[pYb[h2][1]])
                        S.op("pe", lambda E, h2=h2, h=h, c=c, q=q, oc=oc: E.matmul(
                            pYb[h2][0][:, oc], lhsT=MP[h][:, 384:512], rhs=tm[c][q][:, 3, 64 * h2:64 * h2 + 64],
                            start=False, stop=True), reads=[t_MP[h], t_tm[c][q]], writes=[pYb[h2][1]])
                for h2 in range(2):
                    S.op("act", lambda E, h2=h2, yb=yb, q=q: E.activation(
                        out=ytm[yb][:, q, :].rearrange("p (c g v) -> p g c v", g=2, v=64)[:, h2, :, :],
                        in_=pYb[h2][0][:, 0:192].rearrange("p (a b) -> p a b", a=3), func=AF.Copy),
                        reads=[pYb[h2][1]], writes=[t_ytm[yb]])
                pS_, t_pS = getbank()
                for h in range(6):
                    c, h2 = h // 2, h % 2
                    orow = slice(64 * h2, 64 * h2 + 64)
                    S.op("pe", lambda E, pS_=pS_, h=h, c=c, h2=h2, orow=orow, q=q: E.matmul(
                        pS_[orow, 64 * c:64 * c + 64], lhsT=tm[c][q][:, 1, 64 * h2:64 * h2 + 64], rhs=Ub[:, 3 * h2 + c, :],
                        start=True, stop=False), reads=[t_tm[c][q], t_Ub], writes=[t_pS])
                    S.op("pe", lambda E, pS_=pS_, h=h, c=c, h2=h2, orow=orow, q=q: E.matmul(
                        pS_[orow, 64 * c:64 * c + 64], lhsT=tm[c][q][:, 2, 64 * h2:64 * h2 + 64],
                        rhs=tm[c][q][:, 3, 64 * h2:64 * h2 + 64], start=False, stop=True),
                        reads=[t_tm[c][q]], writes=[t_pS])
                for c in range(3):
                    S.op("dve", lambda E, pS_=pS_, c=c, q=q: E.scalar_tensor_tensor(
                        out=Sf[:, c, :], in0=Sf[:, c, :], scalar=WCt[:, c, q:q + 1], in1=pS_[:, 64 * c:64 * c + 64],
                        op0=ALU.mult, op1=ALU.add), reads=[t_pS, t_WC, t_Sf], writes=[t_Sf])
                S.op("act", lambda E: E.activation(out=Sb[:], in_=Sf[:], func=AF.Copy), reads=[t_Sf], writes=[t_Sb])
            def run_until(g, label):
                for lab in g:
                    if lab == label:
                        return True
                return False
            gens = [chunk_gen(q) for q in range(4)]
            run_until(gens[0], "post")
            for q in range(4):
                g, gn = gens[q], (gens[q + 1] if q + 1 < 4 else None)
                if gn is not None:
                    run_until(gn, "scores")
                run_until(g, "q1")
                if gn is not None:
                    run_until(gn, "t0")
                    run_until(gn, "lev")
                    run_until(gn, "lev")
                run_until(g, "never")
                if gn is not None:
                    run_until(gn, "post")
            lb = 512 * bi
            S.dma("sp", ydir[d][lb:lb + 512, :].rearrange("(q p) f -> p q f", p=128), ytm[yb][:], reads=[t_ytm[yb]],
                  writes=[t_out])
    P.close()


LNX_EPS = 64e-5


def phase_rwkv_combine(nc, S, ydir, bon, gfm, lnx_w, lnx_b, mixT, T):
    P = Phase(nc, S)
    NT = T // 128
    ident, identf, t_id = make_ident(nc, S, P)
    t_c = Tok()
    J = P.sb([128, 128], F32)
    S.op("pool", lambda E: E.memset(J[:], 1.0), writes=[t_c])
    S.op("pool", lambda E: E.affine_select(out=J[:], in_=J[:], pattern=[[1, 128]], base=-127, channel_multiplier=1,
                                           compare_op=ALU.is_equal, fill=0.0), reads=[t_c], writes=[t_c])
    lwt = P.sb([128, 384], F32); lbt = P.sb([128, 384], F32)
    S.dma("sp", lwt[:], lnx_w.partition_broadcast(128), writes=[t_c])
    S.dma("sp", lbt[:], lnx_b.partition_broadcast(128), writes=[t_c])
    mh = P.sb([128, 6], F32)
    S.op("pool", lambda E: E.memset(mh[:], -0.5), writes=[t_c])
    NBF = 3
    y0 = [P.sb([128, 384], F32) for _ in range(NBF)]; t_y0 = [Tok() for _ in range(NBF)]
    y1 = [P.sb([128, 384], F32) for _ in range(NBF)]; t_y1 = [Tok() for _ in range(NBF)]
    b0 = [P.sb([128, 3, 128], F32) for _ in range(NBF)]; t_b0 = [Tok() for _ in range(NBF)]
    b1 = [P.sb([128, 3, 128], F32) for _ in range(NBF)]; t_b1 = [Tok() for _ in range(NBF)]
    gt = [P.sb([128, 3, 128], F32) for _ in range(NBF)]; t_gt = [Tok() for _ in range(NBF)]
    ys = [P.sb([128, 6, 64], F32) for _ in range(NBF)]; t_ys = [Tok() for _ in range(NBF)]
    sq = [P.sb([128, 6, 64], F32) for _ in range(NBF)]; t_sq = [Tok() for _ in range(NBF)]
    st = [P.sb([128, 4, 6], F32) for _ in range(NBF)]; t_st = [Tok() for _ in range(NBF)]
    rs = [P.sb([128, 384], F32) for _ in range(NBF)]; t_rs = [Tok() for _ in range(NBF)]
    ob = [P.sb([128, 3, 128], BF16) for _ in range(NBF)]; t_ob = [Tok() for _ in range(NBF)]
    pJ = [P.ps([128, 512], F32) for _ in range(2)]; t_pJ = [Tok(), Tok()]
    pB = [P.ps([128, 512], F32) for _ in range(2)]; t_pB = [Tok(), Tok()]
    pG = [P.ps([128, 512], F32) for _ in range(2)]; t_pG = [Tok(), Tok()]
    pO = [P.ps([128, 512], F32) for _ in range(2)]; t_pO = [Tok(), Tok()]
    t_out = Tok()
    f3 = lambda ap: ap.rearrange("p (a b) -> p a b", a=6)
    def stA(n):
        b = n % NBF
        pb2 = n % 2
        tl = slice(128 * n, 128 * n + 128)
        S.dma("sp", y0[b][:], ydir[0][128 * n:128 * n + 128, :], writes=[t_y0[b]])
        S.dma("sp", y1[b][:], ydir[1][T - 128 * (n + 1):T - 128 * n, :], writes=[t_y1[b]])
        S.dma("sp", b0[b][:], bon[0][:, tl].rearrange("(c p) t -> p c t", p=128), writes=[t_b0[b]])
        S.dma("sp", b1[b][:], bon[1][:, tl].rearrange("(c p) t -> p c t", p=128), writes=[t_b1[b]])
        S.dma("sp", gt[b][:], gfm[:, tl].rearrange("(c p) t -> p c t", p=128), writes=[t_gt[b]])
        S.op("pe", lambda E, b=b: E.matmul(pJ[pb2][:, 0:384], lhsT=J[:], rhs=y1[b][:], start=True, stop=True),
             reads=[t_c, t_y1[b]], writes=[t_pJ[pb2]])
        S.op("dve", lambda E, b=b: E.tensor_tensor(out=ys[b][:], in0=f3(pJ[pb2][:, 0:384]), in1=f3(y0[b][:]), op=ALU.add),
             reads=[t_pJ[pb2], t_y0[b]], writes=[t_ys[b]])
        S.op("dve", lambda E, b=b: E.tensor_reduce(out=st[b][:, 0, :], in_=ys[b][:], axis=AX.X, op=ALU.add),
             reads=[t_ys[b]], writes=[t_st[b]])
        S.op("dve", lambda E, b=b: E.tensor_scalar(out=st[b][:, 0, :], in0=st[b][:, 0, :], scalar1=1.0 / 64, scalar2=None,
                                                   op0=ALU.mult), reads=[t_st[b]], writes=[t_st[b]])
        S.op("dve", lambda E, b=b: E.tensor_tensor(out=ys[b][:], in0=ys[b][:],
                                                   in1=st[b][:, 0, :].unsqueeze(2).to_broadcast([128, 6, 64]),
                                                   op=ALU.subtract), reads=[t_st[b], t_ys[b]], writes=[t_ys[b]])
        S.op("pool", lambda E, b=b: E.tensor_tensor(out=sq[b][:], in0=ys[b][:], in1=ys[b][:], op=ALU.mult),
             reads=[t_ys[b]], writes=[t_sq[b]])
        S.op("dve", lambda E, b=b: E.tensor_reduce(out=st[b][:, 1, :], in_=sq[b][:], axis=AX.X, op=ALU.add),
             reads=[t_sq[b]], writes=[t_st[b]])
        S.op("dve", lambda E, b=b: E.tensor_scalar(out=st[b][:, 2, :], in0=st[b][:, 1, :], scalar1=1.0 / 64,
                                                   scalar2=LNX_EPS, op0=ALU.mult, op1=ALU.add),
             reads=[t_st[b]], writes=[t_st[b]])
        S.op("pool", lambda E, b=b: E.tensor_tensor(out=st[b][:, 3, :], in0=st[b][:, 2, :], in1=mh[:], op=ALU.pow),
             reads=[t_st[b], t_c], writes=[t_st[b]])
        S.op("dve", lambda E, b=b: E.tensor_tensor(out=ys[b][:], in0=ys[b][:],
                                                   in1=st[b][:, 3, :].unsqueeze(2).to_broadcast([128, 6, 64]),
                                                   op=ALU.mult), reads=[t_st[b], t_ys[b]], writes=[t_ys[b]])
        yf = ys[b][:].rearrange("p a b -> p (a b)")
        S.op("pool", lambda E, b=b, yf=yf: E.tensor_tensor(out=yf, in0=yf, in1=lwt[:], op=ALU.mult),
             reads=[t_ys[b], t_c], writes=[t_ys[b]])
        S.op("pool", lambda E, b=b, yf=yf: E.tensor_tensor(out=yf, in0=yf, in1=lbt[:], op=ALU.add),
             reads=[t_ys[b], t_c], writes=[t_ys[b]])
        S.op("dve", lambda E, b=b: E.tensor_tensor(out=b0[b][:], in0=b0[b][:], in1=b1[b][:], op=ALU.add),
             reads=[t_b0[b], t_b1[b]], writes=[t_b0[b]])
    def stB(n):
        b = n % NBF
        pb2 = n % 2
        tl = slice(128 * n, 128 * n + 128)
        yf = ys[b][:].rearrange("p a b -> p (a b)")
        for c in range(3):
            S.op("pe", lambda E, b=b, c=c: E.transpose(out=pB[pb2][:, 128 * c:128 * c + 128], in_=b0[b][:, c, :],
                                                       identity=identf[:]),
                 reads=[t_b0[b], t_id], writes=[t_pB[pb2]])
        for c in range(3):
            S.op("pe", lambda E, b=b, c=c: E.transpose(out=pG[pb2][:, 128 * c:128 * c + 128], in_=gt[b][:, c, :],
                                                       identity=identf[:]),
                 reads=[t_gt[b], t_id], writes=[t_pG[pb2]])
        S.op("dve", lambda E, b=b, yf=yf: E.tensor_tensor(out=rs[b][:], in0=pB[pb2][:, 0:384], in1=yf, op=ALU.add),
             reads=[t_pB[pb2], t_ys[b]], writes=[t_rs[b]])
        S.op("dve", lambda E, b=b: E.tensor_tensor(out=rs[b][:], in0=pG[pb2][:, 0:384], in1=rs[b][:], op=ALU.mult),
             reads=[t_pG[pb2], t_rs[b]], writes=[t_rs[b]])
        for c in range(3):
            S.op("pe", lambda E, b=b, c=c: E.transpose(out=pO[pb2][:, 128 * c:128 * c + 128],
                                                       in_=rs[b][:, 128 * c:128 * c + 128], identity=identf[:]),
                 reads=[t_rs[b], t_id], writes=[t_pO[pb2]])
        S.op("act", lambda E, b=b: E.activation(out=ob[b][:], in_=pO[pb2][:, 0:384].rearrange("p (a b) -> p a b", a=3),
                                                func=AF.Copy), reads=[t_pO[pb2]], writes=[t_ob[b]])
        S.dma("sp", mixT[0:384, tl].rearrange("(c p) t -> p c t", p=128), ob[b][:], reads=[t_ob[b]], writes=[t_out])
    for n in range(NT + 1):
        if n < NT:
            stA(n)
        if n >= 1:
            stB(n - 1)
    P.close()


PARAM_SHAPES = {
    "ffn1_norm_g": [2, 1024], "ffn1_w_gate": [2, 1024, 2816], "ffn1_w_up": [2, 1024, 2816],
    "ffn1_w_down": [2, 2816, 1024], "mix_norm_g": [2, 1024], "w_in": [2, 1024, 2816], "w_out": [2, 1024, 1024],
    "rwkv_mu_prev": [2, 1408], "rwkv_mu_next": [2, 1408], "rwkv_decay_w0": [2, 2, 384],
    "rwkv_decay_w2": [2, 2, 64, 384], "rwkv_iclr_a0": [2, 2, 384], "rwkv_iclr_a2": [2, 2, 64, 384],
    "rwkv_gate_w2": [2, 128, 384], "rwkv_k_k": [2, 384], "rwkv_k_a": [2, 384], "rwkv_r_k": [2, 6, 64],
    "rwkv_lnx_w": [2, 384], "rwkv_lnx_b": [2, 384], "s5_a_re": [2, 2, 16, 64], "s5_a_im": [2, 2, 16, 64],
    "s5_log_step": [2, 2, 16], "s5_b_re": [2, 16, 64, 16], "s5_b_im": [2, 16, 64, 16],
    "s5_c_re": [2, 2, 16, 16, 64], "s5_c_im": [2, 2, 16, 16, 64], "s5_d": [2, 256], "s5_glu_w": [2, 256, 512],
    "s5_glu_b": [2, 512], "ffn2_norm_g": [2, 1024], "ffn2_w_gate": [2, 1024, 2816], "ffn2_w_up": [2, 1024, 2816],
    "ffn2_w_down": [2, 2816, 1024], "final_norm_g": [1024],
}
DEPTH = 2


def build_program(T, depth=DEPTH):
    nc = bass.Bass("TRN2", target_bir_lowering=False)
    x = nc.dram_tensor("x", [T, D], F32, kind="ExternalInput").ap()
    p = {k: nc.dram_tensor(k, list(s), F32, kind="ExternalInput").ap() for k, s in PARAM_SHAPES.items()}
    out = nc.dram_tensor("out", [T, D], F32, kind="ExternalOutput").ap()
    h = nc.dram_tensor("h_res", [T, D], F32).ap()
    zr = nc.dram_tensor("z_rwkv", [1408, T], F32).ap()
    qk = nc.dram_tensor("z_qk", [768, T], BF16).ap()
    vtm = nc.dram_tensor("z_v", [T, 384], BF16).ap()
    us5 = nc.dram_tensor("z_s5", [256, T], F32).ap()
    mixT = nc.dram_tensor("mixT", [1024, T], BF16).ap()
    ydir = nc.dram_tensor("y_dir", [2, T, 384], F32).ap()
    bon = nc.dram_tensor("bonus", [2, 384, T], F32).ap()
    gfm = nc.dram_tensor("gate", [384, T], F32).ap()
    S = SchedI(nc)
    NT = T // 128
    tk = [Tok() for _ in range(NT)]
    for l in range(depth):
        src = x if l == 0 else h
        tk2 = [Tok() for _ in range(NT)]
        phase_ffn(nc, S, src, h, p["ffn1_norm_g"][l], p["ffn1_w_gate"][l], p["ffn1_w_up"][l], p["ffn1_w_down"][l],
                  tk, tk2, T)
        tk = tk2
        phase_win(nc, S, h, p["mix_norm_g"][l], p["w_in"][l], zr, qk, vtm, us5, tk, T)
        phase_rwkv(nc, S, zr, ydir, bon, gfm,
                   [p["rwkv_mu_prev"][l], p["rwkv_mu_next"][l], p["rwkv_decay_w0"][l], p["rwkv_decay_w2"][l],
                    p["rwkv_iclr_a0"][l], p["rwkv_iclr_a2"][l], p["rwkv_gate_w2"][l], p["rwkv_k_k"][l],
                    p["rwkv_k_a"][l], p["rwkv_r_k"][l]], T)
        phase_rwkv_combine(nc, S, ydir, bon, gfm, p["rwkv_lnx_w"][l], p["rwkv_lnx_b"][l], mixT, T)
        phase_attn(nc, S, qk, vtm, mixT, T)
        phase_s5(nc, S, us5, mixT,
                 [p["s5_a_re"][l], p["s5_a_im"][l], p["s5_log_step"][l], p["s5_b_re"][l], p["s5_b_im"][l],
                  p["s5_c_re"][l], p["s5_c_im"][l], p["s5_d"][l], p["s5_glu_w"][l], p["s5_glu_b"][l]], T)
        tk2 = [Tok() for _ in range(NT)]
        phase_wout(nc, S, h, h, mixT, p["w_out"][l], tk, tk2, T)
        tk = tk2
        tk2 = [Tok() for _ in range(NT)]
        phase_ffn(nc, S, h, h, p["ffn2_norm_g"][l], p["ffn2_w_gate"][l], p["ffn2_w_up"][l], p["ffn2_w_down"][l],
                  tk, tk2, T)
        tk = tk2
    tko = [Tok() for _ in range(NT)]
    phase_final(nc, S, h, out, p["final_norm_g"], tk, tko, T)
    S.finish()
    return nc, S


def kernel(**inputs):
    x = np.ascontiguousarray(np.asarray(inputs["x"], dtype=np.float32))
    B, T, _ = x.shape
    nc, S = build_program(T)
    params = {k: np.ascontiguousarray(np.asarray(inputs[k], dtype=np.float32)) for k in PARAM_SHAPES}
    in_maps = []
    for b in range(B):
        m = {"x": x[b]}
        m.update(params)
        in_maps.append(m)
    res = run_bass_kernel_spmd(nc, in_maps, core_ids=list(range(B)))
    return np.stack([np.asarray(r["out"], dtype=np.float32) for r in res.results], axis=0)
```

```python
import numpy as np
import concourse.bass as bass
import concourse.mybir as mybir
from concourse.bass_utils import run_bass_kernel_spmd

F32 = mybir.dt.float32
BF16 = mybir.dt.bfloat16
I32 = mybir.dt.int32
AF = mybir.ActivationFunctionType
ALU = mybir.AluOpType
AX = mybir.AxisListType

ENGS = ("pe", "act", "dve", "pool", "sp")


class Tok:
    __slots__ = ("w", "r", "name")

    def __init__(self, name=""):
        self.w = None
        self.r = {}
        self.name = name


class Sched:
    def __init__(self, nc, lanes_sp=8, lanes_pool=6, lanes_act=2, same_engine_sync=True):
        self.nc = nc
        self.ops = {e: [] for e in ENGS}
        self.cnt = {}
        self.sems = {}
        self.seen = {e: {} for e in ENGS}
        self.same = same_engine_sync
        self._ctx = []
        for e in ("pe", "act", "dve", "pool"):
            self._mksem(e)
        self.lanes = {"sp": [], "pool": [], "act": []}
        for q, n in (("sp", lanes_sp), ("pool", lanes_pool), ("act", lanes_act)):
            for i in range(n):
                nm = f"ln_{q}{i}"
                self._mksem(nm)
                self.lanes[q].append(nm)
        self.lane_rr = {"sp": 0, "pool": 0, "act": 0}
        self.n_instr = 0

    def _mksem(self, name):
        cm = self.nc.semaphore(name)
        s = cm.__enter__()
        self._ctx.append(cm)
        self.sems[name] = s
        self.cnt[name] = 0

    def _collect(self, eng, reads, writes):
        need = {}

        def add(src, val):
            if src == eng and (eng == "pe" or not self.same or eng == "sp"):
                return
            if need.get(src, 0) < val:
                need[src] = val
        for t in reads:
            if t.w is not None:
                add(*t.w)
        for t in writes:
            if t.w is not None:
                add(*t.w)
            for s, v in t.r.items():
                add(s, v)
        out = []
        seen = self.seen[eng]
        for s, v in need.items():
            if seen.get(s, 0) < v:
                seen[s] = v
                out.append((self.sems[s], v))
        return out

    def op(self, eng, fn, reads=(), writes=()):
        waits = self._collect(eng, reads, writes)
        self.cnt[eng] += 1
        c = self.cnt[eng]
        sem = self.sems[eng]

        def emit(E, waits=waits, fn=fn, sem=sem):
            for s, v in waits:
                E.wait_ge(s, v)
            fn(E).then_inc(sem, 1)
        self.ops[eng].append(emit)
        for t in reads:
            t.r[eng] = c
        for t in writes:
            t.w = (eng, c)
            t.r = {}
        self.n_instr += 1

    def dma(self, q, out, in_, reads=(), writes=(), **kw):
        lanes = self.lanes[q]
        ln = lanes[self.lane_rr[q] % len(lanes)]
        self.lane_rr[q] += 1
        waits = self._collect(q, reads, writes)
        prev = self.cnt[ln]
        if prev and self.seen[q].get(ln, 0) < prev:
            self.seen[q][ln] = prev
            waits.append((self.sems[ln], prev))
        self.cnt[ln] += 16
        c = self.cnt[ln]
        sem = self.sems[ln]

        def emit(E, waits=waits, sem=sem, out=out, in_=in_, kw=kw):
            for s, v in waits:
                E.wait_ge(s, v)
            E.dma_start(out=out, in_=in_, **kw).then_inc(sem, 16)
        self.ops[q].append(emit)
        for t in reads:
            t.r[ln] = c
        for t in writes:
            t.w = (ln, c)
            t.r = {}
        self.n_instr += 1

    def finish(self, final_toks):
        nc = self.nc
        fin = []
        need = {}
        for t in final_toks:
            if t.w is not None and need.get(t.w[0], 0) < t.w[1]:
                need[t.w[0]] = t.w[1]
        for s, v in self.cnt.items():
            if v and need.get(s, 0) < v:
                need[s] = v
        for s, v in need.items():
            fin.append((self.sems[s], v))
        ops = self.ops
        with nc.Block() as block:
            @block.tensor
            def _(E):
                for f in ops["pe"]:
                    f(E)

            @block.scalar
            def _(E):
                for f in ops["act"]:
                    f(E)

            @block.vector
            def _(E):
                for f in ops["dve"]:
                    f(E)

            @block.gpsimd
            def _(E):
                for f in ops["pool"]:
                    f(E)

            @block.sync
            def _(E):
                for f in ops["sp"]:
                    f(E)
                for s, v in fin:
                    E.wait_ge(s, v)
        for cm in reversed(self._ctx):
            cm.__exit__(None, None, None)


class Alloc:
    def __init__(self, nc):
        self.nc = nc
        self._ctx = []

    def sb(self, name, shape, dt):
        cm = self.nc.sbuf_tensor(name, list(shape), dt)
        t = cm.__enter__()
        self._ctx.append(cm)
        return t

    def ps(self, name, shape, dt):
        cm = self.nc.psum_tensor(name, list(shape), dt)
        t = cm.__enter__()
        self._ctx.append(cm)
        return t

    def close(self):
        for cm in reversed(self._ctx):
            cm.__exit__(None, None, None)


class SchedI(Sched):
    def __init__(self, nc, **kw):
        super().__init__(nc, **kw)
        self.E = {"pe": nc.tensor, "act": nc.scalar, "dve": nc.vector, "pool": nc.gpsimd, "sp": nc.sync}

    limit = 10 ** 9

    def op(self, eng, fn, reads=(), writes=()):
        if self.n_instr >= self.limit:
            return
        waits = self._collect(eng, reads, writes)
        self.cnt[eng] += 1
        c = self.cnt[eng]
        E = self.E[eng]
        for s, v in waits:
            E.wait_ge(s, v)
        fn(E).then_inc(self.sems[eng], 1)
        for t in reads:
            t.r[eng] = c
        for t in writes:
            t.w = (eng, c)
            t.r = {}
        self.n_instr += 1

    def dma(self, q, out, in_, reads=(), writes=(), **kw):
        if self.n_instr >= self.limit:
            return
        lanes = self.lanes[q]
        ln = lanes[self.lane_rr[q] % len(lanes)]
        self.lane_rr[q] += 1
        waits = self._collect(q, reads, writes)
        prev = self.cnt[ln]
        if prev and self.seen[q].get(ln, 0) < prev:
            self.seen[q][ln] = prev
            waits.append((self.sems[ln], prev))
        self.cnt[ln] += 16
        c = self.cnt[ln]
        E = self.E[q]
        for s, v in waits:
            E.wait_ge(s, v)
        E.dma_start(out=out, in_=in_, **kw).then_inc(self.sems[ln], 16)
        for t in reads:
            t.r[ln] = c
        for t in writes:
            t.w = (ln, c)
            t.r = {}
        self.n_instr += 1

    def barrier(self):
        for e in ("pe", "act", "dve", "pool", "sp"):
            E = self.E[e]
            for s, v in self.cnt.items():
                if v and s != e and self.seen[e].get(s, 0) < v:
                    self.seen[e][s] = v
                    E.wait_ge(self.sems[s], v)
                if s == e and v and e != "sp":
                    if self.seen[e].get(s, 0) < v:
                        self.seen[e][s] = v
                        E.wait_ge(self.sems[s], v)

    def finish(self, final_toks=()):
        self.barrier()
        for cm in reversed(self._ctx):
            cm.__exit__(None, None, None)


from contextlib import ExitStack

D = 1024
DFF = 2816
NFF = DFF // 128
KD = D // 128


class Phase:
    _uid = [0]

    def __init__(self, nc, S):
        self.nc, self.S = nc, S
        self.es = ExitStack()
        self.n = 0
        Phase._uid[0] += 1
        self.uid = Phase._uid[0]

    def sb(self, shape, dt, name=None):
        self.n += 1
        return self.es.enter_context(self.nc.sbuf_tensor(name or f"t{self.uid}_{self.n}", list(shape), dt))

    def ps(self, shape, dt, name=None):
        self.n += 1
        return self.es.enter_context(self.nc.psum_tensor(name or f"p{self.uid}_{self.n}", list(shape), dt))

    def close(self):
        self.S.barrier()
        self.es.close()


def make_ident(nc, S, P, dt=BF16):
    identf = P.sb([128, 128], F32)
    ident = P.sb([128, 128], dt)
    t = Tok()
    S.op("pool", lambda E: E.memset(identf[:], 1.0), writes=[t])
    S.op("pool", lambda E: E.affine_select(out=identf[:], in_=identf[:], pattern=[[-1, 128]], base=0,
                                           channel_multiplier=1, compare_op=ALU.is_equal, fill=0.0),
         reads=[t], writes=[t])
    S.op("dve", lambda E: E.tensor_copy(out=ident[:], in_=identf[:]), reads=[t], writes=[t])
    return ident, identf, t


def load_w_bf16(S, dst, src, tok, rows_per=128, col_split=2):
    K = dst.shape[1]
    N = dst.shape[2]
    cs = N // col_split
    for k in range(K):
        for c in range(col_split):
            S.dma("pool", dst[:, k, c * cs:(c + 1) * cs], src[k * 128:(k + 1) * 128, c * cs:(c + 1) * cs],
                  writes=[tok])


def rms_prep(S, P, ht, t_h, s, gt, t_g, xn, t_xn, junk, t_junk, stat, t_stat, mhalf, t_mh):
    S.op("dve", lambda E: E.scalar_tensor_tensor(out=junk[:], in0=ht[:, s, :], scalar=1.0 / D, in1=ht[:, s, :],
                                                 op0=ALU.mult, op1=ALU.mult, accum_out=stat[:, 0:1]),
         reads=[t_h], writes=[t_junk, t_stat])
    S.op("dve", lambda E: E.tensor_scalar(out=stat[:, 1:2], in0=stat[:, 0:1], scalar1=1e-6, scalar2=None,
                                          op0=ALU.add), reads=[t_stat], writes=[t_stat])
    S.op("pool", lambda E: E.tensor_tensor(out=stat[:, 2:3], in0=stat[:, 1:2], in1=mhalf[:, 0:1], op=ALU.pow),
         reads=[t_stat, t_mh], writes=[t_stat])
    S.op("dve", lambda E: E.scalar_tensor_tensor(out=xn[:], in0=ht[:, s, :], scalar=stat[:, 2:3], in1=gt[:],
                                                 op0=ALU.mult, op1=ALU.mult),
         reads=[t_h, t_stat, t_g], writes=[t_xn])


def phase_ffn(nc, S, h_in, h_out, g, wg, wu, wd, toks_in, toks_out, T):
    P = Phase(nc, S)
    NT = T // 128
    NS = 4
    NSUP = NT // NS
    hv_in = h_in.rearrange("(n p) d -> p n d", p=128)
    hv_out = h_out.rearrange("(n p) d -> p n d", p=128)
    ident, _, t_id = make_ident(nc, S, P)
    wg_b = P.sb([128, KD, DFF], BF16); t_wg = Tok()
    wu_b = P.sb([128, KD, DFF], BF16); t_wu = Tok()
    wd_b = P.sb([128, NFF, D], BF16); t_wd = Tok()
    load_w_bf16(S, wg_b, wg, t_wg)
    load_w_bf16(S, wu_b, wu, t_wu)
    load_w_bf16(S, wd_b, wd, t_wd, col_split=1)
    gt = P.sb([128, D], F32); t_g = Tok()
    S.dma("sp", gt[:], g.partition_broadcast(128), writes=[t_g])
    mhalf = P.sb([128, 1], F32); t_mh = Tok()
    S.op("pool", lambda E: E.memset(mhalf[:], -0.5), writes=[t_mh])
    ht = [P.sb([128, NS, D], F32)] * 2; t_ht = [[Tok() for _ in range(NS)]] * 2
    rl = [P.sb([128, 512], F32) for _ in range(4)]; t_rl = [Tok() for _ in range(4)]
    xn = [P.sb([128, D], BF16) for _ in range(2)]; t_xn = [Tok(), Tok()]
    junk = P.sb([128, D], BF16); t_junk = Tok()
    stat = [P.sb([128, 4], F32) for _ in range(2)]; t_stat = [Tok(), Tok()]
    xnT = [P.sb([128, KD, NS * 128], BF16) for _ in range(2)]; t_xnT = [Tok(), Tok()]
    hT = P.sb([128, NFF, NS * 128], BF16); t_hT = [Tok() for _ in range(NFF)]
    sg = [P.sb([128, NS * 128], BF16) for _ in range(2)]; t_sg = [Tok(), Tok()]
    pT = [P.ps([128, KD, 128], BF16) for _ in range(2)]; t_pT = [Tok(), Tok()]
    pG = [P.ps([128, 512], F32) for _ in range(2)]; t_pG = [Tok(), Tok()]
    pU = [P.ps([128, 512], F32) for _ in range(2)]; t_pU = [Tok(), Tok()]
    pD = [P.ps([128, 512], F32) for _ in range(2)]; t_pD = [Tok(), Tok()]
    itc = [0]

    def load(st):
        hb = st % 2
        for s in range(NS):
            n = st * NS + s
            S.dma("sp", ht[hb][:, s, :], hv_in[:, n, :], reads=[toks_in[n]], writes=[t_ht[hb][s]])

    def prep(st):
        hb = st % 2
        for s in range(NS):
            b = s % 2
            rms_prep(S, P, ht[hb], t_ht[hb][s], s, gt, t_g, xn[b], t_xn[b], junk, t_junk, stat[b], t_stat[b], mhalf,
                     t_mh)
            for k in range(KD):
                S.op("pe", lambda E, k=k, b=b: E.transpose(out=pT[b][:, k, :], in_=xn[b][:, k * 128:(k + 1) * 128],
                                                           identity=ident[:]),
                     reads=[t_xn[b], t_id], writes=[t_pT[b]])
            S.op("dve", lambda E, b=b, s=s, hb=hb: E.tensor_copy(out=xnT[hb][:, :, s * 128:(s + 1) * 128], in_=pT[b][:]),
                 reads=[t_pT[b]], writes=[t_xnT[hb]])

    def gateup(st):
        hb = st % 2
        for f in range(NFF):
            b = f % 2
            for k in range(KD):
                S.op("pe", lambda E, k=k, f=f, b=b: E.matmul(pG[b][:], lhsT=wg_b[:, k, f * 128:(f + 1) * 128],
                                                             rhs=xnT[hb][:, k, :], start=(k == 0), stop=(k == KD - 1)),
                     reads=[t_wg, t_xnT[hb]], writes=[t_pG[b]])
            for k in range(KD):
                S.op("pe", lambda E, k=k, f=f, b=b: E.matmul(pU[b][:], lhsT=wu_b[:, k, f * 128:(f + 1) * 128],
                                                             rhs=xnT[hb][:, k, :], start=(k == 0), stop=(k == KD - 1)),
                     reads=[t_wu, t_xnT[hb]], writes=[t_pU[b]])
            S.op("act", lambda E, b=b: E.activation(out=sg[b][:], in_=pG[b][:], func=AF.Silu),
                 reads=[t_pG[b]], writes=[t_sg[b]])
            S.op("dve", lambda E, b=b, f=f: E.tensor_tensor(out=hT[:, f, :], in0=pU[b][:], in1=sg[b][:], op=ALU.mult),
                 reads=[t_pU[b], t_sg[b]], writes=[t_hT[f]])

    def down(st):
        for s in range(NS):
            n = st * NS + s
            for c in range(2):
                b = itc[0] % 2
                r4 = itc[0] % 4
                itc[0] += 1
                S.dma("sp", rl[r4][:], hv_in[:, n, c * 512:(c + 1) * 512], reads=[toks_in[n]], writes=[t_rl[r4]])
                for f in range(NFF):
                    S.op("pe", lambda E, f=f, s=s, c=c, b=b: E.matmul(
                        pD[b][:], lhsT=hT[:, f, s * 128:(s + 1) * 128], rhs=wd_b[:, f, c * 512:(c + 1) * 512],
                        start=(f == 0), stop=(f == NFF - 1)),
                        reads=[t_wd, t_hT[f]], writes=[t_pD[b]])
                S.op("dve", lambda E, b=b, r4=r4: E.scalar_tensor_tensor(
                    out=rl[r4][:], in0=pD[b][:], scalar=0.5, in1=rl[r4][:], op0=ALU.mult, op1=ALU.add),
                    reads=[t_pD[b], t_rl[r4]], writes=[t_rl[r4]])
                S.dma("sp", hv_out[:, n, c * 512:(c + 1) * 512], rl[r4][:], reads=[t_rl[r4]], writes=[toks_out[n]])

    load(0)
    prep(0)
    for st in range(NSUP):
        if st + 1 < NSUP:
            load(st + 1)
        gateup(st)
        if st + 1 < NSUP:
            prep(st + 1)
        down(st)
    P.close()


RWKV_IN = 1408
ATT_Q0 = 1408
ATT_V0 = 2176
S5_0 = 2560
INW = 2816


def phase_win(nc, S, h_in, g, win, zr, qk, vtm, us5, toks_in, T):
    P = Phase(nc, S)
    NT = T // 128
    NS = 4
    NSUP = NT // NS
    hv_in = h_in.rearrange("(n p) d -> p n d", p=128)
    ident, _, t_id = make_ident(nc, S, P)
    w_b = P.sb([128, KD, INW], BF16); t_w = Tok()
    load_w_bf16(S, w_b, win, t_w)
    gt = P.sb([128, D], F32); t_g = Tok()
    S.dma("sp", gt[:], g.partition_broadcast(128), writes=[t_g])
    mhalf = P.sb([128, 1], F32); t_mh = Tok()
    S.op("pool", lambda E: E.memset(mhalf[:], -0.5), writes=[t_mh])
    ht = [P.sb([128, NS, D], F32) for _ in range(2)]; t_ht = [[Tok() for _ in range(NS)] for _ in range(2)]
    xn = [P.sb([128, D], BF16) for _ in range(2)]; t_xn = [Tok(), Tok()]
    junk = P.sb([128, D], BF16); t_junk = Tok()
    stat = [P.sb([128, 4], F32) for _ in range(2)]; t_stat = [Tok(), Tok()]
    xnT = [P.sb([128, KD, NS * 128], BF16) for _ in range(2)]; t_xnT = [Tok(), Tok()]
    stf = [P.sb([128, 512], F32) for _ in range(4)]; t_stf = [Tok() for _ in range(4)]
    stb = [P.sb([128, 512], BF16) for _ in range(4)]; t_stb = [Tok() for _ in range(4)]
    pT = [P.ps([128, KD, 128], BF16) for _ in range(2)]; t_pT = [Tok(), Tok()]
    pZ = [P.ps([128, 512], F32) for _ in range(4)]; t_pZ = [Tok() for _ in range(4)]
    t_out = Tok()
    chunks = []
    for c in range(11):
        chunks.append((c * 128, zr, c * 128, False))
    for c in range(6):
        chunks.append((ATT_Q0 + c * 128, qk, c * 128, True))
    for c in range(2):
        chunks.append((S5_0 + c * 128, us5, c * 128, False))
    cnt = {"it": 0, "ib": 0, "iff": 0}

    def load(st):
        hb = st % 2
        for s in range(NS):
            n = st * NS + s
            S.dma("sp", ht[hb][:, s, :], hv_in[:, n, :], reads=[toks_in[n]], writes=[t_ht[hb][s]])

    def prep(st):
        hb = st % 2
        for s in range(NS):
            b = s % 2
            rms_prep(S, P, ht[hb], t_ht[hb][s], s, gt, t_g, xn[b], t_xn[b], junk, t_junk, stat[b], t_stat[b], mhalf, t_mh)
            for k in range(KD):
                S.op("pe", lambda E, k=k, b=b: E.transpose(out=pT[b][:, k, :], in_=xn[b][:, k * 128:(k + 1) * 128],
                                                           identity=ident[:]),
                     reads=[t_xn[b], t_id], writes=[t_pT[b]])
            S.op("dve", lambda E, b=b, s=s, hb=hb: E.tensor_copy(out=xnT[hb][:, :, s * 128:(s + 1) * 128], in_=pT[b][:]),
                 reads=[t_pT[b]], writes=[t_xnT[hb]])

    def fm_chunks(st, chs):
        hb = st % 2
        tsl = slice(st * 512, (st + 1) * 512)
        for (c0, dst, r0, isb) in chs:
            pb = cnt["it"] % 4
            cnt["it"] += 1
            for k in range(KD):
                S.op("pe", lambda E, k=k, c0=c0, pb=pb, hb=hb: E.matmul(
                    pZ[pb][:], lhsT=w_b[:, k, c0:c0 + 128], rhs=xnT[hb][:, k, :], start=(k == 0), stop=(k == KD - 1)),
                    reads=[t_w, t_xnT[hb]], writes=[t_pZ[pb]])
            if isb:
                sb_ = cnt["ib"] % 4
                cnt["ib"] += 1
                S.op("act", lambda E, pb=pb, sb_=sb_: E.activation(out=stb[sb_][:], in_=pZ[pb][:], func=AF.Copy),
                     reads=[t_pZ[pb]], writes=[t_stb[sb_]])
                S.dma("sp", dst[r0:r0 + 128, tsl], stb[sb_][:], reads=[t_stb[sb_]], writes=[t_out])
            else:
                sf = cnt["iff"] % 4
                cnt["iff"] += 1
                if cnt["iff"] % 2:
                    S.op("act", lambda E, pb=pb, sf=sf: E.activation(out=stf[sf][:], in_=pZ[pb][:], func=AF.Copy),
                         reads=[t_pZ[pb]], writes=[t_stf[sf]])
                else:
                    S.op("dve", lambda E, pb=pb, sf=sf: E.tensor_copy(out=stf[sf][:], in_=pZ[pb][:]),
                         reads=[t_pZ[pb]], writes=[t_stf[sf]])
                S.dma("sp", dst[r0:r0 + 128, tsl], stf[sf][:], reads=[t_stf[sf]], writes=[t_out])

    def v_chunks(st):
        hb = st % 2
        for s in range(NS):
            n = st * NS + s
            pb = cnt["it"] % 4
            cnt["it"] += 1
            for k in range(KD):
                S.op("pe", lambda E, k=k, s=s, pb=pb, hb=hb: E.matmul(
                    pZ[pb][:, 0:384], lhsT=xnT[hb][:, k, s * 128:(s + 1) * 128], rhs=w_b[:, k, ATT_V0:ATT_V0 + 384],
                    start=(k == 0), stop=(k == KD - 1)),
                    reads=[t_w, t_xnT[hb]], writes=[t_pZ[pb]])
            sb_ = cnt["ib"] % 4
            cnt["ib"] += 1
            S.op("dve", lambda E, pb=pb, sb_=sb_: E.tensor_copy(out=stb[sb_][:, 0:384], in_=pZ[pb][:, 0:384]),
                 reads=[t_pZ[pb]], writes=[t_stb[sb_]])
            S.dma("sp", vtm[n * 128:(n + 1) * 128, :], stb[sb_][:, 0:384], reads=[t_stb[sb_]], writes=[t_out])

    load(0)
    prep(0)
    for st in range(NSUP):
        if st + 1 < NSUP:
            load(st + 1)
        fm_chunks(st, chunks[:10])
        if st + 1 < NSUP:
            prep(st + 1)
        fm_chunks(st, chunks[10:])
        v_chunks(st)
    P.close()


def phase_wout(nc, S, h_in, h_out, mixT, wout, toks_in, toks_out, T):
    P = Phase(nc, S)
    NT = T // 128
    NS = 4
    NSUP = NT // NS
    hv_in = h_in.rearrange("(n p) d -> p n d", p=128)
    hv_out = h_out.rearrange("(n p) d -> p n d", p=128)
    mv = mixT.rearrange("(k p) t -> p k t", p=128)
    w_b = P.sb([128, KD, D], BF16); t_w = Tok()
    load_w_bf16(S, w_b, wout, t_w, col_split=1)
    ht = [P.sb([128, NS, D], F32) for _ in range(2)]; t_ht = [[Tok() for _ in range(NS)] for _ in range(2)]
    ml = [P.sb([128, KD, 512], BF16) for _ in range(2)]; t_ml = [Tok(), Tok()]
    pD = [P.ps([128, 512], F32) for _ in range(4)]; t_pD = [Tok() for _ in range(4)]
    it = 0
    for st in range(NSUP):
        hb = st % 2
        S.dma("sp", ml[hb][:], mv[:, :, st * 512:(st + 1) * 512], writes=[t_ml[hb]])
        for s in range(NS):
            n = st * NS + s
            S.dma("sp", ht[hb][:, s, :], hv_in[:, n, :], reads=[toks_in[n]], writes=[t_ht[hb][s]])
        for s in range(NS):
            n = st * NS + s
            for c in range(2):
                b = it % 4
                it += 1
                for k in range(KD):
                    S.op("pe", lambda E, k=k, s=s, c=c, b=b, hb=hb: E.matmul(
                        pD[b][:], lhsT=ml[hb][:, k, s * 128:(s + 1) * 128], rhs=w_b[:, k, c * 512:(c + 1) * 512],
                        start=(k == 0), stop=(k == KD - 1)),
                        reads=[t_w, t_ml[hb]], writes=[t_pD[b]])
                S.op("dve", lambda E, s=s, c=c, b=b, hb=hb: E.tensor_tensor(
                    out=ht[hb][:, s, c * 512:(c + 1) * 512], in0=pD[b][:], in1=ht[hb][:, s, c * 512:(c + 1) * 512],
                    op=ALU.add), reads=[t_pD[b]], writes=[t_ht[hb][s]])
            S.dma("sp", hv_out[:, n, :], ht[hb][:, s, :], reads=[t_ht[hb][s]], writes=[toks_out[n]])
    P.close()


def phase_final(nc, S, h_in, out, g, toks_in, toks_out, T):
    P = Phase(nc, S)
    NT = T // 128
    hv_in = h_in.rearrange("(n p) d -> p n d", p=128)
    hv_out = out.rearrange("(n p) d -> p n d", p=128)
    gt = P.sb([128, D], F32); t_g = Tok()
    S.dma("sp", gt[:], g.partition_broadcast(128), writes=[t_g])
    mhalf = P.sb([128, 1], F32); t_mh = Tok()
    S.op("pool", lambda E: E.memset(mhalf[:], -0.5), writes=[t_mh])
    ht = [P.sb([128, 1, D], F32) for _ in range(4)]; t_ht = [Tok() for _ in range(4)]
    xo = [P.sb([128, D], F32) for _ in range(4)]; t_xo = [Tok() for _ in range(4)]
    junk = P.sb([128, D], BF16); t_junk = Tok()
    stat = [P.sb([128, 4], F32) for _ in range(4)]; t_stat = [Tok() for _ in range(4)]
    for n in range(NT):
        b = n % 4
        S.dma("sp", ht[b][:, 0, :], hv_in[:, n, :], reads=[toks_in[n]], writes=[t_ht[b]])
        rms_prep(S, P, ht[b], t_ht[b], 0, gt, t_g, xo[b], t_xo[b], junk, t_junk, stat[b], t_stat[b], mhalf, t_mh)
        S.dma("pool", hv_out[:, n, :], xo[b][:], reads=[t_xo[b]], writes=[toks_out[n]])
    P.close()


ALIBI = [0.25, 0.0625, 0.015625, 0.00390625, 0.5, 0.125]
DILS = [1, 4, 16]
NEG = -1.0e30


def phase_attn(nc, S, qk, vtm, mixT, T):
    P = Phase(nc, S)
    NB1 = T // 128
    dfi = P.sb([128, 128], I32)
    dff = P.sb([128, 128], F32)
    Dk = P.sb([128, 3, 128], F32)
    Mk = P.sb([128, 3, 128], F32)
    t_c = Tok()
    S.op("pool", lambda E: E.iota(dfi[:], pattern=[[1, 128]], base=0, channel_multiplier=-1), writes=[t_c])
    S.op("dve", lambda E: E.tensor_copy(out=dff[:], in_=dfi[:]), reads=[t_c], writes=[t_c])
    S.op("dve", lambda E: E.tensor_scalar(out=Dk[:, 0, :], in0=dff[:], scalar1=128.0, scalar2=None, op0=ALU.add),
         reads=[t_c], writes=[t_c])
    S.op("dve", lambda E: E.tensor_scalar(out=Dk[:, 1, :], in0=dff[:], scalar1=-1.0, scalar2=None, op0=ALU.mult),
         reads=[t_c], writes=[t_c])
    S.op("dve", lambda E: E.tensor_tensor(out=Dk[:, 1, :], in0=Dk[:, 1, :], in1=dff[:], op=ALU.max),
         reads=[t_c], writes=[t_c])
    S.op("dve", lambda E: E.tensor_scalar(out=Dk[:, 2, :], in0=dff[:], scalar1=-1.0, scalar2=128.0, op0=ALU.mult,
                                          op1=ALU.add), reads=[t_c], writes=[t_c])
    S.op("dve", lambda E: E.tensor_scalar(out=Mk[:], in0=Dk[:], scalar1=64.0, scalar2=NEG, op0=ALU.is_gt,
                                          op1=ALU.mult), reads=[t_c], writes=[t_c])
    sel = P.sb([65, 64], F32)
    S.op("dve", lambda E: E.memset(sel[:], 0.0), writes=[t_c])
    S.op("dve", lambda E: E.memset(sel[64:65, :], 1.0), reads=[t_c], writes=[t_c])

    qT = P.sb([128, T], BF16); t_q = Tok()
    kT = P.sb([128, T], BF16); t_k = Tok()
    vt = [P.sb([128, NB1, 2, 65], BF16) for _ in range(3)]; t_v = [Tok() for _ in range(3)]
    acc = P.sb([65, 2, T], F32)
    t_acc = [[Tok() for _ in range(NB1)] for _ in range(2)]
    biasT = P.sb([128, 2, 3, 3, 128], F32); t_b = Tok()
    NBF = 4
    sc = [P.sb([128, 3, 128], F32) for _ in range(NBF)]; t_sc = [Tok() for _ in range(NBF)]
    pr = [P.sb([128, 3, 128], BF16) for _ in range(NBF)]; t_pr = [Tok() for _ in range(NBF)]
    rec = [P.sb([64, 512], F32) for _ in range(2)]; t_rec = [Tok(), Tok()]
    ob = [P.sb([64, 512], BF16) for _ in range(2)]; t_ob = [Tok(), Tok()]
    pS = [P.ps([128, 3, 128], F32) for _ in range(NBF)]; t_pS = [Tok() for _ in range(NBF)]
    pOb = [P.ps([128, 512], F32) for _ in range(NBF)]; t_pO = [Tok() for _ in range(NBF)]
    pO = [x[0:65, 0:128] for x in pOb]
    pB = [x[0:64, 0:512] for x in pOb]; t_pB = t_pO
    t_out = Tok()
    it = 0
    for hp in range(3):
        S.dma("sp", qT[:], qk[128 * hp:128 * hp + 128, :], writes=[t_q])
        S.dma("sp", kT[:], qk[384 + 128 * hp:384 + 128 * hp + 128, :], writes=[t_k])
        for pi, d in enumerate(DILS):
            S.op("pool", lambda E, pi=pi: E.memset(vt[pi][:], 1.0), writes=[t_v[pi]])
            nm = T // d // 128
            vv = vtm.rearrange("(m j r) (h c) -> j r m h c", j=128, r=d, c=64)
            for r in range(d):
                for h2 in range(2):
                    S.dma("sp", vt[pi][:, r * nm:(r + 1) * nm, h2, 0:64], vv[:, r, :, 2 * hp + h2, :],
                          writes=[t_v[pi]])
        for h2 in range(2):
            for pi, d in enumerate(DILS):
                sl = -ALIBI[2 * hp + h2] * d
                S.op("dve", lambda E, h2=h2, pi=pi, sl=sl: E.scalar_tensor_tensor(
                    out=biasT[:, h2, pi, :, :], in0=Dk[:], scalar=sl, in1=Mk[:], op0=ALU.mult, op1=ALU.add),
                    reads=[t_c], writes=[t_b])
        for h2 in range(2):
            rows = slice(64 * h2, 64 * h2 + 64)
            stages = []
            for pi, d in enumerate(DILS):
                nb = T // d // 128
                for r in range(d):
                    for b in range(nb):
                        bi = it % NBF
                        it += 1
                        kts = [kt for kt in (b - 1, b, b + 1) if 0 <= kt < nb]
                        k0 = kts[0] - (b - 1)
                        nk = len(kts)
                        qs = slice(r + d * 128 * b, r + d * 128 * b + d * 127 + 1, d)
                        blks = sorted(set(range((r + d * 128 * b) // 128, (r + d * 128 * (b + 1) - d) // 128 + 1)))

                        def stA(bi=bi, kts=kts, k0=k0, nk=nk, qs=qs, r=r, d=d, pi=pi, rows=rows, h2=h2):
                            for ki, kt in enumerate(kts):
                                ks = slice(r + d * 128 * kt, r + d * 128 * kt + d * 127 + 1, d)
                                S.op("pe", lambda E, ki=ki, ks=ks: E.matmul(
                                    pS[bi][:, ki, :], lhsT=kT[rows, ks], rhs=qT[rows, qs], start=True, stop=True),
                                    reads=[t_q, t_k], writes=[t_pS[bi]])
                            S.op("dve", lambda E: E.scalar_tensor_tensor(
                                out=sc[bi][:, 0:nk, :], in0=pS[bi][:, 0:nk, :], scalar=0.125,
                                in1=biasT[:, h2, pi, k0:k0 + nk, :], op0=ALU.mult, op1=ALU.add),
                                reads=[t_pS[bi], t_b], writes=[t_sc[bi]])
                            S.op("act", lambda E: E.activation(out=pr[bi][:, 0:nk, :], in_=sc[bi][:, 0:nk, :],
                                                               func=AF.Exp),
                                 reads=[t_sc[bi]], writes=[t_pr[bi]])

                        def stB(bi=bi, kts=kts, nk=nk, qs=qs, r=r, nb=nb, pi=pi, h2=h2, blks=blks):
                            for ki, kt in enumerate(kts):
                                S.op("pe", lambda E, ki=ki, kt=kt: E.matmul(
                                    pO[bi], lhsT=vt[pi][:, r * nb + kt, h2, :], rhs=pr[bi][:, ki, :],
                                    start=(ki == 0), stop=(ki == nk - 1)),
                                    reads=[t_v[pi], t_pr[bi]], writes=[t_pO[bi]])
                            at = [t_acc[h2][x] for x in blks]
                            if pi == 0:
                                S.op("act", lambda E: E.activation(out=acc[:, h2, qs], in_=pO[bi], func=AF.Copy),
                                     reads=[t_pO[bi]], writes=at)
                            else:
                                S.op("dve", lambda E: E.tensor_tensor(out=acc[:, h2, qs], in0=pO[bi],
                                                                      in1=acc[:, h2, qs], op=ALU.add),
                                     reads=[t_pO[bi]], writes=at)
                        stages.append((stA, stB))
            SKEW = NBF - 1
            for i in range(len(stages) + SKEW):
                if i < len(stages):
                    stages[i][0]()
                if i - SKEW >= 0:
                    stages[i - SKEW][1]()
            head = 2 * hp + h2
            for c in range(T // 512):
                bi = c % 2
                cs = slice(c * 512, (c + 1) * 512)
                at = t_acc[h2][4 * c:4 * c + 4]
                S.op("pe", lambda E, bi=bi, h2=h2, cs=cs: E.matmul(pB[bi], lhsT=sel[:], rhs=acc[:, h2, cs],
                                                                   start=True, stop=True),
                     reads=at + [t_c], writes=[t_pB[bi]])
                S.op("dve", lambda E, bi=bi: E.reciprocal(out=rec[bi][:], in_=pB[bi]),
                     reads=[t_pB[bi]], writes=[t_rec[bi]])
                S.op("dve", lambda E, bi=bi, h2=h2, cs=cs: E.tensor_tensor(out=ob[bi][:], in0=acc[0:64, h2, cs],
                                                                          in1=rec[bi][:], op=ALU.mult),
                     reads=at + [t_rec[bi]], writes=[t_ob[bi]])
                S.dma("sp", mixT[384 + 64 * head:384 + 64 * head + 64, cs], ob[bi][:], reads=[t_ob[bi]],
                      writes=[t_out])
    P.close()


def rsl(lo, hi):
    return slice(hi - 1, (lo - 1) if lo > 0 else None, -1)


TWO_PI = 6.283185307179586
MAGIC = 12582912.0


def phase_s5(nc, S, us5, mixT, prm, T, C=256):
    P = Phase(nc, S)
    NCH = T // C
    NCMB = 16
    a_re, a_im, lstep, b_re, b_im, c_re, c_im, dsk, glu_w, glu_b = prm
    ident, identf, t_id = make_ident(nc, S, P)
    t_s = Tok()

    def dv(fn, eng="dve"):
        S.op(eng, fn, reads=[t_s, t_id], writes=[t_s])

    prs = P.sb([128, 40, 16], F32)
    cosT = P.sb([128, NCMB, C], F32)
    sinT = P.sb([128, NCMB, C], F32)
    BT = P.sb([128, NCMB, 2, 128], BF16)
    CT = P.sb([128, NCMB, 2, 128], BF16)
    gw = P.sb([128, 2, 512], BF16)
    gb = P.sb([128, 4], F32)
    dk = P.sb([128, 2], F32)
    ub = P.sb([128, 2, T], BF16); t_ub = Tok()
    ybwd = P.sb([128, 2, T], F32); t_yb = [Tok() for _ in range(NCH)]
    gi = P.sb([128, 2, NCMB], F32); t_gi = [Tok() for _ in range(NCMB)]
    banks = [P.ps([128, 512], F32) for _ in range(6)]
    P2 = Phase(nc, S)
    stg = P2.sb([16, 3, 128], F32)
    lst = P2.sb([16, 2], F32)
    S.dma("sp", stg[:, 0, :], a_re.rearrange("d (j g) p -> (d j) (g p)", g=2), writes=[t_s])
    S.dma("sp", stg[:, 1, :], a_im.rearrange("d (j g) p -> (d j) (g p)", g=2), writes=[t_s])
    S.dma("sp", lst[:], lstep.rearrange("d (j g) -> (d j) g", g=2), writes=[t_s])
    for g2 in range(2):
        dv(lambda E, g2=g2: E.tensor_copy(out=stg[:, 2, 64 * g2:64 * g2 + 64],
                                          in_=lst[:, g2:g2 + 1].to_broadcast([16, 64])))
    pst = banks[0][:, 0:48].rearrange("p (a b) -> p a b", a=3)
    for i in range(3):
        S.op("pe", lambda E, i=i: E.transpose(out=pst[:, i, :], in_=stg[:, i, :], identity=identf[0:16, 0:16]),
             reads=[t_s, t_id], writes=[t_s])
    nm = {}

    def V(name):
        if name not in nm:
            nm[name] = len(nm)
        return prs[:, nm[name], :]
    dv(lambda E: E.tensor_copy(out=prs[:, 0:3, :], in_=pst))
    nm.update({"are": 0, "aim": 1, "lst": 2})
    S.op("act", lambda E: E.activation(out=V("step"), in_=V("lst"), func=AF.Exp), reads=[t_s], writes=[t_s])
    dv(lambda E: E.tensor_tensor(out=V("ar"), in0=V("are"), in1=V("step"), op=ALU.mult))
    dv(lambda E: E.tensor_tensor(out=V("th"), in0=V("aim"), in1=V("step"), op=ALU.mult))
    S.op("act", lambda E: E.activation(out=V("rho"), in_=V("ar"), func=AF.Exp), reads=[t_s], writes=[t_s])

    def sin_of(dst, src, shift):
        dv(lambda E: E.tensor_scalar(out=V("k1"), in0=V(src), scalar1=1.0 / TWO_PI, scalar2=shift / TWO_PI,
                                     op0=ALU.mult, op1=ALU.add))
        dv(lambda E: E.tensor_scalar(out=V("k2"), in0=V("k1"), scalar1=MAGIC, scalar2=None, op0=ALU.add))
        dv(lambda E: E.tensor_scalar(out=V("k3"), in0=V("k2"), scalar1=-MAGIC, scalar2=None, op0=ALU.add))
        dv(lambda E: E.tensor_tensor(out=V("k1"), in0=V("k1"), in1=V("k3"), op=ALU.subtract))
        S.op("act", lambda E: E.activation(out=V(dst), in_=V("k1"), func=AF.Sin, scale=TWO_PI),
             reads=[t_s], writes=[t_s])
    sin_of("sn", "th", 0.0)
    sin_of("cs", "th", TWO_PI / 4)
    dv(lambda E: E.tensor_tensor(out=V("lr"), in0=V("rho"), in1=V("cs"), op=ALU.mult))
    dv(lambda E: E.tensor_tensor(out=V("li"), in0=V("rho"), in1=V("sn"), op=ALU.mult))
    dv(lambda E: E.tensor_scalar(out=V("nr"), in0=V("lr"), scalar1=-1.0, scalar2=None, op0=ALU.add))
    dv(lambda E: E.tensor_tensor(out=V("d1"), in0=V("are"), in1=V("are"), op=ALU.mult))
    dv(lambda E: E.tensor_tensor(out=V("d2"), in0=V("aim"), in1=V("aim"), op=ALU.mult))
    dv(lambda E: E.tensor_tensor(out=V("d1"), in0=V("d1"), in1=V("d2"), op=ALU.add))
    dv(lambda E: E.reciprocal(out=V("rd"), in_=V("d1")))
    dv(lambda E: E.tensor_tensor(out=V("z1"), in0=V("nr"), in1=V("are"), op=ALU.mult))
    dv(lambda E: E.tensor_tensor(out=V("z2"), in0=V("li"), in1=V("aim"), op=ALU.mult))
    dv(lambda E: E.tensor_tensor(out=V("z1"), in0=V("z1"), in1=V("z2"), op=ALU.add))
    dv(lambda E: E.tensor_tensor(out=V("zr"), in0=V("z1"), in1=V("rd"), op=ALU.mult))
    dv(lambda E: E.tensor_tensor(out=V("z1"), in0=V("li"), in1=V("are"), op=ALU.mult))
    dv(lambda E: E.tensor_tensor(out=V("z2"), in0=V("nr"), in1=V("aim"), op=ALU.mult))
    dv(lambda E: E.tensor_tensor(out=V("z1"), in0=V("z1"), in1=V("z2"), op=ALU.subtract))
    dv(lambda E: E.tensor_tensor(out=V("zi"), in0=V("z1"), in1=V("rd"), op=ALU.mult))

    tmpA = P2.sb([128, NCMB, C // 2], F32)
    tmpB = P2.sb([128, NCMB, C // 2], F32)
    dv(lambda E: E.memset(cosT[:, :, 0:1], 1.0))
    dv(lambda E: E.memset(sinT[:, :, 0:1], 0.0))
    dv(lambda E: E.tensor_copy(out=V("wr"), in_=V("cs")))
    dv(lambda E: E.tensor_copy(out=V("wi"), in_=V("sn")))
    L = 1
    while L < C:
        wrb = V("wr").unsqueeze(2).to_broadcast([128, NCMB, L])
        wib = V("wi").unsqueeze(2).to_broadcast([128, NCMB, L])
        dv(lambda E, L=L, wrb=wrb: E.tensor_tensor(out=tmpA[:, :, 0:L], in0=cosT[:, :, 0:L], in1=wrb, op=ALU.mult))
        dv(lambda E, L=L, wib=wib: E.tensor_tensor(out=tmpB[:, :, 0:L], in0=sinT[:, :, 0:L], in1=wib, op=ALU.mult))
        dv(lambda E, L=L: E.tensor_tensor(out=cosT[:, :, L:2 * L], in0=tmpA[:, :, 0:L], in1=tmpB[:, :, 0:L],
                                          op=ALU.subtract))
        dv(lambda E, L=L, wib=wib: E.tensor_tensor(out=tmpA[:, :, 0:L], in0=cosT[:, :, 0:L], in1=wib, op=ALU.mult))
        dv(lambda E, L=L, wrb=wrb: E.tensor_tensor(out=tmpB[:, :, 0:L], in0=sinT[:, :, 0:L], in1=wrb, op=ALU.mult))
        dv(lambda E, L=L: E.tensor_tensor(out=sinT[:, :, L:2 * L], in0=tmpA[:, :, 0:L], in1=tmpB[:, :, 0:L],
                                          op=ALU.add))
        dv(lambda E: E.tensor_tensor(out=V("q1"), in0=V("wr"), in1=V("wr"), op=ALU.mult))
        dv(lambda E: E.tensor_tensor(out=V("q2"), in0=V("wi"), in1=V("wi"), op=ALU.mult))
        dv(lambda E: E.tensor_tensor(out=V("q3"), in0=V("wr"), in1=V("wi"), op=ALU.mult))
        dv(lambda E: E.tensor_tensor(out=V("wr"), in0=V("q1"), in1=V("q2"), op=ALU.subtract))
        dv(lambda E: E.tensor_scalar(out=V("wi"), in0=V("q3"), scalar1=2.0, scalar2=None, op0=ALU.mult))
        L *= 2
    dv(lambda E: E.tensor_scalar(out=V("nwi"), in0=V("wi"), scalar1=-1.0, scalar2=None, op0=ALU.mult))

    bst = P2.sb([128, 2, 8, 16], F32)
    S.dma("sp", bst[:, 0, :, :], b_re.rearrange("(j g) p c -> (g p) j c", g=2), writes=[t_s])
    S.dma("sp", bst[:, 1, :, :], b_im.rearrange("(j g) p c -> (g p) j c", g=2), writes=[t_s])
    bexp = P2.sb([128, 2, 128], F32)
    btmp = P2.sb([128, 16], F32)
    pX = [banks[1][:, 0:128], banks[2][:, 0:128]]
    for d in range(2):
        for j in range(8):
            cmb = d * 8 + j
            jj = j % 4
            dv(lambda E: E.memset(bexp[:], 0.0))
            for g2 in range(2):
                ps_ = slice(64 * g2, 64 * g2 + 64)
                cs_ = slice(32 * jj + 16 * g2, 32 * jj + 16 * g2 + 16)
                zr_ = prs[ps_, nm["zr"], cmb:cmb + 1]
                zi_ = prs[ps_, nm["zi"], cmb:cmb + 1]
                dv(lambda E, ps_=ps_, zi_=zi_, j=j: E.tensor_scalar(out=btmp[ps_, :], in0=bst[ps_, 1, j, :], scalar1=zi_,
                                                                    scalar2=None, op0=ALU.mult))
                dv(lambda E, ps_=ps_, cs_=cs_, zr_=zr_, j=j: E.scalar_tensor_tensor(
                    out=bexp[ps_, 0, cs_], in0=bst[ps_, 0, j, :], scalar=zr_, in1=btmp[ps_, :], op0=ALU.mult,
                    op1=ALU.subtract))
                dv(lambda E, ps_=ps_, zr_=zr_, j=j: E.tensor_scalar(out=btmp[ps_, :], in0=bst[ps_, 1, j, :], scalar1=zr_,
                                                                    scalar2=None, op0=ALU.mult))
                dv(lambda E, ps_=ps_, cs_=cs_, zi_=zi_, j=j: E.scalar_tensor_tensor(
                    out=bexp[ps_, 1, cs_], in0=bst[ps_, 0, j, :], scalar=zi_, in1=btmp[ps_, :], op0=ALU.mult,
                    op1=ALU.add))
            for ri in range(2):
                S.op("pe", lambda E, ri=ri: E.transpose(out=pX[ri], in_=bexp[:, ri, :], identity=identf[:]),
                     reads=[t_s, t_id], writes=[t_s])
                dv(lambda E, ri=ri, cmb=cmb: E.tensor_copy(out=BT[:, cmb, ri, :], in_=pX[ri]))
    cnat = P2.sb([128, 2, 2, 2, 64], F32)
    for d in range(2):
        for ri, cc in enumerate((c_re, c_im)):
            for ut in range(2):
                S.dma("sp", cnat[:, d, ri, ut, :], cc[d].rearrange("g c p -> (g c) p")[128 * ut:128 * ut + 128, :],
                      writes=[t_s])
    mki = P2.sb([128, 4, 2], I32)
    mk = P2.sb([128, 4, 2], F32)
    mk2 = P2.sb([128, 4, 2], F32)
    S.op("pool", lambda E: E.iota(mki[:], pattern=[[-32, 4], [-16, 2]], base=0, channel_multiplier=1),
         reads=[t_s], writes=[t_s])
    dv(lambda E: E.tensor_copy(out=mk[:], in_=mki[:]))
    dv(lambda E: E.tensor_scalar(out=mk2[:], in0=mk[:], scalar1=0.0, scalar2=None, op0=ALU.is_ge))
    dv(lambda E: E.tensor_scalar(out=mk[:], in0=mk[:], scalar1=15.0, scalar2=None, op0=ALU.is_le))
    dv(lambda E: E.tensor_tensor(out=mk[:], in0=mk[:], in1=mk2[:], op=ALU.mult))
    cx = P2.sb([128, 2, 64], F32)
    for d in range(2):
        for j in range(8):
            cmb = d * 8 + j
            jj = j % 4
            ut = j // 4
            for ri in range(2):
                for g2 in range(2):
                    dv(lambda E, d=d, ri=ri, ut=ut, jj=jj, g2=g2: E.tensor_scalar(
                        out=cx[:, g2, :], in0=cnat[:, d, ri, ut, :], scalar1=mk[:, jj, g2:g2 + 1], scalar2=None,
                        op0=ALU.mult))
                S.op("pe", lambda E, ri=ri: E.transpose(out=pX[ri], in_=cx[:].rearrange("p a b -> p (a b)"),
                                                        identity=identf[:]),
                     reads=[t_s, t_id], writes=[t_s])
                sgn = 1.0 if ri == 0 else -1.0
                dv(lambda E, ri=ri, cmb=cmb, sgn=sgn: E.tensor_scalar(out=CT[:, cmb, ri, :], in0=pX[ri], scalar1=sgn,
                                                                      scalar2=None, op0=ALU.mult))
    load_w_bf16(S, gw, glu_w, t_s, col_split=1)
    S.dma("sp", gb[:], glu_b.rearrange("(o p) -> p o", p=128), writes=[t_s], allow_slow_non_contiguous=True)
    S.dma("sp", dk[:], dsk.rearrange("(o p) -> p o", p=128), writes=[t_s], allow_slow_non_contiguous=True)

    for ut in range(2):
        for c4 in range(T // 2048 if T >= 2048 else 1):
            w_ = min(2048, T)
            S.dma("pool", ub[:, ut, c4 * w_:(c4 + 1) * w_], us5[128 * ut:128 * ut + 128, c4 * w_:(c4 + 1) * w_],
                  writes=[t_ub])
    S.op("dve", lambda E: E.memset(gi[:], 0.0), writes=t_gi)
    NB = 3
    P2.close()
    pBU = [banks[i][:, 0:2 * C].rearrange("p (a b) -> p a b", a=2) for i in range(2)]; t_pBU = [Tok() for _ in range(2)]
    m1 = [P.sb([128, 4, C], F32) for _ in range(NB)]; t_m1 = [Tok() for _ in range(NB)]
    gin = [P.sb([128, 2, C], F32) for _ in range(NB)]; t_gin = [Tok() for _ in range(NB)]
    gg = [P.sb([128, 2, C], F32) for _ in range(NB)]; t_gg = [Tok() for _ in range(NB)]
    m2 = [P.sb([128, 4, C], F32) for _ in range(NB)]; t_m2 = [Tok() for _ in range(NB)]
    ctmp = [P.sb([128, 2], F32) for _ in range(NB)]; t_ct = [Tok() for _ in range(NB)]
    hh = [P.sb([128, 4, 2, C], BF16) for _ in range(2)]; t_hh = [[Tok() for _ in range(4)] for _ in range(2)]
    pY = [banks[2 + i][:, 0:C] for i in range(2)]; t_pY = [Tok(), Tok()]
    uf = [P.sb([128, 2, C], F32) for _ in range(2)]; t_uf = [Tok(), Tok()]
    yv = [P.sb([128, C], F32) for _ in range(2)]; t_yv = [Tok(), Tok()]
    y2 = [P.sb([128, C], F32) for _ in range(2)]; t_y2 = [Tok(), Tok()]
    ygl = [P.sb([128, 2, C], BF16) for _ in range(2)]; t_yg = [[Tok(), Tok()] for _ in range(2)]
    pZ = [banks[4 + i][:, 0:C] for i in range(2)]; t_pZ = [Tok(), Tok()]
    sg = [P.sb([128, C], F32) for _ in range(2)]; t_sg = [Tok(), Tok()]
    oo = [P.sb([128, C], BF16) for _ in range(2)]; t_oo = [Tok(), Tok()]
    t_out = Tok()
    iy = [0]
    items = []
    for d in (1, 0):
        for ci in range(NCH):
            for ut in range(2):
                for jj in range(4):
                    items.append((d, ci, ut, jj))

    def geom(d, ci):
        if d == 1:
            lo, hi = T - (ci + 1) * C, T - ci * C
            return lo, hi, rsl(lo, hi)
        lo, hi = ci * C, (ci + 1) * C
        return lo, hi, slice(lo, hi)

    def stA(i):
        d, ci, ut, jj = items[i]
        lo, hi, tsl = geom(d, ci)
        cmb = d * 8 + ut * 4 + jj
        b = i % NB
        pb = i % 2
        if d == 0 and ut == 0 and jj == 0:
            ufb = ci % 2
            S.dma("sp", uf[ufb][:], us5.rearrange("(u p) t -> p u t", p=128)[:, :, lo:hi], writes=[t_uf[ufb]])
        for ri in range(2):
            S.op("pe", lambda E, ri=ri: E.matmul(pBU[pb][:, ri, :], lhsT=BT[:, cmb, ri, :], rhs=ub[:, ut, tsl],
                                                 start=True, stop=True), reads=[t_s, t_ub], writes=[t_pBU[pb]])
        cs_ = cosT[:, cmb, :]
        sn_ = sinT[:, cmb, :]
        for k, (src, tab) in enumerate(((0, cs_), (1, sn_), (1, cs_), (0, sn_))):
            S.op("dve", lambda E, k=k, src=src, tab=tab: E.tensor_tensor(out=m1[b][:, k, :], in0=pBU[pb][:, src, :],
                                                                         in1=tab, op=ALU.mult),
                 reads=[t_pBU[pb], t_s], writes=[t_m1[b]])
        S.op("pool", lambda E: E.tensor_tensor(out=gin[b][:, 0, :], in0=m1[b][:, 0, :], in1=m1[b][:, 1, :], op=ALU.add),
             reads=[t_m1[b]], writes=[t_gin[b]])
        S.op("dve", lambda E: E.tensor_tensor(out=gin[b][:, 1, :], in0=m1[b][:, 2, :], in1=m1[b][:, 3, :],
                                              op=ALU.subtract), reads=[t_m1[b]], writes=[t_gin[b]])

    def stB(i):
        d, ci, ut, jj = items[i]
        cmb = d * 8 + ut * 4 + jj
        b = i % NB
        rho_b = prs[:, nm["rho"], cmb:cmb + 1].to_broadcast([128, C])
        for ri in range(2):
            S.op("dve", lambda E, ri=ri: E.tensor_tensor_scan(
                out=gg[b][:, ri, :], data0=rho_b, data1=gin[b][:, ri, :], initial=gi[:, ri, cmb:cmb + 1],
                op0=ALU.mult, op1=ALU.add), reads=[t_gin[b], t_gi[cmb], t_s], writes=[t_gg[b]])
        wr_ = prs[:, nm["wr"], cmb:cmb + 1]
        wi_ = prs[:, nm["wi"], cmb:cmb + 1]
        nwi_ = prs[:, nm["nwi"], cmb:cmb + 1]
        S.op("act", lambda E: E.activation(out=ctmp[b][:, 0:1], in_=gg[b][:, 1, C - 1:C], func=AF.Copy, scale=nwi_),
             reads=[t_gg[b], t_s], writes=[t_ct[b]])
        S.op("act", lambda E: E.activation(out=ctmp[b][:, 1:2], in_=gg[b][:, 1, C - 1:C], func=AF.Copy, scale=wr_),
             reads=[t_gg[b], t_s], writes=[t_ct[b]])
        S.op("act", lambda E: E.activation(out=gi[:, 0, cmb:cmb + 1], in_=gg[b][:, 0, C - 1:C], func=AF.Identity,
                                           scale=wr_, bias=ctmp[b][:, 0:1]),
             reads=[t_gg[b], t_ct[b], t_s], writes=[t_gi[cmb]])
        S.op("act", lambda E: E.activation(out=gi[:, 1, cmb:cmb + 1], in_=gg[b][:, 0, C - 1:C], func=AF.Identity,
                                           scale=wi_, bias=ctmp[b][:, 1:2]),
             reads=[t_gg[b], t_ct[b], t_s], writes=[t_gi[cmb]])

    def stC(i):
        d, ci, ut, jj = items[i]
        lo, hi, tsl = geom(d, ci)
        chn = lo // C
        cmb = d * 8 + ut * 4 + jj
        b = i % NB
        hb = (ci * 2 + ut) % 2
        cs_ = cosT[:, cmb, :]
        sn_ = sinT[:, cmb, :]
        S.op("pool", lambda E: E.tensor_tensor(out=m2[b][:, 0, :], in0=gg[b][:, 0, :], in1=cs_, op=ALU.mult),
             reads=[t_gg[b], t_s], writes=[t_m2[b]])
        S.op("pool", lambda E: E.tensor_tensor(out=m2[b][:, 1, :], in0=gg[b][:, 1, :], in1=sn_, op=ALU.mult),
             reads=[t_gg[b], t_s], writes=[t_m2[b]])
        S.op("dve", lambda E: E.tensor_tensor(out=m2[b][:, 2, :], in0=gg[b][:, 1, :], in1=cs_, op=ALU.mult),
             reads=[t_gg[b], t_s], writes=[t_m2[b]])
        S.op("dve", lambda E: E.tensor_tensor(out=m2[b][:, 3, :], in0=gg[b][:, 0, :], in1=sn_, op=ALU.mult),
             reads=[t_gg[b], t_s], writes=[t_m2[b]])
        S.op("pool", lambda E: E.tensor_tensor(out=hh[hb][:, jj, 0, :], in0=m2[b][:, 0, :], in1=m2[b][:, 1, :],
                                               op=ALU.subtract), reads=[t_m2[b]], writes=[t_hh[hb][jj]])
        S.op("pool", lambda E: E.tensor_tensor(out=hh[hb][:, jj, 1, :], in0=m2[b][:, 2, :], in1=m2[b][:, 3, :],
                                               op=ALU.add), reads=[t_m2[b]], writes=[t_hh[hb][jj]])
        if jj != 3:
            return
        yb_ = iy[0] % 2
        iy[0] += 1
        n_mm = 0
        for j4 in range(4):
            cm2 = d * 8 + ut * 4 + j4
            for ri in range(2):
                S.op("pe", lambda E, cm2=cm2, ri=ri, j4=j4, n_mm=n_mm: E.matmul(
                    pY[yb_], lhsT=CT[:, cm2, ri, :], rhs=hh[hb][:, j4, ri, :], start=(n_mm == 0), stop=(n_mm == 7)),
                    reads=[t_s, t_hh[hb][j4]], writes=[t_pY[yb_]])
                n_mm += 1
        if d == 1:
            S.op("act", lambda E: E.activation(out=ybwd[:, ut, tsl], in_=pY[yb_], func=AF.Copy),
                 reads=[t_pY[yb_]], writes=[t_yb[chn]])
            return
        ufb = ci % 2
        gb_ = ci % 2
        S.op("dve", lambda E: E.tensor_tensor(out=yv[yb_][:], in0=pY[yb_], in1=ybwd[:, ut, tsl], op=ALU.add),
             reads=[t_pY[yb_], t_yb[chn]], writes=[t_yv[yb_]])
        S.op("dve", lambda E: E.scalar_tensor_tensor(out=yv[yb_][:], in0=uf[ufb][:, ut, :], scalar=dk[:, ut:ut + 1],
                                                     in1=yv[yb_][:], op0=ALU.mult, op1=ALU.add),
             reads=[t_uf[ufb], t_s, t_yv[yb_]], writes=[t_yv[yb_]])
        S.op("act", lambda E: E.activation(out=y2[yb_][:], in_=yv[yb_][:], func=AF.Square),
             reads=[t_yv[yb_]], writes=[t_y2[yb_]])
        S.op("pool", lambda E: E.tensor_scalar(out=y2[yb_][:], in0=y2[yb_][:], scalar1=0.044715, scalar2=1.0,
                                               op0=ALU.mult, op1=ALU.add), reads=[t_y2[yb_]], writes=[t_y2[yb_]])
        S.op("pool", lambda E: E.tensor_tensor(out=y2[yb_][:], in0=y2[yb_][:], in1=yv[yb_][:], op=ALU.mult),
             reads=[t_y2[yb_], t_yv[yb_]], writes=[t_y2[yb_]])
        S.op("act", lambda E: E.activation(out=y2[yb_][:], in_=y2[yb_][:], func=AF.Sigmoid, scale=1.5957691216057308),
             reads=[t_y2[yb_]], writes=[t_y2[yb_]])
        S.op("dve", lambda E: E.tensor_tensor(out=ygl[gb_][:, ut, :], in0=y2[yb_][:], in1=yv[yb_][:], op=ALU.mult),
             reads=[t_y2[yb_], t_yv[yb_]], writes=[t_yg[gb_][ut]])
        if ut != 1:
            return
        for o in range(2):
            for half in range(2):
                oc = o + 2 * half
                for u2 in range(2):
                    S.op("pe", lambda E, half=half, oc=oc, u2=u2: E.matmul(
                        pZ[half], lhsT=gw[:, u2, oc * 128:(oc + 1) * 128], rhs=ygl[gb_][:, u2, :], start=(u2 == 0),
                        stop=(u2 == 1)), reads=[t_s, t_yg[gb_][u2]], writes=[t_pZ[half]])
            S.op("act", lambda E, o=o: E.activation(out=sg[o][:], in_=pZ[1], func=AF.Sigmoid, bias=gb[:, 2 + o:3 + o]),
                 reads=[t_pZ[1], t_s], writes=[t_sg[o]])
            S.op("dve", lambda E, o=o: E.scalar_tensor_tensor(out=oo[o][:], in0=pZ[0], scalar=gb[:, o:o + 1],
                                                              in1=sg[o][:], op0=ALU.add, op1=ALU.mult),
                 reads=[t_pZ[0], t_sg[o], t_s], writes=[t_oo[o]])
            S.dma("sp", mixT[768 + 128 * o:768 + 128 * o + 128, lo:hi], oo[o][:], reads=[t_oo[o]], writes=[t_out])

    N = len(items)
    for i in range(N + 2):
        if i < N:
            stA(i)
        if 0 <= i - 1 < N:
            stB(i - 1)
        if 0 <= i - 2 < N:
            stC(i - 2)
    P.close()


DEC = 0.6065306597126334
RWKV_DBG = [9]


def phase_rwkv(nc, S, zr, ydir, bon, gfm, prm, T):
    mu_p, mu_n, w0, w2, a0, a2, g2, k_k, k_a, r_k = prm
    P = Phase(nc, S)
    NBLK = T // 512
    ident, identf, t_id = make_ident(nc, S, P)
    t_c = Tok()

    def cst(fn, eng="dve"):
        S.op(eng, fn, reads=[t_c, t_id], writes=[t_c])
    onesb = P.sb([128, 128], F32)
    cst(lambda E: E.memset(onesb[:], 0.0))
    cst(lambda E: E.memset(onesb[0:64, 0:64], 1.0))
    cst(lambda E: E.memset(onesb[64:128, 64:128], 1.0))
    mskU = P.sb([128, 512], F32)
    mskL = P.sb([128, 3, 128], F32)
    cst(lambda E: E.memset(mskU[:], 1.0), "pool")
    cst(lambda E: E.memset(mskL[:], 1.0), "pool")
    for i in range(4):
        cmp_ = ALU.is_gt if i % 2 == 0 else ALU.is_ge
        cst(lambda E, i=i, cmp_=cmp_: E.affine_select(out=mskU[:, 128 * i:128 * i + 128], in_=mskU[:, 128 * i:128 * i + 128],
                                                      pattern=[[1, 128]], base=0, channel_multiplier=-1, compare_op=cmp_,
                                                      fill=0.0), "pool")
    for i in range(3):
        cst(lambda E, i=i: E.affine_select(out=mskL[:, i, :], in_=mskL[:, i, :], pattern=[[-1, 128]], base=0,
                                           channel_multiplier=1, compare_op=ALU.is_gt, fill=0.0), "pool")
    rst = P.sb([128, 512], F32)
    cst(lambda E: E.memset(rst[:], 1.0))
    for q in range(4):
        cst(lambda E, q=q: E.memset(rst[:, 128 * q:128 * q + 1], 0.0))
    mh512 = P.sb([128, 512], F32)
    cst(lambda E: E.memset(mh512[:], -0.5), "pool")
    cmu = P.sb([128, 3, 11], F32)
    S.dma("sp", cmu[:, 1, :], mu_p.rearrange("(c p) -> p c", p=128), writes=[t_c], allow_slow_non_contiguous=True)
    S.dma("sp", cmu[:, 2, :], mu_n.rearrange("(c p) -> p c", p=128), writes=[t_c], allow_slow_non_contiguous=True)
    cst(lambda E: E.tensor_tensor(out=cmu[:, 0, :], in0=cmu[:, 1, :], in1=cmu[:, 2, :], op=ALU.add))
    cst(lambda E: E.tensor_scalar(out=cmu[:, 0, :], in0=cmu[:, 0, :], scalar1=-1.0, scalar2=1.0, op0=ALU.mult, op1=ALU.add))
    w0c = P.sb([128, 2, 3], F32); a0c = P.sb([128, 2, 3], F32)
    for d in range(2):
        S.dma("sp", w0c[:, d, :], w0[d].rearrange("(c p) -> p c", p=128), writes=[t_c], allow_slow_non_contiguous=True)
        S.dma("sp", a0c[:, d, :], a0[d].rearrange("(c p) -> p c", p=128), writes=[t_c], allow_slow_non_contiguous=True)
    kkc = P.sb([128, 3], F32); kac = P.sb([128, 3], F32); omka = P.sb([128, 3], F32); rkc = P.sb([128, 3], F32)
    S.dma("sp", kkc[:], k_k.rearrange("(c p) -> p c", p=128), writes=[t_c], allow_slow_non_contiguous=True)
    S.dma("sp", kac[:], k_a.rearrange("(c p) -> p c", p=128), writes=[t_c], allow_slow_non_contiguous=True)
    S.dma("sp", rkc[:], r_k.rearrange("h k -> (h k)").rearrange("(c p) -> p c", p=128), writes=[t_c],
          allow_slow_non_contiguous=True)
    cst(lambda E: E.tensor_scalar(out=omka[:], in0=kac[:], scalar1=-1.0, scalar2=1.0, op0=ALU.mult, op1=ALU.add))
    w2a2 = P.sb([128, 2, 384], BF16)
    for d in range(2):
        S.dma("pool", w2a2[0:64, d, :], w2[d], writes=[t_c])
        S.dma("pool", w2a2[64:128, d, :], a2[d], writes=[t_c])
    g2b = P.sb([128, 384], BF16)
    S.dma("pool", g2b[:], g2, writes=[t_c])

    banks = [P.ps([128, 512], F32) for _ in range(6)]
    t_bk = [Tok() for _ in range(6)]
    bkrr = [0]

    def getbank():
        i = bkrr[0] % 6
        bkrr[0] += 1
        return banks[i], t_bk[i]
    pTr = [P.ps([128, 4, 128], BF16) for _ in range(2)]; t_pTr = [Tok(), Tok()]
    trr = [0]

    NZS = 4
    zraw = P.sb([128, NZS, 514], F32); t_zraw = [Tok() for _ in range(NZS)]
    zsi = [0]
    zm = P.sb([128, 11, 512], F32); t_zm = [Tok() for _ in range(11)]
    tmp0 = [P.sb([128, 512], F32) for _ in range(2)]; t_tmp0 = [Tok(), Tok()]
    tmp1 = [P.sb([128, 512], F32) for _ in range(2)]; t_tmp1 = [Tok(), Tok()]
    tz = P.sb([128, 512], BF16); t_tz = Tok()
    sgz = P.sb([128, 512], BF16); t_sgz = Tok()
    B1 = [P.sb([128, 512], F32) for _ in range(3)]; tB1 = [Tok() for _ in range(3)]
    B2 = [P.sb([128, 512], F32) for _ in range(3)]; tB2 = [Tok() for _ in range(3)]
    B3 = [P.sb([128, 512], F32) for _ in range(3)]; tB3 = [Tok() for _ in range(3)]
    B4 = [P.sb([128, 512], F32) for _ in range(3)]; tB4 = [Tok() for _ in range(3)]
    B5 = [P.sb([128, 512], F32) for _ in range(3)]; tB5 = [Tok() for _ in range(3)]
    B6 = [P.sb([128, 512], F32) for _ in range(3)]; tB6 = [Tok() for _ in range(3)]
    B7 = [P.sb([128, 512], F32) for _ in range(3)]; tB7 = [Tok() for _ in range(3)]
    ARb = [P.sb([128, 4, 2, 128], BF16) for _ in range(3)]; t_AR = [Tok() for _ in range(3)]
    Bt = [P.sb([128, 512], BF16) for _ in range(3)]; t_Bt = [Tok() for _ in range(3)]
    Kt = [P.sb([128, 512], BF16) for _ in range(3)]; t_Kt = [Tok() for _ in range(3)]
    Bb = [P.sb([128, 512], BF16) for _ in range(3)]; t_Bb = [Tok() for _ in range(3)]
    Kb = [P.sb([128, 512], BF16) for _ in range(3)]; t_Kb = [Tok() for _ in range(3)]
    vb = [P.sb([128, 512], BF16) for _ in range(3)]; t_vb = [Tok() for _ in range(3)]
    WCt = P.sb([128, 3, 4], F32); t_WC = Tok()
    tm = [[P.sb([128, 4, 128], BF16) for _ in range(4)] for _ in range(3)]
    t_tm = [[Tok() for _ in range(4)] for _ in range(3)]
    stg = [P.sb([128, 512], F32) for _ in range(3)]; t_stg = [Tok() for _ in range(3)]
    sgi = [0]
    MPs = [[P.sb([128, 512], BF16) for _ in range(6)] for _ in range(2)]
    t_MPs = [[Tok() for _ in range(6)] for _ in range(2)]
    MTs = [[P.sb([128, 3, 128], BF16) for _ in range(2)] for _ in range(2)]
    t_MTs = [[Tok(), Tok()], [Tok(), Tok()]]
    Pm = [[P.sb([128, 3, 128], BF16) for _ in range(2)] for _ in range(2)]; t_Pm = [[Tok(), Tok()], [Tok(), Tok()]]
    PmT = [[P.sb([128, 3, 128], BF16) for _ in range(2)] for _ in range(2)]; t_PmT = [[Tok(), Tok()], [Tok(), Tok()]]
    Tm = [[P.sb([128, 3, 128], BF16) for _ in range(2)] for _ in range(2)]; t_Tm = [[Tok(), Tok()], [Tok(), Tok()]]
    X0 = P.sb([128, 6, 64], BF16); t_X0 = Tok()
    Uv = P.sb([128, 6, 64], F32); t_Uv = Tok()
    Ahb = P.sb([128, 3, 128], BF16); t_Ah = Tok()
    Ub = P.sb([128, 6, 64], BF16); t_Ub = Tok()
    Sf = P.sb([128, 3, 64], F32); t_Sf = Tok()
    Sb = P.sb([128, 3, 64], BF16); t_Sb = Tok()
    ytm = [P.sb([128, 4, 384], F32) for _ in range(2)]; t_ytm = [Tok(), Tok()]
    t_out = Tok()
    zv = zr.rearrange("(c p) t -> p c t", p=128)

    for d in range(2 if RWKV_DBG[0] > -1 else 0):
        S.op("dve", lambda E: E.memset(Sf[:], 0.0), writes=[t_Sf])
        S.op("dve", lambda E: E.memset(Sb[:], 0.0), writes=[t_Sb])
        for bi in range(NBLK):
            if d == 0:
                lo, hi = 512 * bi, 512 * bi + 512
            else:
                lo, hi = T - 512 * (bi + 1), T - 512 * bi
            loc = (lambda ap: ap) if d == 0 else None
            s0 = 1 if lo == 0 else 0
            s1 = 513 if hi == T else 514
            osl = slice(0, 512) if d == 0 else rsl(0, 512)
            for c in range(11):
                zs = zsi[0] % NZS
                zsi[0] += 1
                b = c % 2
                if lo == 0:
                    S.op("pool", lambda E, zs=zs: E.memset(zraw[:, zs, 0:1], 0.0), writes=[t_zraw[zs]])
                if hi == T:
                    S.op("pool", lambda E, zs=zs: E.memset(zraw[:, zs, 513:514], 0.0), writes=[t_zraw[zs]])
                S.dma("sp", zraw[:, zs, s0:s1], zv[:, c, lo - 1 + s0:lo - 1 + s1], writes=[t_zraw[zs]])
                S.op("act", lambda E, c=c, b=b, zs=zs: E.activation(out=tmp0[b][:], in_=zraw[:, zs, 1:513], func=AF.Copy,
                                                                   scale=cmu[:, 0, c:c + 1]),
                     reads=[t_zraw[zs], t_c], writes=[t_tmp0[b]])
                S.op("dve", lambda E, c=c, b=b, zs=zs: E.scalar_tensor_tensor(out=tmp1[b][:], in0=zraw[:, zs, 0:512],
                                                                              scalar=cmu[:, 1, c:c + 1], in1=tmp0[b][:],
                                                                              op0=ALU.mult, op1=ALU.add),
                     reads=[t_zraw[zs], t_c, t_tmp0[b]], writes=[t_tmp1[b]])
                S.op("dve", lambda E, c=c, b=b, zs=zs, osl=osl: E.scalar_tensor_tensor(
                    out=zm[:, c, osl], in0=zraw[:, zs, 2:514], scalar=cmu[:, 2, c:c + 1], in1=tmp1[b][:],
                    op0=ALU.mult, op1=ALU.add),
                    reads=[t_zraw[zs], t_c, t_tmp1[b]], writes=[t_zm[c]])
            if RWKV_DBG[0] == 0:
                continue
            S.op("act", lambda E: E.activation(out=tz[0:64, :], in_=zm[0:64, 9, :], func=AF.Tanh),
                 reads=[t_zm[9]], writes=[t_tz])
            S.op("act", lambda E: E.activation(out=tz[64:128, :], in_=zm[64:128, 9, :], func=AF.Copy),
                 reads=[t_zm[9]], writes=[t_tz])
            if d == 0:
                S.op("act", lambda E: E.activation(out=sgz[:], in_=zm[:, 10, :], func=AF.Sigmoid),
                     reads=[t_zm[10]], writes=[t_sgz])
            v4 = lambda ap: ap.rearrange("p (q t) -> p q t", q=4)
            steps = []
            cb = {}

            def ST(f):
                steps.append(f)
            for c in range(3):
                cb[c] = dict(zr=zm[:, c, :], tr=t_zm[c], zk=zm[:, 3 + c, :], tk=t_zm[3 + c], zv=zm[:, 6 + c, :],
                             tv=t_zm[6 + c])

            def s_mm(c):
                X = cb[c]
                X["pW"], X["tpW"] = getbank()
                S.op("pe", lambda E: E.matmul(X["pW"][:], lhsT=w2a2[0:64, d, 128 * c:128 * c + 128], rhs=tz[0:64, :],
                                              start=True, stop=True), reads=[t_c, t_tz], writes=[X["tpW"]])
                X["pA"], X["tpA"] = getbank()
                S.op("pe", lambda E: E.matmul(X["pA"][:], lhsT=w2a2[64:128, d, 128 * c:128 * c + 128], rhs=tz[64:128, :],
                                              start=True, stop=True), reads=[t_c, t_tz], writes=[X["tpA"]])
            ST(s_mm)

            def s_sig(c):
                X = cb[c]
                S.op("act", lambda E: E.activation(out=B1[c][:], in_=X["pW"][:], func=AF.Sigmoid, bias=w0c[:, d, c:c + 1]),
                     reads=[X["tpW"], t_c], writes=[tB1[c]])
                S.op("act", lambda E: E.activation(out=B6[c][:], in_=X["pA"][:], func=AF.Sigmoid, bias=a0c[:, d, c:c + 1]),
                     reads=[X["tpA"], t_c], writes=[tB6[c]])
            ST(s_sig)

            def s_kkv(c):
                X = cb[c]
                S.op("dve", lambda E: E.tensor_scalar(out=B4[c][:], in0=X["zk"], scalar1=kkc[:, c:c + 1], scalar2=None,
                                                      op0=ALU.mult), reads=[X["tk"], t_c], writes=[tB4[c]])
                S.op("pool", lambda E: E.tensor_tensor(out=B5[c][:], in0=B4[c][:], in1=B4[c][:], op=ALU.mult),
                     reads=[tB4[c]], writes=[tB5[c]])
                X["pN"], X["tpN"] = getbank()
                S.op("pe", lambda E: E.matmul(X["pN"][:], lhsT=onesb[:], rhs=B5[c][:], start=True, stop=True),
                     reads=[t_c, tB5[c]], writes=[X["tpN"]])
            ST(s_kkv)

            def s_cls(c):
                S.op("dve", lambda E: E.tensor_tensor_scan(out=B2[c][:], data0=rst[:], data1=B1[c][:], initial=0.0,
                                                           op0=ALU.mult, op1=ALU.add),
                     reads=[tB1[c], t_c], writes=[tB2[c]])
                S.op("pool", lambda E: E.tensor_tensor(out=B1[c][:], in0=B2[c][:], in1=B1[c][:], op=ALU.subtract),
                     reads=[tB2[c]], writes=[tB1[c]])
            ST(s_cls)

            def s_exp(c):
                S.op("act", lambda E: E.activation(out=B3[c][:], in_=B2[c][:], func=AF.Exp, scale=DEC),
                     reads=[tB2[c]], writes=[tB3[c]])
                S.op("act", lambda E: E.activation(out=B2[c][:], in_=B2[c][:], func=AF.Exp, scale=-DEC),
                     reads=[tB2[c]], writes=[tB2[c]])
                S.op("act", lambda E: E.activation(out=B1[c][:], in_=B1[c][:], func=AF.Exp, scale=-DEC),
                     reads=[tB1[c]], writes=[tB1[c]])
            ST(s_exp)

            def s_rn(c):
                X = cb[c]
                S.op("dve", lambda E: E.tensor_scalar(out=B5[c][:], in0=X["pN"][:], scalar1=1e-12, scalar2=None,
                                                      op0=ALU.add), reads=[X["tpN"]], writes=[tB5[c]])
                S.op("act", lambda E: E.activation(out=B5[c][:], in_=B5[c][:], func=AF.Sqrt),
                     reads=[tB5[c]], writes=[tB5[c]])
                S.op("dve", lambda E: E.reciprocal(out=B5[c][:], in_=B5[c][:]),
                     reads=[tB5[c]], writes=[tB5[c]])
                S.op("dve", lambda E: E.tensor_tensor(out=B4[c][:], in0=B4[c][:], in1=B5[c][:], op=ALU.mult),
                     reads=[tB4[c], tB5[c]], writes=[tB4[c]])
                S.op("dve", lambda E: E.tensor_copy(out=WCt[:, c, :], in_=B2[c][:, 127:512:128]),
                     reads=[tB2[c]], writes=[t_WC])
            ST(s_rn)

            def s_kd(c):
                X = cb[c]
                S.op("dve", lambda E: E.tensor_scalar(out=B7[c][:], in0=B6[c][:], scalar1=kac[:, c:c + 1],
                                                      scalar2=omka[:, c:c + 1], op0=ALU.mult, op1=ALU.add),
                     reads=[tB6[c], t_c], writes=[tB7[c]])
                S.op("dve", lambda E: E.tensor_tensor(out=B7[c][:], in0=B7[c][:], in1=X["zk"], op=ALU.mult),
                     reads=[tB7[c], X["tk"]], writes=[tB7[c]])
                S.op("pool", lambda E: E.tensor_tensor(out=B6[c][:], in0=B4[c][:], in1=B6[c][:], op=ALU.mult),
                     reads=[tB4[c], tB6[c]], writes=[tB6[c]])
            ST(s_kd)

            def s_ar(c):
                X = cb[c]
                S.op("dve", lambda E: E.scalar_tensor_tensor(out=ARb[c][:, :, 0, :], in0=v4(B4[c][:]), scalar=-1.0,
                                                             in1=v4(B1[c][:]), op0=ALU.mult, op1=ALU.mult),
                     reads=[tB4[c], tB1[c]], writes=[t_AR[c]])
                S.op("pool", lambda E: E.tensor_tensor(out=ARb[c][:, :, 1, :], in0=v4(X["zr"]), in1=v4(B2[c][:]),
                                                       op=ALU.mult), reads=[X["tr"], tB2[c]], writes=[t_AR[c]])
                S.op("pool", lambda E: E.tensor_tensor(out=Bt[c][:], in0=B6[c][:], in1=B3[c][:], op=ALU.mult),
                     reads=[tB6[c], tB3[c]], writes=[t_Bt[c]])
                S.op("pool", lambda E: E.tensor_tensor(out=Kt[c][:], in0=B7[c][:], in1=B3[c][:], op=ALU.mult),
                     reads=[tB7[c], tB3[c]], writes=[t_Kt[c]])
                S.op("act", lambda E: E.activation(out=vb[c][:], in_=X["zv"], func=AF.Copy),
                     reads=[X["tv"]], writes=[t_vb[c]])
            ST(s_ar)

            def s_bb(c):
                wcb = WCt[:, c, :].unsqueeze(2).to_broadcast([128, 4, 128])
                S.op("pool", lambda E: E.tensor_tensor(out=v4(Bb[c][:]), in0=v4(Bt[c][:]), in1=wcb, op=ALU.mult),
                     reads=[t_Bt[c], t_WC], writes=[t_Bb[c]])
                S.op("pool", lambda E: E.tensor_tensor(out=v4(Kb[c][:]), in0=v4(Kt[c][:]), in1=wcb, op=ALU.mult),
                     reads=[t_Kt[c], t_WC], writes=[t_Kb[c]])
            ST(s_bb)

            def s_bonus(c):
                X = cb[c]
                S.op("dve", lambda E: E.scalar_tensor_tensor(out=B5[c][:], in0=X["zr"], scalar=rkc[:, c:c + 1],
                                                             in1=B7[c][:], op0=ALU.mult, op1=ALU.mult),
                     reads=[X["tr"], tB7[c], t_c], writes=[tB5[c]])
                pBn, t_pBn = getbank()
                S.op("pe", lambda E: E.matmul(pBn[:], lhsT=onesb[:], rhs=B5[c][:], start=True, stop=True),
                     reads=[t_c, tB5[c]], writes=[t_pBn])
                si = sgi[0] % 3
                sgi[0] += 1
                S.op("dve", lambda E: E.tensor_tensor(out=stg[si][:, osl], in0=pBn[:], in1=X["zv"], op=ALU.mult),
                     reads=[t_pBn, X["tv"]], writes=[t_stg[si]])
                S.dma("sp", bon[d][128 * c:128 * c + 128, lo:hi], stg[si][:], reads=[t_stg[si]], writes=[t_out])
                if d == 0:
                    pG, t_pG = getbank()
                    S.op("pe", lambda E: E.matmul(pG[:], lhsT=g2b[:, 128 * c:128 * c + 128], rhs=sgz[:], start=True,
                                                  stop=True), reads=[t_c, t_sgz], writes=[t_pG])
                    si2 = sgi[0] % 3
                    sgi[0] += 1
                    S.op("act", lambda E: E.activation(out=stg[si2][:], in_=pG[:], func=AF.Copy),
                         reads=[t_pG], writes=[t_stg[si2]])
                    S.dma("sp", gfm[128 * c:128 * c + 128, lo:hi], stg[si2][:], reads=[t_stg[si2]], writes=[t_out])
            ST(s_bonus)

            def s_tr(c):
                for q in range(4):
                    ts_ = trr[0] % 2
                    trr[0] += 1
                    srcs = [(ARb[c][:, q, 0, :], t_AR[c]), (Bb[c][:, 128 * q:128 * q + 128], t_Bb[c]),
                            (Kb[c][:, 128 * q:128 * q + 128], t_Kb[c]), (vb[c][:, 128 * q:128 * q + 128], t_vb[c])]
                    for ai, (src, tk) in enumerate(srcs):
                        S.op("pe", lambda E, ts_=ts_, ai=ai, src=src: E.transpose(out=pTr[ts_][:, ai, :], in_=src,
                                                                                 identity=ident[:]),
                             reads=[tk, t_id], writes=[t_pTr[ts_]])
                    if q % 2 == 0:
                        S.op("act", lambda E, ts_=ts_, q=q: E.activation(out=tm[c][q][:], in_=pTr[ts_][:], func=AF.Copy),
                             reads=[t_pTr[ts_]], writes=[t_tm[c][q]])
                    else:
                        S.op("dve", lambda E, ts_=ts_, q=q: E.tensor_copy(out=tm[c][q][:], in_=pTr[ts_][:]),
                             reads=[t_pTr[ts_]], writes=[t_tm[c][q]])
            ST(s_tr)
            for st_ in steps:
                for c in range(3):
                    st_(c)
            yb = bi % 2
            def chunk_gen(q):
                MP, t_MP, MT, t_MT = MPs[q % 2], t_MPs[q % 2], MTs[q % 2], t_MTs[q % 2]
                qs = slice(128 * q, 128 * q + 128)
                for h in range(6):
                    c, h2 = h // 2, h % 2
                    rows = slice(64 * h2, 64 * h2 + 64)
                    pM, t_pM = getbank()
                    S.op("pe", lambda E, pM=pM, c=c, rows=rows, qs=qs, q=q: E.matmul(
                        pM[:, 0:256], lhsT=Bt[c][rows, qs], rhs=ARb[c][rows, q, :, :].rearrange("p a b -> p (a b)"), start=True, stop=True),
                        reads=[t_Bt[c], t_AR[c]], writes=[t_pM])
                    S.op("pe", lambda E, pM=pM, c=c, rows=rows, qs=qs, q=q: E.matmul(
                        pM[:, 256:512], lhsT=Kt[c][rows, qs], rhs=ARb[c][rows, q, :, :].rearrange("p a b -> p (a b)"), start=True, stop=True),
                        reads=[t_Kt[c], t_AR[c]], writes=[t_pM])
                    S.op("dve", lambda E, pM=pM, h=h: E.tensor_tensor(out=MP[h][:], in0=pM[:], in1=mskU[:], op=ALU.mult),
                         reads=[t_pM, t_c], writes=[t_MP[h]])
                for hg in range(2):
                    pM3, t_pM3 = getbank()
                    for j in range(3):
                        h = 2 * j + hg
                        c, h2 = j, hg
                        rows = slice(64 * h2, 64 * h2 + 64)
                        S.op("pe", lambda E, pM3=pM3, j=j, c=c, rows=rows, qs=qs, q=q: E.matmul(
                            pM3[:, 128 * j:128 * j + 128], lhsT=ARb[c][rows, q, 0, :], rhs=Bt[c][rows, qs],
                            start=True, stop=True), reads=[t_Bt[c], t_AR[c]], writes=[t_pM3])
                    S.op("dve", lambda E, pM3=pM3, hg=hg: E.tensor_tensor(
                        out=MT[hg][:], in0=pM3[:, 0:384].rearrange("p (a b) -> p a b", a=3), in1=mskL[:], op=ALU.mult),
                        reads=[t_pM3, t_c], writes=[t_MT[hg]])
                yield "scores"
                cur = [0, 0]
                for hg in range(2):
                    for j in range(3):
                        h = 2 * j + hg
                        S.op("pool", lambda E, hg=hg, j=j, h=h: E.tensor_tensor(out=Tm[hg][0][:, j, :], in0=MP[h][:, 0:128],
                                                                              in1=identf[:], op=ALU.add),
                             reads=[t_MP[h], t_id], writes=[t_Tm[hg][0]])
                v3 = lambda ap: ap[:, 0:384].rearrange("p (a b) -> p a b", a=3)
                yield "t0"
                for lev in range(1, 7):
                    if lev > 1:
                        yield "lev"
                    bk = {}
                    for hg in range(2):
                        pv = cur[hg]
                        pP, t_pP = getbank()
                        pPT, t_pPT = getbank()
                        bk[hg] = (pP, t_pP, pPT, t_pPT)
                        for j in range(3):
                            h = 2 * j + hg
                            if lev == 1:
                                Pprev, tP = MP[h][:, 0:128], t_MP[h]
                                PTprev, tPT = MT[hg][:, j, :], t_MT[hg]
                            else:
                                Pprev, tP = Pm[hg][pv][:, j, :], t_Pm[hg][pv]
                                PTprev, tPT = PmT[hg][pv][:, j, :], t_PmT[hg][pv]
                            if lev < 6:
                                S.op("pe", lambda E, pP=pP, j=j, Pprev=Pprev, PTprev=PTprev: E.matmul(
                                    pP[:, 128 * j:128 * j + 128], lhsT=PTprev, rhs=Pprev, start=True, stop=True),
                                    reads=[tP, tPT], writes=[t_pP])
                            S.op("pe", lambda E, pPT=pPT, j=j, Pprev=Pprev, PTprev=PTprev: E.matmul(
                                pPT[:, 128 * j:128 * j + 128], lhsT=Pprev, rhs=PTprev, start=True, stop=True),
                                reads=[tP, tPT], writes=[t_pPT])
                    for hg in range(2):
                        pP, t_pP, pPT, t_pPT = bk[hg]
                        nx = 1 - cur[hg]
                        if lev < 6:
                            S.op("act", lambda E, pP=pP, hg=hg, nx=nx: E.activation(out=Pm[hg][nx][:], in_=v3(pP),
                                                                                    func=AF.Copy),
                                 reads=[t_pP], writes=[t_Pm[hg][nx]])
                        S.op("dve", lambda E, pPT=pPT, hg=hg, nx=nx: E.tensor_copy(out=PmT[hg][nx][:], in_=v3(pPT)),
                             reads=[t_pPT], writes=[t_PmT[hg][nx]])
                    bt = {}
                    for hg in range(2):
                        pv = cur[hg]
                        nx = 1 - pv
                        pTT, t_pTT = getbank()
                        bt[hg] = (pTT, t_pTT)
                        for j in range(3):
                            S.op("pe", lambda E, pTT=pTT, j=j, hg=hg, nx=nx, pv=pv: E.matmul(
                                pTT[:, 128 * j:128 * j + 128], lhsT=PmT[hg][nx][:, j, :], rhs=Tm[hg][pv][:, j, :],
                                start=True, stop=True), reads=[t_PmT[hg][nx], t_Tm[hg][pv]], writes=[t_pTT])
                    for hg in range(2):
                        pv = cur[hg]
                        nx = 1 - pv
                        pTT, t_pTT = bt[hg]
                        S.op("dve", lambda E, pTT=pTT, hg=hg, nx=nx, pv=pv: E.tensor_tensor(
                            out=Tm[hg][nx][:], in0=v3(pTT), in1=Tm[hg][pv][:], op=ALU.add),
                            reads=[t_pTT, t_Tm[hg][pv]], writes=[t_Tm[hg][nx]])
                        cur[hg] = nx
                TF = [Tm[0][cur[0]], Tm[1][cur[1]]]
                tTF = [t_Tm[0][cur[0]], t_Tm[1][cur[1]]]
                pX, t_pX = getbank()
                for h in range(6):
                    c, h2 = h // 2, h % 2
                    sl_ = 3 * h2 + c
                    S.op("pe", lambda E, pX=pX, h=h, c=c, h2=h2, q=q, sl_=sl_: E.matmul(
                        pX[:, 64 * sl_:64 * sl_ + 64], lhsT=MP[h][:, 256:384], rhs=tm[c][q][:, 3, 64 * h2:64 * h2 + 64],
                        start=True, stop=True), reads=[t_MP[h], t_tm[c][q]], writes=[t_pX])
                S.op("act", lambda E, pX=pX: E.activation(out=X0[:], in_=pX[:, 0:384].rearrange("p (a b) -> p a b", a=6),
                                                         func=AF.Copy), reads=[t_pX], writes=[t_X0])
                pV, t_pV = getbank()
                for h in range(6):
                    c, h2 = h // 2, h % 2
                    sl_ = 3 * h2 + c
                    S.op("pe", lambda E, pV=pV, c=c, h2=h2, sl_=sl_: E.matmul(
                        pV[:, 64 * sl_:64 * sl_ + 64], lhsT=TF[h2][:, c, :], rhs=X0[:, sl_, :], start=True, stop=True),
                        reads=[tTF[h2], t_X0], writes=[t_pV])
                S.op("act", lambda E, pV=pV: E.activation(out=Uv[:], in_=pV[:, 0:384].rearrange("p (a b) -> p a b", a=6),
                                                         func=AF.Copy), reads=[t_pV], writes=[t_Uv])
                pH, t_pH = getbank()
                for h in range(6):
                    c, h2 = h // 2, h % 2
                    S.op("pe", lambda E, pH=pH, c=c, h2=h2, q=q: E.matmul(
                        pH[64 * h2:64 * h2 + 64, 128 * c:128 * c + 128], lhsT=tm[c][q][:, 0, 64 * h2:64 * h2 + 64],
                        rhs=TF[h2][:, c, :], start=True, stop=True), reads=[tTF[h2], t_tm[c][q]], writes=[t_pH])
                S.op("dve", lambda E, pH=pH: E.tensor_copy(out=Ahb[:], in_=pH[:, 0:384].rearrange("p (a b) -> p a b", a=3)),
                     reads=[t_pH], writes=[t_Ah])
                yield "post"
                pUb = [getbank(), getbank()]
                for h2 in range(2):
                    rows = slice(64 * h2, 64 * h2 + 64)
                    for c in range(3):
                        S.op("pe", lambda E, h2=h2, c=c, rows=rows: E.matmul(
                            pUb[h2][0][:, 64 * c:64 * c + 64], lhsT=Ahb[rows, c, :], rhs=Sb[rows, c, :], start=True,
                            stop=True), reads=[t_Ah, t_Sb], writes=[pUb[h2][1]])
                for h2 in range(2):
                    S.op("dve", lambda E, h2=h2: E.tensor_tensor(
                        out=Ub[:, 3 * h2:3 * h2 + 3, :], in0=pUb[h2][0][:, 0:192].rearrange("p (a b) -> p a b", a=3),
                        in1=Uv[:, 3 * h2:3 * h2 + 3, :], op=ALU.add),
                        reads=[pUb[h2][1], t_Uv], writes=[t_Ub])
                yield "q1"
                pYb = [getbank(), getbank()]
                for h2 in range(2):
                    rows = slice(64 * h2, 64 * h2 + 64)
                    for c in range(3):
                        h = 2 * c + h2
                        sl_ = 3 * h2 + c
                        oc = slice(64 * c, 64 * c + 64)
                        S.op("pe", lambda E, h2=h2, c=c, rows=rows, q=q, oc=oc: E.matmul(
                            pYb[h2][0][:, oc], lhsT=ARb[c][rows, q, 1, :], rhs=Sb[rows, c, :], start=True, stop=False),
                            reads=[t_AR[c], t_Sb], writes=[pYb[h2][1]])
                        S.op("pe", lambda E, h2=h2, h=h, sl_=sl_, oc=oc: E.matmul(
                            pYb[h2][0][:, oc], lhsT=MP[h][:, 128:256], rhs=Ub[:, sl_, :], start=False, stop=False),
                            reads=[t_MP[h], t_Ub], writes=[pYb[h2][1]])
                        S.op("pe", lambda E, h2=h2, h=h, c=c, q=q, oc=oc: E.matmul(
                            pYb[h2][0][:, oc], lhsT=MP[h][:, 384:512], rhs=tm[c][q][:, 3, 64 * h2:64 * h2 + 64],
                            start=False, stop=True), reads=[t_MP[h], t_tm[c][q]], writes=[pYb[h2][1]])
                for h2 in range(2):
                    S.op("act", lambda E, h2=h2, yb=yb, q=q: E.activation(
                        out=ytm[yb][:, q, :].rearrange("p (c g v) -> p g c v", g=2, v=64)[:, h2, :, :],
                        in_=pYb[h2][0][:, 0:192].rearrange("p (a b) -> p a b", a=3), func=AF.Copy),
                        reads=[pYb[h2][1]], writes=[t_ytm[yb]])
                pS_, t_pS = getbank()
                for h in range(6):
                    c, h2 = h // 2, h % 2
                    orow = slice(64 * h2, 64 * h2 + 64)
                    S.op("pe", lambda E, pS_=pS_, h=h, c=c, h2=h2, orow=orow, q=q: E.matmul(
                        pS_[orow, 64 * c:64 * c + 64], lhsT=tm[c][q][:, 1, 64 * h2:64 * h2 + 64], rhs=Ub[:, 3 * h2 + c, :],
                        start=True, stop=False), reads=[t_tm[c][q], t_Ub], writes=[t_pS])
                    S.op("pe", lambda E, pS_=pS_, h=h, c=c, h2=h2, orow=orow, q=q: E.matmul(
                        pS_[orow, 64 * c:64 * c + 64], lhsT=tm[c][q][:, 2, 64 * h2:64 * h2 + 64],
                        rhs=tm[c][q][:, 3, 64 * h2:64 * h2 + 64], start=False, stop=True),
                        reads=[t_tm[c][q]], writes=[t_pS])
                for c in range(3):
                    S.op("dve", lambda E, pS_=pS_, c=c, q=q: E.scalar_tensor_tensor(
                        out=Sf[:, c, :], in0=Sf[:, c, :], scalar=WCt[:, c, q:q + 1], in1=pS_[:, 64 * c:64 * c + 64],
                        op0=ALU.mult, op1=ALU.add), reads=[t_pS, t_WC, t_Sf], writes=[t_Sf])
                S.op("act", lambda E: E.activation(out=Sb[:], in_=Sf[:], func=AF.Copy), reads=[t_Sf], writes=[t_Sb])
            def run_until(g, label):
                for lab in g:
                    if lab == label:
                        return True
                return False
            gens = [chunk_gen(q) for q in range(4)]
            run_until(gens[0], "post")
            for q in range(4):
                g, gn = gens[q], (gens[q + 1] if q + 1 < 4 else None)
                if gn is not None:
                    run_until(gn, "scores")
                run_until(g, "q1")
                if gn is not None:
                    run_until(gn, "t0")
                    run_until(gn, "lev")
                    run_until(gn, "lev")
                run_until(g, "never")
                if gn is not None:
                    run_until(gn, "post")
            lb = 512 * bi
            S.dma("sp", ydir[d][lb:lb + 512, :].rearrange("(q p) f -> p q f", p=128), ytm[yb][:], reads=[t_ytm[yb]],
                  writes=[t_out])
    P.close()


LNX_EPS = 64e-5


def phase_rwkv_combine(nc, S, ydir, bon, gfm, lnx_w, lnx_b, mixT, T):
    P = Phase(nc, S)
    NT = T // 128
    ident, identf, t_id = make_ident(nc, S, P)
    t_c = Tok()
    J = P.sb([128, 128], F32)
    S.op("pool", lambda E: E.memset(J[:], 1.0), writes=[t_c])
    S.op("pool", lambda E: E.affine_select(out=J[:], in_=J[:], pattern=[[1, 128]], base=-127, channel_multiplier=1,
                                           compare_op=ALU.is_equal, fill=0.0), reads=[t_c], writes=[t_c])
    lwt = P.sb([128, 384], F32); lbt = P.sb([128, 384], F32)
    S.dma("sp", lwt[:], lnx_w.partition_broadcast(128), writes=[t_c])
    S.dma("sp", lbt[:], lnx_b.partition_broadcast(128), writes=[t_c])
    mh = P.sb([128, 6], F32)
    S.op("pool", lambda E: E.memset(mh[:], -0.5), writes=[t_c])
    NBF = 3
    y0 = [P.sb([128, 384], F32) for _ in range(NBF)]; t_y0 = [Tok() for _ in range(NBF)]
    y1 = [P.sb([128, 384], F32) for _ in range(NBF)]; t_y1 = [Tok() for _ in range(NBF)]
    b0 = [P.sb([128, 3, 128], F32) for _ in range(NBF)]; t_b0 = [Tok() for _ in range(NBF)]
    b1 = [P.sb([128, 3, 128], F32) for _ in range(NBF)]; t_b1 = [Tok() for _ in range(NBF)]
    gt = [P.sb([128, 3, 128], F32) for _ in range(NBF)]; t_gt = [Tok() for _ in range(NBF)]
    ys = [P.sb([128, 6, 64], F32) for _ in range(NBF)]; t_ys = [Tok() for _ in range(NBF)]
    sq = [P.sb([128, 6, 64], F32) for _ in range(NBF)]; t_sq = [Tok() for _ in range(NBF)]
    st = [P.sb([128, 4, 6], F32) for _ in range(NBF)]; t_st = [Tok() for _ in range(NBF)]
    rs = [P.sb([128, 384], F32) for _ in range(NBF)]; t_rs = [Tok() for _ in range(NBF)]
    ob = [P.sb([128, 3, 128], BF16) for _ in range(NBF)]; t_ob = [Tok() for _ in range(NBF)]
    pJ = [P.ps([128, 512], F32) for _ in range(2)]; t_pJ = [Tok(), Tok()]
    pB = [P.ps([128, 512], F32) for _ in range(2)]; t_pB = [Tok(), Tok()]
    pG = [P.ps([128, 512], F32) for _ in range(2)]; t_pG = [Tok(), Tok()]
    pO = [P.ps([128, 512], F32) for _ in range(2)]; t_pO = [Tok(), Tok()]
    t_out = Tok()
    f3 = lambda ap: ap.rearrange("p (a b) -> p a b", a=6)
    def stA(n):
        b = n % NBF
        pb2 = n % 2
        tl = slice(128 * n, 128 * n + 128)
        S.dma("sp", y0[b][:], ydir[0][128 * n:128 * n + 128, :], writes=[t_y0[b]])
        S.dma("sp", y1[b][:], ydir[1][T - 128 * (n + 1):T - 128 * n, :], writes=[t_y1[b]])
        S.dma("sp", b0[b][:], bon[0][:, tl].rearrange("(c p) t -> p c t", p=128), writes=[t_b0[b]])
        S.dma("sp", b1[b][:], bon[1][:, tl].rearrange("(c p) t -> p c t", p=128), writes=[t_b1[b]])
        S.dma("sp", gt[b][:], gfm[:, tl].rearrange("(c p) t -> p c t", p=128), writes=[t_gt[b]])
        S.op("pe", lambda E, b=b: E.matmul(pJ[pb2][:, 0:384], lhsT=J[:], rhs=y1[b][:], start=True, stop=True),
             reads=[t_c, t_y1[b]], writes=[t_pJ[pb2]])
        S.op("dve", lambda E, b=b: E.tensor_tensor(out=ys[b][:], in0=f3(pJ[pb2][:, 0:384]), in1=f3(y0[b][:]), op=ALU.add),
             reads=[t_pJ[pb2], t_y0[b]], writes=[t_ys[b]])
        S.op("dve", lambda E, b=b: E.tensor_reduce(out=st[b][:, 0, :], in_=ys[b][:], axis=AX.X, op=ALU.add),
             reads=[t_ys[b]], writes=[t_st[b]])
        S.op("dve", lambda E, b=b: E.tensor_scalar(out=st[b][:, 0, :], in0=st[b][:, 0, :], scalar1=1.0 / 64, scalar2=None,
                                                   op0=ALU.mult), reads=[t_st[b]], writes=[t_st[b]])
        S.op("dve", lambda E, b=b: E.tensor_tensor(out=ys[b][:], in0=ys[b][:],
                                                   in1=st[b][:, 0, :].unsqueeze(2).to_broadcast([128, 6, 64]),
                                                   op=ALU.subtract), reads=[t_st[b], t_ys[b]], writes=[t_ys[b]])
        S.op("pool", lambda E, b=b: E.tensor_tensor(out=sq[b][:], in0=ys[b][:], in1=ys[b][:], op=ALU.mult),
             reads=[t_ys[b]], writes=[t_sq[b]])
        S.op("dve", lambda E, b=b: E.tensor_reduce(out=st[b][:, 1, :], in_=sq[b][:], axis=AX.X, op=ALU.add),
             reads=[t_sq[b]], writes=[t_st[b]])
        S.op("dve", lambda E, b=b: E.tensor_scalar(out=st[b][:, 2, :], in0=st[b][:, 1, :], scalar1=1.0 / 64,
                                                   scalar2=LNX_EPS, op0=ALU.mult, op1=ALU.add),
             reads=[t_st[b]], writes=[t_st[b]])
        S.op("pool", lambda E, b=b: E.tensor_tensor(out=st[b][:, 3, :], in0=st[b][:, 2, :], in1=mh[:], op=ALU.pow),
             reads=[t_st[b], t_c], writes=[t_st[b]])
        S.op("dve", lambda E, b=b: E.tensor_tensor(out=ys[b][:], in0=ys[b][:],
                                                   in1=st[b][:, 3, :].unsqueeze(2).to_broadcast([128, 6, 64]),
                                                   op=ALU.mult), reads=[t_st[b], t_ys[b]], writes=[t_ys[b]])
        yf = ys[b][:].rearrange("p a b -> p (a b)")
        S.op("pool", lambda E, b=b, yf=yf: E.tensor_tensor(out=yf, in0=yf, in1=lwt[:], op=ALU.mult),
             reads=[t_ys[b], t_c], writes=[t_ys[b]])
        S.op("pool", lambda E, b=b, yf=yf: E.tensor_tensor(out=yf, in0=yf, in1=lbt[:], op=ALU.add),
             reads=[t_ys[b], t_c], writes=[t_ys[b]])
        S.op("dve", lambda E, b=b: E.tensor_tensor(out=b0[b][:], in0=b0[b][:], in1=b1[b][:], op=ALU.add),
             reads=[t_b0[b], t_b1[b]], writes=[t_b0[b]])
    def stB(n):
        b = n % NBF
        pb2 = n % 2
        tl = slice(128 * n, 128 * n + 128)
        yf = ys[b][:].rearrange("p a b -> p (a b)")
        for c in range(3):
            S.op("pe", lambda E, b=b, c=c: E.transpose(out=pB[pb2][:, 128 * c:128 * c + 128], in_=b0[b][:, c, :],
                                                       identity=identf[:]),
                 reads=[t_b0[b], t_id], writes=[t_pB[pb2]])
        for c in range(3):
            S.op("pe", lambda E, b=b, c=c: E.transpose(out=pG[pb2][:, 128 * c:128 * c + 128], in_=gt[b][:, c, :],
                                                       identity=identf[:]),
                 reads=[t_gt[b], t_id], writes=[t_pG[pb2]])
        S.op("dve", lambda E, b=b, yf=yf: E.tensor_tensor(out=rs[b][:], in0=pB[pb2][:, 0:384], in1=yf, op=ALU.add),
             reads=[t_pB[pb2], t_ys[b]], writes=[t_rs[b]])
        S.op("dve", lambda E, b=b: E.tensor_tensor(out=rs[b][:], in0=pG[pb2][:, 0:384], in1=rs[b][:], op=ALU.mult),
             reads=[t_pG[pb2], t_rs[b]], writes=[t_rs[b]])
        for c in range(3):
            S.op("pe", lambda E, b=b, c=c: E.transpose(out=pO[pb2][:, 128 * c:128 * c + 128],
                                                       in_=rs[b][:, 128 * c:128 * c + 128], identity=identf[:]),
                 reads=[t_rs[b], t_id], writes=[t_pO[pb2]])
        S.op("act", lambda E, b=b: E.activation(out=ob[b][:], in_=pO[pb2][:, 0:384].rearrange("p (a b) -> p a b", a=3),
                                                func=AF.Copy), reads=[t_pO[pb2]], writes=[t_ob[b]])
        S.dma("sp", mixT[0:384, tl].rearrange("(c p) t -> p c t", p=128), ob[b][:], reads=[t_ob[b]], writes=[t_out])
    for n in range(NT + 1):
        if n < NT:
            stA(n)
        if n >= 1:
            stB(n - 1)
    P.close()


PARAM_SHAPES = {
    "ffn1_norm_g": [2, 1024], "ffn1_w_gate": [2, 1024, 2816], "ffn1_w_up": [2, 1024, 2816],
    "ffn1_w_down": [2, 2816, 1024], "mix_norm_g": [2, 1024], "w_in": [2, 1024, 2816], "w_out": [2, 1024, 1024],
    "rwkv_mu_prev": [2, 1408], "rwkv_mu_next": [2, 1408], "rwkv_decay_w0": [2, 2, 384],
    "rwkv_decay_w2": [2, 2, 64, 384], "rwkv_iclr_a0": [2, 2, 384], "rwkv_iclr_a2": [2, 2, 64, 384],
    "rwkv_gate_w2": [2, 128, 384], "rwkv_k_k": [2, 384], "rwkv_k_a": [2, 384], "rwkv_r_k": [2, 6, 64],
    "rwkv_lnx_w": [2, 384], "rwkv_lnx_b": [2, 384], "s5_a_re": [2, 2, 16, 64], "s5_a_im": [2, 2, 16, 64],
    "s5_log_step": [2, 2, 16], "s5_b_re": [2, 16, 64, 16], "s5_b_im": [2, 16, 64, 16],
    "s5_c_re": [2, 2, 16, 16, 64], "s5_c_im": [2, 2, 16, 16, 64], "s5_d": [2, 256], "s5_glu_w": [2, 256, 512],
    "s5_glu_b": [2, 512], "ffn2_norm_g": [2, 1024], "ffn2_w_gate": [2, 1024, 2816], "ffn2_w_up": [2, 1024, 2816],
    "ffn2_w_down": [2, 2816, 1024], "final_norm_g": [1024],
}
DEPTH = 2


def build_program(T, depth=DEPTH):
    nc = bass.Bass("TRN2", target_bir_lowering=False)
    x = nc.dram_tensor("x", [T, D], F32, kind="ExternalInput").ap()
    p = {k: nc.dram_tensor(k, list(s), F32, kind="ExternalInput").ap() for k, s in PARAM_SHAPES.items()}
    out = nc.dram_tensor("out", [T, D], F32, kind="ExternalOutput").ap()
    h = nc.dram_tensor("h_res", [T, D], F32).ap()
    zr = nc.dram_tensor("z_rwkv", [1408, T], F32).ap()
    qk = nc.dram_tensor("z_qk", [768, T], BF16).ap()
    vtm = nc.dram_tensor("z_v", [T, 384], BF16).ap()
    us5 = nc.dram_tensor("z_s5", [256, T], F32).ap()
    mixT = nc.dram_tensor("mixT", [1024, T], BF16).ap()
    ydir = nc.dram_tensor("y_dir", [2, T, 384], F32).ap()
    bon = nc.dram_tensor("bonus", [2, 384, T], F32).ap()
    gfm = nc.dram_tensor("gate", [384, T], F32).ap()
    S = SchedI(nc)
    NT = T // 128
    tk = [Tok() for _ in range(NT)]
    for l in range(depth):
        src = x if l == 0 else h
        tk2 = [Tok() for _ in range(NT)]
        phase_ffn(nc, S, src, h, p["ffn1_norm_g"][l], p["ffn1_w_gate"][l], p["ffn1_w_up"][l], p["ffn1_w_down"][l],
                  tk, tk2, T)
        tk = tk2
        phase_win(nc, S, h, p["mix_norm_g"][l], p["w_in"][l], zr, qk, vtm, us5, tk, T)
        phase_rwkv(nc, S, zr, ydir, bon, gfm,
                   [p["rwkv_mu_prev"][l], p["rwkv_mu_next"][l], p["rwkv_decay_w0"][l], p["rwkv_decay_w2"][l],
                    p["rwkv_iclr_a0"][l], p["rwkv_iclr_a2"][l], p["rwkv_gate_w2"][l], p["rwkv_k_k"][l],
                    p["rwkv_k_a"][l], p["rwkv_r_k"][l]], T)
        phase_rwkv_combine(nc, S, ydir, bon, gfm, p["rwkv_lnx_w"][l], p["rwkv_lnx_b"][l], mixT, T)
        phase_attn(nc, S, qk, vtm, mixT, T)
        phase_s5(nc, S, us5, mixT,
                 [p["s5_a_re"][l], p["s5_a_im"][l], p["s5_log_step"][l], p["s5_b_re"][l], p["s5_b_im"][l],
                  p["s5_c_re"][l], p["s5_c_im"][l], p["s5_d"][l], p["s5_glu_w"][l], p["s5_glu_b"][l]], T)
        tk2 = [Tok() for _ in range(NT)]
        phase_wout(nc, S, h, h, mixT, p["w_out"][l], tk, tk2, T)
        tk = tk2
        tk2 = [Tok() for _ in range(NT)]
        phase_ffn(nc, S, h, h, p["ffn2_norm_g"][l], p["ffn2_w_gate"][l], p["ffn2_w_up"][l], p["ffn2_w_down"][l],
                  tk, tk2, T)
        tk = tk2
    tko = [Tok() for _ in range(NT)]
    phase_final(nc, S, h, out, p["final_norm_g"], tk, tko, T)
    S.finish()
    return nc, S


def kernel(**inputs):
    x = np.ascontiguousarray(np.asarray(inputs["x"], dtype=np.float32))
    B, T, _ = x.shape
    nc, S = build_program(T)
    params = {k: np.ascontiguousarray(np.asarray(inputs[k], dtype=np.float32)) for k in PARAM_SHAPES}
    in_maps = []
    for b in range(B):
        m = {"x": x[b]}
        m.update(params)
        in_maps.append(m)
    res = run_bass_kernel_spmd(nc, in_maps, core_ids=list(range(B)))
    return np.stack([np.asarray(r["out"], dtype=np.float32) for r in res.results], axis=0)
```

```python
import numpy as np
import concourse.bass as bass
import concourse.mybir as mybir
from concourse.bass_utils import run_bass_kernel_spmd

F32 = mybir.dt.float32
BF16 = mybir.dt.bfloat16
I32 = mybir.dt.int32
AF = mybir.ActivationFunctionType
ALU = mybir.AluOpType
AX = mybir.AxisListType

ENGS = ("pe", "act", "dve", "pool", "sp")


class Tok:
    __slots__ = ("w", "r", "name")

    def __init__(self, name=""):
        self.w = None
        self.r = {}
        self.name = name


class Sched:
    def __init__(self, nc, lanes_sp=8, lanes_pool=6, lanes_act=2, same_engine_sync=True):
        self.nc = nc
        self.ops = {e: [] for e in ENGS}
        self.cnt = {}
        self.sems = {}
        self.seen = {e: {} for e in ENGS}
        self.same = same_engine_sync
        self._ctx = []
        for e in ("pe", "act", "dve", "pool"):
            self._mksem(e)
        self.lanes = {"sp": [], "pool": [], "act": []}
        for q, n in (("sp", lanes_sp), ("pool", lanes_pool), ("act", lanes_act)):
            for i in range(n):
                nm = f"ln_{q}{i}"
                self._mksem(nm)
                self.lanes[q].append(nm)
        self.lane_rr = {"sp": 0, "pool": 0, "act": 0}
        self.n_instr = 0

    def _mksem(self, name):
        cm = self.nc.semaphore(name)
        s = cm.__enter__()
        self._ctx.append(cm)
        self.sems[name] = s
        self.cnt[name] = 0

    def _collect(self, eng, reads, writes):
        need = {}

        def add(src, val):
            if src == eng and (eng == "pe" or not self.same or eng == "sp"):
                return
            if need.get(src, 0) < val:
                need[src] = val
        for t in reads:
            if t.w is not None:
                add(*t.w)
        for t in writes:
            if t.w is not None:
                add(*t.w)
            for s, v in t.r.items():
                add(s, v)
        out = []
        seen = self.seen[eng]
        for s, v in need.items():
            if seen.get(s, 0) < v:
                seen[s] = v
                out.append((self.sems[s], v))
        return out

    def op(self, eng, fn, reads=(), writes=()):
        waits = self._collect(eng, reads, writes)
        self.cnt[eng] += 1
        c = self.cnt[eng]
        sem = self.sems[eng]

        def emit(E, waits=waits, fn=fn, sem=sem):
            for s, v in waits:
                E.wait_ge(s, v)
            fn(E).then_inc(sem, 1)
        self.ops[eng].append(emit)
        for t in reads:
            t.r[eng] = c
        for t in writes:
            t.w = (eng, c)
            t.r = {}
        self.n_instr += 1

    def dma(self, q, out, in_, reads=(), writes=(), **kw):
        lanes = self.lanes[q]
        ln = lanes[self.lane_rr[q] % len(lanes)]
        self.lane_rr[q] += 1
        waits = self._collect(q, reads, writes)
        prev = self.cnt[ln]
        if prev and self.seen[q].get(ln, 0) < prev:
            self.seen[q][ln] = prev
            waits.append((self.sems[ln], prev))
        self.cnt[ln] += 16
        c = self.cnt[ln]
        sem = self.sems[ln]

        def emit(E, waits=waits, sem=sem, out=out, in_=in_, kw=kw):
            for s, v in waits:
                E.wait_ge(s, v)
            E.dma_start(out=out, in_=in_, **kw).then_inc(sem, 16)
        self.ops[q].append(emit)
        for t in reads:
            t.r[ln] = c
        for t in writes:
            t.w = (ln, c)
            t.r = {}
        self.n_instr += 1

    def finish(self, final_toks):
        nc = self.nc
        fin = []
        need = {}
        for t in final_toks:
            if t.w is not None and need.get(t.w[0], 0) < t.w[1]:
                need[t.w[0]] = t.w[1]
        for s, v in self.cnt.items():
            if v and need.get(s, 0) < v:
                need[s] = v
        for s, v in need.items():
            fin.append((self.sems[s], v))
        ops = self.ops
        with nc.Block() as block:
            @block.tensor
            def _(E):
                for f in ops["pe"]:
                    f(E)

            @block.scalar
            def _(E):
                for f in ops["act"]:
                    f(E)

            @block.vector
            def _(E):
                for f in ops["dve"]:
                    f(E)

            @block.gpsimd
            def _(E):
                for f in ops["pool"]:
                    f(E)

            @block.sync
            def _(E):
                for f in ops["sp"]:
                    f(E)
                for s, v in fin:
                    E.wait_ge(s, v)
        for cm in reversed(self._ctx):
            cm.__exit__(None, None, None)


class Alloc:
    def __init__(self, nc):
        self.nc = nc
        self._ctx = []

    def sb(self, name, shape, dt):
        cm = self.nc.sbuf_tensor(name, list(shape), dt)
        t = cm.__enter__()
        self._ctx.append(cm)
        return t

    def ps(self, name, shape, dt):
        cm = self.nc.psum_tensor(name, list(shape), dt)
        t = cm.__enter__()
        self._ctx.append(cm)
        return t

    def close(self):
        for cm in reversed(self._ctx):
            cm.__exit__(None, None, None)


class SchedI(Sched):
    def __init__(self, nc, **kw):
        super().__init__(nc, **kw)
        self.E = {"pe": nc.tensor, "act": nc.scalar, "dve": nc.vector, "pool": nc.gpsimd, "sp": nc.sync}

    limit = 10 ** 9

    def op(self, eng, fn, reads=(), writes=()):
        if self.n_instr >= self.limit:
            return
        waits = self._collect(eng, reads, writes)
        self.cnt[eng] += 1
        c = self.cnt[eng]
        E = self.E[eng]
        for s, v in waits:
            E.wait_ge(s, v)
        fn(E).then_inc(self.sems[eng], 1)
        for t in reads:
            t.r[eng] = c
        for t in writes:
            t.w = (eng, c)
            t.r = {}
        self.n_instr += 1

    def dma(self, q, out, in_, reads=(), writes=(), **kw):
        if self.n_instr >= self.limit:
            return
        lanes = self.lanes[q]
        ln = lanes[self.lane_rr[q] % len(lanes)]
        self.lane_rr[q] += 1
        waits = self._collect(q, reads, writes)
        prev = self.cnt[ln]
        if prev and self.seen[q].get(ln, 0) < prev:
            self.seen[q][ln] = prev
            waits.append((self.sems[ln], prev))
        self.cnt[ln] += 16
        c = self.cnt[ln]
        E = self.E[q]
        for s, v in waits:
            E.wait_ge(s, v)
        E.dma_start(out=out, in_=in_, **kw).then_inc(self.sems[ln], 16)
        for t in reads:
            t.r[ln] = c
        for t in writes:
            t.w = (ln, c)
            t.r = {}
        self.n_instr += 1

    def barrier(self):
        for e in ("pe", "act", "dve", "pool", "sp"):
            E = self.E[e]
            for s, v in self.cnt.items():
                if v and s != e and self.seen[e].get(s, 0) < v:
                    self.seen[e][s] = v
                    E.wait_ge(self.sems[s], v)
                if s == e and v and e != "sp":
                    if self.seen[e].get(s, 0) < v:
                        self.seen[e][s] = v
                        E.wait_ge(self.sems[s], v)

    def finish(self, final_toks=()):
        self.barrier()
        for cm in reversed(self._ctx):
            cm.__exit__(None, None, None)


from contextlib import ExitStack

D = 1024
DFF = 2816
NFF = DFF // 128
KD = D // 128


class Phase:
    _uid = [0]

    def __init__(self, nc, S):
        self.nc, self.S = nc, S
        self.es = ExitStack()
        self.n = 0
        Phase._uid[0] += 1
        self.uid = Phase._uid[0]

    def sb(self, shape, dt, name=None):
        self.n += 1
        return self.es.enter_context(self.nc.sbuf_tensor(name or f"t{self.uid}_{self.n}", list(shape), dt))

    def ps(self, shape, dt, name=None):
        self.n += 1
        return self.es.enter_context(self.nc.psum_tensor(name or f"p{self.uid}_{self.n}", list(shape), dt))

    def close(self):
        self.S.barrier()
        self.es.close()


def make_ident(nc, S, P, dt=BF16):
    identf = P.sb([128, 128], F32)
    ident = P.sb([128, 128], dt)
    t = Tok()
    S.op("pool", lambda E: E.memset(identf[:], 1.0), writes=[t])
    S.op("pool", lambda E: E.affine_select(out=identf[:], in_=identf[:], pattern=[[-1, 128]], base=0,
                                           channel_multiplier=1, compare_op=ALU.is_equal, fill=0.0),
         reads=[t], writes=[t])
    S.op("dve", lambda E: E.tensor_copy(out=ident[:], in_=identf[:]), reads=[t], writes=[t])
    return ident, identf, t


def load_w_bf16(S, dst, src, tok, rows_per=128, col_split=2):
    K = dst.shape[1]
    N = dst.shape[2]
    cs = N // col_split
    for k in range(K):
        for c in range(col_split):
            S.dma("pool", dst[:, k, c * cs:(c + 1) * cs], src[k * 128:(k + 1) * 128, c * cs:(c + 1) * cs],
                  writes=[tok])


def rms_prep(S, P, ht, t_h, s, gt, t_g, xn, t_xn, junk, t_junk, stat, t_stat, mhalf, t_mh):
    S.op("dve", lambda E: E.scalar_tensor_tensor(out=junk[:], in0=ht[:, s, :], scalar=1.0 / D, in1=ht[:, s, :],
                                                 op0=ALU.mult, op1=ALU.mult, accum_out=stat[:, 0:1]),
         reads=[t_h], writes=[t_junk, t_stat])
    S.op("dve", lambda E: E.tensor_scalar(out=stat[:, 1:2], in0=stat[:, 0:1], scalar1=1e-6, scalar2=None,
                                          op0=ALU.add), reads=[t_stat], writes=[t_stat])
    S.op("pool", lambda E: E.tensor_tensor(out=stat[:, 2:3], in0=stat[:, 1:2], in1=mhalf[:, 0:1], op=ALU.pow),
         reads=[t_stat, t_mh], writes=[t_stat])
    S.op("dve", lambda E: E.scalar_tensor_tensor(out=xn[:], in0=ht[:, s, :], scalar=stat[:, 2:3], in1=gt[:],
                                                 op0=ALU.mult, op1=ALU.mult),
         reads=[t_h, t_stat, t_g], writes=[t_xn])


def phase_ffn(nc, S, h_in, h_out, g, wg, wu, wd, toks_in, toks_out, T):
    P = Phase(nc, S)
    NT = T // 128
    NS = 4
    NSUP = NT // NS
    hv_in = h_in.rearrange("(n p) d -> p n d", p=128)
    hv_out = h_out.rearrange("(n p) d -> p n d", p=128)
    ident, _, t_id = make_ident(nc, S, P)
    wg_b = P.sb([128, KD, DFF], BF16); t_wg = Tok()
    wu_b = P.sb([128, KD, DFF], BF16); t_wu = Tok()
    wd_b = P.sb([128, NFF, D], BF16); t_wd = Tok()
    load_w_bf16(S, wg_b, wg, t_wg)
    load_w_bf16(S, wu_b, wu, t_wu)
    load_w_bf16(S, wd_b, wd, t_wd, col_split=1)
    gt = P.sb([128, D], F32); t_g = Tok()
    S.dma("sp", gt[:], g.partition_broadcast(128), writes=[t_g])
    mhalf = P.sb([128, 1], F32); t_mh = Tok()
    S.op("pool", lambda E: E.memset(mhalf[:], -0.5), writes=[t_mh])
    ht = [P.sb([128, NS, D], F32)] * 2; t_ht = [[Tok() for _ in range(NS)]] * 2
    rl = [P.sb([128, 512], F32) for _ in range(4)]; t_rl = [Tok() for _ in range(4)]
    xn = [P.sb([128, D], BF16) for _ in range(2)]; t_xn = [Tok(), Tok()]
    junk = P.sb([128, D], BF16); t_junk = Tok()
    stat = [P.sb([128, 4], F32) for _ in range(2)]; t_stat = [Tok(), Tok()]
    xnT = [P.sb([128, KD, NS * 128], BF16) for _ in range(2)]; t_xnT = [Tok(), Tok()]
    hT = P.sb([128, NFF, NS * 128], BF16); t_hT = [Tok() for _ in range(NFF)]
    sg = [P.sb([128, NS * 128], BF16) for _ in range(2)]; t_sg = [Tok(), Tok()]
    pT = [P.ps([128, KD, 128], BF16) for _ in range(2)]; t_pT = [Tok(), Tok()]
    pG = [P.ps([128, 512], F32) for _ in range(2)]; t_pG = [Tok(), Tok()]
    pU = [P.ps([128, 512], F32) for _ in range(2)]; t_pU = [Tok(), Tok()]
    pD = [P.ps([128, 512], F32) for _ in range(2)]; t_pD = [Tok(), Tok()]
    itc = [0]

    def load(st):
        hb = st % 2
        for s in range(NS):
            n = st * NS + s
            S.dma("sp", ht[hb][:, s, :], hv_in[:, n, :], reads=[toks_in[n]], writes=[t_ht[hb][s]])

    def prep(st):
        hb = st % 2
        for s in range(NS):
            b = s % 2
            rms_prep(S, P, ht[hb], t_ht[hb][s], s, gt, t_g, xn[b], t_xn[b], junk, t_junk, stat[b], t_stat[b], mhalf,
                     t_mh)
            for k in range(KD):
                S.op("pe", lambda E, k=k, b=b: E.transpose(out=pT[b][:, k, :], in_=xn[b][:, k * 128:(k + 1) * 128],
                                                           identity=ident[:]),
                     reads=[t_xn[b], t_id], writes=[t_pT[b]])
            S.op("dve", lambda E, b=b, s=s, hb=hb: E.tensor_copy(out=xnT[hb][:, :, s * 128:(s + 1) * 128], in_=pT[b][:]),
                 reads=[t_pT[b]], writes=[t_xnT[hb]])

    def gateup(st):
        hb = st % 2
        for f in range(NFF):
            b = f % 2
            for k in range(KD):
                S.op("pe", lambda E, k=k, f=f, b=b: E.matmul(pG[b][:], lhsT=wg_b[:, k, f * 128:(f + 1) * 128],
                                                             rhs=xnT[hb][:, k, :], start=(k == 0), stop=(k == KD - 1)),
                     reads=[t_wg, t_xnT[hb]], writes=[t_pG[b]])
            for k in range(KD):
                S.op("pe", lambda E, k=k, f=f, b=b: E.matmul(pU[b][:], lhsT=wu_b[:, k, f * 128:(f + 1) * 128],
                                                             rhs=xnT[hb][:, k, :], start=(k == 0), stop=(k == KD - 1)),
                     reads=[t_wu, t_xnT[hb]], writes=[t_pU[b]])
            S.op("act", lambda E, b=b: E.activation(out=sg[b][:], in_=pG[b][:], func=AF.Silu),
                 reads=[t_pG[b]], writes=[t_sg[b]])
            S.op("dve", lambda E, b=b, f=f: E.tensor_tensor(out=hT[:, f, :], in0=pU[b][:], in1=sg[b][:], op=ALU.mult),
                 reads=[t_pU[b], t_sg[b]], writes=[t_hT[f]])

    def down(st):
        for s in range(NS):
            n = st * NS + s
            for c in range(2):
                b = itc[0] % 2
                r4 = itc[0] % 4
                itc[0] += 1
                S.dma("sp", rl[r4][:], hv_in[:, n, c * 512:(c + 1) * 512], reads=[toks_in[n]], writes=[t_rl[r4]])
                for f in range(NFF):
                    S.op("pe", lambda E, f=f, s=s, c=c, b=b: E.matmul(
                        pD[b][:], lhsT=hT[:, f, s * 128:(s + 1) * 128], rhs=wd_b[:, f, c * 512:(c + 1) * 512],
                        start=(f == 0), stop=(f == NFF - 1)),
                        reads=[t_wd, t_hT[f]], writes=[t_pD[b]])
                S.op("dve", lambda E, b=b, r4=r4: E.scalar_tensor_tensor(
                    out=rl[r4][:], in0=pD[b][:], scalar=0.5, in1=rl[r4][:], op0=ALU.mult, op1=ALU.add),
                    reads=[t_pD[b], t_rl[r4]], writes=[t_rl[r4]])
                S.dma("sp", hv_out[:, n, c * 512:(c + 1) * 512], rl[r4][:], reads=[t_rl[r4]], writes=[toks_out[n]])

    load(0)
    prep(0)
    for st in range(NSUP):
        if st + 1 < NSUP:
            load(st + 1)
        gateup(st)
        if st + 1 < NSUP:
            prep(st + 1)
        down(st)
    P.close()


RWKV_IN = 1408
ATT_Q0 = 1408
ATT_V0 = 2176
S5_0 = 2560
INW = 2816


def phase_win(nc, S, h_in, g, win, zr, qk, vtm, us5, toks_in, T):
    P = Phase(nc, S)
    NT = T // 128
    NS = 4
    NSUP = NT // NS
    hv_in = h_in.rearrange("(n p) d -> p n d", p=128)
    ident, _, t_id = make_ident(nc, S, P)
    w_b = P.sb([128, KD, INW], BF16); t_w = Tok()
    load_w_bf16(S, w_b, win, t_w)
    gt = P.sb([128, D], F32); t_g = Tok()
    S.dma("sp", gt[:], g.partition_broadcast(128), writes=[t_g])
    mhalf = P.sb([128, 1], F32); t_mh = Tok()
    S.op("pool", lambda E: E.memset(mhalf[:], -0.5), writes=[t_mh])
    ht = [P.sb([128, NS, D], F32) for _ in range(2)]; t_ht = [[Tok() for _ in range(NS)] for _ in range(2)]
    xn = [P.sb([128, D], BF16) for _ in range(2)]; t_xn = [Tok(), Tok()]
    junk = P.sb([128, D], BF16); t_junk = Tok()
    stat = [P.sb([128, 4], F32) for _ in range(2)]; t_stat = [Tok(), Tok()]
    xnT = [P.sb([128, KD, NS * 128], BF16) for _ in range(2)]; t_xnT = [Tok(), Tok()]
    stf = [P.sb([128, 512], F32) for _ in range(4)]; t_stf = [Tok() for _ in range(4)]
    stb = [P.sb([128, 512], BF16) for _ in range(4)]; t_stb = [Tok() for _ in range(4)]
    pT = [P.ps([128, KD, 128], BF16) for _ in range(2)]; t_pT = [Tok(), Tok()]
    pZ = [P.ps([128, 512], F32) for _ in range(4)]; t_pZ = [Tok() for _ in range(4)]
    t_out = Tok()
    chunks = []
    for c in range(11):
        chunks.append((c * 128, zr, c * 128, False))
    for c in range(6):
        chunks.append((ATT_Q0 + c * 128, qk, c * 128, True))
    for c in range(2):
        chunks.append((S5_0 + c * 128, us5, c * 128, False))
    cnt = {"it": 0, "ib": 0, "iff": 0}

    def load(st):
        hb = st % 2
        for s in range(NS):
            n = st * NS + s
            S.dma("sp", ht[hb][:, s, :], hv_in[:, n, :], reads=[toks_in[n]], writes=[t_ht[hb][s]])

    def prep(st):
        hb = st % 2
        for s in range(NS):
            b = s % 2
            rms_prep(S, P, ht[hb], t_ht[hb][s], s, gt, t_g, xn[b], t_xn[b], junk, t_junk, stat[b], t_stat[b], mhalf, t_mh)
            for k in range(KD):
                S.op("pe", lambda E, k=k, b=b: E.transpose(out=pT[b][:, k, :], in_=xn[b][:, k * 128:(k + 1) * 128],
                                                           identity=ident[:]),
                     reads=[t_xn[b], t_id], writes=[t_pT[b]])
            S.op("dve", lambda E, b=b, s=s, hb=hb: E.tensor_copy(out=xnT[hb][:, :, s * 128:(s + 1) * 128], in_=pT[b][:]),
                 reads=[t_pT[b]], writes=[t_xnT[hb]])

    def fm_chunks(st, chs):
        hb = st % 2
        tsl = slice(st * 512, (st + 1) * 512)
        for (c0, dst, r0, isb) in chs:
            pb = cnt["it"] % 4
            cnt["it"] += 1
            for k in range(KD):
                S.op("pe", lambda E, k=k, c0=c0, pb=pb, hb=hb: E.matmul(
                    pZ[pb][:], lhsT=w_b[:, k, c0:c0 + 128], rhs=xnT[hb][:, k, :], start=(k == 0), stop=(k == KD - 1)),
                    reads=[t_w, t_xnT[hb]], writes=[t_pZ[pb]])
            if isb:
                sb_ = cnt["ib"] % 4
                cnt["ib"] += 1
                S.op("act", lambda E, pb=pb, sb_=sb_: E.activation(out=stb[sb_][:], in_=pZ[pb][:], func=AF.Copy),
                     reads=[t_pZ[pb]], writes=[t_stb[sb_]])
                S.dma("sp", dst[r0:r0 + 128, tsl], stb[sb_][:], reads=[t_stb[sb_]], writes=[t_out])
            else:
                sf = cnt["iff"] % 4
                cnt["iff"] += 1
                if cnt["iff"] % 2:
                    S.op("act", lambda E, pb=pb, sf=sf: E.activation(out=stf[sf][:], in_=pZ[pb][:], func=AF.Copy),
                         reads=[t_pZ[pb]], writes=[t_stf[sf]])
                else:
                    S.op("dve", lambda E, pb=pb, sf=sf: E.tensor_copy(out=stf[sf][:], in_=pZ[pb][:]),
                         reads=[t_pZ[pb]], writes=[t_stf[sf]])
                S.dma("sp", dst[r0:r0 + 128, tsl], stf[sf][:], reads=[t_stf[sf]], writes=[t_out])

    def v_chunks(st):
        hb = st % 2
        for s in range(NS):
            n = st * NS + s
            pb = cnt["it"] % 4
            cnt["it"] += 1
            for k in range(KD):
                S.op("pe", lambda E, k=k, s=s, pb=pb, hb=hb: E.matmul(
                    pZ[pb][:, 0:384], lhsT=xnT[hb][:, k, s * 128:(s + 1) * 128], rhs=w_b[:, k, ATT_V0:ATT_V0 + 384],
                    start=(k == 0), stop=(k == KD - 1)),
                    reads=[t_w, t_xnT[hb]], writes=[t_pZ[pb]])
            sb_ = cnt["ib"] % 4
            cnt["ib"] += 1
            S.op("dve", lambda E, pb=pb, sb_=sb_: E.tensor_copy(out=stb[sb_][:, 0:384], in_=pZ[pb][:, 0:384]),
                 reads=[t_pZ[pb]], writes=[t_stb[sb_]])
            S.dma("sp", vtm[n * 128:(n + 1) * 128, :], stb[sb_][:, 0:384], reads=[t_stb[sb_]], writes=[t_out])

    load(0)
    prep(0)
    for st in range(NSUP):
        if st + 1 < NSUP:
            load(st + 1)
        fm_chunks(st, chunks[:10])
        if st + 1 < NSUP:
            prep(st + 1)
        fm_chunks(st, chunks[10:])
        v_chunks(st)
    P.close()


def phase_wout(nc, S, h_in, h_out, mixT, wout, toks_in, toks_out, T):
    P = Phase(nc, S)
    NT = T // 128
    NS = 4
    NSUP = NT // NS
    hv_in = h_in.rearrange("(n p) d -> p n d", p=128)
    hv_out = h_out.rearrange("(n p) d -> p n d", p=128)
    mv = mixT.rearrange("(k p) t -> p k t", p=128)
    w_b = P.sb([128, KD, D], BF16); t_w = Tok()
    load_w_bf16(S, w_b, wout, t_w, col_split=1)
    ht = [P.sb([128, NS, D], F32) for _ in range(2)]; t_ht = [[Tok() for _ in range(NS)] for _ in range(2)]
    ml = [P.sb([128, KD, 512], BF16) for _ in range(2)]; t_ml = [Tok(), Tok()]
    pD = [P.ps([128, 512], F32) for _ in range(4)]; t_pD = [Tok() for _ in range(4)]
    it = 0
    for st in range(NSUP):
        hb = st % 2
        S.dma("sp", ml[hb][:], mv[:, :, st * 512:(st + 1) * 512], writes=[t_ml[hb]])
        for s in range(NS):
            n = st * NS + s
            S.dma("sp", ht[hb][:, s, :], hv_in[:, n, :], reads=[toks_in[n]], writes=[t_ht[hb][s]])
        for s in range(NS):
            n = st * NS + s
            for c in range(2):
                b = it % 4
                it += 1
                for k in range(KD):
                    S.op("pe", lambda E, k=k, s=s, c=c, b=b, hb=hb: E.matmul(
                        pD[b][:], lhsT=ml[hb][:, k, s * 128:(s + 1) * 128], rhs=w_b[:, k, c * 512:(c + 1) * 512],
                        start=(k == 0), stop=(k == KD - 1)),
                        reads=[t_w, t_ml[hb]], writes=[t_pD[b]])
                S.op("dve", lambda E, s=s, c=c, b=b, hb=hb: E.tensor_tensor(
                    out=ht[hb][:, s, c * 512:(c + 1) * 512], in0=pD[b][:], in1=ht[hb][:, s, c * 512:(c + 1) * 512],
                    op=ALU.add), reads=[t_pD[b]], writes=[t_ht[hb][s]])
            S.dma("sp", hv_out[:, n, :], ht[hb][:, s, :], reads=[t_ht[hb][s]], writes=[toks_out[n]])
    P.close()


def phase_final(nc, S, h_in, out, g, toks_in, toks_out, T):
    P = Phase(nc, S)
    NT = T // 128
    hv_in = h_in.rearrange("(n p) d -> p n d", p=128)
    hv_out = out.rearrange("(n p) d -> p n d", p=128)
    gt = P.sb([128, D], F32); t_g = Tok()
    S.dma("sp", gt[:], g.partition_broadcast(128), writes=[t_g])
    mhalf = P.sb([128, 1], F32); t_mh = Tok()
    S.op("pool", lambda E: E.memset(mhalf[:], -0.5), writes=[t_mh])
    ht = [P.sb([128, 1, D], F32) for _ in range(4)]; t_ht = [Tok() for _ in range(4)]
    xo = [P.sb([128, D], F32) for _ in range(4)]; t_xo = [Tok() for _ in range(4)]
    junk = P.sb([128, D], BF16); t_junk = Tok()
    stat = [P.sb([128, 4], F32) for _ in range(4)]; t_stat = [Tok() for _ in range(4)]
    for n in range(NT):
        b = n % 4
        S.dma("sp", ht[b][:, 0, :], hv_in[:, n, :], reads=[toks_in[n]], writes=[t_ht[b]])
        rms_prep(S, P, ht[b], t_ht[b], 0, gt, t_g, xo[b], t_xo[b], junk, t_junk, stat[b], t_stat[b], mhalf, t_mh)
        S.dma("pool", hv_out[:, n, :], xo[b][:], reads=[t_xo[b]], writes=[toks_out[n]])
    P.close()


ALIBI = [0.25, 0.0625, 0.015625, 0.00390625, 0.5, 0.125]
DILS = [1, 4, 16]
NEG = -1.0e30


def phase_attn(nc, S, qk, vtm, mixT, T):
    P = Phase(nc, S)
    NB1 = T // 128
    dfi = P.sb([128, 128], I32)
    dff = P.sb([128, 128], F32)
    Dk = P.sb([128, 3, 128], F32)
    Mk = P.sb([128, 3, 128], F32)
    t_c = Tok()
    S.op("pool", lambda E: E.iota(dfi[:], pattern=[[1, 128]], base=0, channel_multiplier=-1), writes=[t_c])
    S.op("dve", lambda E: E.tensor_copy(out=dff[:], in_=dfi[:]), reads=[t_c], writes=[t_c])
    S.op("dve", lambda E: E.tensor_scalar(out=Dk[:, 0, :], in0=dff[:], scalar1=128.0, scalar2=None, op0=ALU.add),
         reads=[t_c], writes=[t_c])
    S.op("dve", lambda E: E.tensor_scalar(out=Dk[:, 1, :], in0=dff[:], scalar1=-1.0, scalar2=None, op0=ALU.mult),
         reads=[t_c], writes=[t_c])
    S.op("dve", lambda E: E.tensor_tensor(out=Dk[:, 1, :], in0=Dk[:, 1, :], in1=dff[:], op=ALU.max),
         reads=[t_c], writes=[t_c])
    S.op("dve", lambda E: E.tensor_scalar(out=Dk[:, 2, :], in0=dff[:], scalar1=-1.0, scalar2=128.0, op0=ALU.mult,
                                          op1=ALU.add), reads=[t_c], writes=[t_c])
    S.op("dve", lambda E: E.tensor_scalar(out=Mk[:], in0=Dk[:], scalar1=64.0, scalar2=NEG, op0=ALU.is_gt,
                                          op1=ALU.mult), reads=[t_c], writes=[t_c])
    sel = P.sb([65, 64], F32)
    S.op("dve", lambda E: E.memset(sel[:], 0.0), writes=[t_c])
    S.op("dve", lambda E: E.memset(sel[64:65, :], 1.0), reads=[t_c], writes=[t_c])

    qT = P.sb([128, T], BF16); t_q = Tok()
    kT = P.sb([128, T], BF16); t_k = Tok()
    vt = [P.sb([128, NB1, 2, 65], BF16) for _ in range(3)]; t_v = [Tok() for _ in range(3)]
    acc = P.sb([65, 2, T], F32)
    t_acc = [[Tok() for _ in range(NB1)] for _ in range(2)]
    biasT = P.sb([128, 2, 3, 3, 128], F32); t_b = Tok()
    NBF = 4
    sc = [P.sb([128, 3, 128], F32) for _ in range(NBF)]; t_sc = [Tok() for _ in range(NBF)]
    pr = [P.sb([128, 3, 128], BF16) for _ in range(NBF)]; t_pr = [Tok() for _ in range(NBF)]
    rec = [P.sb([64, 512], F32) for _ in range(2)]; t_rec = [Tok(), Tok()]
    ob = [P.sb([64, 512], BF16) for _ in range(2)]; t_ob = [Tok(), Tok()]
    pS = [P.ps([128, 3, 128], F32) for _ in range(NBF)]; t_pS = [Tok() for _ in range(NBF)]
    pOb = [P.ps([128, 512], F32) for _ in range(NBF)]; t_pO = [Tok() for _ in range(NBF)]
    pO = [x[0:65, 0:128] for x in pOb]
    pB = [x[0:64, 0:512] for x in pOb]; t_pB = t_pO
    t_out = Tok()
    it = 0
    for hp in range(3):
        S.dma("sp", qT[:], qk[128 * hp:128 * hp + 128, :], writes=[t_q])
        S.dma("sp", kT[:], qk[384 + 128 * hp:384 + 128 * hp + 128, :], writes=[t_k])
        for pi, d in enumerate(DILS):
            S.op("pool", lambda E, pi=pi: E.memset(vt[pi][:], 1.0), writes=[t_v[pi]])
            nm = T // d // 128
            vv = vtm.rearrange("(m j r) (h c) -> j r m h c", j=128, r=d, c=64)
            for r in range(d):
                for h2 in range(2):
                    S.dma("sp", vt[pi][:, r * nm:(r + 1) * nm, h2, 0:64], vv[:, r, :, 2 * hp + h2, :],
                          writes=[t_v[pi]])
        for h2 in range(2):
            for pi, d in enumerate(DILS):
                sl = -ALIBI[2 * hp + h2] * d
                S.op("dve", lambda E, h2=h2, pi=pi, sl=sl: E.scalar_tensor_tensor(
                    out=biasT[:, h2, pi, :, :], in0=Dk[:], scalar=sl, in1=Mk[:], op0=ALU.mult, op1=ALU.add),
                    reads=[t_c], writes=[t_b])
        for h2 in range(2):
            rows = slice(64 * h2, 64 * h2 + 64)
            stages = []
            for pi, d in enumerate(DILS):
                nb = T // d // 128
                for r in range(d):
                    for b in range(nb):
                        bi = it % NBF
                        it += 1
                        kts = [kt for kt in (b - 1, b, b + 1) if 0 <= kt < nb]
                        k0 = kts[0] - (b - 1)
                        nk = len(kts)
                        qs = slice(r + d * 128 * b, r + d * 128 * b + d * 127 + 1, d)
                        blks = sorted(set(range((r + d * 128 * b) // 128, (r + d * 128 * (b + 1) - d) // 128 + 1)))

                        def stA(bi=bi, kts=kts, k0=k0, nk=nk, qs=qs, r=r, d=d, pi=pi, rows=rows, h2=h2):
                            for ki, kt in enumerate(kts):
                                ks = slice(r + d * 128 * kt, r + d * 128 * kt + d * 127 + 1, d)
                                S.op("pe", lambda E, ki=ki, ks=ks: E.matmul(
                                    pS[bi][:, ki, :], lhsT=kT[rows, ks], rhs=qT[rows, qs], start=True, stop=True),
                                    reads=[t_q, t_k], writes=[t_pS[bi]])
                            S.op("dve", lambda E: E.scalar_tensor_tensor(
                                out=sc[bi][:, 0:nk, :], in0=pS[bi][:, 0:nk, :], scalar=0.125,
                                in1=biasT[:, h2, pi, k0:k0 + nk, :], op0=ALU.mult, op1=ALU.add),
                                reads=[t_pS[bi], t_b], writes=[t_sc[bi]])
                            S.op("act", lambda E: E.activation(out=pr[bi][:, 0:nk, :], in_=sc[bi][:, 0:nk, :],
                                                               func=AF.Exp),
                                 reads=[t_sc[bi]], writes=[t_pr[bi]])

                        def stB(bi=bi, kts=kts, nk=nk, qs=qs, r=r, nb=nb, pi=pi, h2=h2, blks=blks):
                            for ki, kt in enumerate(kts):
                                S.op("pe", lambda E, ki=ki, kt=kt: E.matmul(
                                    pO[bi], lhsT=vt[pi][:, r * nb + kt, h2, :], rhs=pr[bi][:, ki, :],
                                    start=(ki == 0), stop=(ki == nk - 1)),
                                    reads=[t_v[pi], t_pr[bi]], writes=[t_pO[bi]])
                            at = [t_acc[h2][x] for x in blks]
                            if pi == 0:
                                S.op("act", lambda E: E.activation(out=acc[:, h2, qs], in_=pO[bi], func=AF.Copy),
                                     reads=[t_pO[bi]], writes=at)
                            else:
                                S.op("dve", lambda E: E.tensor_tensor(out=acc[:, h2, qs], in0=pO[bi],
                                                                      in1=acc[:, h2, qs], op=ALU.add),
                                     reads=[t_pO[bi]], writes=at)
                        stages.append((stA, stB))
            SKEW = NBF - 1
            for i in range(len(stages) + SKEW):
                if i < len(stages):
                    stages[i][0]()
                if i - SKEW >= 0:
                    stages[i - SKEW][1]()
            head = 2 * hp + h2
            for c in range(T // 512):
                bi = c % 2
                cs = slice(c * 512, (c + 1) * 512)
                at = t_acc[h2][4 * c:4 * c + 4]
                S.op("pe", lambda E, bi=bi, h2=h2, cs=cs: E.matmul(pB[bi], lhsT=sel[:], rhs=acc[:, h2, cs],
                                                                   start=True, stop=True),
                     reads=at + [t_c], writes=[t_pB[bi]])
                S.op("dve", lambda E, bi=bi: E.reciprocal(out=rec[bi][:], in_=pB[bi]),
                     reads=[t_pB[bi]], writes=[t_rec[bi]])
                S.op("dve", lambda E, bi=bi, h2=h2, cs=cs: E.tensor_tensor(out=ob[bi][:], in0=acc[0:64, h2, cs],
                                                                          in1=rec[bi][:], op=ALU.mult),
                     reads=at + [t_rec[bi]], writes=[t_ob[bi]])
                S.dma("sp", mixT[384 + 64 * head:384 + 64 * head + 64, cs], ob[bi][:], reads=[t_ob[bi]],
                      writes=[t_out])
    P.close()


def rsl(lo, hi):
    return slice(hi - 1, (lo - 1) if lo > 0 else None, -1)


TWO_PI = 6.283185307179586
MAGIC = 12582912.0


def phase_s5(nc, S, us5, mixT, prm, T, C=256):
    P = Phase(nc, S)
    NCH = T // C
    NCMB = 16
    a_re, a_im, lstep, b_re, b_im, c_re, c_im, dsk, glu_w, glu_b = prm
    ident, identf, t_id = make_ident(nc, S, P)
    t_s = Tok()

    def dv(fn, eng="dve"):
        S.op(eng, fn, reads=[t_s, t_id], writes=[t_s])

    prs = P.sb([128, 40, 16], F32)
    cosT = P.sb([128, NCMB, C], F32)
    sinT = P.sb([128, NCMB, C], F32)
    BT = P.sb([128, NCMB, 2, 128], BF16)
    CT = P.sb([128, NCMB, 2, 128], BF16)
    gw = P.sb([128, 2, 512], BF16)
    gb = P.sb([128, 4], F32)
    dk = P.sb([128, 2], F32)
    ub = P.sb([128, 2, T], BF16); t_ub = Tok()
    ybwd = P.sb([128, 2, T], F32); t_yb = [Tok() for _ in range(NCH)]
    gi = P.sb([128, 2, NCMB], F32); t_gi = [Tok() for _ in range(NCMB)]
    banks = [P.ps([128, 512], F32) for _ in range(6)]
    P2 = Phase(nc, S)
    stg = P2.sb([16, 3, 128], F32)
    lst = P2.sb([16, 2], F32)
    S.dma("sp", stg[:, 0, :], a_re.rearrange("d (j g) p -> (d j) (g p)", g=2), writes=[t_s])
    S.dma("sp", stg[:, 1, :], a_im.rearrange("d (j g) p -> (d j) (g p)", g=2), writes=[t_s])
    S.dma("sp", lst[:], lstep.rearrange("d (j g) -> (d j) g", g=2), writes=[t_s])
    for g2 in range(2):
        dv(lambda E, g2=g2: E.tensor_copy(out=stg[:, 2, 64 * g2:64 * g2 + 64],
                                          in_=lst[:, g2:g2 + 1].to_broadcast([16, 64])))
    pst = banks[0][:, 0:48].rearrange("p (a b) -> p a b", a=3)
    for i in range(3):
        S.op("pe", lambda E, i=i: E.transpose(out=pst[:, i, :], in_=stg[:, i, :], identity=identf[0:16, 0:16]),
             reads=[t_s, t_id], writes=[t_s])
    nm = {}

    def V(name):
        if name not in nm:
            nm[name] = len(nm)
        return prs[:, nm[name], :]
    dv(lambda E: E.tensor_copy(out=prs[:, 0:3, :], in_=pst))
    nm.update({"are": 0, "aim": 1, "lst": 2})
    S.op("act", lambda E: E.activation(out=V("step"), in_=V("lst"), func=AF.Exp), reads=[t_s], writes=[t_s])
    dv(lambda E: E.tensor_tensor(out=V("ar"), in0=V("are"), in1=V("step"), op=ALU.mult))
    dv(lambda E: E.tensor_tensor(out=V("th"), in0=V("aim"), in1=V("step"), op=ALU.mult))
    S.op("act", lambda E: E.activation(out=V("rho"), in_=V("ar"), func=AF.Exp), reads=[t_s], writes=[t_s])

    def sin_of(dst, src, shift):
        dv(lambda E: E.tensor_scalar(out=V("k1"), in0=V(src), scalar1=1.0 / TWO_PI, scalar2=shift / TWO_PI,
                                     op0=ALU.mult, op1=ALU.add))
        dv(lambda E: E.tensor_scalar(out=V("k2"), in0=V("k1"), scalar1=MAGIC, scalar2=None, op0=ALU.add))
        dv(lambda E: E.tensor_scalar(out=V("k3"), in0=V("k2"), scalar1=-MAGIC, scalar2=None, op0=ALU.add))
        dv(lambda E: E.tensor_tensor(out=V("k1"), in0=V("k1"), in1=V("k3"), op=ALU.subtract))
        S.op("act", lambda E: E.activation(out=V(dst), in_=V("k1"), func=AF.Sin, scale=TWO_PI),
             reads=[t_s], writes=[t_s])
    sin_of("sn", "th", 0.0)
    sin_of("cs", "th", TWO_PI / 4)
    dv(lambda E: E.tensor_tensor(out=V("lr"), in0=V("rho"), in1=V("cs"), op=ALU.mult))
    dv(lambda E: E.tensor_tensor(out=V("li"), in0=V("rho"), in1=V("sn"), op=ALU.mult))
    dv(lambda E: E.tensor_scalar(out=V("nr"), in0=V("lr"), scalar1=-1.0, scalar2=None, op0=ALU.add))
    dv(lambda E: E.tensor_tensor(out=V("d1"), in0=V("are"), in1=V("are"), op=ALU.mult))
    dv(lambda E: E.tensor_tensor(out=V("d2"), in0=V("aim"), in1=V("aim"), op=ALU.mult))
    dv(lambda E: E.tensor_tensor(out=V("d1"), in0=V("d1"), in1=V("d2"), op=ALU.add))
    dv(lambda E: E.reciprocal(out=V("rd"), in_=V("d1")))
    dv(lambda E: E.tensor_tensor(out=V("z1"), in0=V("nr"), in1=V("are"), op=ALU.mult))
    dv(lambda E: E.tensor_tensor(out=V("z2"), in0=V("li"), in1=V("aim"), op=ALU.mult))
    dv(lambda E: E.tensor_tensor(out=V("z1"), in0=V("z1"), in1=V("z2"), op=ALU.add))
    dv(lambda E: E.tensor_tensor(out=V("zr"), in0=V("z1"), in1=V("rd"), op=ALU.mult))
    dv(lambda E: E.tensor_tensor(out=V("z1"), in0=V("li"), in1=V("are"), op=ALU.mult))
    dv(lambda E: E.tensor_tensor(out=V("z2"), in0=V("nr"), in1=V("aim"), op=ALU.mult))
    dv(lambda E: E.tensor_tensor(out=V("z1"), in0=V("z1"), in1=V("z2"), op=ALU.subtract))
    dv(lambda E: E.tensor_tensor(out=V("zi"), in0=V("z1"), in1=V("rd"), op=ALU.mult))

    tmpA = P2.sb([128, NCMB, C // 2], F32)
    tmpB = P2.sb([128, NCMB, C // 2], F32)
    dv(lambda E: E.memset(cosT[:, :, 0:1], 1.0))
    dv(lambda E: E.memset(sinT[:, :, 0:1], 0.0))
    dv(lambda E: E.tensor_copy(out=V("wr"), in_=V("cs")))
    dv(lambda E: E.tensor_copy(out=V("wi"), in_=V("sn")))
    L = 1
    while L < C:
        wrb = V("wr").unsqueeze(2).to_broadcast([128, NCMB, L])
        wib = V("wi").unsqueeze(2).to_broadcast([128, NCMB, L])
        dv(lambda E, L=L, wrb=wrb: E.tensor_tensor(out=tmpA[:, :, 0:L], in0=cosT[:, :, 0:L], in1=wrb, op=ALU.mult))
        dv(lambda E, L=L, wib=wib: E.tensor_tensor(out=tmpB[:, :, 0:L], in0=sinT[:, :, 0:L], in1=wib, op=ALU.mult))
        dv(lambda E, L=L: E.tensor_tensor(out=cosT[:, :, L:2 * L], in0=tmpA[:, :, 0:L], in1=tmpB[:, :, 0:L],
                                          op=ALU.subtract))
        dv(lambda E, L=L, wib=wib: E.tensor_tensor(out=tmpA[:, :, 0:L], in0=cosT[:, :, 0:L], in1=wib, op=ALU.mult))
        dv(lambda E, L=L, wrb=wrb: E.tensor_tensor(out=tmpB[:, :, 0:L], in0=sinT[:, :, 0:L], in1=wrb, op=ALU.mult))
        dv(lambda E, L=L: E.tensor_tensor(out=sinT[:, :, L:2 * L], in0=tmpA[:, :, 0:L], in1=tmpB[:, :, 0:L],
                                          op=ALU.add))
        dv(lambda E: E.tensor_tensor(out=V("q1"), in0=V("wr"), in1=V("wr"), op=ALU.mult))
        dv(lambda E: E.tensor_tensor(out=V("q2"), in0=V("wi"), in1=V("wi"), op=ALU.mult))
        dv(lambda E: E.tensor_tensor(out=V("q3"), in0=V("wr"), in1=V("wi"), op=ALU.mult))
        dv(lambda E: E.tensor_tensor(out=V("wr"), in0=V("q1"), in1=V("q2"), op=ALU.subtract))
        dv(lambda E: E.tensor_scalar(out=V("wi"), in0=V("q3"), scalar1=2.0, scalar2=None, op0=ALU.mult))
        L *= 2
    dv(lambda E: E.tensor_scalar(out=V("nwi"), in0=V("wi"), scalar1=-1.0, scalar2=None, op0=ALU.mult))

    bst = P2.sb([128, 2, 8, 16], F32)
    S.dma("sp", bst[:, 0, :, :], b_re.rearrange("(j g) p c -> (g p) j c", g=2), writes=[t_s])
    S.dma("sp", bst[:, 1, :, :], b_im.rearrange("(j g) p c -> (g p) j c", g=2), writes=[t_s])
    bexp = P2.sb([128, 2, 128], F32)
    btmp = P2.sb([128, 16], F32)
    pX = [banks[1][:, 0:128], banks[2][:, 0:128]]
    for d in range(2):
        for j in range(8):
            cmb = d * 8 + j
            jj = j % 4
            dv(lambda E: E.memset(bexp[:], 0.0))
            for g2 in range(2):
                ps_ = slice(64 * g2, 64 * g2 + 64)
                cs_ = slice(32 * jj + 16 * g2, 32 * jj + 16 * g2 + 16)
                zr_ = prs[ps_, nm["zr"], cmb:cmb + 1]
                zi_ = prs[ps_, nm["zi"], cmb:cmb + 1]
                dv(lambda E, ps_=ps_, zi_=zi_, j=j: E.tensor_scalar(out=btmp[ps_, :], in0=bst[ps_, 1, j, :], scalar1=zi_,
                                                                    scalar2=None, op0=ALU.mult))
                dv(lambda E, ps_=ps_, cs_=cs_, zr_=zr_, j=j: E.scalar_tensor_tensor(
                    out=bexp[ps_, 0, cs_], in0=bst[ps_, 0, j, :], scalar=zr_, in1=btmp[ps_, :], op0=ALU.mult,
                    op1=ALU.subtract))
                dv(lambda E, ps_=ps_, zr_=zr_, j=j: E.tensor_scalar(out=btmp[ps_, :], in0=bst[ps_, 1, j, :], scalar1=zr_,
                                                                    scalar2=None, op0=ALU.mult))
                dv(lambda E, ps_=ps_, cs_=cs_, zi_=zi_, j=j: E.scalar_tensor_tensor(
                    out=bexp[ps_, 1, cs_], in0=bst[ps_, 0, j, :], scalar=zi_, in1=btmp[ps_, :], op0=ALU.mult,
                    op1=ALU.add))
            for ri in range(2):
                S.op("pe", lambda E, ri=ri: E.transpose(out=pX[ri], in_=bexp[:, ri, :], identity=identf[:]),
                     reads=[t_s, t_id], writes=[t_s])
                dv(lambda E, ri=ri, cmb=cmb: E.tensor_copy(out=BT[:, cmb, ri, :], in_=pX[ri]))
    cnat = P2.sb([128, 2, 2, 2, 64], F32)
    for d in range(2):
        for ri, cc in enumerate((c_re, c_im)):
            for ut in range(2):
                S.dma("sp", cnat[:, d, ri, ut, :], cc[d].rearrange("g c p -> (g c) p")[128 * ut:128 * ut + 128, :],
                      writes=[t_s])
    mki = P2.sb([128, 4, 2], I32)
    mk = P2.sb([128, 4, 2], F32)
    mk2 = P2.sb([128, 4, 2], F32)
    S.op("pool", lambda E: E.iota(mki[:], pattern=[[-32, 4], [-16, 2]], base=0, channel_multiplier=1),
         reads=[t_s], writes=[t_s])
    dv(lambda E: E.tensor_copy(out=mk[:], in_=mki[:]))
    dv(lambda E: E.tensor_scalar(out=mk2[:], in0=mk[:], scalar1=0.0, scalar2=None, op0=ALU.is_ge))
    dv(lambda E: E.tensor_scalar(out=mk[:], in0=mk[:], scalar1=15.0, scalar2=None, op0=ALU.is_le))
    dv(lambda E: E.tensor_tensor(out=mk[:], in0=mk[:], in1=mk2[:], op=ALU.mult))
    cx = P2.sb([128, 2, 64], F32)
    for d in range(2):
        for j in range(8):
            cmb = d * 8 + j
            jj = j % 4
            ut = j // 4
            for ri in range(2):
                for g2 in range(2):
                    dv(lambda E, d=d, ri=ri, ut=ut, jj=jj, g2=g2: E.tensor_scalar(
                        out=cx[:, g2, :], in0=cnat[:, d, ri, ut, :], scalar1=mk[:, jj, g2:g2 + 1], scalar2=None,
                        op0=ALU.mult))
                S.op("pe", lambda E, ri=ri: E.transpose(out=pX[ri], in_=cx[:].rearrange("p a b -> p (a b)"),
                                                        identity=identf[:]),
                     reads=[t_s, t_id], writes=[t_s])
                sgn = 1.0 if ri == 0 else -1.0
                dv(lambda E, ri=ri, cmb=cmb, sgn=sgn: E.tensor_scalar(out=CT[:, cmb, ri, :], in0=pX[ri], scalar1=sgn,
                                                                      scalar2=None, op0=ALU.mult))
    load_w_bf16(S, gw, glu_w, t_s, col_split=1)
    S.dma("sp", gb[:], glu_b.rearrange("(o p) -> p o", p=128), writes=[t_s], allow_slow_non_contiguous=True)
    S.dma("sp", dk[:], dsk.rearrange("(o p) -> p o", p=128), writes=[t_s], allow_slow_non_contiguous=True)

    for ut in range(2):
        for c4 in range(T // 2048 if T >= 2048 else 1):
            w_ = min(2048, T)
            S.dma("pool", ub[:, ut, c4 * w_:(c4 + 1) * w_], us5[128 * ut:128 * ut + 128, c4 * w_:(c4 + 1) * w_],
                  writes=[t_ub])
    S.op("dve", lambda E: E.memset(gi[:], 0.0), writes=t_gi)
    NB = 3
    P2.close()
    pBU = [banks[i][:, 0:2 * C].rearrange("p (a b) -> p a b", a=2) for i in range(2)]; t_pBU = [Tok() for _ in range(2)]
    m1 = [P.sb([128, 4, C], F32) for _ in range(NB)]; t_m1 = [Tok() for _ in range(NB)]
    gin = [P.sb([128, 2, C], F32) for _ in range(NB)]; t_gin = [Tok() for _ in range(NB)]
    gg = [P.sb([128, 2, C], F32) for _ in range(NB)]; t_gg = [Tok() for _ in range(NB)]
    m2 = [P.sb([128, 4, C], F32) for _ in range(NB)]; t_m2 = [Tok() for _ in range(NB)]
    ctmp = [P.sb([128, 2], F32) for _ in range(NB)]; t_ct = [Tok() for _ in range(NB)]
    hh = [P.sb([128, 4, 2, C], BF16) for _ in range(2)]; t_hh = [[Tok() for _ in range(4)] for _ in range(2)]
    pY = [banks[2 + i][:, 0:C] for i in range(2)]; t_pY = [Tok(), Tok()]
    uf = [P.sb([128, 2, C], F32) for _ in range(2)]; t_uf = [Tok(), Tok()]
    yv = [P.sb([128, C], F32) for _ in range(2)]; t_yv = [Tok(), Tok()]
    y2 = [P.sb([128, C], F32) for _ in range(2)]; t_y2 = [Tok(), Tok()]
    ygl = [P.sb([128, 2, C], BF16) for _ in range(2)]; t_yg = [[Tok(), Tok()] for _ in range(2)]
    pZ = [banks[4 + i][:, 0:C] for i in range(2)]; t_pZ = [Tok(), Tok()]
    sg = [P.sb([128, C], F32) for _ in range(2)]; t_sg = [Tok(), Tok()]
    oo = [P.sb([128, C], BF16) for _ in range(2)]; t_oo = [Tok(), Tok()]
    t_out = Tok()
    iy = [0]
    items = []
    for d in (1, 0):
        for ci in range(NCH):
            for ut in range(2):
                for jj in range(4):
                    items.append((d, ci, ut, jj))

    def geom(d, ci):
        if d == 1:
            lo, hi = T - (ci + 1) * C, T - ci * C
            return lo, hi, rsl(lo, hi)
        lo, hi = ci * C, (ci + 1) * C
        return lo, hi, slice(lo, hi)

    def stA(i):
        d, ci, ut, jj = items[i]
        lo, hi, tsl = geom(d, ci)
        cmb = d * 8 + ut * 4 + jj
        b = i % NB
        pb = i % 2
        if d == 0 and ut == 0 and jj == 0:
            ufb = ci % 2
            S.dma("sp", uf[ufb][:], us5.rearrange("(u p) t -> p u t", p=128)[:, :, lo:hi], writes=[t_uf[ufb]])
        for ri in range(2):
            S.op("pe", lambda E, ri=ri: E.matmul(pBU[pb][:, ri, :], lhsT=BT[:, cmb, ri, :], rhs=ub[:, ut, tsl],
                                                 start=True, stop=True), reads=[t_s, t_ub], writes=[t_pBU[pb]])
        cs_ = cosT[:, cmb, :]
        sn_ = sinT[:, cmb, :]
        for k, (src, tab) in enumerate(((0, cs_), (1, sn_), (1, cs_), (0, sn_))):
            S.op("dve", lambda E, k=k, src=src, tab=tab: E.tensor_tensor(out=m1[b][:, k, :], in0=pBU[pb][:, src, :],
                                                                         in1=tab, op=ALU.mult),
                 reads=[t_pBU[pb], t_s], writes=[t_m1[b]])
        S.op("pool", lambda E: E.tensor_tensor(out=gin[b][:, 0, :], in0=m1[b][:, 0, :], in1=m1[b][:, 1, :], op=ALU.add),
             reads=[t_m1[b]], writes=[t_gin[b]])
        S.op("dve", lambda E: E.tensor_tensor(out=gin[b][:, 1, :], in0=m1[b][:, 2, :], in1=m1[b][:, 3, :],
                                              op=ALU.subtract), reads=[t_m1[b]], writes=[t_gin[b]])

    def stB(i):
        d, ci, ut, jj = items[i]
        cmb = d * 8 + ut * 4 + jj
        b = i % NB
        rho_b = prs[:, nm["rho"], cmb:cmb + 1].to_broadcast([128, C])
        for ri in range(2):
            S.op("dve", lambda E, ri=ri: E.tensor_tensor_scan(
                out=gg[b][:, ri, :], data0=rho_b, data1=gin[b][:, ri, :], initial=gi[:, ri, cmb:cmb + 1],
                op0=ALU.mult, op1=ALU.add), reads=[t_gin[b], t_gi[cmb], t_s], writes=[t_gg[b]])
        wr_ = prs[:, nm["wr"], cmb:cmb + 1]
        wi_ = prs[:, nm["wi"], cmb:cmb + 1]
        nwi_ = prs[:, nm["nwi"], cmb:cmb + 1]
        S.op("act", lambda E: E.activation(out=ctmp[b][:, 0:1], in_=gg[b][:, 1, C - 1:C], func=AF.Copy, scale=nwi_),
             reads=[t_gg[b], t_s], writes=[t_ct[b]])
        S.op("act", lambda E: E.activation(out=ctmp[b][:, 1:2], in_=gg[b][:, 1, C - 1:C], func=AF.Copy, scale=wr_),
             reads=[t_gg[b], t_s], writes=[t_ct[b]])
        S.op("act", lambda E: E.activation(out=gi[:, 0, cmb:cmb + 1], in_=gg[b][:, 0, C - 1:C], func=AF.Identity,
                                           scale=wr_, bias=ctmp[b][:, 0:1]),
             reads=[t_gg[b], t_ct[b], t_s], writes=[t_gi[cmb]])
        S.op("act", lambda E: E.activation(out=gi[:, 1, cmb:cmb + 1], in_=gg[b][:, 0, C - 1:C], func=AF.Identity,
                                           scale=wi_, bias=ctmp[b][:, 1:2]),
             reads=[t_gg[b], t_ct[b], t_s], writes=[t_gi[cmb]])

    def stC(i):
        d, ci, ut, jj = items[i]
        lo, hi, tsl = geom(d, ci)
        chn = lo // C
        cmb = d * 8 + ut * 4 + jj
        b = i % NB
        hb = (ci * 2 + ut) % 2
        cs_ = cosT[:, cmb, :]
        sn_ = sinT[:, cmb, :]
        S.op("pool", lambda E: E.tensor_tensor(out=m2[b][:, 0, :], in0=gg[b][:, 0, :], in1=cs_, op=ALU.mult),
             reads=[t_gg[b], t_s], writes=[t_m2[b]])
        S.op("pool", lambda E: E.tensor_tensor(out=m2[b][:, 1, :], in0=gg[b][:, 1, :], in1=sn_, op=ALU.mult),
             reads=[t_gg[b], t_s], writes=[t_m2[b]])
        S.op("dve", lambda E: E.tensor_tensor(out=m2[b][:, 2, :], in0=gg[b][:, 1, :], in1=cs_, op=ALU.mult),
             reads=[t_gg[b], t_s], writes=[t_m2[b]])
        S.op("dve", lambda E: E.tensor_tensor(out=m2[b][:, 3, :], in0=gg[b][:, 0, :], in1=sn_, op=ALU.mult),
             reads=[t_gg[b], t_s], writes=[t_m2[b]])
        S.op("pool", lambda E: E.tensor_tensor(out=hh[hb][:, jj, 0, :], in0=m2[b][:, 0, :], in1=m2[b][:, 1, :],
                                               op=ALU.subtract), reads=[t_m2[b]], writes=[t_hh[hb][jj]])
        S.op("pool", lambda E: E.tensor_tensor(out=hh[hb][:, jj, 1, :], in0=m2[b][:, 2, :], in1=m2[b][:, 3, :],
                                               op=ALU.add), reads=[t_m2[b]], writes=[t_hh[hb][jj]])
        if jj != 3:
            return
        yb_ = iy[0] % 2
        iy[0] += 1
        n_mm = 0
        for j4 in range(4):
            cm2 = d * 8 + ut * 4 + j4
            for ri in range(2):
                S.op("pe", lambda E, cm2=cm2, ri=ri, j4=j4, n_mm=n_mm: E.matmul(
                    pY[yb_], lhsT=CT[:, cm2, ri, :], rhs=hh[hb][:, j4, ri, :], start=(n_mm == 0), stop=(n_mm == 7)),
                    reads=[t_s, t_hh[hb][j4]], writes=[t_pY[yb_]])
                n_mm += 1
        if d == 1:
            S.op("act", lambda E: E.activation(out=ybwd[:, ut, tsl], in_=pY[yb_], func=AF.Copy),
                 reads=[t_pY[yb_]], writes=[t_yb[chn]])
            return
        ufb = ci % 2
        gb_ = ci % 2
        S.op("dve", lambda E: E.tensor_tensor(out=yv[yb_][:], in0=pY[yb_], in1=ybwd[:, ut, tsl], op=ALU.add),
             reads=[t_pY[yb_], t_yb[chn]], writes=[t_yv[yb_]])
        S.op("dve", lambda E: E.scalar_tensor_tensor(out=yv[yb_][:], in0=uf[ufb][:, ut, :], scalar=dk[:, ut:ut + 1],
                                                     in1=yv[yb_][:], op0=ALU.mult, op1=ALU.add),
             reads=[t_uf[ufb], t_s, t_yv[yb_]], writes=[t_yv[yb_]])
        S.op("act", lambda E: E.activation(out=y2[yb_][:], in_=yv[yb_][:], func=AF.Square),
             reads=[t_yv[yb_]], writes=[t_y2[yb_]])
        S.op("pool", lambda E: E.tensor_scalar(out=y2[yb_][:], in0=y2[yb_][:], scalar1=0.044715, scalar2=1.0,
                                               op0=ALU.mult, op1=ALU.add), reads=[t_y2[yb_]], writes=[t_y2[yb_]])
        S.op("pool", lambda E: E.tensor_tensor(out=y2[yb_][:], in0=y2[yb_][:], in1=yv[yb_][:], op=ALU.mult),
             reads=[t_y2[yb_], t_yv[yb_]], writes=[t_y2[yb_]])
        S.op("act", lambda E: E.activation(out=y2[yb_][:], in_=y2[yb_][:], func=AF.Sigmoid, scale=1.5957691216057308),
             reads=[t_y2[yb_]], writes=[t_y2[yb_]])
        S.op("dve", lambda E: E.tensor_tensor(out=ygl[gb_][:, ut, :], in0=y2[yb_][:], in1=yv[yb_][:], op=ALU.mult),
             reads=[t_y2[yb_], t_yv[yb_]], writes=[t_yg[gb_][ut]])
        if ut != 1:
            return
        for o in range(2):
            for half in range(2):
                oc = o + 2 * half
                for u2 in range(2):
                    S.op("pe", lambda E, half=half, oc=oc, u2=u2: E.matmul(
                        pZ[half], lhsT=gw[:, u2, oc * 128:(oc + 1) * 128], rhs=ygl[gb_][:, u2, :], start=(u2 == 0),
                        stop=(u2 == 1)), reads=[t_s, t_yg[gb_][u2]], writes=[t_pZ[half]])
            S.op("act", lambda E, o=o: E.activation(out=sg[o][:], in_=pZ[1], func=AF.Sigmoid, bias=gb[:, 2 + o:3 + o]),
                 reads=[t_pZ[1], t_s], writes=[t_sg[o]])
            S.op("dve", lambda E, o=o: E.scalar_tensor_tensor(out=oo[o][:], in0=pZ[0], scalar=gb[:, o:o + 1],
                                                              in1=sg[o][:], op0=ALU.add, op1=ALU.mult),
                 reads=[t_pZ[0], t_sg[o], t_s], writes=[t_oo[o]])
            S.dma("sp", mixT[768 + 128 * o:768 + 128 * o + 128, lo:hi], oo[o][:], reads=[t_oo[o]], writes=[t_out])

    N = len(items)
    for i in range(N + 2):
        if i < N:
            stA(i)
        if 0 <= i - 1 < N:
            stB(i - 1)
        if 0 <= i - 2 < N:
            stC(i - 2)
    P.close()


DEC = 0.6065306597126334
RWKV_DBG = [9]


def phase_rwkv(nc, S, zr, ydir, bon, gfm, prm, T):
    mu_p, mu_n, w0, w2, a0, a2, g2, k_k, k_a, r_k = prm
    P = Phase(nc, S)
    NBLK = T // 512
    ident, identf, t_id = make_ident(nc, S, P)
    t_c = Tok()

    def cst(fn, eng="dve"):
        S.op(eng, fn, reads=[t_c, t_id], writes=[t_c])
    onesb = P.sb([128, 128], F32)
    cst(lambda E: E.memset(onesb[:], 0.0))
    cst(lambda E: E.memset(onesb[0:64, 0:64], 1.0))
    cst(lambda E: E.memset(onesb[64:128, 64:128], 1.0))
    mskU = P.sb([128, 512], F32)
    mskL = P.sb([128, 3, 128], F32)
    cst(lambda E: E.memset(mskU[:], 1.0), "pool")
    cst(lambda E: E.memset(mskL[:], 1.0), "pool")
    for i in range(4):
        cmp_ = ALU.is_gt if i % 2 == 0 else ALU.is_ge
        cst(lambda E, i=i, cmp_=cmp_: E.affine_select(out=mskU[:, 128 * i:128 * i + 128], in_=mskU[:, 128 * i:128 * i + 128],
                                                      pattern=[[1, 128]], base=0, channel_multiplier=-1, compare_op=cmp_,
                                                      fill=0.0), "pool")
    for i in range(3):
        cst(lambda E, i=i: E.affine_select(out=mskL[:, i, :], in_=mskL[:, i, :], pattern=[[-1, 128]], base=0,
                                           channel_multiplier=1, compare_op=ALU.is_gt, fill=0.0), "pool")
    rst = P.sb([128, 512], F32)
    cst(lambda E: E.memset(rst[:], 1.0))
    for q in range(4):
        cst(lambda E, q=q: E.memset(rst[:, 128 * q:128 * q + 1], 0.0))
    mh512 = P.sb([128, 512], F32)
    cst(lambda E: E.memset(mh512[:], -0.5), "pool")
    cmu = P.sb([128, 3, 11], F32)
    S.dma("sp", cmu[:, 1, :], mu_p.rearrange("(c p) -> p c", p=128), writes=[t_c], allow_slow_non_contiguous=True)
    S.dma("sp", cmu[:, 2, :], mu_n.rearrange("(c p) -> p c", p=128), writes=[t_c], allow_slow_non_contiguous=True)
    cst(lambda E: E.tensor_tensor(out=cmu[:, 0, :], in0=cmu[:, 1, :], in1=cmu[:, 2, :], op=ALU.add))
    cst(lambda E: E.tensor_scalar(out=cmu[:, 0, :], in0=cmu[:, 0, :], scalar1=-1.0, scalar2=1.0, op0=ALU.mult, op1=ALU.add))
    w0c = P.sb([128, 2, 3], F32); a0c = P.sb([128, 2, 3], F32)
    for d in range(2):
        S.dma("sp", w0c[:, d, :], w0[d].rearrange("(c p) -> p c", p=128), writes=[t_c], allow_slow_non_contiguous=True)
        S.dma("sp", a0c[:, d, :], a0[d].rearrange("(c p) -> p c", p=128), writes=[t_c], allow_slow_non_contiguous=True)
    kkc = P.sb([128, 3], F32); kac = P.sb([128, 3], F32); omka = P.sb([128, 3], F32); rkc = P.sb([128, 3], F32)
    S.dma("sp", kkc[:], k_k.rearrange("(c p) -> p c", p=128), writes=[t_c], allow_slow_non_contiguous=True)
    S.dma("sp", kac[:], k_a.rearrange("(c p) -> p c", p=128), writes=[t_c], allow_slow_non_contiguous=True)
    S.dma("sp", rkc[:], r_k.rearrange("h k -> (h k)").rearrange("(c p) -> p c", p=128), writes=[t_c],
          allow_slow_non_contiguous=True)
    cst(lambda E: E.tensor_scalar(out=omka[:], in0=kac[:], scalar1=-1.0, scalar2=1.0, op0=ALU.mult, op1=ALU.add))
    w2a2 = P.sb([128, 2, 384], BF16)
    for d in range(2):
        S.dma("pool", w2a2[0:64, d, :], w2[d], writes=[t_c])
        S.dma("pool", w2a2[64:128, d, :], a2[d], writes=[t_c])
    g2b = P.sb([128, 384], BF16)
    S.dma("pool", g2b[:], g2, writes=[t_c])

    banks = [P.ps([128, 512], F32) for _ in range(6)]
    t_bk = [Tok() for _ in range(6)]
    bkrr = [0]

    def getbank():
        i = bkrr[0] % 6
        bkrr[0] += 1
        return banks[i], t_bk[i]
    pTr = [P.ps([128, 4, 128], BF16) for _ in range(2)]; t_pTr = [Tok(), Tok()]
    trr = [0]

    NZS = 6
    zraw = P.sb([128, NZS, 514], F32); t_zraw = [Tok() for _ in range(NZS)]
    zsi = [0]
    zm = P.sb([128, 11, 512], F32); t_zm = [Tok() for _ in range(11)]
    tmp0 = [P.sb([128, 512], F32) for _ in range(4)]; t_tmp0 = [Tok() for _ in range(4)]
    tmp1 = [P.sb([128, 512], F32) for _ in range(4)]; t_tmp1 = [Tok() for _ in range(4)]
    tz = P.sb([128, 512], BF16); t_tz = Tok()
    sgz = P.sb([128, 512], BF16); t_sgz = Tok()
    B1 = [P.sb([128, 512], F32) for _ in range(3)]; tB1 = [Tok() for _ in range(3)]
    B2 = [P.sb([128, 512], F32) for _ in range(3)]; tB2 = [Tok() for _ in range(3)]
    B3 = [P.sb([128, 512], F32) for _ in range(3)]; tB3 = [Tok() for _ in range(3)]
    B4 = [P.sb([128, 512], F32) for _ in range(3)]; tB4 = [Tok() for _ in range(3)]
    B5 = [P.sb([128, 512], F32) for _ in range(3)]; tB5 = [Tok() for _ in range(3)]
    B6 = [P.sb([128, 512], F32) for _ in range(3)]; tB6 = [Tok() for _ in range(3)]
    B7 = [P.sb([128, 512], F32) for _ in range(3)]; tB7 = [Tok() for _ in range(3)]
    ARb = [P.sb([128, 4, 2, 128], BF16) for _ in range(3)]; t_AR = [Tok() for _ in range(3)]
    Bt = [P.sb([128, 512], BF16) for _ in range(3)]; t_Bt = [Tok() for _ in range(3)]
    Kt = [P.sb([128, 512], BF16) for _ in range(3)]; t_Kt = [Tok() for _ in range(3)]
    Bb = [P.sb([128, 512], BF16) for _ in range(3)]; t_Bb = [Tok() for _ in range(3)]
    Kb = [P.sb([128, 512], BF16) for _ in range(3)]; t_Kb = [Tok() for _ in range(3)]
    vb = [P.sb([128, 512], BF16) for _ in range(3)]; t_vb = [Tok() for _ in range(3)]
    WCt = P.sb([128, 3, 4], F32); t_WC = Tok()
    tm = [[P.sb([128, 4, 128], BF16) for _ in range(4)] for _ in range(3)]
    t_tm = [[Tok() for _ in range(4)] for _ in range(3)]
    stg = [P.sb([128, 512], F32) for _ in range(3)]; t_stg = [Tok() for _ in range(3)]
    sgi = [0]
    MPs = [[P.sb([128, 512], BF16) for _ in range(6)] for _ in range(2)]
    t_MPs = [[Tok() for _ in range(6)] for _ in range(2)]
    MTs = [[P.sb([128, 3, 128], BF16) for _ in range(2)] for _ in range(2)]
    t_MTs = [[Tok(), Tok()], [Tok(), Tok()]]
    Pm = [[P.sb([128, 3, 128], BF16) for _ in range(2)] for _ in range(2)]; t_Pm = [[Tok(), Tok()], [Tok(), Tok()]]
    PmT = [[P.sb([128, 3, 128], BF16) for _ in range(2)] for _ in range(2)]; t_PmT = [[Tok(), Tok()], [Tok(), Tok()]]
    Tm = [[P.sb([128, 3, 128], BF16) for _ in range(2)] for _ in range(2)]; t_Tm = [[Tok(), Tok()], [Tok(), Tok()]]
    X0 = P.sb([128, 6, 64], BF16); t_X0 = Tok()
    Uv = P.sb([128, 6, 64], F32); t_Uv = Tok()
    Ahb = P.sb([128, 3, 128], BF16); t_Ah = Tok()
    Ub = P.sb([128, 6, 64], BF16); t_Ub = Tok()
    Sf = P.sb([128, 3, 64], F32); t_Sf = Tok()
    Sb = P.sb([128, 3, 64], BF16); t_Sb = Tok()
    ytm = [P.sb([128, 4, 384], F32) for _ in range(2)]; t_ytm = [Tok(), Tok()]
    t_out = Tok()
    zv = zr.rearrange("(c p) t -> p c t", p=128)

    for d in range(2 if RWKV_DBG[0] > -1 else 0):
        S.op("dve", lambda E: E.memset(Sf[:], 0.0), writes=[t_Sf])
        S.op("dve", lambda E: E.memset(Sb[:], 0.0), writes=[t_Sb])
        for bi in range(NBLK):
            if d == 0:
                lo, hi = 512 * bi, 512 * bi + 512
            else:
                lo, hi = T - 512 * (bi + 1), T - 512 * bi
            loc = (lambda ap: ap) if d == 0 else None
            s0 = 1 if lo == 0 else 0
            s1 = 513 if hi == T else 514
            osl = slice(0, 512) if d == 0 else rsl(0, 512)
            for c in range(11):
                zs = zsi[0] % NZS
                zsi[0] += 1
                b = c % 4
                if lo == 0:
                    S.op("pool", lambda E, zs=zs: E.memset(zraw[:, zs, 0:1], 0.0), writes=[t_zraw[zs]])
                if hi == T:
                    S.op("pool", lambda E, zs=zs: E.memset(zraw[:, zs, 513:514], 0.0), writes=[t_zraw[zs]])
                S.dma("sp", zraw[:, zs, s0:s1], zv[:, c, lo - 1 + s0:lo - 1 + s1], writes=[t_zraw[zs]])
                S.op("act", lambda E, c=c, b=b, zs=zs: E.activation(out=tmp0[b][:], in_=zraw[:, zs, 1:513], func=AF.Copy,
                                                                   scale=cmu[:, 0, c:c + 1]),
                     reads=[t_zraw[zs], t_c], writes=[t_tmp0[b]])
                S.op("dve", lambda E, c=c, b=b, zs=zs: E.scalar_tensor_tensor(out=tmp1[b][:], in0=zraw[:, zs, 0:512],
                                                                              scalar=cmu[:, 1, c:c + 1], in1=tmp0[b][:],
                                                                              op0=ALU.mult, op1=ALU.add),
                     reads=[t_zraw[zs], t_c, t_tmp0[b]], writes=[t_tmp1[b]])
                S.op("dve", lambda E, c=c, b=b, zs=zs, osl=osl: E.scalar_tensor_tensor(
                    out=zm[:, c, osl], in0=zraw[:, zs, 2:514], scalar=cmu[:, 2, c:c + 1], in1=tmp1[b][:],
                    op0=ALU.mult, op1=ALU.add),
                    reads=[t_zraw[zs], t_c, t_tmp1[b]], writes=[t_zm[c]])
            if RWKV_DBG[0] == 0:
                continue
            S.op("act", lambda E: E.activation(out=tz[0:64, :], in_=zm[0:64, 9, :], func=AF.Tanh),
                 reads=[t_zm[9]], writes=[t_tz])
            S.op("act", lambda E: E.activation(out=tz[64:128, :], in_=zm[64:128, 9, :], func=AF.Copy),
                 reads=[t_zm[9]], writes=[t_tz])
            if d == 0:
                S.op("act", lambda E: E.activation(out=sgz[:], in_=zm[:, 10, :], func=AF.Sigmoid),
                     reads=[t_zm[10]], writes=[t_sgz])
            v4 = lambda ap: ap.rearrange("p (q t) -> p q t", q=4)
            steps = []
            cb = {}

            def ST(f):
                steps.append(f)
            for c in range(3):
                cb[c] = dict(zr=zm[:, c, :], tr=t_zm[c], zk=zm[:, 3 + c, :], tk=t_zm[3 + c], zv=zm[:, 6 + c, :],
                             tv=t_zm[6 + c])

            def s_mm(c):
                X = cb[c]
                X["pW"], X["tpW"] = getbank()
                S.op("pe", lambda E: E.matmul(X["pW"][:], lhsT=w2a2[0:64, d, 128 * c:128 * c + 128], rhs=tz[0:64, :],
                                              start=True, stop=True), reads=[t_c, t_tz], writes=[X["tpW"]])
                X["pA"], X["tpA"] = getbank()
                S.op("pe", lambda E: E.matmul(X["pA"][:], lhsT=w2a2[64:128, d, 128 * c:128 * c + 128], rhs=tz[64:128, :],
                                              start=True, stop=True), reads=[t_c, t_tz], writes=[X["tpA"]])
            ST(s_mm)

            def s_sig(c):
                X = cb[c]
                S.op("act", lambda E: E.activation(out=B1[c][:], in_=X["pW"][:], func=AF.Sigmoid, bias=w0c[:, d, c:c + 1]),
                     reads=[X["tpW"], t_c], writes=[tB1[c]])
                S.op("act", lambda E: E.activation(out=B6[c][:], in_=X["pA"][:], func=AF.Sigmoid, bias=a0c[:, d, c:c + 1]),
                     reads=[X["tpA"], t_c], writes=[tB6[c]])
            ST(s_sig)

            def s_kkv(c):
                X = cb[c]
                S.op("dve", lambda E: E.tensor_scalar(out=B4[c][:], in0=X["zk"], scalar1=kkc[:, c:c + 1], scalar2=None,
                                                      op0=ALU.mult), reads=[X["tk"], t_c], writes=[tB4[c]])
                S.op("pool", lambda E: E.tensor_tensor(out=B5[c][:], in0=B4[c][:], in1=B4[c][:], op=ALU.mult),
                     reads=[tB4[c]], writes=[tB5[c]])
                X["pN"], X["tpN"] = getbank()
                S.op("pe", lambda E: E.matmul(X["pN"][:], lhsT=onesb[:], rhs=B5[c][:], start=True, stop=True),
                     reads=[t_c, tB5[c]], writes=[X["tpN"]])
            ST(s_kkv)

            def s_cls(c):
                S.op("dve", lambda E: E.tensor_tensor_scan(out=B2[c][:], data0=rst[:], data1=B1[c][:], initial=0.0,
                                                           op0=ALU.mult, op1=ALU.add),
                     reads=[tB1[c], t_c], writes=[tB2[c]])
                S.op("pool", lambda E: E.tensor_tensor(out=B1[c][:], in0=B2[c][:], in1=B1[c][:], op=ALU.subtract),
                     reads=[tB2[c]], writes=[tB1[c]])
            ST(s_cls)

            def s_exp(c):
                S.op("act", lambda E: E.activation(out=B3[c][:], in_=B2[c][:], func=AF.Exp, scale=DEC),
                     reads=[tB2[c]], writes=[tB3[c]])
                S.op("act", lambda E: E.activation(out=B2[c][:], in_=B2[c][:], func=AF.Exp, scale=-DEC),
                     reads=[tB2[c]], writes=[tB2[c]])
                S.op("act", lambda E: E.activation(out=B1[c][:], in_=B1[c][:], func=AF.Exp, scale=-DEC),
                     reads=[tB1[c]], writes=[tB1[c]])
            ST(s_exp)

            def s_rn(c):
                X = cb[c]
                S.op("dve", lambda E: E.tensor_scalar(out=B5[c][:], in0=X["pN"][:], scalar1=1e-12, scalar2=None,
                                                      op0=ALU.add), reads=[X["tpN"]], writes=[tB5[c]])
                S.op("act", lambda E: E.activation(out=B5[c][:], in_=B5[c][:], func=AF.Sqrt),
                     reads=[tB5[c]], writes=[tB5[c]])
                S.op("dve", lambda E: E.reciprocal(out=B5[c][:], in_=B5[c][:]),
                     reads=[tB5[c]], writes=[tB5[c]])
                S.op("dve", lambda E: E.tensor_tensor(out=B4[c][:], in0=B4[c][:], in1=B5[c][:], op=ALU.mult),
                     reads=[tB4[c], tB5[c]], writes=[tB4[c]])
                S.op("dve", lambda E: E.tensor_copy(out=WCt[:, c, :], in_=B2[c][:, 127:512:128]),
                     reads=[tB2[c]], writes=[t_WC])
            ST(s_rn)

            def s_kd(c):
                X = cb[c]
                S.op("dve", lambda E: E.tensor_scalar(out=B7[c][:], in0=B6[c][:], scalar1=kac[:, c:c + 1],
                                                      scalar2=omka[:, c:c + 1], op0=ALU.mult, op1=ALU.add),
                     reads=[tB6[c], t_c], writes=[tB7[c]])
                S.op("dve", lambda E: E.tensor_tensor(out=B7[c][:], in0=B7[c][:], in1=X["zk"], op=ALU.mult),
                     reads=[tB7[c], X["tk"]], writes=[tB7[c]])
                S.op("pool", lambda E: E.tensor_tensor(out=B6[c][:], in0=B4[c][:], in1=B6[c][:], op=ALU.mult),
                     reads=[tB4[c], tB6[c]], writes=[tB6[c]])
            ST(s_kd)

            def s_ar(c):
                X = cb[c]
                S.op("dve", lambda E: E.scalar_tensor_tensor(out=ARb[c][:, :, 0, :], in0=v4(B4[c][:]), scalar=-1.0,
                                                             in1=v4(B1[c][:]), op0=ALU.mult, op1=ALU.mult),
                     reads=[tB4[c], tB1[c]], writes=[t_AR[c]])
                S.op("pool", lambda E: E.tensor_tensor(out=ARb[c][:, :, 1, :], in0=v4(X["zr"]), in1=v4(B2[c][:]),
                                                       op=ALU.mult), reads=[X["tr"], tB2[c]], writes=[t_AR[c]])
                S.op("pool", lambda E: E.tensor_tensor(out=Bt[c][:], in0=B6[c][:], in1=B3[c][:], op=ALU.mult),
                     reads=[tB6[c], tB3[c]], writes=[t_Bt[c]])
                S.op("pool", lambda E: E.tensor_tensor(out=Kt[c][:], in0=B7[c][:], in1=B3[c][:], op=ALU.mult),
                     reads=[tB7[c], tB3[c]], writes=[t_Kt[c]])
                S.op("act", lambda E: E.activation(out=vb[c][:], in_=X["zv"], func=AF.Copy),
                     reads=[X["tv"]], writes=[t_vb[c]])
            ST(s_ar)

            def s_bb(c):
                wcb = WCt[:, c, :].unsqueeze(2).to_broadcast([128, 4, 128])
                S.op("pool", lambda E: E.tensor_tensor(out=v4(Bb[c][:]), in0=v4(Bt[c][:]), in1=wcb, op=ALU.mult),
                     reads=[t_Bt[c], t_WC], writes=[t_Bb[c]])
                S.op("pool", lambda E: E.tensor_tensor(out=v4(Kb[c][:]), in0=v4(Kt[c][:]), in1=wcb, op=ALU.mult),
                     reads=[t_Kt[c], t_WC], writes=[t_Kb[c]])
            ST(s_bb)

            def s_bonus(c):
                X = cb[c]
                S.op("dve", lambda E: E.scalar_tensor_tensor(out=B5[c][:], in0=X["zr"], scalar=rkc[:, c:c + 1],
                                                             in1=B7[c][:], op0=ALU.mult, op1=ALU.mult),
                     reads=[X["tr"], tB7[c], t_c], writes=[tB5[c]])
                pBn, t_pBn = getbank()
                S.op("pe", lambda E: E.matmul(pBn[:], lhsT=onesb[:], rhs=B5[c][:], start=True, stop=True),
                     reads=[t_c, tB5[c]], writes=[t_pBn])
                si = sgi[0] % 3
                sgi[0] += 1
                S.op("dve", lambda E: E.tensor_tensor(out=stg[si][:, osl], in0=pBn[:], in1=X["zv"], op=ALU.mult),
                     reads=[t_pBn, X["tv"]], writes=[t_stg[si]])
                S.dma("sp", bon[d][128 * c:128 * c + 128, lo:hi], stg[si][:], reads=[t_stg[si]], writes=[t_out])
                if d == 0:
                    pG, t_pG = getbank()
                    S.op("pe", lambda E: E.matmul(pG[:], lhsT=g2b[:, 128 * c:128 * c + 128], rhs=sgz[:], start=True,
                                                  stop=True), reads=[t_c, t_sgz], writes=[t_pG])
                    si2 = sgi[0] % 3
                    sgi[0] += 1
                    S.op("act", lambda E: E.activation(out=stg[si2][:], in_=pG[:], func=AF.Copy),
                         reads=[t_pG], writes=[t_stg[si2]])
                    S.dma("sp", gfm[128 * c:128 * c + 128, lo:hi], stg[si2][:], reads=[t_stg[si2]], writes=[t_out])
            ST(s_bonus)

            def s_tr(c):
                for q in range(4):
                    ts_ = trr[0] % 2
                    trr[0] += 1
                    srcs = [(ARb[c][:, q, 0, :], t_AR[c]), (Bb[c][:, 128 * q:128 * q + 128], t_Bb[c]),
                            (Kb[c][:, 128 * q:128 * q + 128], t_Kb[c]), (vb[c][:, 128 * q:128 * q + 128], t_vb[c])]
                    for ai, (src, tk) in enumerate(srcs):
                        S.op("pe", lambda E, ts_=ts_, ai=ai, src=src: E.transpose(out=pTr[ts_][:, ai, :], in_=src,
                                                                                 identity=ident[:]),
                             reads=[tk, t_id], writes=[t_pTr[ts_]])
                    if q % 2 == 0:
                        S.op("act", lambda E, ts_=ts_, q=q: E.activation(out=tm[c][q][:], in_=pTr[ts_][:], func=AF.Copy),
                             reads=[t_pTr[ts_]], writes=[t_tm[c][q]])
                    else:
                        S.op("dve", lambda E, ts_=ts_, q=q: E.tensor_copy(out=tm[c][q][:], in_=pTr[ts_][:]),
                             reads=[t_pTr[ts_]], writes=[t_tm[c][q]])
            ST(s_tr)
            for st_ in steps:
                for c in range(3):
                    st_(c)
            yb = bi % 2
            def chunk_gen(q):
                MP, t_MP, MT, t_MT = MPs[q % 2], t_MPs[q % 2], MTs[q % 2], t_MTs[q % 2]
                qs = slice(128 * q, 128 * q + 128)
                for h in range(6):
                    c, h2 = h // 2, h % 2
                    rows = slice(64 * h2, 64 * h2 + 64)
                    pM, t_pM = getbank()
                    S.op("pe", lambda E, pM=pM, c=c, rows=rows, qs=qs, q=q: E.matmul(
                        pM[:, 0:256], lhsT=Bt[c][rows, qs], rhs=ARb[c][rows, q, :, :].rearrange("p a b -> p (a b)"), start=True, stop=True),
                        reads=[t_Bt[c], t_AR[c]], writes=[t_pM])
                    S.op("pe", lambda E, pM=pM, c=c, rows=rows, qs=qs, q=q: E.matmul(
                        pM[:, 256:512], lhsT=Kt[c][rows, qs], rhs=ARb[c][rows, q, :, :].rearrange("p a b -> p (a b)"), start=True, stop=True),
                        reads=[t_Kt[c], t_AR[c]], writes=[t_pM])
                    S.op("dve", lambda E, pM=pM, h=h: E.tensor_tensor(out=MP[h][:], in0=pM[:], in1=mskU[:], op=ALU.mult),
                         reads=[t_pM, t_c], writes=[t_MP[h]])
                for hg in range(2):
                    pM3, t_pM3 = getbank()
                    for j in range(3):
                        h = 2 * j + hg
                        c, h2 = j, hg
                        rows = slice(64 * h2, 64 * h2 + 64)
                        S.op("pe", lambda E, pM3=pM3, j=j, c=c, rows=rows, qs=qs, q=q: E.matmul(
                            pM3[:, 128 * j:128 * j + 128], lhsT=ARb[c][rows, q, 0, :], rhs=Bt[c][rows, qs],
                            start=True, stop=True), reads=[t_Bt[c], t_AR[c]], writes=[t_pM3])
                    S.op("dve", lambda E, pM3=pM3, hg=hg: E.tensor_tensor(
                        out=MT[hg][:], in0=pM3[:, 0:384].rearrange("p (a b) -> p a b", a=3), in1=mskL[:], op=ALU.mult),
                        reads=[t_pM3, t_c], writes=[t_MT[hg]])
                yield "scores"
                cur = [0, 0]
                for hg in range(2):
                    for j in range(3):
                        h = 2 * j + hg
                        S.op("pool", lambda E, hg=hg, j=j, h=h: E.tensor_tensor(out=Tm[hg][0][:, j, :], in0=MP[h][:, 0:128],
                                                                              in1=identf[:], op=ALU.add),
                             reads=[t_MP[h], t_id], writes=[t_Tm[hg][0]])
                v3 = lambda ap: ap[:, 0:384].rearrange("p (a b) -> p a b", a=3)
                yield "t0"
                for lev in range(1, 7):
                    if lev > 1:
                        yield "lev"
                    bk = {}
                    for hg in range(2):
                        pv = cur[hg]
                        pP, t_pP = getbank()
                        pPT, t_pPT = getbank()
                        bk[hg] = (pP, t_pP, pPT, t_pPT)
                        for j in range(3):
                            h = 2 * j + hg
                            if lev == 1:
                                Pprev, tP = MP[h][:, 0:128], t_MP[h]
                                PTprev, tPT = MT[hg][:, j, :], t_MT[hg]
                            else:
                                Pprev, tP = Pm[hg][pv][:, j, :], t_Pm[hg][pv]
                                PTprev, tPT = PmT[hg][pv][:, j, :], t_PmT[hg][pv]
                            if lev < 6:
                                S.op("pe", lambda E, pP=pP, j=j, Pprev=Pprev, PTprev=PTprev: E.matmul(
                                    pP[:, 128 * j:128 * j + 128], lhsT=PTprev, rhs=Pprev, start=True, stop=True),
                                    reads=[tP, tPT], writes=[t_pP])
                            S.op("pe", lambda E, pPT=pPT, j=j, Pprev=Pprev, PTprev=PTprev: E.matmul(
                                pPT[:, 128 * j:128 * j + 128], lhsT=Pprev, rhs=PTprev, start=True, stop=True),
                                reads=[tP, tPT], writes=[t_pPT])
                    for hg in range(2):
                        pP, t_pP, pPT, t_pPT = bk[hg]
                        nx = 1 - cur[hg]
                        if lev < 6:
                            S.op("act", lambda E, pP=pP, hg=hg, nx=nx: E.activation(out=Pm[hg][nx][:], in_=v3(pP),
                                                                                    func=AF.Copy),
                                 reads=[t_pP], writes=[t_Pm[hg][nx]])
                        S.op("dve", lambda E, pPT=pPT, hg=hg, nx=nx: E.tensor_copy(out=PmT[hg][nx][:], in_=v3(pPT)),
                             reads=[t_pPT], writes=[t_PmT[hg][nx]])
                    bt = {}
                    for hg in range(2):
                        pv = cur[hg]
                        nx = 1 - pv
                        pTT, t_pTT = getbank()
                        bt[hg] = (pTT, t_pTT)
                        for j in range(3):
                            S.op("pe", lambda E, pTT=pTT, j=j, hg=hg, nx=nx, pv=pv: E.matmul(
                                pTT[:, 128 * j:128 * j + 128], lhsT=PmT[hg][nx][:, j, :], rhs=Tm[hg][pv][:, j, :],
                                start=True, stop=True), reads=[t_PmT[hg][nx], t_Tm[hg][pv]], writes=[t_pTT])
                    for hg in range(2):
                        pv = cur[hg]
                        nx = 1 - pv
                        pTT, t_pTT = bt[hg]
                        S.op("dve", lambda E, pTT=pTT, hg=hg, nx=nx, pv=pv: E.tensor_tensor(
                            out=Tm[hg][nx][:], in0=v3(pTT), in1=Tm[hg][pv][:], op=ALU.add),
                            reads=[t_pTT, t_Tm[hg][pv]], writes=[t_Tm[hg][nx]])
                        cur[hg] = nx
                TF = [Tm[0][cur[0]], Tm[1][cur[1]]]
                tTF = [t_Tm[0][cur[0]], t_Tm[1][cur[1]]]
                pX, t_pX = getbank()
                for h in range(6):
                    c, h2 = h // 2, h % 2
                    sl_ = 3 * h2 + c
                    S.op("pe", lambda E, pX=pX, h=h, c=c, h2=h2, q=q, sl_=sl_: E.matmul(
                        pX[:, 64 * sl_:64 * sl_ + 64], lhsT=MP[h][:, 256:384], rhs=tm[c][q][:, 3, 64 * h2:64 * h2 + 64],
                        start=True, stop=True), reads=[t_MP[h], t_tm[c][q]], writes=[t_pX])
                S.op("act", lambda E, pX=pX: E.activation(out=X0[:], in_=pX[:, 0:384].rearrange("p (a b) -> p a b", a=6),
                                                         func=AF.Copy), reads=[t_pX], writes=[t_X0])
                pV, t_pV = getbank()
                for h in range(6):
                    c, h2 = h // 2, h % 2
                    sl_ = 3 * h2 + c
                    S.op("pe", lambda E, pV=pV, c=c, h2=h2, sl_=sl_: E.matmul(
                        pV[:, 64 * sl_:64 * sl_ + 64], lhsT=TF[h2][:, c, :], rhs=X0[:, sl_, :], start=True, stop=True),
                        reads=[tTF[h2], t_X0], writes=[t_pV])
                S.op("act", lambda E, pV=pV: E.activation(out=Uv[:], in_=pV[:, 0:384].rearrange("p (a b) -> p a b", a=6),
                                                         func=AF.Copy), reads=[t_pV], writes=[t_Uv])
                pH, t_pH = getbank()
                for h in range(6):
                    c, h2 = h // 2, h % 2
                    S.op("pe", lambda E, pH=pH, c=c, h2=h2, q=q: E.matmul(
                        pH[64 * h2:64 * h2 + 64, 128 * c:128 * c + 128], lhsT=tm[c][q][:, 0, 64 * h2:64 * h2 + 64],
                        rhs=TF[h2][:, c, :], start=True, stop=True), reads=[tTF[h2], t_tm[c][q]], writes=[t_pH])
                S.op("dve", lambda E, pH=pH: E.tensor_copy(out=Ahb[:], in_=pH[:, 0:384].rearrange("p (a b) -> p a b", a=3)),
                     reads=[t_pH], writes=[t_Ah])
                yield "post"
                pUb = [getbank(), getbank()]
                for h2 in range(2):
                    rows = slice(64 * h2, 64 * h2 + 64)
                    for c in range(3):
                        S.op("pe", lambda E, h2=h2, c=c, rows=rows: E.matmul(
                            pUb[h2][0][:, 64 * c:64 * c + 64], lhsT=Ahb[rows, c, :], rhs=Sb[rows, c, :], start=True,
                            stop=True), reads=[t_Ah, t_Sb], writes=[pUb[h2][1]])
                for h2 in range(2):
                    S.op("dve", lambda E, h2=h2: E.tensor_tensor(
                        out=Ub[:, 3 * h2:3 * h2 + 3, :], in0=pUb[h2][0][:, 0:192].rearrange("p (a b) -> p a b", a=3),
                        in1=Uv[:, 3 * h2:3 * h2 + 3, :], op=ALU.add),
                        reads=[pUb[h2][1], t_Uv], writes=[t_Ub])
                yield "q1"
                pYb = [getbank(), getbank()]
                for h2 in range(2):
                    rows = slice(64 * h2, 64 * h2 + 64)
                    for c in range(3):
                        h = 2 * c + h2
                        sl_ = 3 * h2 + c
                        oc = slice(64 * c, 64 * c + 64)
                        S.op("pe", lambda E, h2=h2, c=c, rows=rows, q=q, oc=oc: E.matmul(
                            pYb[h2][0][:, oc], lhsT=ARb[c][rows, q, 1, :], rhs=Sb[rows, c, :], start=True, stop=False),
                            reads=[t_AR[c], t_Sb], writes=[pYb[h2][1]])
                        S.op("pe", lambda E, h2=h2, h=h, sl_=sl_, oc=oc: E.matmul(
                            pYb[h2][0][:, oc], lhsT=MP[h][:, 128:256], rhs=Ub[:, sl_, :], start=False, stop=False),
                            reads=[t_MP[h], t_Ub], writes=[pYb[h2][1]])
                        S.op("pe", lambda E, h2=h2, h=h, c=c, q=q, oc=oc: E.matmul(
                            pYb[h2][0][:, oc], lhsT=MP[h][:, 384:512], rhs=tm[c][q][:, 3, 64 * h2:64 * h2 + 64],
                            start=False, stop=True), reads=[t_MP[h], t_tm[c][q]], writes=[pYb[h2][1]])
                for h2 in range(2):
                    S.op("act", lambda E, h2=h2, yb=yb, q=q: E.activation(
                        out=ytm[yb][:, q, :].rearrange("p (c g v) -> p g c v", g=2, v=64)[:, h2, :, :],
                        in_=pYb[h2][0][:, 0:192].rearrange("p (a b) -> p a b", a=3), func=AF.Copy),
                        reads=[pYb[h2][1]], writes=[t_ytm[yb]])
                pS_, t_pS = getbank()
                for h in range(6):
                    c, h2 = h // 2, h % 2
                    orow = slice(64 * h2, 64 * h2 + 64)
                    S.op("pe", lambda E, pS_=pS_, h=h, c=c, h2=h2, orow=orow, q=q: E.matmul(
                        pS_[orow, 64 * c:64 * c + 64], lhsT=tm[c][q][:, 1, 64 * h2:64 * h2 + 64], rhs=Ub[:, 3 * h2 + c, :],
                        start=True, stop=False), reads=[t_tm[c][q], t_Ub], writes=[t_pS])
                    S.op("pe", lambda E, pS_=pS_, h=h, c=c, h2=h2, orow=orow, q=q: E.matmul(
                        pS_[orow, 64 * c:64 * c + 64], lhsT=tm[c][q][:, 2, 64 * h2:64 * h2 + 64],
                        rhs=tm[c][q][:, 3, 64 * h2:64 * h2 + 64], start=False, stop=True),
                        reads=[t_tm[c][q]], writes=[t_pS])
                for c in range(3):
                    S.op("dve", lambda E, pS_=pS_, c=c, q=q: E.scalar_tensor_tensor(
                        out=Sf[:, c, :], in0=Sf[:, c, :], scalar=WCt[:, c, q:q + 1], in1=pS_[:, 64 * c:64 * c + 64],
                        op0=ALU.mult, op1=ALU.add), reads=[t_pS, t_WC, t_Sf], writes=[t_Sf])
                S.op("act", lambda E: E.activation(out=Sb[:], in_=Sf[:], func=AF.Copy), reads=[t_Sf], writes=[t_Sb])
            def run_until(g, label):
                for lab in g:
                    if lab == label:
                        return True
                return False
            gens = [chunk_gen(q) for q in range(4)]
            run_until(gens[0], "post")
            for q in range(4):
                g, gn = gens[q], (gens[q + 1] if q + 1 < 4 else None)
                if gn is not None:
                    run_until(gn, "scores")
                run_until(g, "q1")
                if gn is not None:
                    run_until(gn, "t0")
                    run_until(gn, "lev")
                    run_until(gn, "lev")
                run_until(g, "never")
                if gn is not None:
                    run_until(gn, "post")
            lb = 512 * bi
            S.dma("sp", ydir[d][lb:lb + 512, :].rearrange("(q p) f -> p q f", p=128), ytm[yb][:], reads=[t_ytm[yb]],
                  writes=[t_out])
    P.close()


LNX_EPS = 64e-5


def phase_rwkv_combine(nc, S, ydir, bon, gfm, lnx_w, lnx_b, mixT, T):
    P = Phase(nc, S)
    NT = T // 128
    ident, identf, t_id = make_ident(nc, S, P)
    t_c = Tok()
    J = P.sb([128, 128], F32)
    S.op("pool", lambda E: E.memset(J[:], 1.0), writes=[t_c])
    S.op("pool", lambda E: E.affine_select(out=J[:], in_=J[:], pattern=[[1, 128]], base=-127, channel_multiplier=1,
                                           compare_op=ALU.is_equal, fill=0.0), reads=[t_c], writes=[t_c])
    lwt = P.sb([128, 384], F32); lbt = P.sb([128, 384], F32)
    S.dma("sp", lwt[:], lnx_w.partition_broadcast(128), writes=[t_c])
    S.dma("sp", lbt[:], lnx_b.partition_broadcast(128), writes=[t_c])
    mh = P.sb([128, 6], F32)
    S.op("pool", lambda E: E.memset(mh[:], -0.5), writes=[t_c])
    NBF = 3
    y0 = [P.sb([128, 384], F32) for _ in range(NBF)]; t_y0 = [Tok() for _ in range(NBF)]
    y1 = [P.sb([128, 384], F32) for _ in range(NBF)]; t_y1 = [Tok() for _ in range(NBF)]
    b0 = [P.sb([128, 3, 128], F32) for _ in range(NBF)]; t_b0 = [Tok() for _ in range(NBF)]
    b1 = [P.sb([128, 3, 128], F32) for _ in range(NBF)]; t_b1 = [Tok() for _ in range(NBF)]
    gt = [P.sb([128, 3, 128], F32) for _ in range(NBF)]; t_gt = [Tok() for _ in range(NBF)]
    ys = [P.sb([128, 6, 64], F32) for _ in range(NBF)]; t_ys = [Tok() for _ in range(NBF)]
    sq = [P.sb([128, 6, 64], F32) for _ in range(NBF)]; t_sq = [Tok() for _ in range(NBF)]
    st = [P.sb([128, 4, 6], F32) for _ in range(NBF)]; t_st = [Tok() for _ in range(NBF)]
    rs = [P.sb([128, 384], F32) for _ in range(NBF)]; t_rs = [Tok() for _ in range(NBF)]
    ob = [P.sb([128, 3, 128], BF16) for _ in range(NBF)]; t_ob = [Tok() for _ in range(NBF)]
    pJ = [P.ps([128, 512], F32) for _ in range(2)]; t_pJ = [Tok(), Tok()]
    pB = [P.ps([128, 512], F32) for _ in range(2)]; t_pB = [Tok(), Tok()]
    pG = [P.ps([128, 512], F32) for _ in range(2)]; t_pG = [Tok(), Tok()]
    pO = [P.ps([128, 512], F32) for _ in range(2)]; t_pO = [Tok(), Tok()]
    t_out = Tok()
    f3 = lambda ap: ap.rearrange("p (a b) -> p a b", a=6)
    def stA(n):
        b = n % NBF
        pb2 = n % 2
        tl = slice(128 * n, 128 * n + 128)
        S.dma("sp", y0[b][:], ydir[0][128 * n:128 * n + 128, :], writes=[t_y0[b]])
        S.dma("sp", y1[b][:], ydir[1][T - 128 * (n + 1):T - 128 * n, :], writes=[t_y1[b]])
        S.dma("sp", b0[b][:], bon[0][:, tl].rearrange("(c p) t -> p c t", p=128), writes=[t_b0[b]])
        S.dma("sp", b1[b][:], bon[1][:, tl].rearrange("(c p) t -> p c t", p=128), writes=[t_b1[b]])
        S.dma("sp", gt[b][:], gfm[:, tl].rearrange("(c p) t -> p c t", p=128), writes=[t_gt[b]])
        S.op("pe", lambda E, b=b: E.matmul(pJ[pb2][:, 0:384], lhsT=J[:], rhs=y1[b][:], start=True, stop=True),
             reads=[t_c, t_y1[b]], writes=[t_pJ[pb2]])
        S.op("dve", lambda E, b=b: E.tensor_tensor(out=ys[b][:], in0=f3(pJ[pb2][:, 0:384]), in1=f3(y0[b][:]), op=ALU.add),
             reads=[t_pJ[pb2], t_y0[b]], writes=[t_ys[b]])
        S.op("dve", lambda E, b=b: E.tensor_reduce(out=st[b][:, 0, :], in_=ys[b][:], axis=AX.X, op=ALU.add),
             reads=[t_ys[b]], writes=[t_st[b]])
        S.op("dve", lambda E, b=b: E.tensor_scalar(out=st[b][:, 0, :], in0=st[b][:, 0, :], scalar1=1.0 / 64, scalar2=None,
                                                   op0=ALU.mult), reads=[t_st[b]], writes=[t_st[b]])
        S.op("dve", lambda E, b=b: E.tensor_tensor(out=ys[b][:], in0=ys[b][:],
                                                   in1=st[b][:, 0, :].unsqueeze(2).to_broadcast([128, 6, 64]),
                                                   op=ALU.subtract), reads=[t_st[b], t_ys[b]], writes=[t_ys[b]])
        S.op("pool", lambda E, b=b: E.tensor_tensor(out=sq[b][:], in0=ys[b][:], in1=ys[b][:], op=ALU.mult),
             reads=[t_ys[b]], writes=[t_sq[b]])
        S.op("dve", lambda E, b=b: E.tensor_reduce(out=st[b][:, 1, :], in_=sq[b][:], axis=AX.X, op=ALU.add),
             reads=[t_sq[b]], writes=[t_st[b]])
        S.op("dve", lambda E, b=b: E.tensor_scalar(out=st[b][:, 2, :], in0=st[b][:, 1, :], scalar1=1.0 / 64,
                                                   scalar2=LNX_EPS, op0=ALU.mult, op1=ALU.add),
             reads=[t_st[b]], writes=[t_st[b]])
        S.op("pool", lambda E, b=b: E.tensor_tensor(out=st[b][:, 3, :], in0=st[b][:, 2, :], in1=mh[:], op=ALU.pow),
             reads=[t_st[b], t_c], writes=[t_st[b]])
        S.op("dve", lambda E, b=b: E.tensor_tensor(out=ys[b][:], in0=ys[b][:],
                                                   in1=st[b][:, 3, :].unsqueeze(2).to_broadcast([128, 6, 64]),
                                                   op=ALU.mult), reads=[t_st[b], t_ys[b]], writes=[t_ys[b]])
        yf = ys[b][:].rearrange("p a b -> p (a b)")
        S.op("pool", lambda E, b=b, yf=yf: E.tensor_tensor(out=yf, in0=yf, in1=lwt[:], op=ALU.mult),
             reads=[t_ys[b], t_c], writes=[t_ys[b]])
        S.op("pool", lambda E, b=b, yf=yf: E.tensor_tensor(out=yf, in0=yf, in1=lbt[:], op=ALU.add),
             reads=[t_ys[b], t_c], writes=[t_ys[b]])
        S.op("dve", lambda E, b=b: E.tensor_tensor(out=b0[b][:], in0=b0[b][:], in1=b1[b][:], op=ALU.add),
             reads=[t_b0[b], t_b1[b]], writes=[t_b0[b]])
    def stB(n):
        b = n % NBF
        pb2 = n % 2
        tl = slice(128 * n, 128 * n + 128)
        yf = ys[b][:].rearrange("p a b -> p (a b)")
        for c in range(3):
            S.op("pe", lambda E, b=b, c=c: E.transpose(out=pB[pb2][:, 128 * c:128 * c + 128], in_=b0[b][:, c, :],
                                                       identity=identf[:]),
                 reads=[t_b0[b], t_id], writes=[t_pB[pb2]])
        for c in range(3):
            S.op("pe", lambda E, b=b, c=c: E.transpose(out=pG[pb2][:, 128 * c:128 * c + 128], in_=gt[b][:, c, :],
                                                       identity=identf[:]),
                 reads=[t_gt[b], t_id], writes=[t_pG[pb2]])
        S.op("dve", lambda E, b=b, yf=yf: E.tensor_tensor(out=rs[b][:], in0=pB[pb2][:, 0:384], in1=yf, op=ALU.add),
             reads=[t_pB[pb2], t_ys[b]], writes=[t_rs[b]])
        S.op("dve", lambda E, b=b: E.tensor_tensor(out=rs[b][:], in0=pG[pb2][:, 0:384], in1=rs[b][:], op=ALU.mult),
             reads=[t_pG[pb2], t_rs[b]], writes=[t_rs[b]])
        for c in range(3):
            S.op("pe", lambda E, b=b, c=c: E.transpose(out=pO[pb2][:, 128 * c:128 * c + 128],
                                                       in_=rs[b][:, 128 * c:128 * c + 128], identity=identf[:]),
                 reads=[t_rs[b], t_id], writes=[t_pO[pb2]])
        S.op("act", lambda E, b=b: E.activation(out=ob[b][:], in_=pO[pb2][:, 0:384].rearrange("p (a b) -> p a b", a=3),
                                                func=AF.Copy), reads=[t_pO[pb2]], writes=[t_ob[b]])
        S.dma("sp", mixT[0:384, tl].rearrange("(c p) t -> p c t", p=128), ob[b][:], reads=[t_ob[b]], writes=[t_out])
    for n in range(NT + 1):
        if n < NT:
            stA(n)
        if n >= 1:
            stB(n - 1)
    P.close()


PARAM_SHAPES = {
    "ffn1_norm_g": [2, 1024], "ffn1_w_gate": [2, 1024, 2816], "ffn1_w_up": [2, 1024, 2816],
    "ffn1_w_down": [2, 2816, 1024], "mix_norm_g": [2, 1024], "w_in": [2, 1024, 2816], "w_out": [2, 1024, 1024],
    "rwkv_mu_prev": [2, 1408], "rwkv_mu_next": [2, 1408], "rwkv_decay_w0": [2, 2, 384],
    "rwkv_decay_w2": [2, 2, 64, 384], "rwkv_iclr_a0": [2, 2, 384], "rwkv_iclr_a2": [2, 2, 64, 384],
    "rwkv_gate_w2": [2, 128, 384], "rwkv_k_k": [2, 384], "rwkv_k_a": [2, 384], "rwkv_r_k": [2, 6, 64],
    "rwkv_lnx_w": [2, 384], "rwkv_lnx_b": [2, 384], "s5_a_re": [2, 2, 16, 64], "s5_a_im": [2, 2, 16, 64],
    "s5_log_step": [2, 2, 16], "s5_b_re": [2, 16, 64, 16], "s5_b_im": [2, 16, 64, 16],
    "s5_c_re": [2, 2, 16, 16, 64], "s5_c_im": [2, 2, 16, 16, 64], "s5_d": [2, 256], "s5_glu_w": [2, 256, 512],
    "s5_glu_b": [2, 512], "ffn2_norm_g": [2, 1024], "ffn2_w_gate": [2, 1024, 2816], "ffn2_w_up": [2, 1024, 2816],
    "ffn2_w_down": [2, 2816, 1024], "final_norm_g": [1024],
}
DEPTH = 2


def build_program(T, depth=DEPTH):
    nc = bass.Bass("TRN2", target_bir_lowering=False)
    x = nc.dram_tensor("x", [T, D], F32, kind="ExternalInput").ap()
    p = {k: nc.dram_tensor(k, list(s), F32, kind="ExternalInput").ap() for k, s in PARAM_SHAPES.items()}
    out = nc.dram_tensor("out", [T, D], F32, kind="ExternalOutput").ap()
    h = nc.dram_tensor("h_res", [T, D], F32).ap()
    zr = nc.dram_tensor("z_rwkv", [1408, T], F32).ap()
    qk = nc.dram_tensor("z_qk", [768, T], BF16).ap()
    vtm = nc.dram_tensor("z_v", [T, 384], BF16).ap()
    us5 = nc.dram_tensor("z_s5", [256, T], F32).ap()
    mixT = nc.dram_tensor("mixT", [1024, T], BF16).ap()
    ydir = nc.dram_tensor("y_dir", [2, T, 384], F32).ap()
    bon = nc.dram_tensor("bonus", [2, 384, T], F32).ap()
    gfm = nc.dram_tensor("gate", [384, T], F32).ap()
    S = SchedI(nc)
    NT = T // 128
    tk = [Tok() for _ in range(NT)]
    for l in range(depth):
        src = x if l == 0 else h
        tk2 = [Tok() for _ in range(NT)]
        phase_ffn(nc, S, src, h, p["ffn1_norm_g"][l], p["ffn1_w_gate"][l], p["ffn1_w_up"][l], p["ffn1_w_down"][l],
                  tk, tk2, T)
        tk = tk2
        phase_win(nc, S, h, p["mix_norm_g"][l], p["w_in"][l], zr, qk, vtm, us5, tk, T)
        phase_rwkv(nc, S, zr, ydir, bon, gfm,
                   [p["rwkv_mu_prev"][l], p["rwkv_mu_next"][l], p["rwkv_decay_w0"][l], p["rwkv_decay_w2"][l],
                    p["rwkv_iclr_a0"][l], p["rwkv_iclr_a2"][l], p["rwkv_gate_w2"][l], p["rwkv_k_k"][l],
                    p["rwkv_k_a"][l], p["rwkv_r_k"][l]], T)
        phase_rwkv_combine(nc, S, ydir, bon, gfm, p["rwkv_lnx_w"][l], p["rwkv_lnx_b"][l], mixT, T)
        phase_attn(nc, S, qk, vtm, mixT, T)
        phase_s5(nc, S, us5, mixT,
                 [p["s5_a_re"][l], p["s5_a_im"][l], p["s5_log_step"][l], p["s5_b_re"][l], p["s5_b_im"][l],
                  p["s5_c_re"][l], p["s5_c_im"][l], p["s5_d"][l], p["s5_glu_w"][l], p["s5_glu_b"][l]], T)
        tk2 = [Tok() for _ in range(NT)]
        phase_wout(nc, S, h, h, mixT, p["w_out"][l], tk, tk2, T)
        tk = tk2
        tk2 = [Tok() for _ in range(NT)]
        phase_ffn(nc, S, h, h, p["ffn2_norm_g"][l], p["ffn2_w_gate"][l], p["ffn2_w_up"][l], p["ffn2_w_down"][l],
                  tk, tk2, T)
        tk = tk2
    tko = [Tok() for _ in range(NT)]
    phase_final(nc, S, h, out, p["final_norm_g"], tk, tko, T)
    S.finish()
    return nc, S


def kernel(**inputs):
    x = np.ascontiguousarray(np.asarray(inputs["x"], dtype=np.float32))
    B, T, _ = x.shape
    nc, S = build_program(T)
    params = {k: np.ascontiguousarray(np.asarray(inputs[k], dtype=np.float32)) for k in PARAM_SHAPES}
    in_maps = []
    for b in range(B):
        m = {"x": x[b]}
        m.update(params)
        in_maps.append(m)
    res = run_bass_kernel_spmd(nc, in_maps, core_ids=list(range(B)))
    return np.stack([np.asarray(r["out"], dtype=np.float32) for r in res.results], axis=0)
```
